# Optimizing a Trainium2 kernel written in Bass

```python
import math
import jax, jax.numpy as jnp
from jax import lax
import numpy as np

D_MODEL = 1024
BATCH = 2
SEQ = 8192
DEPTH = 2
DEC_BATCH = 128
DEC_SEQ = 8
PAST_LEN = 8192
PAGE_SIZE = 128

HEAD_DIM = 64
ATTN_HEADS = D_MODEL // HEAD_DIM
ATTN_KV_HEADS = ATTN_HEADS // 4
WINDOW = 128
D_RNN = D_MODEL
RNN_BLOCKS = 16
RNN_BLOCK = D_RNN // RNN_BLOCKS
CONV_W = 4
RG_C = 8.0
RWKV_HEADS = D_MODEL // HEAD_DIM
RW_D = RWKV_HEADS * HEAD_DIM
LORA_W = 64
LORA_A = 64
LORA_G = 160
GN_EPS = 64e-5
LN_EPS = 1e-5
_FF_RAW = (8 * D_MODEL + 2) // 3
D_FF = ((_FF_RAW + 255) // 256) * 256
ALPHA = (2 * DEPTH) ** 0.25
BETA = (8 * DEPTH) ** -0.25

Q_W = ATTN_HEADS * HEAD_DIM
KV_W = ATTN_KV_HEADS * HEAD_DIM
OFF_Q = 0
OFF_K = OFF_Q + Q_W
OFF_V = OFF_K + KV_W
OFF_RNN = OFF_V + KV_W
OFF_RW = OFF_RNN + D_RNN
SHIFT_W = 3 * RW_D + LORA_W + LORA_A + LORA_G
OFF_GATE = OFF_RW + SHIFT_W
N_BRANCH = 3
IN_COLS = OFF_GATE + N_BRANCH * D_MODEL

kernel_name = 'hybrid_swa_rglru_rwkv7_step'


def layer_norm(x, g, b):
    xf = x.astype(jnp.float32)
    mu = xf.mean(-1, keepdims=True)
    var = jnp.square(xf - mu).mean(-1, keepdims=True)
    return ((xf - mu) * lax.rsqrt(var + LN_EPS) * g.astype(jnp.float32) + b.astype(jnp.float32)).astype(x.dtype)


def alibi_slopes():
    h = jnp.arange(1, ATTN_HEADS + 1, dtype=jnp.float32)
    return 2.0 ** (-8.0 * h / ATTN_HEADS)


def sink_attend(q, k, v, sinks, rel, valid):
    n, tq, h, d = q.shape
    kvh = k.shape[2]
    g = h // kvh
    qg = q.reshape(n, tq, kvh, g, d)
    s = jnp.einsum('nqcgd,nscd->ncgqs', qg, k).astype(jnp.float32) * (d ** -0.5)
    s = s - alibi_slopes().reshape(kvh, g, 1, 1) * rel
    s = jnp.where(valid, s, -jnp.inf)
    sink = sinks.astype(jnp.float32).reshape(1, kvh, g, 1, 1)
    m = jnp.maximum(s.max(-1, keepdims=True), sink)
    e = jnp.exp(s - m)
    den = e.sum(-1, keepdims=True) + jnp.exp(sink - m)
    p = (e / den).astype(v.dtype)
    o = jnp.einsum('ncgqs,nscd->nqcgd', p, v)
    return o.reshape(n, tq, h * d)


def attn_prompt(q, k, v, sinks):
    b, t, h, d = q.shape
    w = WINDOW
    nb = t // w
    kvh = k.shape[2]
    qb = q.reshape(b * nb, w, h, d)
    kb = k.reshape(b, nb, w, kvh, d)
    vb = v.reshape(b, nb, w, kvh, d)
    kc = jnp.concatenate([jnp.concatenate([jnp.zeros_like(kb[:, :1]), kb[:, :-1]], axis=1), kb], axis=2)
    vc = jnp.concatenate([jnp.concatenate([jnp.zeros_like(vb[:, :1]), vb[:, :-1]], axis=1), vb], axis=2)
    kc = kc.reshape(b * nb, 2 * w, kvh, d)
    vc = vc.reshape(b * nb, 2 * w, kvh, d)
    tpos = w + jnp.arange(w)
    spos = jnp.arange(2 * w)
    rel_i = tpos[:, None] - spos[None, :]
    band = (rel_i >= 0) & (rel_i < WINDOW)
    has_prev = (jnp.arange(nb) > 0)[:, None, None] | (spos >= w)[None, None, :]
    valid = band[None] & has_prev
    valid = jnp.tile(valid, (b, 1, 1)).reshape(b * nb, 1, 1, w, 2 * w)
    o = sink_attend(qb, kc, vc, sinks, rel_i.astype(jnp.float32), valid)
    return o.reshape(b, t, h * d)


def attn_sample(q, k, v, sinks, win_k, win_v):
    t = q.shape[1]
    wb = win_k.shape[1]
    kc = jnp.concatenate([win_k.astype(k.dtype), k], axis=1)
    vc = jnp.concatenate([win_v.astype(v.dtype), v], axis=1)
    rel_i = (wb + jnp.arange(t))[:, None] - jnp.arange(wb + t)[None, :]
    valid = ((rel_i >= 0) & (rel_i < WINDOW))[None, None, None]
    o = sink_attend(q, kc, vc, sinks, rel_i.astype(jnp.float32), valid)
    return o, kc[:, -wb:], vc[:, -wb:]


def causal_conv(x, buf, w, b):
    t = x.shape[1]
    xp = jnp.concatenate([buf.astype(x.dtype), x], axis=1)
    y = b + sum(w[i] * xp[:, i:i + t] for i in range(CONV_W))
    return y, xp[:, -(CONV_W - 1):]


def rglru(x, conv_buf, h0, conv_w, conv_b, wa, ba, wx, bx, lam):
    xc, new_buf = causal_conv(x, conv_buf, conv_w, conv_b)
    n, t, _ = xc.shape
    xb = xc.reshape(n, t, RNN_BLOCKS, RNN_BLOCK)
    r = jax.nn.sigmoid(jnp.einsum('ntbi,bij->ntbj', xb, wa).reshape(n, t, D_RNN) + ba)
    i = jax.nn.sigmoid(jnp.einsum('ntbi,bij->ntbj', xb, wx).reshape(n, t, D_RNN) + bx)
    log_a = -RG_C * r.astype(jnp.float32) * jax.nn.softplus(-lam.astype(jnp.float32))
    a = jnp.exp(log_a)
    u = jnp.sqrt(-jnp.expm1(2.0 * log_a)) * (i * xc).astype(jnp.float32)
    u = u.at[:, 0].add(a[:, 0] * h0.astype(jnp.float32))

    def combine(c1, c2):
        a1, b1 = c1
        a2, b2 = c2
        return a1 * a2, a2 * b1 + b2

    _, h = lax.associative_scan(combine, (a, u), axis=1)
    return h.astype(x.dtype), new_buf, h[:, -1].astype(h0.dtype)


def wkv_scan(r, w, k, v, kk, a, s0):
    xs = tuple(jnp.moveaxis(z.astype(jnp.float32), 1, 0) for z in (r, w, k, v, kk, a))

    def step(s, inp):
        rt, wt, kt, vt, kkt, at = inp
        sa = jnp.einsum('nhvk,nhk->nhv', s, -kkt)
        s = s * wt[:, :, None, :] + sa[..., None] * (kkt * at)[:, :, None, :] + vt[..., None] * kt[:, :, None, :]
        return s, jnp.einsum('nhvk,nhk->nhv', s, rt)

    s_fin, y = lax.scan(step, s0.astype(jnp.float32), xs)
    return jnp.moveaxis(y, 0, 1), s_fin.astype(s0.dtype)


def rwkv_tmix(pc, shift0, s0, p):
    n, t, _ = pc.shape
    prev = jnp.concatenate([shift0[:, None].astype(pc.dtype), pc[:, :-1]], axis=1)
    ps = pc + p['rw_mu'] * (prev - pc)
    new_shift = pc[:, -1]
    o1 = 3 * RW_D
    r = ps[..., :RW_D]
    k = ps[..., RW_D:2 * RW_D]
    v = ps[..., 2 * RW_D:o1]
    lw = ps[..., o1:o1 + LORA_W]
    la = ps[..., o1 + LORA_W:o1 + LORA_W + LORA_A]
    lg = ps[..., o1 + LORA_W + LORA_A:]
    wlog = -jax.nn.softplus(-(p['rw_w0'] + jnp.tanh(lw) @ p['rw_wup'])) - 0.5
    decay = jnp.exp(-jnp.exp(wlog.astype(jnp.float32)))
    a = jax.nn.sigmoid(p['rw_a0'] + la @ p['rw_aup'])
    g = jax.nn.sigmoid(lg) @ p['rw_gup']

    def hs(z):
        return z.reshape(n, t, RWKV_HEADS, HEAD_DIM)

    kk = hs(k * p['rw_kk']).astype(jnp.float32)
    kk = kk / jnp.maximum(jnp.sqrt(jnp.sum(kk * kk, -1, keepdims=True)), 1e-12)
    k = k * (1.0 + (a - 1.0) * p['rw_ka'])
    y, s_new = wkv_scan(hs(r), hs(decay), hs(k), hs(v), kk, hs(a), s0)
    mu = y.mean(-1, keepdims=True)
    var = jnp.square(y - mu).mean(-1, keepdims=True)
    y = ((y - mu) * lax.rsqrt(var + GN_EPS)).reshape(n, t, RW_D)
    y = y * p['rw_lnw'].astype(jnp.float32) + p['rw_lnb'].astype(jnp.float32)
    bonus = (jnp.sum(hs(r).astype(jnp.float32) * hs(k).astype(jnp.float32) * p['rw_rk'].astype(jnp.float32), -1, keepdims=True)
             * hs(v).astype(jnp.float32)).reshape(n, t, RW_D)
    o = ((y + bonus) * g.astype(jnp.float32)).astype(pc.dtype)
    return o, new_shift, s_new


def hybrid_layer(x, win_k, win_v, conv_buf, h0, shift0, s0, win_rows, p):
    n, t, _ = x.shape
    proj = x @ p['w_in']
    q = proj[..., OFF_Q:OFF_K].reshape(n, t, ATTN_HEADS, HEAD_DIM)
    k = proj[..., OFF_K:OFF_V].reshape(n, t, ATTN_KV_HEADS, HEAD_DIM)
    v = proj[..., OFF_V:OFF_RNN].reshape(n, t, ATTN_KV_HEADS, HEAD_DIM)
    if win_k is None:
        o_a = attn_prompt(q, k, v, p['attn_sinks'])
        new_k, new_v = k[:, -win_rows:], v[:, -win_rows:]
    else:
        o_a, new_k, new_v = attn_sample(q, k, v, p['attn_sinks'], win_k, win_v)
    o_b, new_conv, new_h = rglru(proj[..., OFF_RNN:OFF_RW], conv_buf, h0, p['conv_w'], p['conv_b'],
                                 p['rg_wa'], p['rg_ba'], p['rg_wx'], p['rg_bx'], p['rg_lambda'])
    o_c, new_shift, new_s = rwkv_tmix(proj[..., OFF_RW:OFF_GATE], shift0, s0, p)
    gates = jax.nn.sigmoid(proj[..., OFF_GATE:]).reshape(n, t, N_BRANCH, D_MODEL)
    mixed = gates[..., 0, :] * o_a + gates[..., 1, :] * o_b + gates[..., 2, :] * o_c
    x = layer_norm(ALPHA * x + mixed @ p['w_out'], p['ln1_g'], p['ln1_b'])
    gu = x @ p['w_gu']
    ff = (jax.nn.silu(gu[..., :D_FF]) * gu[..., D_FF:]) @ p['w_down']
    x = layer_norm(ALPHA * x + ff, p['ln2_g'], p['ln2_b'])
    return x, (new_k, new_v, new_conv, new_h, new_shift, new_s)


def _nrm(key, shape, scale):
    return scale * jax.random.normal(key, shape, jnp.float32)


def setup_inputs(seed: int = 0) -> dict:
    key = jax.random.key(seed)
    ks = jax.random.split(key, 40)
    win_buf = min(WINDOW, PAST_LEN)
    col_scale = jnp.ones((IN_COLS,), jnp.float32).at[OFF_V:OFF_RNN].set(BETA).at[OFF_RW + 2 * RW_D:OFF_RW + 3 * RW_D].set(BETA)
    a_init = jax.random.uniform(ks[16], (DEPTH, D_RNN), jnp.float32, 0.9, 0.999)
    sig_l = a_init ** (1.0 / RG_C)
    ramp = -6.5 + 5.0 * jnp.linspace(0.0, 1.0, RW_D) ** 0.85
    return {
        'x_prompt': _nrm(ks[0], (BATCH, SEQ, D_MODEL), 1.0),
        'x_sample': _nrm(ks[1], (DEC_BATCH, DEC_SEQ, D_MODEL), 1.0),
        'cache_k': _nrm(ks[2], (DEPTH, DEC_BATCH, win_buf, ATTN_KV_HEADS, HEAD_DIM), 1.0),
        'cache_v': _nrm(ks[3], (DEPTH, DEC_BATCH, win_buf, ATTN_KV_HEADS, HEAD_DIM), BETA),
        'state_conv': _nrm(ks[4], (DEPTH, DEC_BATCH, CONV_W - 1, D_RNN), 1.0),
        'state_rglru': _nrm(ks[5], (DEPTH, DEC_BATCH, D_RNN), 0.5),
        'state_shift': _nrm(ks[6], (DEPTH, DEC_BATCH, SHIFT_W), 1.0),
        'state_wkv': _nrm(ks[7], (DEPTH, DEC_BATCH, RWKV_HEADS, HEAD_DIM, HEAD_DIM), 0.3),
        'w_in': _nrm(ks[8], (DEPTH, D_MODEL, IN_COLS), D_MODEL ** -0.5) * col_scale,
        'attn_sinks': _nrm(ks[9], (DEPTH, ATTN_HEADS), 0.5),
        'conv_w': _nrm(ks[10], (DEPTH, CONV_W, D_RNN), CONV_W ** -0.5),
        'conv_b': _nrm(ks[11], (DEPTH, D_RNN), 0.02),
        'rg_wa': _nrm(ks[12], (DEPTH, RNN_BLOCKS, RNN_BLOCK, RNN_BLOCK), RNN_BLOCK ** -0.5),
        'rg_ba': _nrm(ks[13], (DEPTH, D_RNN), 0.02),
        'rg_wx': _nrm(ks[14], (DEPTH, RNN_BLOCKS, RNN_BLOCK, RNN_BLOCK), RNN_BLOCK ** -0.5),
        'rg_bx': _nrm(ks[15], (DEPTH, D_RNN), 0.02),
        'rg_lambda': jnp.log(sig_l) - jnp.log1p(-sig_l),
        'rw_mu': jax.random.uniform(ks[17], (DEPTH, SHIFT_W), jnp.float32),
        'rw_w0': ramp[None, :] + _nrm(ks[18], (DEPTH, RW_D), 0.1),
        'rw_wup': _nrm(ks[19], (DEPTH, LORA_W, RW_D), 0.1 * LORA_W ** -0.5),
        'rw_a0': _nrm(ks[20], (DEPTH, RW_D), 0.1),
        'rw_aup': _nrm(ks[21], (DEPTH, LORA_A, RW_D), 0.5 * LORA_A ** -0.5),
        'rw_gup': _nrm(ks[22], (DEPTH, LORA_G, RW_D), LORA_G ** -0.5),
        'rw_kk': 0.85 + _nrm(ks[23], (DEPTH, RW_D), 0.05),
        'rw_ka': 1.0 + _nrm(ks[24], (DEPTH, RW_D), 0.05),
        'rw_rk': _nrm(ks[25], (DEPTH, RWKV_HEADS, HEAD_DIM), 0.1),
        'rw_lnw': 1.0 + _nrm(ks[26], (DEPTH, RW_D), 0.05),
        'rw_lnb': _nrm(ks[27], (DEPTH, RW_D), 0.02),
        'w_out': _nrm(ks[28], (DEPTH, D_MODEL, D_MODEL), BETA * D_MODEL ** -0.5),
        'ln1_g': 1.0 + _nrm(ks[29], (DEPTH, D_MODEL), 0.05),
        'ln1_b': _nrm(ks[30], (DEPTH, D_MODEL), 0.02),
        'w_gu': _nrm(ks[31], (DEPTH, D_MODEL, 2 * D_FF), D_MODEL ** -0.5),
        'w_down': _nrm(ks[32], (DEPTH, D_FF, D_MODEL), BETA * D_FF ** -0.5),
        'ln2_g': 1.0 + _nrm(ks[33], (DEPTH, D_MODEL), 0.05),
        'ln2_b': _nrm(ks[34], (DEPTH, D_MODEL), 0.02),
    }


def reference(x_prompt, x_sample, cache_k, cache_v, state_conv, state_rglru, state_shift, state_wkv,
              w_in, attn_sinks, conv_w, conv_b, rg_wa, rg_ba, rg_wx, rg_bx, rg_lambda,
              rw_mu, rw_w0, rw_wup, rw_a0, rw_aup, rw_gup, rw_kk, rw_ka, rw_rk, rw_lnw, rw_lnb,
              w_out, ln1_g, ln1_b, w_gu, w_down, ln2_g, ln2_b):
    win_rows = cache_k.shape[2]
    bp = x_prompt.shape[0]
    dt = x_prompt.dtype
    yp = x_prompt
    ys = x_sample
    st_p = []
    st_s = []
    for l in range(DEPTH):
        p = dict(w_in=w_in[l], attn_sinks=attn_sinks[l], conv_w=conv_w[l], conv_b=conv_b[l],
                 rg_wa=rg_wa[l], rg_ba=rg_ba[l], rg_wx=rg_wx[l], rg_bx=rg_bx[l], rg_lambda=rg_lambda[l],
                 rw_mu=rw_mu[l], rw_w0=rw_w0[l], rw_wup=rw_wup[l], rw_a0=rw_a0[l], rw_aup=rw_aup[l],
                 rw_gup=rw_gup[l], rw_kk=rw_kk[l], rw_ka=rw_ka[l], rw_rk=rw_rk[l], rw_lnw=rw_lnw[l],
                 rw_lnb=rw_lnb[l], w_out=w_out[l], ln1_g=ln1_g[l], ln1_b=ln1_b[l], w_gu=w_gu[l],
                 w_down=w_down[l], ln2_g=ln2_g[l], ln2_b=ln2_b[l])
        yp, sp = hybrid_layer(yp, None, None,
                              jnp.zeros((bp, CONV_W - 1, D_RNN), dt), jnp.zeros((bp, D_RNN), dt),
                              jnp.zeros((bp, SHIFT_W), dt),
                              jnp.zeros((bp, RWKV_HEADS, HEAD_DIM, HEAD_DIM), jnp.float32), win_rows, p)
        ys, ss = hybrid_layer(ys, cache_k[l], cache_v[l], state_conv[l], state_rglru[l], state_shift[l],
                              state_wkv[l], win_rows, p)
        st_p.append(sp)
        st_s.append(ss)
    p_k = jnp.stack([s[0] for s in st_p])
    p_v = jnp.stack([s[1] for s in st_p])
    p_conv = jnp.stack([s[2] for s in st_p])
    p_h = jnp.stack([s[3] for s in st_p])
    p_shift = jnp.stack([s[4] for s in st_p])
    p_wkv = jnp.stack([s[5] for s in st_p])
    s_k = jnp.stack([s[0] for s in st_s])
    s_v = jnp.stack([s[1] for s in st_s])
    s_conv = jnp.stack([s[2] for s in st_s])
    s_h = jnp.stack([s[3] for s in st_s])
    s_shift = jnp.stack([s[4] for s in st_s])
    s_wkv = jnp.stack([s[5] for s in st_s])
    return (yp, ys, p_k, p_v, p_conv, p_h, p_shift, p_wkv, s_k, s_v, s_conv, s_h, s_shift, s_wkv)
```

```python
import math
import numpy as np
from contextlib import ExitStack
import concourse.bass as bass
import concourse.mybir as mybir
from concourse.bass_utils import run_bass_kernel_spmd

F32 = mybir.dt.float32
BF16 = mybir.dt.bfloat16
ALU = mybir.AluOpType
AF = mybir.ActivationFunctionType
AX = mybir.AxisListType

D = 1024
NH = 16
HD = 64
KVH = 4
DFF = 2816
INC = 8992
OFF_K, OFF_V, OFF_RNN, OFF_RW, OFF_GATE = 1024, 1280, 1536, 2560, 5920
SHW = 3360
DEPTH = 2
ALPHA = (2 * DEPTH) ** 0.25
LN_EPS = 1e-5
GN_EPS = 64e-5
NEG = -30000.0
NCORES = 8
NSEQ_S = 16
T_S = 8


class Sched:
    def __init__(self, nc, es, n_dma_sems=32):
        self.nc = nc
        self.es = es
        self.engs = {'pe': nc.tensor, 'dve': nc.vector, 'act': nc.scalar, 'pool': nc.gpsimd, 'sp': nc.sync}
        self.csem = {e: es.enter_context(nc.semaphore("cs_" + e)) for e in ('pe', 'dve', 'act', 'pool')}
        self.ccnt = {e: 0 for e in self.csem}
        self.dsem = {q: [es.enter_context(nc.semaphore("ds%s%d" % (q, i))) for i in range(n)]
                     for q, n in (('sp', 16), ('act', 12), ('pool', 4))}
        self.dcnt = {q: [0] * len(v) for q, v in self.dsem.items()}
        self.dnext = {q: 0 for q in self.dsem}
        self.known = {e: {} for e in self.engs}
        self.lastw = {}
        self.reads = {}
        self.ninst = 0
        self.psn = 0

    def _wait(self, e, tok):
        if tok is None:
            return
        sem, val, owner = tok
        if owner == 'pe' and e == 'pe':
            return
        if self.known[e].get(sem.name, 0) >= val:
            return
        self.engs[e].wait_ge(sem, val)
        self.known[e][sem.name] = val
        self.ninst += 1

    def _deps(self, e, reads, writes):
        for k in reads:
            self._wait(e, self.lastw.get(k))
        for k in writes:
            self._wait(e, self.lastw.get(k))
            for t in self.reads.get(k, ()):
                self._wait(e, t)

    def _commit(self, tok, reads, writes):
        for k in writes:
            self.lastw[k] = tok
            self.reads[k] = []
        for k in reads:
            if k in writes:
                continue
            lst = self.reads.setdefault(k, [])
            lst[:] = [t for t in lst if t[0].name != tok[0].name]
            lst.append(tok)

    def op(self, e, fn, reads=(), writes=()):
        self._deps(e, reads, writes)
        ins = fn(self.engs[e])
        self.ccnt[e] += 1
        ins.then_inc(self.csem[e], 1)
        tok = (self.csem[e], self.ccnt[e], e)
        self._commit(tok, reads, writes)
        self.ninst += 1
        return tok

    def dma(self, e, out, in_, reads=(), writes=(), **kw):
        i = self.dnext[e]
        self.dnext[e] = (i + 1) % len(self.dsem[e])
        if self.dcnt[e][i] > 0:
            self._wait(e, (self.dsem[e][i], self.dcnt[e][i], 'dma'))
        self._deps(e, reads, writes)
        ins = self.engs[e].dma_start(out=out, in_=in_, **kw)
        self.dcnt[e][i] += 16
        ins.then_inc(self.dsem[e][i], 16)
        tok = (self.dsem[e][i], self.dcnt[e][i], 'dma')
        self._commit(tok, reads, writes)
        self.ninst += 1
        return tok

    def release(self, names, engines=('pe', 'dve', 'act', 'pool', 'sp')):
        names = set(names)
        keys = [k for k in list(self.lastw.keys()) + list(self.reads.keys())
                if (k in names) or (isinstance(k, tuple) and k[0] in names)]
        for e in engines:
            for k in keys:
                self._wait(e, self.lastw.get(k))
                for t in self.reads.get(k, ()):
                    self._wait(e, t)
        for k in set(keys):
            self.lastw.pop(k, None)
            self.reads.pop(k, None)

    def finish(self, e='sp'):
        for k, t in list(self.lastw.items()):
            self._wait(e, t)


class Phase:
    def __init__(self, kb):
        self.kb = kb
        self.es = ExitStack()
        self.names = []
        kb.open_phases.append(self)

    _uid = [0]

    def T(self, name, shape, dt):
        self.names.append(name)
        Phase._uid[0] += 1
        return self.es.enter_context(self.kb.nc.sbuf_tensor("%s_%d" % (name, Phase._uid[0]), list(shape), dt))

    def close(self):
        self.kb.open_phases.remove(self)
        self.kb.S.release(self.names)
        self.es.close()


def w_in_tiles():
    t = []
    t.append(('q0', [(0, 512)]))
    t.append(('q1', [(512, 512)]))
    t.append(('kdup', None))
    t.append(('kv', [(OFF_K, 512)]))
    t.append(('gA0', [(OFF_GATE, 512)]))
    t.append(('gA1', [(OFF_GATE + 512, 512)]))
    return t


class _Stop(Exception):
    pass


class KB:
    def __init__(self, seq, depth=DEPTH, do_sample=True, debug=(), stop_after=None):
        self.seq = seq
        self.depth = depth
        self.do_sample = do_sample
        self.debug = set(debug)
        self.stop_after = stop_after
        self.dbg_outs = {}
        self.open_phases = []
        self.nc = bass.Bass("TRN2", target_bir_lowering=False)
        self.es = ExitStack()

    def chk(self, name):
        if self.stop_after == name:
            raise _Stop()

    def din(self, name, shape, dt=F32):
        return self.nc.dram_tensor(name, list(shape), dt, kind="ExternalInput").ap()

    def dout(self, name, shape, dt=F32):
        return self.nc.dram_tensor(name, list(shape), dt, kind="ExternalOutput").ap()

    def dscr(self, name, shape, dt):
        return self.nc.dram_tensor(name, list(shape), dt, kind="Internal").ap()

    def T(self, name, shape, dt):
        return self.es.enter_context(self.nc.sbuf_tensor(name, list(shape), dt))

    def ps(self):
        i = self.S.psn
        self.S.psn = (i + 1) % 8
        return self.psb[i], 'ps%d' % i

    def dump(self, name, ap, shape, key):
        if name not in self.debug:
            return
        o = self.dout("dbg_" + name, shape, ap.dtype if hasattr(ap, 'dtype') else F32)
        self.dbg_outs[name] = shape
        self.S.dma('act', o, ap, reads=key if isinstance(key, list) else [key], writes=['dbg_' + name])

    def mm(self, out, lhsT, rhs, start, stop, reads, writes):
        return self.S.op('pe', lambda e: e.matmul(out, lhsT=lhsT, rhs=rhs, start=start, stop=stop),
                         reads=reads, writes=writes)

    def evac_copy(self, eng, out, in_, reads, writes, scale=None):
        if eng == 'act':
            if scale is None:
                return self.S.op('act', lambda e: e.copy(out=out, in_=in_), reads=reads, writes=writes)
            return self.S.op('act', lambda e: e.activation(out=out, in_=in_, func=AF.Copy, scale=scale),
                             reads=reads, writes=writes)
        if scale is None:
            return self.S.op(eng, lambda e: e.tensor_copy(out=out, in_=in_), reads=reads, writes=writes)
        return self.S.op(eng, lambda e: e.tensor_scalar(out=out, in0=in_, scalar1=scale, scalar2=None, op0=ALU.mult),
                         reads=reads, writes=writes)

    def build(self):
        nc, es = self.nc, self.es
        self.S = S = Sched(nc, es)
        seq = self.seq
        L = self.depth
        I = self.I = {}
        I['xp'] = self.din('xp', [seq, D])
        I['xs'] = self.din('xs', [128, D])
        I['ck'] = self.din('ck', [DEPTH, NSEQ_S, 128, 256])
        I['cv'] = self.din('cv', [DEPTH, NSEQ_S, 128, 256])
        I['sconv'] = self.din('sconv', [DEPTH, NSEQ_S, 3, D])
        I['srg'] = self.din('srg', [DEPTH, NSEQ_S, D])
        I['sshift'] = self.din('sshift', [DEPTH, NSEQ_S, SHW])
        I['swkv'] = self.din('swkv', [DEPTH, NSEQ_S, NH, HD, HD])
        for nm, shp in [('w_in', [DEPTH, D, INC]), ('attn_sinks', [DEPTH, NH]), ('conv_w', [DEPTH, 4, D]),
                        ('conv_b', [DEPTH, D]), ('rg_wa', [DEPTH, 16, 64, 64]), ('rg_ba', [DEPTH, D]),
                        ('rg_wx', [DEPTH, 16, 64, 64]), ('rg_bx', [DEPTH, D]), ('rg_lambda', [DEPTH, D]),
                        ('rw_mu', [DEPTH, SHW]), ('rw_w0', [DEPTH, D]), ('rw_wup', [DEPTH, 64, D]),
                        ('rw_a0', [DEPTH, D]), ('rw_aup', [DEPTH, 64, D]), ('rw_gup', [DEPTH, 160, D]),
                        ('rw_kk', [DEPTH, D]), ('rw_ka', [DEPTH, D]), ('rw_rk', [DEPTH, D]),
                        ('rw_lnw', [DEPTH, D]), ('rw_lnb', [DEPTH, D]), ('w_out', [DEPTH, D, D]),
                        ('ln1_g', [DEPTH, D]), ('ln1_b', [DEPTH, D]), ('w_gu', [DEPTH, D, 2 * DFF]),
                        ('w_down', [DEPTH, DFF, D]), ('ln2_g', [DEPTH, D]), ('ln2_b', [DEPTH, D])]:
            I[nm] = self.din(nm, shp)
        I['c_ident'] = self.din('c_ident', [128, 128])
        I['c_biasP'] = self.din('c_biasP', [128, 2, NH, 128])
        I['c_biasSn'] = self.din('c_biasSn', [128, NH, 128])
        I['c_biasSc'] = self.din('c_biasSc', [128, KVH, 2, 256])
        I['c_mask'] = self.din('c_mask', [128, 4, 512])
        I['c_rmask'] = self.din('c_rmask', [128, 512])
        I['c_blk'] = self.din('c_blk', [128, 128])
        O = self.O = {}
        O['yp'] = self.dout('yp', [seq, D])
        O['ys'] = self.dout('ys', [128, D])
        O['pk'] = self.dout('pk', [DEPTH, 128, 256])
        O['pv'] = self.dout('pv', [DEPTH, 128, 256])
        O['pconv'] = self.dout('pconv', [DEPTH, 3, D])
        O['prg'] = self.dout('prg', [DEPTH, D])
        O['pshift'] = self.dout('pshift', [DEPTH, SHW])
        O['pwkv'] = self.dout('pwkv', [DEPTH, NH, HD, HD])
        O['sk'] = self.dout('sk', [DEPTH, NSEQ_S, 128, 256])
        O['sv'] = self.dout('sv', [DEPTH, NSEQ_S, 128, 256])
        O['sconv_o'] = self.dout('sconv_o', [DEPTH, NSEQ_S, 3, D])
        O['srg_o'] = self.dout('srg_o', [DEPTH, NSEQ_S, D])
        O['sshift_o'] = self.dout('sshift_o', [DEPTH, NSEQ_S, SHW])
        O['swkv_o'] = self.dout('swkv_o', [DEPTH, NSEQ_S, NH, HD, HD])
        W = self.W = {}
        W['w_in'] = self.dscr('wb_in', [DEPTH, D, INC], BF16)
        W['w_out'] = self.dscr('wb_out', [DEPTH, D, D], BF16)
        W['w_gu'] = self.dscr('wb_gu', [DEPTH, D, 2 * DFF], BF16)
        W['w_down'] = self.dscr('wb_down', [DEPTH, DFF, D], BF16)
        self.scr_v = self.dscr('scr_v', [DEPTH, NSEQ_S, 8, 2, 6, T_S, 64], F32)
        self.scr_y = self.dscr('scr_y', [DEPTH, NSEQ_S, 8, 2, T_S, 64], F32)

        self.psb = [es.enter_context(nc.psum_tensor("psb%d" % i, [128, 512], F32)) for i in range(8)]
        self.wbuf = [self.T("wbuf%d" % i, [128, 8, 512], BF16) for i in range(3)]
        self.wn = 0
        self.xres = self.T("xres", [128, 4, D], F32)
        self.xT = self.T("xT", [128, 8, 512], BF16)
        self.mixacc = self.T("mixacc", [128, 8, 512], F32)
        self.identf = self.T("identf", [128, 128], F32)
        self.xb_tmp = [self.T("xb_tmp%d" % i, [128, D], BF16) for i in range(1)]
        self.identb = self.T("identb", [128, 128], BF16)
        self.onesb = self.T("onesb", [128, 64], BF16)
        self.esink = self.T("esink", [128, DEPTH, 8], F32)
        self.epsln = self.T("epsln", [128, 2], F32)
        self.p_cw = self.T("p_cw", [128, DEPTH, 4, 8], F32)
        self.p_cb = self.T("p_cb", [128, DEPTH, 8], F32)
        self.p_ba = self.T("p_ba", [128, DEPTH, 8], F32)
        self.p_bx = self.T("p_bx", [128, DEPTH, 8], F32)
        self.p_c8 = self.T("p_c8", [128, DEPTH, 8], F32)
        self.p_mu = self.T("p_mu", [128, DEPTH, 27], F32)
        self.p_w0 = self.T("p_w0", [128, DEPTH, 8], F32)
        self.p_a0 = self.T("p_a0", [128, DEPTH, 8], F32)
        self.p_kk = self.T("p_kk", [128, DEPTH, 8], F32)
        self.p_ka = self.T("p_ka", [128, DEPTH, 8], F32)
        self.p_rk = self.T("p_rk", [128, DEPTH, 8], F32)
        self.wa_bd = self.T("wa_bd", [128, DEPTH, 2, 8, 128], BF16)
        self.p_omka = self.T("p_omka", [128, DEPTH, 8], F32)
        self.wlora = self.T("wlora", [128, DEPTH, D], BF16)
        self.wgup = self.T("wgup", [128, DEPTH, 2, D], BF16)
        self.blkb = self.T("blkb", [128, 128], BF16)
        self.blkf = self.T("blkf", [128, 256], F32)
        self.convst = {'P': self.T("convstP", [128, DEPTH, 8, 1, 3], F32), 'S': self.T("convstS", [128, DEPTH, 8, NSEQ_S, 3], F32)}
        self.hst = {'P': self.T("hstP", [128, DEPTH, 8, 1], F32), 'S': self.T("hstS", [128, DEPTH, 8, NSEQ_S], F32)}
        self.shst = {'P': self.T("shstP", [128, DEPTH, 27, 1], F32), 'S': self.T("shstS", [128, DEPTH, 27, NSEQ_S], F32)}

        self.phP = Phase(self)
        self.kT_p = [self.phP.T("kTp%d" % l, [128, KVH, 128 + 512], BF16) for l in range(L)]
        self.vtok_p = [self.phP.T("vtokp%d" % l, [128, 5, 256], BF16) for l in range(L)]
        self.H32 = self.phP.T("H32", [128, DEPTH, 8, 128], F32)
        self.Hbf = self.phP.T("Hbf", [128, DEPTH, 8, 128], BF16)
        self.maskb = self.phP.T("maskb", [128, 4, 512], BF16)
        self.rmaskP = self.phP.T("rmaskP", [128, 512], F32)
        self.phP.names = ['kTp', 'vtokp', 'H32', 'Hbf', 'maskb', 'rmaskP']
        try:
            self.main_body()
        except _Stop:
            for ph in reversed(list(self.open_phases)):
                ph.close()
        S.finish('sp')
        S.finish('act')
        self.es.close()
        return nc

    def main_body(self):
        S, I, O, L, seq = self.S, self.I, self.O, self.depth, self.seq
        self.setup()
        self.chk('setup')
        ngroups = seq // 512
        for g in range(ngroups):
            self.load_x(I['xp'][g * 512:(g + 1) * 512, :], 4)
            self.chk('loadx')
            for l in range(L):
                self.layer(l, 'P', g, ngroups)
            self.store_y(O['yp'][g * 512:(g + 1) * 512, :], 4)
        self.chk('pgroups')
        for l in range(L):
            self.prompt_state_out(l)
        self.chk('pstates')
        self.phP.close()
        if self.do_sample:
            self.load_x(I['xs'], 1)
            for l in range(L):
                self.sample_state_in(l)
            self.chk('S_in')
            for l in range(L):
                self.layer(l, 'S', 0, 1)
                self.sample_state_out(l)
            self.store_y(O['ys'], 1)

    def setup(self):
        S, I, W = self.S, self.I, self.W
        ph = Phase(self)
        st32 = [ph.T("wst32_%d" % i, [128, 2048], F32) for i in range(3)]
        st16 = [ph.T("wst16_%d" % i, [128, 2048], BF16) for i in range(3)]
        pc = 0
        for nm in ['w_in', 'w_out', 'w_gu', 'w_down']:
            src = I[nm]
            dst = W[nm]
            tot = 1
            for s_ in src.shape:
                tot *= s_
            sf = src.rearrange("l a b -> (l a b)").rearrange("(r c) -> r c", c=2048)
            df = dst.rearrange("l a b -> (l a b)").rearrange("(r c) -> r c", c=2048)
            R = tot // 2048
            r0 = 0
            while r0 < R:
                rr = min(128, R - r0)
                b = pc % 3
                S.dma('sp', st32[b][0:rr, :], sf[r0:r0 + rr, :], writes=[('wst32', b)])
                ce = ('dve', 'act', 'pool')[pc % 3]
                if ce == 'act':
                    S.op('act', lambda e: e.copy(out=st16[b][0:rr, :], in_=st32[b][0:rr, :]),
                         reads=[('wst32', b)], writes=[('wst16', b)])
                else:
                    S.op(ce, lambda e: e.tensor_copy(out=st16[b][0:rr, :], in_=st32[b][0:rr, :]),
                         reads=[('wst32', b)], writes=[('wst16', b)])
                S.dma('act', df[r0:r0 + rr, :], st16[b][0:rr, :], reads=[('wst16', b)], writes=[('W', nm, pc)])
                r0 += rr
                pc += 1
        ph.names = ['wst32', 'wst16']
        ph.close()
        for k, t in list(S.lastw.items()):
            if isinstance(k, tuple) and k[0] == 'W':
                S._wait('sp', t)
        S.dma('sp', self.identf[:], I['c_ident'], writes=['identf'])
        S.op('dve', lambda e: e.tensor_copy(out=self.identb[:], in_=self.identf[:]), reads=['identf'], writes=['identb'])
        S.op('dve', lambda e: e.memset(self.onesb[:], 1.0), writes=['onesb'])
        S.op('dve', lambda e: e.memset(self.epsln[:, 0:1], LN_EPS), writes=['epsln'])
        S.op('dve', lambda e: e.memset(self.epsln[:, 1:2], GN_EPS), writes=['epsln'])
        sk = I['attn_sinks'].rearrange("l (j two) -> two l j", two=2)
        for half in range(2):
            S.dma('sp', self.esink[half * 64:(half + 1) * 64, :, :],
                  sk[half:half + 1, :, :].broadcast_to([64, DEPTH, 8]), writes=['esink'],
                  allow_slow_non_contiguous=True)
        S.op('act', lambda e: e.activation(out=self.esink[:], in_=self.esink[:], func=AF.Exp),
             reads=['esink'], writes=['esink'])
        self.setup_params()
        for l in range(self.depth):
            S.op('dve', lambda e: e.memset(self.kT_p[l][:], 0.0), writes=[('kTp', l)])
            S.op('pool', lambda e: e.memset(self.vtok_p[l][:], 0.0), writes=[('vtokp', l)])


    def colload(self, dst, src_vec, key, nblk=8):
        self.S.dma('act', dst, src_vec.rearrange("(j p) -> p j", p=128), writes=[key], allow_slow_non_contiguous=True)

    def setup_params(self):
        S, I = self.S, self.I
        ph = Phase(self)
        for l in range(DEPTH):
            for i in range(4):
                self.colload(self.p_cw[:, l, i, :], I['conv_w'][l, i], 'p_cw')
            self.colload(self.p_cb[:, l, :], I['conv_b'][l], 'p_cb')
            self.colload(self.p_ba[:, l, :], I['rg_ba'][l], 'p_ba')
            self.colload(self.p_bx[:, l, :], I['rg_bx'][l], 'p_bx')
            self.colload(self.p_c8[:, l, :], I['rg_lambda'][l], 'p_c8')
            self.colload(self.p_mu[:, l, 0:26], I['rw_mu'][l, 0:3328], 'p_mu', 26)
            S.dma('act', self.p_mu[0:32, l, 26:27], I['rw_mu'][l, 3328:3360].rearrange("(p o) -> p o", o=1), writes=['p_mu'],
                  allow_slow_non_contiguous=True)
            for nm, t in (('rw_w0', self.p_w0), ('rw_a0', self.p_a0), ('rw_kk', self.p_kk), ('rw_ka', self.p_ka),
                          ('rw_rk', self.p_rk)):
                self.colload(t[:, l, :], I[nm][l], nm)
        c8 = self.p_c8[:].rearrange("p l j -> p (l j)")
        x = ph.T("sp_x", [128, 16], F32)
        z = ph.T("sp_z", [128, 16], F32)
        z2 = ph.T("sp_z2", [128, 16], F32)
        acc = ph.T("sp_acc", [128, 16], F32)
        S.op('act', lambda e: e.activation(out=x[:], in_=c8, func=AF.Exp, scale=-1.0), reads=['p_c8'], writes=['sp_x'])
        S.op('dve', lambda e: e.tensor_scalar(out=z[:], in0=x[:], scalar1=2.0, scalar2=None, op0=ALU.add), reads=['sp_x'], writes=['sp_z'])
        S.op('dve', lambda e: e.reciprocal(out=z[:], in_=z[:]), reads=['sp_z'], writes=['sp_z'])
        S.op('dve', lambda e: e.tensor_tensor(out=z[:], in0=z[:], in1=x[:], op=ALU.mult), reads=['sp_z', 'sp_x'], writes=['sp_z'])
        S.op('dve', lambda e: e.tensor_tensor(out=z2[:], in0=z[:], in1=z[:], op=ALU.mult), reads=['sp_z'], writes=['sp_z2'])
        S.op('dve', lambda e: e.tensor_scalar(out=acc[:], in0=z2[:], scalar1=1.0 / 9, scalar2=1.0 / 7, op0=ALU.mult, op1=ALU.add),
             reads=['sp_z2'], writes=['sp_acc'])
        for cst in (1.0 / 5, 1.0 / 3, 1.0):
            S.op('dve', lambda e: e.tensor_tensor(out=acc[:], in0=acc[:], in1=z2[:], op=ALU.mult), reads=['sp_acc', 'sp_z2'], writes=['sp_acc'])
            S.op('dve', lambda e: e.tensor_scalar(out=acc[:], in0=acc[:], scalar1=cst, scalar2=None, op0=ALU.add), reads=['sp_acc'], writes=['sp_acc'])
        S.op('dve', lambda e: e.tensor_tensor(out=acc[:], in0=acc[:], in1=z[:], op=ALU.mult), reads=['sp_acc', 'sp_z'], writes=['sp_acc'])
        S.op('dve', lambda e: e.tensor_scalar(out=c8, in0=acc[:], scalar1=-16.0, scalar2=None, op0=ALU.mult), reads=['sp_acc'], writes=['p_c8'])
        ph.close()
        ph = Phase(self)
        st = ph.T("wbd_st", [128, DEPTH, 2, 8, 128], F32)
        S.op('pool', lambda e: e.memset(st[:], 0.0), writes=['wbd_st'])
        for l in range(DEPTH):
            for gi, nm in enumerate(('rg_wa', 'rg_wx')):
                src = I[nm][l].rearrange("(j hh) i c -> hh i j c", hh=2)
                for hh in range(2):
                    S.dma('act', st[hh * 64:(hh + 1) * 64, l, gi, :, hh * 64:(hh + 1) * 64], src[hh], writes=['wbd_st'])
        S.op('dve', lambda e: e.tensor_copy(out=self.wa_bd[:], in_=st[:]), reads=['wbd_st'], writes=['wa_bd'])
        for st_ in ('P', 'S'):
            S.op('pool', lambda e: e.memset(self.convst[st_][:], 0.0), writes=[('convst', st_)])
            S.op('pool', lambda e: e.memset(self.hst[st_][:], 0.0), writes=[('hst', st_)])
            S.op('pool', lambda e: e.memset(self.shst[st_][:], 0.0), writes=[('shst', st_)])
        ph.close()
        ph = Phase(self)
        st2 = ph.T("lora_st", [128, DEPTH, D], F32)
        st3 = ph.T("gup_st", [128, DEPTH, 2, D], F32)
        S.op('pool', lambda e: e.memset(st3[:], 0.0), writes=['gup_st'])
        for l in range(DEPTH):
            S.dma('act', st2[0:64, l, :], I['rw_wup'][l], writes=['lora_st'])
            S.dma('act', st2[64:128, l, :], I['rw_aup'][l], writes=['lora_st'])
            S.dma('act', st3[:, l, 0, :], I['rw_gup'][l, 0:128, :], writes=['gup_st'])
            S.dma('act', st3[0:32, l, 1, :], I['rw_gup'][l, 128:160, :], writes=['gup_st'])
        S.op('dve', lambda e: e.tensor_copy(out=self.wlora[:], in_=st2[:]), reads=['lora_st'], writes=['wlora'])
        S.op('dve', lambda e: e.tensor_copy(out=self.wgup[:], in_=st3[:]), reads=['gup_st'], writes=['wgup'])
        S.op('dve', lambda e: e.tensor_scalar(out=self.p_omka[:], in0=self.p_ka[:], scalar1=-1.0, scalar2=1.0, op0=ALU.mult, op1=ALU.add),
             reads=['rw_ka'], writes=['p_omka'])
        S.op('pool', lambda e: e.memset(self.H32[:], 0.0), writes=['H32'])
        S.op('pool', lambda e: e.memset(self.Hbf[:], 0.0), writes=['Hbf'])
        ph.close()
        ph = Phase(self)
        mk = ph.T("mask_st", [128, 4, 512], F32)
        S.dma('act', mk[:], I['c_mask'], writes=['mask_st'])
        S.op('dve', lambda e: e.tensor_copy(out=self.maskb[:], in_=mk[:]), reads=['mask_st'], writes=['maskb'])
        bk = ph.T("blk_st", [128, 128], F32)
        S.dma('act', bk[:], I['c_blk'], writes=['blk_st'])
        S.op('dve', lambda e: e.tensor_copy(out=self.blkb[:], in_=bk[:]), reads=['blk_st'], writes=['blkb'])
        for r2 in range(2):
            S.op('dve', lambda e: e.tensor_copy(out=self.blkf[:, r2 * 128:(r2 + 1) * 128], in_=bk[:]), reads=['blk_st'], writes=['blkf'])
        S.dma('act', self.rmaskP[:], I['c_rmask'], writes=['rmaskP'])
        ph.close()


    def prompt_state_out(self, l):
        S, O = self.S, self.O
        ph = Phase(self)
        cv, hs, sh = self.convst['P'], self.hst['P'], self.shst['P']
        self.fm_to_rows(ph, lambda blk, w: cv[0:w, l, blk, 0, :], 3, D, O['pconv'][l], [('convst', 'P')], ('o_pconv', l))
        self.fm_to_rows(ph, lambda blk, w: hs[0:w, l, blk, :], 1, D, O['prg'][l:l + 1, :], [('hst', 'P')], ('o_prg', l))
        self.fm_to_rows(ph, lambda blk, w: sh[0:w, l, blk, :], 1, SHW, O['pshift'][l:l + 1, :], [('shst', 'P')], ('o_pshift', l))
        for j in range(8):
            ps, pk = self.ps()
            S.op('pe', lambda e: e.transpose(out=ps[:, 0:128], in_=self.H32[:, l, j, :], identity=self.identf[:]),
                 reads=[('H32', l, j), 'identf'], writes=[pk])
            tmp = ph.T("pw_tmp%d" % j, [128, 128], F32)
            S.op('act', lambda e: e.copy(out=tmp[:], in_=ps[:, 0:128]), reads=[pk], writes=[('pw_tmp', j)])
            for hh in range(2):
                S.dma('act', O['pwkv'][l, 2 * j + hh], tmp[hh * 64:(hh + 1) * 64, hh * 64:(hh + 1) * 64],
                      reads=[('pw_tmp', j)], writes=[('o_pwkv', l, j, hh)])
        ph.names.append('pw_tmp')
        ph.close()

    def sample_state_in(self, l):
        S, I = self.S, self.I
        ph = Phase(self)
        cv, hs, sh = self.convst['S'], self.hst['S'], self.shst['S']
        self.rows_to_fm(ph, I['sconv'][l].rearrange("n i c -> (n i) c"), 48, D,
                        lambda b0, nb, w: cv[0:w, l, b0:b0 + nb, :, :].rearrange("p b n i -> p b (n i)"), [('convst', 'S')])
        ph.close()
        ph = Phase(self)
        self.rows_to_fm(ph, I['srg'][l], NSEQ_S, D, lambda b0, nb, w: hs[0:w, l, b0:b0 + nb, :], [('hst', 'S')])
        ph.close()
        ph = Phase(self)
        self.rows_to_fm(ph, I['sshift'][l], NSEQ_S, SHW, lambda b0, nb, w: sh[0:w, l, b0:b0 + nb, :], [('shst', 'S')])
        ph.close()

    def sample_state_out(self, l):
        S, O = self.S, self.O
        cv, hs, sh = self.convst['S'], self.hst['S'], self.shst['S']
        ph = Phase(self)
        self.fm_to_rows(ph, lambda blk, w: cv[0:w, l, blk, :, :].rearrange("p n i -> p (n i)"), 48, D,
                        O['sconv_o'][l].rearrange("n i c -> (n i) c"), [('convst', 'S')], ('o_sconv', l))
        ph.close()
        ph = Phase(self)
        self.fm_to_rows(ph, lambda blk, w: hs[0:w, l, blk, :], NSEQ_S, D, O['srg_o'][l], [('hst', 'S')], ('o_srg', l))
        ph.close()
        ph = Phase(self)
        self.fm_to_rows(ph, lambda blk, w: sh[0:w, l, blk, :], NSEQ_S, SHW, O['sshift_o'][l], [('shst', 'S')], ('o_sshift', l))
        ph.close()

    def rows_to_fm(self, ph, src_rows, R, F, dst_fn, keys_w):
        S = self.S
        tmp = ph.T("r2f_tmp", [128, F], F32)
        S.dma('act', tmp[0:R, :], src_rows, writes=['r2f_tmp'])
        nblk = (F + 127) // 128
        per = max(1, 512 // R)
        b0 = 0
        while b0 < nblk:
            nb = min(per, nblk - b0)
            ps, pk = self.ps()
            wlast = 128
            for b in range(nb):
                w = min(128, F - (b0 + b) * 128)
                wlast = w
                S.op('pe', lambda e: e.transpose(out=ps[0:w, b * R:(b + 1) * R], in_=tmp[0:R, (b0 + b) * 128:(b0 + b) * 128 + w],
                                                 identity=self.identf[0:R, 0:R]), reads=['r2f_tmp', 'identf'], writes=[pk])
            nfull = nb if wlast == 128 else nb - 1
            if nfull > 0:
                S.op('act', lambda e: e.copy(out=dst_fn(b0, nfull, 128), in_=ps[:, 0:nfull * R].rearrange("p (b r) -> p b r", r=R)),
                     reads=[pk], writes=keys_w)
            if nfull < nb:
                S.op('act', lambda e: e.copy(out=dst_fn(b0 + nfull, 1, wlast),
                                             in_=ps[0:wlast, nfull * R:(nfull + 1) * R].rearrange("p (b r) -> p b r", r=R)),
                     reads=[pk], writes=keys_w)
            b0 += nb

    def fm_to_rows(self, ph, src_fn, R, F, dst_rows, keys_r, okey):
        S = self.S
        tmp = ph.T("f2r_tmp", [128, F], F32)
        nblk = (F + 127) // 128
        b0 = 0
        while b0 < nblk:
            nb = min(4, nblk - b0)
            ps, pk = self.ps()
            tot = 0
            for b in range(nb):
                w = min(128, F - (b0 + b) * 128)
                S.op('pe', lambda e: e.transpose(out=ps[0:R, b * 128:b * 128 + w], in_=src_fn(b0 + b, w),
                                                 identity=self.identf[0:w, 0:w]), reads=keys_r + ['identf'], writes=[pk])
                tot += w
            S.op('act', lambda e: e.copy(out=tmp[0:R, b0 * 128:b0 * 128 + tot], in_=ps[0:R, 0:tot]), reads=[pk], writes=['f2r_tmp'])
            b0 += nb
        S.dma('act', dst_rows, tmp[0:R, :], reads=['f2r_tmp'], writes=[okey])

    def load_x(self, src, ntile):
        self.S.dma('act', self.xres[:, 0:ntile, :], src.rearrange("(t p) d -> p t d", p=128),
                   writes=[('xres', t) for t in range(ntile)])
        for t in range(ntile):
            self.make_xT(t)

    def store_y(self, dst, ntile):
        self.S.dma('act', dst.rearrange("(t p) d -> p t d", p=128), self.xres[:, 0:ntile, :],
                   reads=[('xres', t) for t in range(ntile)], writes=[('yout', id(dst))])

    def make_xT(self, t):
        S = self.S
        self.xbn = 0
        xb = self.xb_tmp[self.xbn]
        xbk = ('xb_tmp', self.xbn)
        S.op('act', lambda e: e.copy(out=xb[:], in_=self.xres[:, t, :]), reads=[('xres', t)], writes=[xbk])
        ps, pk = self.ps()
        psb = ps[:].bitcast(BF16)
        for k in range(8):
            S.op('pe', lambda e: e.transpose(out=psb[:, k * 128:(k + 1) * 128], in_=xb[:, k * 128:(k + 1) * 128],
                                             identity=self.identb[:]),
                 reads=[xbk, 'identb'], writes=[pk])
        S.op('dve', lambda e: e.tensor_copy(out=self.xT[:, :, t * 128:(t + 1) * 128],
                                            in_=psb.rearrange("p (k c) -> p k c", k=8)),
             reads=[pk], writes=[('xT', t)])

    def wtile(self, wname, l, cols, rows=(0, 8)):
        S = self.S
        i = self.wn
        self.wn = (self.wn + 1) % 3
        buf = self.wbuf[i]
        src = self.W[wname][l]
        k0, k1 = rows
        off = 0
        keys = []
        for ci, (c0, n) in enumerate(cols):
            key = ('wbuf', i) if ci == 0 else ('wbufx', i, ci)
            S.dma('sp', buf[:, 0:k1 - k0, off:off + n],
                  src[k0 * 128:k1 * 128, c0:c0 + n].rearrange("(k p) c -> p k c", p=128), writes=[key])
            keys.append(key)
            off += n
        return buf, keys

    def layer(self, l, stream, g, ngroups):
        S = self.S
        nt = 4 if stream == 'P' else 1
        N = nt * 128
        self.attention(l, stream, g, ngroups, nt, N)
        self.chk('attn')
        self.chk(stream + '_attn')
        self.rglru(l, stream, g, ngroups, nt, N)
        self.chk('rglru')
        self.chk(stream + '_rglru')
        self.rwkv(l, stream, g, ngroups, nt, N)
        self.chk('rwkv')
        self.chk(stream + '_rwkv')
        self.dense_tail(l, stream, nt, N)
        self.chk('dense')

    def load_ln(self, l):
        S, I = self.S, self.I
        for i, nm in enumerate(['ln1_g', 'ln1_b', 'ln2_g', 'ln2_b']):
            S.dma('act', self.lnbuf[:, i, :], I[nm][l:l + 1, :].broadcast_to([128, D]), writes=[('lnbuf', i)])

    def layer_norm_tile(self, ph, t, which):
        S = self.S
        x = self.xres[:, t, :]
        key = ('xres', t)
        st = ph.T("lnst", [128, 2, 6], F32)
        mv = ph.T("lnmv", [128, 2], F32)
        rs = ph.T("lnrs", [128, 1], F32)
        for hf in range(2):
            S.op('dve', lambda e: e.bn_stats(out=st[:, hf, :], in_=self.xres[:, t, hf * 512:(hf + 1) * 512]),
                 reads=[key], writes=['lnst'])
        S.op('dve', lambda e: e.bn_aggr(out=mv[:], in_=st[:].rearrange("p a b -> p (a b)")), reads=['lnst'], writes=['lnmv'])
        S.op('act', lambda e: e.activation(out=rs[:], in_=mv[:, 1:2], func=AF.Sqrt, bias=self.epsln[:, 0:1], scale=1.0),
             reads=['lnmv', 'epsln'], writes=['lnrs'])
        S.op('dve', lambda e: e.reciprocal(out=rs[:], in_=rs[:]), reads=['lnrs'], writes=['lnrs'])
        S.op('dve', lambda e: e.tensor_scalar(out=x, in0=x, scalar1=mv[:, 0:1], scalar2=rs[:, 0:1],
                                              op0=ALU.subtract, op1=ALU.mult), reads=[key, 'lnmv', 'lnrs'], writes=[key])
        S.op('pool', lambda e: e.tensor_tensor(out=x, in0=x, in1=self.lnbuf[:, 2 * which, :], op=ALU.mult),
             reads=[key, ('lnbuf', 2 * which)], writes=[key])
        S.op('pool', lambda e: e.tensor_tensor(out=x, in0=x, in1=self.lnbuf[:, 2 * which + 1, :], op=ALU.add),
             reads=[key, ('lnbuf', 2 * which + 1)], writes=[key])

    def dense_tail(self, l, stream, nt, N):
        S = self.S
        ph = Phase(self)
        self.lnbuf = ph.T("lnbuf", [128, 4, D], F32)
        self.load_ln(l)
        self.mixT = ph.T("mixT", [128, 8, N], BF16)
        for j in range(8):
            S.op('act', lambda e: e.copy(out=self.mixT[:, j, 0:N], in_=self.mixacc[:, j, 0:N]),
                 reads=[('mixacc', j)], writes=[('mixT', j)])
        mkeys = [('mixT', j) for j in range(8)]
        wbs = [self.wtile('w_out', l, [(half * 512, 512)]) for half in range(2)]
        for t in range(nt):
            for half in range(2):
                wb, wk = wbs[half]
                ps, pk = self.ps()
                for k in range(8):
                    self.mm(ps[:, :], self.mixT[:, k, t * 128:(t + 1) * 128], wb[:, k, :], k == 0, k == 7,
                            reads=wk + [('mixT', k)], writes=[pk])
                xs = self.xres[:, t, half * 512:(half + 1) * 512]
                S.op('dve', lambda e: e.scalar_tensor_tensor(out=xs, in0=xs, scalar=ALPHA, in1=ps[:, :],
                                                             op0=ALU.mult, op1=ALU.add),
                     reads=[pk, ('xres', t)], writes=[('xres', t)])
            self.layer_norm_tile(ph, t, 0)
            self.make_xT(t)
        self.dump('x1', self.xres[:, 0:nt, :], [128, nt, D], [('xres', t) for t in range(nt)])
        hT = ph.T("hT", [128, 22, N], BF16)
        sg = ph.T("sgt", [128, N], F32)
        xkeys = [('xT', t) for t in range(nt)]
        NB = DFF // 128
        for j0 in range(0, NB, 4):
            nb = min(4, NB - j0)
            wg, wgk = self.wtile('w_gu', l, [(j0 * 128, nb * 128)])
            wu, wuk = self.wtile('w_gu', l, [(DFF + j0 * 128, nb * 128)])
            for b in range(nb):
                j = j0 + b
                psg, pgk = self.ps()
                for k in range(8):
                    self.mm(psg[:, 0:N], wg[:, k, b * 128:(b + 1) * 128], self.xT[:, k, 0:N], k == 0, k == 7,
                            reads=wgk + xkeys, writes=[pgk])
                psu, puk = self.ps()
                for k in range(8):
                    self.mm(psu[:, 0:N], wu[:, k, b * 128:(b + 1) * 128], self.xT[:, k, 0:N], k == 0, k == 7,
                            reads=wuk + xkeys, writes=[puk])
                S.op('act', lambda e: e.activation(out=sg[:], in_=psg[:, 0:N], func=AF.Silu), reads=[pgk], writes=['sgt'])
                S.op('dve', lambda e: e.tensor_tensor(out=hT[:, j, :], in0=sg[:], in1=psu[:, 0:N], op=ALU.mult),
                     reads=['sgt', puk], writes=[('hT', j)])
        for half in range(2):
            accs = [self.ps() for _ in range(nt)]
            pieces = [(0, 8), (8, 16), (16, 22)]
            for pi, (k0, k1) in enumerate(pieces):
                wb, wk = self.wtile('w_down', l, [(half * 512, 512)], rows=(k0, k1))
                for t in range(nt):
                    ps, pk = accs[t]
                    for k in range(k0, k1):
                        self.mm(ps[:, :], hT[:, k, t * 128:(t + 1) * 128], wb[:, k - k0, :], k == 0, k == NB - 1,
                                reads=wk + [('hT', k)], writes=[pk])
            for t in range(nt):
                ps, pk = accs[t]
                xs = self.xres[:, t, half * 512:(half + 1) * 512]
                S.op('dve', lambda e: e.scalar_tensor_tensor(out=xs, in0=xs, scalar=ALPHA, in1=ps[:, :],
                                                             op0=ALU.mult, op1=ALU.add),
                     reads=[pk, ('xres', t)], writes=[('xres', t)])
        for t in range(nt):
            self.layer_norm_tile(ph, t, 1)
            if l < self.depth - 1:
                self.make_xT(t)
        self.dump('x2', self.xres[:, 0:nt, :], [128, nt, D], [('xres', t) for t in range(nt)])
        ph.close()

    def interleave(self, gens):
        gens = list(gens)
        while gens:
            for g_ in list(gens):
                try:
                    next(g_)
                except StopIteration:
                    gens.remove(g_)

    def rglru(self, l, stream, g, ngroups, nt, N):
        S, I = self.S, self.I
        ph = Phase(self)
        nseq, T = (1, 512) if stream == 'P' else (NSEQ_S, T_S)
        convst, hst = self.convst[stream], self.hst[stream]
        ckey, hkey = ('convst', stream), ('hst', stream)
        xkeys = [('xT', t) for t in range(nt)]
        NW = 4
        xrh = [ph.T("xrh%d" % i, [128, nseq, 3 + T], F32) for i in range(NW)]
        xc = [ph.T("xc%d" % i, [128, nseq, T], F32) for i in range(NW)]
        xcb = [ph.T("xcb%d" % i, [128, N], BF16) for i in range(NW)]
        rr = [ph.T("rg_r%d" % i, [128, N], F32) for i in range(NW)]
        ii = [ph.T("rg_i%d" % i, [128, N], F32) for i in range(NW)]
        aa = [ph.T("rg_a%d" % i, [128, nseq, T], F32) for i in range(NW)]
        uu = [ph.T("rg_u%d" % i, [128, nseq, T], F32) for i in range(NW)]
        hh = ph.T("rg_h", [128, 4, nseq, T], F32)
        ph.names += ['xrh', 'xc', 'xcb', 'rg_r', 'rg_i', 'rg_a', 'rg_u']
        flat = lambda t3: t3[:].rearrange("p n t -> p (n t)")

        def block(half, b, wb, wk, wg, wgk):
            j = half * 4 + b
            K_ = lambda nm: (nm, b)
            ps, pk = self.ps()
            for k in range(8):
                self.mm(ps[:, 0:N], wb[:, k, b * 128:(b + 1) * 128], self.xT[:, k, 0:N], k == 0, k == 7,
                        reads=wk + xkeys, writes=[pk])
            yield
            S.op('act', lambda e: e.copy(out=xrh[b][:, :, 3:3 + T], in_=ps[:, 0:N].rearrange("p (n t) -> p n t", n=nseq)),
                 reads=[pk], writes=[K_('xrh')])
            S.op('pool', lambda e: e.tensor_copy(out=xrh[b][:, :, 0:3], in_=convst[:, l, j, :, :]), reads=[ckey], writes=[K_('xrh')])
            yield
            cw = self.p_cw
            S.op('dve', lambda e: e.tensor_scalar(out=xc[b][:], in0=xrh[b][:, :, 0:T], scalar1=cw[:, l, 0, j:j + 1],
                                                  scalar2=self.p_cb[:, l, j:j + 1], op0=ALU.mult, op1=ALU.add),
                 reads=[K_('xrh'), 'p_cw', 'p_cb'], writes=[K_('xc')])
            yield
            for i in range(1, 4):
                S.op('dve', lambda e: e.scalar_tensor_tensor(out=xc[b][:], in0=xrh[b][:, :, i:i + T], scalar=cw[:, l, i, j:j + 1],
                                                             in1=xc[b][:], op0=ALU.mult, op1=ALU.add),
                     reads=[K_('xrh'), 'p_cw', K_('xc')], writes=[K_('xc')])
                yield
            S.op('pool', lambda e: e.tensor_copy(out=convst[:, l, j, :, :], in_=xrh[b][:, :, T:T + 3]), reads=[K_('xrh')], writes=[ckey])
            S.op('act', lambda e: e.copy(out=xcb[b][:], in_=flat(xc[b])), reads=[K_('xc')], writes=[K_('xcb')])
            yield
            pss = []
            for gi in range(2):
                ps2, pk2 = self.ps()
                self.mm(ps2[:, 0:N], self.wa_bd[:, l, gi, j, :], xcb[b][:], True, True, reads=['wa_bd', K_('xcb')], writes=[pk2])
                pss.append((ps2, pk2))
            yield
            for gi, (dst, dk, bias) in enumerate(((rr[b], K_('rg_r'), self.p_ba), (ii[b], K_('rg_i'), self.p_bx))):
                ps2, pk2 = pss[gi]
                S.op('act', lambda e: e.activation(out=dst[:], in_=ps2[:, 0:N], func=AF.Sigmoid, bias=bias[:, l, j:j + 1], scale=1.0),
                     reads=[pk2, 'p_ba', 'p_bx'], writes=[dk])
            yield
            S.op('act', lambda e: e.activation(out=flat(aa[b]), in_=rr[b][:], func=AF.Exp, scale=self.p_c8[:, l, j:j + 1]),
                 reads=[K_('rg_r'), 'p_c8'], writes=[K_('rg_a')])
            S.op('pool', lambda e: e.tensor_tensor(out=ii[b][:], in0=ii[b][:], in1=flat(xc[b]), op=ALU.mult), reads=[K_('rg_i'), K_('xc')], writes=[K_('rg_i')])
            yield
            S.op('dve', lambda e: e.tensor_tensor(out=rr[b][:], in0=flat(aa[b]), in1=flat(aa[b]), op=ALU.mult), reads=[K_('rg_a')], writes=[K_('rg_r')])
            yield
            S.op('dve', lambda e: e.tensor_scalar(out=rr[b][:], in0=rr[b][:], scalar1=-1.0, scalar2=1.0, op0=ALU.mult, op1=ALU.add),
                 reads=[K_('rg_r')], writes=[K_('rg_r')])
            yield
            S.op('dve', lambda e: e.tensor_scalar(out=rr[b][:], in0=rr[b][:], scalar1=1e-30, scalar2=None, op0=ALU.max), reads=[K_('rg_r')], writes=[K_('rg_r')])
            yield
            S.op('act', lambda e: e.activation(out=rr[b][:], in_=rr[b][:], func=AF.Sqrt), reads=[K_('rg_r')], writes=[K_('rg_r')])
            yield
            S.op('dve', lambda e: e.tensor_tensor(out=flat(uu[b]), in0=ii[b][:], in1=rr[b][:], op=ALU.mult), reads=[K_('rg_i'), K_('rg_r')], writes=[K_('rg_u')])
            yield
            for n in range(nseq):
                S.op('dve', lambda e: e.tensor_tensor_scan(out=hh[:, b, n, :], data0=aa[b][:, n, :], data1=uu[b][:, n, :],
                                                           initial=hst[:, l, j, n:n + 1], op0=ALU.mult, op1=ALU.add),
                     reads=[K_('rg_a'), K_('rg_u'), hkey], writes=[('rg_h', b)])
            yield
            S.op('pool', lambda e: e.tensor_copy(out=hst[:, l, j, :], in_=hh[:, b, :, T - 1]), reads=[('rg_h', b)], writes=[hkey])
            ps, pk = self.ps()
            for k in range(8):
                self.mm(ps[:, 0:N], wg[:, k, b * 128:(b + 1) * 128], self.xT[:, k, 0:N], k == 0, k == 7,
                        reads=wgk + xkeys, writes=[pk])
            yield
            gbt = ii[b]
            S.op('act', lambda e: e.activation(out=gbt[:], in_=ps[:, 0:N], func=AF.Sigmoid), reads=[pk], writes=[K_('rg_i')])
            yield
            S.op('dve', lambda e: e.tensor_tensor(out=gbt[:], in0=gbt[:], in1=hh[:, b, :, :].rearrange("p n t -> p (n t)"), op=ALU.mult),
                 reads=[K_('rg_i'), ('rg_h', b)], writes=[K_('rg_i')])
            yield
            S.op('pool', lambda e: e.tensor_tensor(out=self.mixacc[:, j, 0:N], in0=self.mixacc[:, j, 0:N], in1=gbt[:], op=ALU.add),
                 reads=[K_('rg_i'), ('mixacc', j)], writes=[('mixacc', j)])
            yield

        for half in range(2):
            wb, wk = self.wtile('w_in', l, [(OFF_RNN + half * 512, 512)])
            wg, wgk = self.wtile('w_in', l, [(OFF_GATE + D + half * 512, 512)])
            self.interleave([block(half, b, wb, wk, wg, wgk) for b in range(4)])
        self.dump('mixAB' if stream == 'P' else 's_mixAB', self.mixacc[:, :, 0:N], [128, 8, N], [('mixacc', j) for j in range(8)])
        ph.close()

    def rwkv(self, l, stream, g, ngroups, nt, N):
        S, I = self.S, self.I
        ph = Phase(self)
        is_s = stream == 'S'
        nseq, T = (NSEQ_S, T_S) if is_s else (1, 512)
        shst = self.shst[stream]
        skey = ('shst', stream)
        xkeys = [('xT', t) for t in range(nt)]
        pch = ph.T("pch", [128, nseq, 1 + T], F32)
        dd = ph.T("rw_d", [128, N], F32)
        lwla = ph.T("lwla", [128, N], BF16)
        siglg = ph.T("siglg", [128, 2, N], BF16)
        lnw = ph.T("lnw_t", [128, D], F32)
        lnb = ph.T("lnb_t", [128, D], F32)
        S.dma('act', lnw[:], I['rw_lnw'][l:l + 1, :].broadcast_to([128, D]), writes=['lnw_t'])
        S.dma('act', lnb[:], I['rw_lnb'][l:l + 1, :].broadcast_to([128, D]), writes=['lnb_t'])
        flat = lambda t3: t3[:].rearrange("p n t -> p (n t)")

        def shifted(ps, pk, blk, w, out_ap, out_key, func=None):
            S.op('act', lambda e: e.copy(out=pch[0:w, :, 1:1 + T], in_=ps[0:w, 0:N].rearrange("p (n t) -> p n t", n=nseq)),
                 reads=[pk], writes=['pch'])
            S.op('pool', lambda e: e.tensor_copy(out=pch[0:w, :, 0], in_=shst[0:w, l, blk, :]), reads=[skey], writes=['pch'])
            S.op('pool', lambda e: e.tensor_copy(out=shst[0:w, l, blk, :], in_=pch[0:w, :, T]), reads=['pch'], writes=[skey])
            S.op('dve', lambda e: e.tensor_tensor(out=dd[0:w, :].rearrange("p (n t) -> p n t", n=nseq), in0=pch[0:w, :, 0:T],
                                                  in1=pch[0:w, :, 1:1 + T], op=ALU.subtract), reads=['pch'], writes=['rw_d'])
            if func is None:
                S.op('dve', lambda e: e.scalar_tensor_tensor(out=out_ap.rearrange("p (n t) -> p n t", n=nseq), in0=dd[0:w, :].rearrange("p (n t) -> p n t", n=nseq),
                                                             scalar=self.p_mu[0:w, l, blk:blk + 1], in1=pch[0:w, :, 1:1 + T],
                                                             op0=ALU.mult, op1=ALU.add), reads=['rw_d', 'pch', 'p_mu'], writes=[out_key])
            else:
                S.op('dve', lambda e: e.scalar_tensor_tensor(out=dd[0:w, :].rearrange("p (n t) -> p n t", n=nseq), in0=dd[0:w, :].rearrange("p (n t) -> p n t", n=nseq),
                                                             scalar=self.p_mu[0:w, l, blk:blk + 1], in1=pch[0:w, :, 1:1 + T],
                                                             op0=ALU.mult, op1=ALU.add), reads=['rw_d', 'pch', 'p_mu'], writes=['rw_d'])
                func(dd)

        wb, wk = self.wtile('w_in', l, [(OFF_RW + 3072, 288)])
        for bi, (c0, w) in enumerate(((0, 128), (128, 128), (256, 32))):
            ps, pk = self.ps()
            for k in range(8):
                self.mm(ps[0:w, 0:N], wb[:, k, c0:c0 + w], self.xT[:, k, 0:N], k == 0, k == 7, reads=wk + xkeys, writes=[pk])
            if bi == 0:
                def f0(d_):
                    S.op('act', lambda e: e.activation(out=lwla[0:64, :], in_=d_[0:64, :], func=AF.Tanh), reads=['rw_d'], writes=['lwla'])
                    S.op('act', lambda e: e.copy(out=lwla[64:128, :], in_=d_[64:128, :]), reads=['rw_d'], writes=['lwla'])
                shifted(ps, pk, 24, 128, None, None, f0)
            else:
                def f1(d_, bi=bi, w=w):
                    S.op('act', lambda e: e.activation(out=siglg[0:w, bi - 1, :], in_=d_[0:w, :], func=AF.Sigmoid), reads=['rw_d'], writes=['siglg'])
                shifted(ps, pk, 24 + bi, w, None, None, f1)

        if not is_s:
            for rnd in range(4):
                self.rwkv_round(ph, l, stream, g, nt, N, rnd, shifted, lwla, siglg, lnw, lnb, xkeys, None)
        else:
            ytok = ph.T("s2_ytok", [128, D], F32)
            nats = []
            self.pre_nat = [(ph.T("w_vT%d" % r_, [128, 2, N], BF16), ph.T("w_pT%d" % r_, [128, 2, N], BF16),
                             ph.T("w_ggT%d" % r_, [128, 2, N], F32)) for r_ in range(4)]
            phA = Phase(self)
            V6 = phA.T("s2_V6", [128, 6, D], F32)
            for rnd in range(4):
                nats.append(self.rwkv_round(ph, l, stream, g, nt, N, rnd, shifted, lwla, siglg, lnw, lnb, xkeys, V6))
            for n in range(NSEQ_S):
                dstn = self.scr_v[l, n].rearrange("j hh q t d -> q t (j hh) d")
                for q in range(6):
                    S.dma('act', dstn[q], V6[n * 8:(n + 1) * 8, q, :].rearrange("t (jh d) -> t jh d", d=64),
                          reads=[('s2_V6', jx) for jx in range(8)], writes=[('scr_v', l, n, q)])
            phA.names.append('s2_V6')
            phA.close()
            svkeys = [('scr_v', l, n, q) for n in range(NSEQ_S) for q in range(6)]
            for hh in range(2):
                phB = Phase(self)
                Sst = phB.T("s2_S", [128, 64, 64], F32)
                tmp = phB.T("s2_tmp", [128, 64, 64], F32)
                vec = phB.T("s2_vec", [128, 6, T_S, 64], F32)
                yS = phB.T("s2_y", [128, T_S, 64], F32)
                sa = phB.T("s2_sa", [128, 64], F32)
                S.dma('act', Sst[:].rearrange("p v k -> p (v k)"),
                      self.I['swkv'][l].rearrange("n (j hh) v k -> hh (n j) (v k)", hh=2)[hh], writes=['s2_S'])
                S.dma('act', vec[:].rearrange("p q t d -> p (q t d)"),
                      self.scr_v[l].rearrange("n j hh q t d -> hh (n j) (q t d)")[hh], reads=svkeys, writes=['s2_vec'])
                bv = lambda q, t: vec[:, q, t, :].unsqueeze(1).to_broadcast([128, 64, 64])
                bo = lambda ap: ap.unsqueeze(2).to_broadcast([128, 64, 64])
                for t in range(T_S):
                    S.op('dve', lambda e: e.tensor_tensor(out=tmp[:], in0=Sst[:], in1=bv(4, t), op=ALU.mult), reads=['s2_S', 's2_vec'], writes=['s2_tmp'])
                    S.op('dve', lambda e: e.tensor_reduce(out=sa[:], in_=tmp[:], axis=AX.X, op=ALU.add), reads=['s2_tmp'], writes=['s2_sa'])
                    S.op('pool', lambda e: e.tensor_tensor(out=Sst[:], in0=Sst[:], in1=bv(1, t), op=ALU.mult), reads=['s2_S', 's2_vec'], writes=['s2_S'])
                    S.op('dve', lambda e: e.tensor_tensor(out=tmp[:], in0=bo(sa[:]), in1=bv(5, t), op=ALU.mult), reads=['s2_sa', 's2_vec'], writes=['s2_tmp'])
                    S.op('pool', lambda e: e.tensor_tensor(out=Sst[:], in0=Sst[:], in1=tmp[:], op=ALU.subtract), reads=['s2_S', 's2_tmp'], writes=['s2_S'])
                    S.op('dve', lambda e: e.tensor_tensor(out=tmp[:], in0=bo(vec[:, 3, t, :]), in1=bv(2, t), op=ALU.mult), reads=['s2_vec'], writes=['s2_tmp'])
                    S.op('pool', lambda e: e.tensor_tensor(out=Sst[:], in0=Sst[:], in1=tmp[:], op=ALU.add), reads=['s2_S', 's2_tmp'], writes=['s2_S'])
                    S.op('dve', lambda e: e.tensor_tensor(out=tmp[:], in0=Sst[:], in1=bv(0, t), op=ALU.mult), reads=['s2_S', 's2_vec'], writes=['s2_tmp'])
                    S.op('dve', lambda e: e.tensor_reduce(out=yS[:, t, :], in_=tmp[:], axis=AX.X, op=ALU.add), reads=['s2_tmp'], writes=['s2_y'])
                S.dma('act', self.O['swkv_o'][l].rearrange("n (j hh) v k -> hh (n j) (v k)", hh=2)[hh], Sst[:].rearrange("p v k -> p (v k)"),
                      reads=['s2_S'], writes=[('o_swkv', l, hh)])
                S.dma('act', self.scr_y[l].rearrange("n j hh t v -> hh (n j) (t v)")[hh], yS[:].rearrange("p t v -> p (t v)"),
                      reads=['s2_y'], writes=[('scr_y', l, hh)])
                phB.close()
            for n in range(NSEQ_S):
                S.dma('act', ytok[n * 8:(n + 1) * 8, :].rearrange("t (jh v) -> t jh v", v=64),
                      self.scr_y[l, n].rearrange("j hh t v -> t (j hh) v"), reads=[('scr_y', l, 0), ('scr_y', l, 1)], writes=[('s2_ytok', n)])
            for rnd in range(4):
                self.wkv_tile(ph, l, 2 * rnd, 0, nats[rnd], lnw, lnb,
                              s2_y=(ytok[:, rnd * 256:(rnd + 1) * 256], [('s2_ytok', n) for n in range(NSEQ_S)]))
        self.dump('mixed' if not is_s else 's_mixed', self.mixacc[:, :, 0:N], [128, 8, N], [('mixacc', j) for j in range(8)])
        ph.names.append('s2_ytok')
        ph.close()

    def rwkv_round(self, ph0, l, stream, g, nt, N, rnd, shifted, lwla, siglg, lnw, lnb, xkeys, V6):
        S, I = self.S, self.I
        is_s = stream == 'S'
        ph = Phase(self)
        j0 = 2 * rnd
        sf = str(rnd) if is_s else ''
        pho = ph0 if is_s else ph
        aT = ph.T("w_aT", [128, 2, N], BF16)
        rT = ph.T("w_rT", [128, 2, N], BF16)
        bT = ph.T("w_bT", [128, 2, N], BF16)
        kT = ph.T("w_kT", [128, 2, N], BF16)
        if is_s:
            vT, pT, ggT = self.pre_nat[rnd]
        else:
            vT = ph.T("w_vT", [128, 2, N], BF16)
            pT = ph.T("w_pT", [128, 2, N], BF16)
            ggT = ph.T("w_ggT", [128, 2, N], F32)
        gC = ph.T("w_gC", [128, 2, 4], F32)
        r32 = ph.T("w_r32", [128, N], F32)
        k32 = ph.T("w_k32", [128, N], F32)
        v32 = ph.T("w_v32", [128, N], F32)
        lw32 = ph.T("w_lw32", [128, N], F32)
        cl = ph.T("w_cl", [128, N], F32)
        a32 = ph.T("w_a32", [128, N], F32)
        kk32 = ph.T("w_kk32", [128, N], F32)
        t32 = ph.T("w_t32", [128, N], F32)
        tb = ph.T("w_tb", [128, N], BF16)
        eI = ph.T("w_eI", [128, N], F32)
        eN = ph.T("w_eN", [128, N], F32)
        c0 = OFF_RW + j0 * 128
        wb, wk = self.wtile('w_in', l, [(c0, 256), (c0 + D, 256)])
        wb2, wk2 = self.wtile('w_in', l, [(c0 + 2 * D, 256), (OFF_GATE + 2 * D + j0 * 128, 256)])
        for jj in range(2):
            j = j0 + jj
            for qi, (dst, dkey, wbx, wkx, coff, blk) in enumerate(((r32, 'w_r32', wb, wk, jj * 128, j),
                                                                  (k32, 'w_k32', wb, wk, 256 + jj * 128, 8 + j),
                                                                  (v32, 'w_v32', wb2, wk2, jj * 128, 16 + j))):
                ps, pk = self.ps()
                for k in range(8):
                    self.mm(ps[:, 0:N], wbx[:, k, coff:coff + 128], self.xT[:, k, 0:N], k == 0, k == 7, reads=wkx + xkeys, writes=[pk])
                shifted(ps, pk, blk, 128, dst[:], dkey)
            ps, pk = self.ps()
            self.mm(ps[:, 0:N], self.wlora[0:64, l, j * 128:(j + 1) * 128], lwla[0:64, :], True, True, reads=['wlora', 'lwla'], writes=[pk])
            S.op('act', lambda e: e.activation(out=lw32[:], in_=ps[:, 0:N], func=AF.Sigmoid, bias=self.p_w0[:, l, j:j + 1], scale=1.0),
                 reads=[pk, 'rw_w0'], writes=['w_lw32'])
            S.op('pool', lambda e: e.tensor_scalar(out=lw32[:], in0=lw32[:], scalar1=-math.exp(-0.5), scalar2=None, op0=ALU.mult),
                 reads=['w_lw32'], writes=['w_lw32'])
            ps, pk = self.ps()
            self.mm(ps[:, 0:N], self.wlora[64:128, l, j * 128:(j + 1) * 128], lwla[64:128, :], True, True, reads=['wlora', 'lwla'], writes=[pk])
            S.op('act', lambda e: e.activation(out=a32[:], in_=ps[:, 0:N], func=AF.Sigmoid, bias=self.p_a0[:, l, j:j + 1], scale=1.0),
                 reads=[pk, 'rw_a0'], writes=['w_a32'])
            S.op('dve', lambda e: e.tensor_scalar(out=kk32[:], in0=k32[:], scalar1=self.p_kk[:, l, j:j + 1], scalar2=None, op0=ALU.mult),
                 reads=['w_k32', 'rw_kk'], writes=['w_kk32'])
            S.op('act', lambda e: e.activation(out=tb[:], in_=kk32[:], func=AF.Square), reads=['w_kk32'], writes=['w_tb'])
            ps, pk = self.ps()
            self.mm(ps[:, 0:N], self.blkb[:, :], tb[:], True, True, reads=['blkb', 'w_tb'], writes=[pk])
            S.op('act', lambda e: e.activation(out=t32[:], in_=ps[:, 0:N], func=AF.Sqrt), reads=[pk], writes=['w_t32'])
            S.op('dve', lambda e: e.tensor_scalar(out=t32[:], in0=t32[:], scalar1=1e-12, scalar2=None, op0=ALU.max), reads=['w_t32'], writes=['w_t32'])
            S.op('dve', lambda e: e.reciprocal(out=t32[:], in_=t32[:]), reads=['w_t32'], writes=['w_t32'])
            S.op('dve', lambda e: e.tensor_tensor(out=kk32[:], in0=kk32[:], in1=t32[:], op=ALU.mult), reads=['w_kk32', 'w_t32'], writes=['w_kk32'])
            S.op('dve', lambda e: e.tensor_scalar(out=t32[:], in0=a32[:], scalar1=self.p_ka[:, l, j:j + 1], scalar2=self.p_omka[:, l, j:j + 1],
                                                  op0=ALU.mult, op1=ALU.add), reads=['w_a32', 'rw_ka', 'p_omka'], writes=['w_t32'])
            S.op('pool', lambda e: e.tensor_tensor(out=k32[:], in0=k32[:], in1=t32[:], op=ALU.mult), reads=['w_k32', 'w_t32'], writes=['w_k32'])
            if is_s:
                S.op('pool', lambda e: e.tensor_tensor(out=t32[:], in0=r32[:], in1=k32[:], op=ALU.mult), reads=['w_r32', 'w_k32'], writes=['w_t32'])
                S.op('dve', lambda e: e.tensor_scalar(out=pT[:, jj, :], in0=t32[:], scalar1=self.p_rk[:, l, j:j + 1], scalar2=None, op0=ALU.mult),
                     reads=['w_t32', 'rw_rk'], writes=['w_pT' + sf])
                S.op('act', lambda e: e.copy(out=vT[:, jj, :], in_=v32[:]), reads=['w_v32'], writes=['w_vT' + sf])
                S.op('act', lambda e: e.activation(out=eI[:], in_=lw32[:], func=AF.Exp), reads=['w_lw32'], writes=['w_eI'])
                S.op('dve', lambda e: e.tensor_tensor(out=eN[:], in0=kk32[:], in1=a32[:], op=ALU.mult), reads=['w_kk32', 'w_a32'], writes=['w_eN'])
                psa, pka = self.ps()
                for q, (src, skey) in enumerate(((r32, 'w_r32'), (eI, 'w_eI'), (k32, 'w_k32'), (v32, 'w_v32'))):
                    S.op('pe', lambda e: e.transpose(out=psa[:, q * 128:(q + 1) * 128], in_=src[:, :], identity=self.identf[:]),
                         reads=[skey, 'identf'], writes=[pka])
                psb_, pkb = self.ps()
                for q, (src, skey) in enumerate(((kk32, 'w_kk32'), (eN, 'w_eN'))):
                    S.op('pe', lambda e: e.transpose(out=psb_[:, q * 128:(q + 1) * 128], in_=src[:, :], identity=self.identf[:]),
                         reads=[skey, 'identf'], writes=[pkb])
                S.op('act', lambda e: e.copy(out=V6[:, 0:4, j * 128:(j + 1) * 128], in_=psa[:, :].rearrange("p (q c) -> p q c", q=4)),
                     reads=[pka], writes=[('s2_V6', j)])
                S.op('dve', lambda e: e.tensor_copy(out=V6[:, 4:6, j * 128:(j + 1) * 128], in_=psb_[:, 0:256].rearrange("p (q c) -> p q c", q=2)),
                     reads=[pkb], writes=[('s2_V6', j)])
            if not is_s:
                S.op('dve', lambda e: e.tensor_tensor_scan(out=cl[:], data0=self.rmaskP[:, 0:N], data1=lw32[:], initial=0.0, op0=ALU.mult, op1=ALU.add),
                     reads=['rmaskP', 'w_lw32'], writes=['w_cl'])
                S.op('act', lambda e: e.activation(out=eI[:], in_=cl[:], func=AF.Exp), reads=['w_cl'], writes=['w_eI'])
                S.op('act', lambda e: e.activation(out=eN[:], in_=cl[:], func=AF.Exp, scale=-1.0), reads=['w_cl'], writes=['w_eN'])
                S.op('pool', lambda e: e.tensor_copy(out=gC[:, jj, :], in_=eI[:, 127:N:128]), reads=['w_eI'], writes=['w_gC'])
                S.op('dve', lambda e: e.tensor_tensor(out=rT[:, jj, :], in0=r32[:], in1=eI[:], op=ALU.mult), reads=['w_r32', 'w_eI'], writes=['w_rT'])
                S.op('pool', lambda e: e.tensor_tensor(out=t32[:], in0=r32[:], in1=k32[:], op=ALU.mult), reads=['w_r32', 'w_k32'], writes=['w_t32'])
                S.op('dve', lambda e: e.tensor_scalar(out=pT[:, jj, :], in0=t32[:], scalar1=self.p_rk[:, l, j:j + 1], scalar2=None, op0=ALU.mult),
                     reads=['w_t32', 'rw_rk'], writes=['w_pT' + sf])
                S.op('dve', lambda e: e.tensor_tensor(out=kT[:, jj, :], in0=k32[:], in1=eN[:], op=ALU.mult), reads=['w_k32', 'w_eN'], writes=['w_kT'])
                S.op('pool', lambda e: e.tensor_tensor(out=t32[:], in0=kk32[:], in1=a32[:], op=ALU.mult), reads=['w_kk32', 'w_a32'], writes=['w_t32'])
                S.op('dve', lambda e: e.tensor_tensor(out=bT[:, jj, :], in0=t32[:], in1=eN[:], op=ALU.mult), reads=['w_t32', 'w_eN'], writes=['w_bT'])
                S.op('pool', lambda e: e.tensor_tensor(out=t32[:], in0=cl[:], in1=lw32[:], op=ALU.subtract), reads=['w_cl', 'w_lw32'], writes=['w_t32'])
                S.op('act', lambda e: e.activation(out=t32[:], in_=t32[:], func=AF.Exp), reads=['w_t32'], writes=['w_t32'])
                S.op('dve', lambda e: e.scalar_tensor_tensor(out=aT[:, jj, :], in0=kk32[:], scalar=-1.0, in1=t32[:], op0=ALU.mult, op1=ALU.mult),
                     reads=['w_kk32', 'w_t32'], writes=['w_aT'])
                S.op('act', lambda e: e.copy(out=vT[:, jj, :], in_=v32[:]), reads=['w_v32'], writes=['w_vT' + sf])
            ps, pk = self.ps()
            self.mm(ps[:, 0:N], self.wgup[:, l, 0, j * 128:(j + 1) * 128], siglg[:, 0, :], True, False, reads=['wgup', 'siglg'], writes=[pk])
            self.mm(ps[:, 0:N], self.wgup[0:32, l, 1, j * 128:(j + 1) * 128], siglg[0:32, 1, :], False, True, reads=['wgup', 'siglg'], writes=[pk])
            ps2, pk2 = self.ps()
            for k in range(8):
                self.mm(ps2[:, 0:N], wb2[:, k, 256 + jj * 128:256 + (jj + 1) * 128], self.xT[:, k, 0:N], k == 0, k == 7, reads=wk2 + xkeys, writes=[pk2])
            S.op('act', lambda e: e.activation(out=t32[:], in_=ps2[:, 0:N], func=AF.Sigmoid), reads=[pk2], writes=['w_t32'])
            S.op('dve', lambda e: e.tensor_tensor(out=ggT[:, jj, :], in0=ps[:, 0:N], in1=t32[:], op=ALU.mult), reads=[pk, 'w_t32'], writes=['w_ggT' + sf])
        nat = dict(aT=aT, rT=rT, bT=bT, kT=kT, vT=vT, pT=pT, ggT=ggT, gC=gC, sf=sf)
        if not is_s:
            for t in range(nt):
                self.wkv_tile(ph, l, j0, t, nat, lnw, lnb)
        ph.close()
        return nat

    def wkv_tile(self, ph0, l, j0, t, nat, lnw, lnb, s2_y=None):
        S = self.S

        class _Reuse:
            def T(_s, name, shape, dt):
                cache = ph0.__dict__.setdefault('_tile_cache', {})
                if name not in cache:
                    cache[name] = ph0.T(name, shape, dt)
                return cache[name]

            def close(_s):
                pass
        ph = _Reuse()
        aT, rT, bT, kT, vT, pT, ggT, gC = (nat[k] for k in ('aT', 'rT', 'bT', 'kT', 'vT', 'pT', 'ggT', 'gC'))
        sf = nat['sf']
        cs = slice(t * 128, (t + 1) * 128)
        tok = ph.T("k_tok", [128, 4, 2, 128], BF16)
        ps, pk = self.ps()
        psb = ps[:].bitcast(BF16)
        for qi, (src, skey) in enumerate(((vT, 'w_vT' + sf), (bT, 'w_bT'), (kT, 'w_kT'), (pT, 'w_pT' + sf))):
            if s2_y is not None and qi in (1, 2):
                src, skey = vT, 'w_vT' + sf
            for jj in range(2):
                S.op('pe', lambda e: e.transpose(out=psb[:, (qi * 2 + jj) * 128:(qi * 2 + jj + 1) * 128], in_=src[:, jj, cs],
                                                 identity=self.identb[:]), reads=[skey, 'identb'], writes=[pk])
        S.op('act', lambda e: e.copy(out=tok[:].rearrange("p q j c -> p (q j c)"), in_=psb), reads=[pk], writes=['k_tok'])
        if s2_y is None:
            blks = {}
            for nm, (src, skey) in (('a', (aT, 'w_aT')), ('r', (rT, 'w_rT')), ('b', (bT, 'w_bT'))):
                isnew = ("k_blk" + nm) not in ph0.__dict__.get('_tile_cache', {})
                bl = ph.T("k_blk" + nm, [128, 2, 2, 128], BF16)
                if isnew:
                    S.op('pool', lambda e: e.memset(bl[:], 0.0), writes=['k_blk' + nm])
                for hh in range(2):
                    eng = 'act' if hh else 'pool'
                    if eng == 'act':
                        S.op('act', lambda e: e.copy(out=bl[hh * 64:(hh + 1) * 64, :, hh, :], in_=src[hh * 64:(hh + 1) * 64, :, cs]),
                             reads=[skey, 'k_blk' + nm], writes=['k_blk' + nm])
                    else:
                        S.op('pool', lambda e: e.tensor_copy(out=bl[hh * 64:(hh + 1) * 64, :, hh, :], in_=src[hh * 64:(hh + 1) * 64, :, cs]),
                             reads=[skey, 'k_blk' + nm], writes=['k_blk' + nm])
                blks[nm] = bl
            mats = {}
            specs = (('AabT', bT, 'w_bT', 'a', 0), ('ArbT', bT, 'w_bT', 'r', 1), ('AakT', kT, 'w_kT', 'a', 0),
                     ('ArkT', kT, 'w_kT', 'r', 1), ('Aab', aT, 'w_aT', 'b', 2))
            for mi, (nm, lsrc, lkey, rb, mk) in enumerate(specs):
                ps, pk = self.ps()
                for jj in range(2):
                    self.mm(ps[:, jj * 256:(jj + 1) * 256], lsrc[:, jj, cs], blks[rb][:, jj, :, :].rearrange("p h c -> p (h c)"),
                            True, True, reads=[lkey, 'k_blk' + rb], writes=[pk])
                m = ph.T("k_" + nm, [128, 4, 128], BF16)
                S.op('dve', lambda e: e.tensor_tensor(out=m[:].rearrange("p h c -> p (h c)"), in0=ps[:, :], in1=self.maskb[:, mk, :], op=ALU.mult),
                     reads=[pk, 'maskb'], writes=['k_' + nm])
                mats[nm] = m
            Pm = [ph.T("k_P%d" % i, [128, 4, 128], BF16) for i in range(2)]
            Am = [ph.T("k_A%d" % i, [128, 4, 128], BF16) for i in range(2)]
            Bm = [ph.T("k_B%d" % i, [128, 4, 128], BF16) for i in range(2)]
            fl = lambda m_: m_[:].rearrange("p h c -> p (h c)")
            S.op('pool', lambda e: e.tensor_tensor(out=fl(Pm[0]), in0=fl(mats['AabT']), in1=self.maskb[:, 3, :], op=ALU.add),
                 reads=['k_AabT', 'maskb'], writes=['k_P0'])
            Acur, Akey, Bcur, Bkey = mats['Aab'], 'k_Aab', mats['AabT'], 'k_AabT'
            pcur = 0
            pend = None

            def p_update(Ap, Apk, pcur_):
                ps_, pk_ = self.ps()
                for h in range(4):
                    self.mm(ps_[:, h * 128:(h + 1) * 128], Ap[:, h, :], Pm[pcur_][:, h, :], True, True, reads=[Apk, 'k_P%d' % pcur_], writes=[pk_])
                S.op('dve', lambda e: e.tensor_tensor(out=fl(Pm[1 - pcur_]), in0=ps_[:, :], in1=fl(Pm[pcur_]), op=ALU.add),
                     reads=[pk_, 'k_P%d' % pcur_], writes=['k_P%d' % (1 - pcur_)])
                return 1 - pcur_
            Am3 = Am + [ph.T("k_A2", [128, 4, 128], BF16)]
            for step in range(6):
                An, Ank = Am3[step % 3], 'k_A%d' % (step % 3)
                psA, pkA = self.ps()
                for h in range(4):
                    self.mm(psA[:, h * 128:(h + 1) * 128], Bcur[:, h, :], Acur[:, h, :], True, True, reads=[Akey, Bkey], writes=[pkA])
                if step < 5:
                    Bn, Bnk = Bm[step % 2], 'k_B%d' % (step % 2)
                    psB, pkB = self.ps()
                    for h in range(4):
                        self.mm(psB[:, h * 128:(h + 1) * 128], Acur[:, h, :], Bcur[:, h, :], True, True, reads=[Akey, Bkey], writes=[pkB])
                if pend is not None:
                    pcur = p_update(pend[0], pend[1], pcur)
                S.op('act', lambda e: e.copy(out=fl(An), in_=psA[:, :]), reads=[pkA], writes=[Ank])
                if step < 5:
                    S.op('dve', lambda e: e.tensor_copy(out=fl(Bn), in_=psB[:, :]), reads=[pkB], writes=[Bnk])
                pend = (An, Ank)
                Acur, Akey = An, Ank
                if step < 5:
                    Bcur, Bkey = Bn, Bnk
            pcur = p_update(pend[0], pend[1], pcur)
            Pf, Pkey = Pm[pcur], 'k_P%d' % pcur
            Hb = self.Hbf
            Xs = ph.T("k_Xs", [128, 4, 64], BF16)
            Us = ph.T("k_Us", [128, 4, 64], BF16)
            psX, pkX = self.ps()
            for jj in range(2):
                j = j0 + jj
                self.mm(psX[:, jj * 128:(jj + 1) * 128], aT[:, jj, cs], Hb[:, l, j, :], True, False, reads=['w_aT', ('Hbf', l, j)], writes=[pkX])
                for hh in range(2):
                    h = jj * 2 + hh
                    self.mm(psX[:, h * 64:(h + 1) * 64], mats['AakT'][:, h, :], tok[:, 0, jj, hh * 64:(hh + 1) * 64], False, hh == 1,
                            reads=['k_AakT', 'k_tok'], writes=[pkX])
            S.op('act', lambda e: e.copy(out=Xs[:].rearrange("p h c -> p (h c)"), in_=psX[:, 0:256]), reads=[pkX], writes=['k_Xs'])
            psU, pkU = self.ps()
            for h in range(4):
                self.mm(psU[:, h * 64:(h + 1) * 64], Pf[:, h, :], Xs[:, h, :], True, True, reads=[Pkey, 'k_Xs'], writes=[pkU])
            S.op('dve', lambda e: e.tensor_copy(out=Us[:].rearrange("p h c -> p (h c)"), in_=psU[:, 0:256]), reads=[pkU], writes=['k_Us'])
            psY, pkY = self.ps()
            for jj in range(2):
                j = j0 + jj
                self.mm(psY[:, jj * 128:(jj + 1) * 128], rT[:, jj, cs], Hb[:, l, j, :], True, False, reads=['w_rT', ('Hbf', l, j)], writes=[pkY])
                for hh in range(2):
                    h = jj * 2 + hh
                    self.mm(psY[:, h * 64:(h + 1) * 64], mats['ArbT'][:, h, :], Us[:, h, :], False, False, reads=['k_ArbT', 'k_Us'], writes=[pkY])
                    self.mm(psY[:, h * 64:(h + 1) * 64], mats['ArkT'][:, h, :], tok[:, 0, jj, hh * 64:(hh + 1) * 64], False, hh == 1,
                            reads=['k_ArkT', 'k_tok'], writes=[pkY])
            psH, pkH = self.ps()
            for jj in range(2):
                self.mm(psH[:, jj * 128:(jj + 1) * 128], tok[:, 1, jj, :], Us[:, 2 * jj:2 * jj + 2, :].rearrange("p h c -> p (h c)"), True, False,
                        reads=['k_tok', 'k_Us'], writes=[pkH])
                self.mm(psH[:, jj * 128:(jj + 1) * 128], tok[:, 2, jj, :], tok[:, 0, jj, :], False, True, reads=['k_tok'], writes=[pkH])
            hd = ph.T("k_hd", [128, 256], F32)
            S.op('dve', lambda e: e.tensor_tensor(out=hd[:], in0=psH[:, 0:256], in1=self.blkf[:], op=ALU.mult), reads=[pkH, 'blkf'], writes=['k_hd'])
            for jj in range(2):
                j = j0 + jj
                S.op('pool', lambda e: e.tensor_tensor(out=hd[:, jj * 128:(jj + 1) * 128], in0=hd[:, jj * 128:(jj + 1) * 128], in1=self.H32[:, l, j, :], op=ALU.add),
                     reads=['k_hd', ('H32', l, j)], writes=['k_hd'])
                S.op('dve', lambda e: e.tensor_scalar(out=self.H32[:, l, j, :], in0=hd[:, jj * 128:(jj + 1) * 128], scalar1=gC[:, jj, t:t + 1], scalar2=None, op0=ALU.mult),
                     reads=['k_hd', 'w_gC'], writes=[('H32', l, j)])
                S.op('act', lambda e: e.copy(out=Hb[:, l, j, :], in_=self.H32[:, l, j, :]), reads=[('H32', l, j)], writes=[('Hbf', l, j)])
        y = ph.T("k_y", [128, 4, 64], F32)
        sq = ph.T("k_sq", [128, 4, 64], F32)
        st = ph.T("k_st", [128, 4], F32)
        st2 = ph.T("k_st2", [128, 4], F32)
        if s2_y is None:
            S.op('act', lambda e: e.copy(out=y[:].rearrange("p h c -> p (h c)"), in_=psY[:, 0:256]), reads=[pkY], writes=['k_y'])
        else:
            S.op('act', lambda e: e.copy(out=y[:].rearrange("p h c -> p (h c)"), in_=s2_y[0]), reads=s2_y[1], writes=['k_y'])
        S.op('dve', lambda e: e.tensor_reduce(out=st[:], in_=y[:], axis=AX.X, op=ALU.add), reads=['k_y'], writes=['k_st'])
        S.op('dve', lambda e: e.tensor_scalar(out=st[:], in0=st[:], scalar1=-1.0 / 64, scalar2=None, op0=ALU.mult), reads=['k_st'], writes=['k_st'])
        S.op('dve', lambda e: e.tensor_tensor(out=y[:], in0=y[:], in1=st[:].unsqueeze(2).to_broadcast([128, 4, 64]), op=ALU.add),
             reads=['k_y', 'k_st'], writes=['k_y'])
        S.op('pool', lambda e: e.tensor_tensor(out=sq[:], in0=y[:], in1=y[:], op=ALU.mult), reads=['k_y'], writes=['k_sq'])
        S.op('dve', lambda e: e.tensor_reduce(out=st2[:], in_=sq[:], axis=AX.X, op=ALU.add), reads=['k_sq'], writes=['k_st2'])
        S.op('act', lambda e: e.activation(out=st2[:], in_=st2[:], func=AF.Sqrt, bias=self.epsln[:, 1:2], scale=1.0 / 64),
             reads=['k_st2', 'epsln'], writes=['k_st2'])
        S.op('dve', lambda e: e.reciprocal(out=st2[:], in_=st2[:]), reads=['k_st2'], writes=['k_st2'])
        S.op('dve', lambda e: e.tensor_tensor(out=y[:], in0=y[:], in1=st2[:].unsqueeze(2).to_broadcast([128, 4, 64]), op=ALU.mult),
             reads=['k_y', 'k_st2'], writes=['k_y'])
        yf = y[:].rearrange("p h c -> p (h c)")
        fsl = slice(j0 * 128, j0 * 128 + 256)
        S.op('pool', lambda e: e.tensor_tensor(out=yf, in0=yf, in1=lnw[:, fsl], op=ALU.mult), reads=['k_y', 'lnw_t'], writes=['k_y'])
        S.op('pool', lambda e: e.tensor_tensor(out=yf, in0=yf, in1=lnb[:, fsl], op=ALU.add), reads=['k_y', 'lnb_t'], writes=['k_y'])
        S.op('dve', lambda e: e.tensor_reduce(out=st[:], in_=tok[:, 3, :, :].rearrange("p j (h c) -> p (j h) c", h=2), axis=AX.X, op=ALU.add),
             reads=['k_tok'], writes=['k_st'])
        S.op('dve', lambda e: e.tensor_tensor(out=sq[:], in0=tok[:, 0, :, :].rearrange("p j (h c) -> p (j h) c", h=2),
                                              in1=st[:].unsqueeze(2).to_broadcast([128, 4, 64]), op=ALU.mult), reads=['k_tok', 'k_st'], writes=['k_sq'])
        S.op('pool', lambda e: e.tensor_tensor(out=y[:], in0=y[:], in1=sq[:], op=ALU.add), reads=['k_y', 'k_sq'], writes=['k_y'])
        psT, pkT = self.ps()
        for jj in range(2):
            S.op('pe', lambda e: e.transpose(out=psT[:, jj * 128:(jj + 1) * 128], in_=yf[:, jj * 128:(jj + 1) * 128], identity=self.identf[:]),
                 reads=['k_y', 'identf'], writes=[pkT])
        og = ph.T("k_og", [128, 2, 128], F32)
        S.op('dve', lambda e: e.tensor_tensor(out=og[:], in0=psT[:, 0:256].rearrange("p (j c) -> p j c", j=2), in1=ggT[:, :, cs], op=ALU.mult),
             reads=[pkT, 'w_ggT' + sf], writes=['k_og'])
        S.op('pool', lambda e: e.tensor_tensor(out=self.mixacc[:, j0:j0 + 2, cs], in0=self.mixacc[:, j0:j0 + 2, cs], in1=og[:], op=ALU.add),
             reads=['k_og', ('mixacc', j0), ('mixacc', j0 + 1)], writes=[('mixacc', j0), ('mixacc', j0 + 1)])
        ph.close()

    def attention(self, l, stream, g, ngroups, nt, N):
        S, I = self.S, self.I
        ph = Phase(self)
        qT = ph.T("qT", [128, 8, N], BF16)
        ET = [ph.T("ET%d" % b, [128, 512], BF16) for b in range(2)]
        sb = ph.T("sbt", [128, 512], F32)
        den = ph.T("den", [128, 256], F32)
        osb = ph.T("osb", [128, 256], F32)
        if stream == 'P':
            self.biasP = ph.T("biasP", [128, 2, NH, 128], F32)
            S.dma('act', self.biasP[:], I['c_biasP'], writes=['biasP'])
        kv32_pre = ph.T("kv32", [128, 512], F32)
        if stream == 'P':
            kT = self.kT_p[l]
            vtok = self.vtok_p[l]
            kkey, vkey = ('kTp', l), ('vtokp', l)
        else:
            kT = ph.T("kTs", [128, KVH, 128 + 128], BF16)
            vtok = ph.T("vtoks", [128, 2, 256], BF16)
            kkey, vkey = 'kTs', 'vtoks'
        xkeys = [('xT', t) for t in range(nt)]
        for half in range(2):
            wb, wk = self.wtile('w_in', l, [(half * 512, 512)])
            for b in range(4):
                ps, pk = self.ps()
                for k in range(8):
                    self.mm(ps[:, 0:N], wb[:, k, b * 128:(b + 1) * 128], self.xT[:, k, 0:N], k == 0, k == 7,
                            reads=wk + xkeys, writes=[pk])
                self.evac_copy('act' if b % 2 else 'dve', qT[:, half * 4 + b, :], ps[:, 0:N], [pk], ['qT'], scale=0.125)
        self.chk('q')
        i = self.wn
        self.wn = (self.wn + 1) % 3
        wb = self.wbuf[i]
        wk = [('wbuf', i)]
        srcw = self.W['w_in'][l]
        kv = srcw[:, OFF_K:OFF_K + 256].rearrange("(k p) (c d) -> p k c d", p=128, d=64)
        wbv = wb[:, :, :].rearrange("p k (c two d) -> p k c two d", two=2, d=64)
        for dup in range(2):
            for k in range(8):
                S.dma('sp', wbv[:, k, :, dup, :], kv[:, k, :, :],
                      writes=wk if (dup == 0 and k == 0) else [('wbufx', i, dup, k)])
        wk_all = wk + [('wbufx', i, dup, k) for dup in range(2) for k in range(8) if not (dup == 0 and k == 0)]
        for c in range(KVH):
            ps, pk = self.ps()
            for k in range(8):
                self.mm(ps[:, 0:N], wb[:, k, c * 128:(c + 1) * 128], self.xT[:, k, 0:N], k == 0, k == 7,
                        reads=wk_all + xkeys, writes=[pk])
            self.evac_copy('act' if c % 2 else 'dve', kT[:, c, 128:128 + N], ps[:, 0:N], [pk], [kkey])
        self.chk('kdup')
        wb, wk = self.wtile('w_in', l, [(OFF_K, 512)])
        last_tile_needs_k = (stream == 'S') or (g == ngroups - 1)
        kv32 = None
        import os
        kvvar = os.environ.get('KVVAR', '')
        for t in range(nt):
            if kvvar == 'dmaonly':
                break
            ps, pk = self.ps()
            for k in range(8):
                self.mm(ps[:, :], self.xT[:, k, t * 128:(t + 1) * 128], wb[:, k, :], k == 0, k == 7,
                        reads=wk + [('xT', t)], writes=[pk])
            if kvvar == 'mmonly':
                continue
            if kvvar != 'noact' and not (kvvar == 'noact3' and t == 3):
                self.evac_copy('act', vtok[:, 1 + t, :], ps[:, 256:512], [pk], [vkey])
            if last_tile_needs_k and t == nt - 1 and kvvar != 'nokv32':
                kv32 = kv32_pre
                self.evac_copy('act', kv32[:], ps[:, :], [pk], ['kv32'])
        import os
        if kv32 is not None and stream == 'P' and not os.environ.get('NOKVOUT'):
            S.dma('act', self.O['pk'][l], kv32[:, 0:256], reads=['kv32'], writes=[('o_pk', l)])
            S.dma('act', self.O['pv'][l], kv32[:, 256:512], reads=['kv32'], writes=[('o_pv', l)])
        self.chk('kvtok')
        for half in range(2):
            wb, wk = self.wtile('w_in', l, [(OFF_GATE + half * 512, 512)])
            for b in range(4):
                ps, pk = self.ps()
                for k in range(8):
                    self.mm(ps[:, 0:N], wb[:, k, b * 128:(b + 1) * 128], self.xT[:, k, 0:N], k == 0, k == 7,
                            reads=wk + xkeys, writes=[pk])
                j = half * 4 + b
                S.op('act', lambda e: e.activation(out=self.mixacc[:, j, 0:N], in_=ps[:, 0:N], func=AF.Sigmoid),
                     reads=[pk], writes=[('mixacc', j)])
        self.chk('proj')
        if stream == 'P':
            for t in range(nt):
                first = (g == 0 and t == 0)
                blks = [1] if first else [0, 1]
                nb = len(blks)
                for c in range(KVH):
                    for par in range(2):
                        ps, pk = self.ps()
                        hp = par * 64
                        for bi, blk in enumerate(blks):
                            koff = 128 * t if blk == 0 else 128 * (t + 1)
                            for i2 in range(2):
                                h = 4 * c + 2 * i2 + par
                                col = (bi * 2 + i2) * 128
                                self.mm(ps[:, col:col + 128], kT[hp:hp + 64, c, koff:koff + 128],
                                        qT[hp:hp + 64, h // 2, t * 128:(t + 1) * 128], True, True,
                                        reads=[kkey, 'qT'], writes=[pk])
                        W_ = nb * 256
                        bias_ap = self.biasP[:, blks[0]:blks[0] + nb, 4 * c + par:4 * c + 4:2, :]
                        S.op('dve', lambda e: e.tensor_tensor(out=sb[:, 0:W_].rearrange("p (b i q) -> p b i q", b=nb, i=2),
                                                              in0=ps[:, 0:W_].rearrange("p (b i q) -> p b i q", b=nb, i=2),
                                                              in1=bias_ap, op=ALU.add),
                             reads=[pk, 'biasP'], writes=['sbt'])
                        S.op('act', lambda e: e.activation(out=ET[par][:, 0:W_], in_=sb[:, 0:W_], func=AF.Exp),
                             reads=['sbt'], writes=[('ET', par)])
                    ps, pk = self.ps()
                    for i4 in range(4):
                        par = i4 % 2
                        hp = par * 64
                        i2 = i4 // 2
                        jj = i4 // 2
                        for part in range(2):
                            for bi, blk in enumerate(blks):
                                slot = t if blk == 0 else t + 1
                                lhsT = vtok[:, slot, c * 64:(c + 1) * 64] if part == 0 else self.onesb[:, :]
                                col = (bi * 2 + i2) * 128
                                self.mm(ps[hp:hp + 64, part * 256 + jj * 128: part * 256 + (jj + 1) * 128], lhsT,
                                        ET[par][:, col:col + 128], bi == 0, bi == nb - 1,
                                        reads=[vkey, 'onesb', ('ET', par)], writes=[pk])
                    for jj in range(2):
                        j = 2 * c + jj
                        S.op('dve', lambda e: e.tensor_scalar(out=den[:, jj * 128:(jj + 1) * 128],
                                                              in0=ps[:, 256 + jj * 128:256 + (jj + 1) * 128],
                                                              scalar1=self.esink[:, l, j:j + 1], scalar2=None, op0=ALU.add),
                             reads=[pk, 'esink'], writes=['den'])
                    S.op('dve', lambda e: e.reciprocal(out=den[:], in_=den[:]), reads=['den'], writes=['den'])
                    S.op('dve', lambda e: e.tensor_tensor(out=osb[:], in0=ps[:, 0:256], in1=den[:], op=ALU.mult),
                         reads=[pk, 'den'], writes=['osb'])
                    S.op('pool', lambda e: e.tensor_tensor(out=self.mixacc[:, 2 * c:2 * c + 2, t * 128:(t + 1) * 128],
                                                           in0=self.mixacc[:, 2 * c:2 * c + 2, t * 128:(t + 1) * 128],
                                                           in1=osb[:].rearrange("p (j q) -> p j q", j=2), op=ALU.mult),
                         reads=['osb', ('mixacc', 2 * c), ('mixacc', 2 * c + 1)],
                         writes=[('mixacc', 2 * c), ('mixacc', 2 * c + 1)])
            S.op('pool', lambda e: e.tensor_copy(out=kT[:, :, 0:128], in_=kT[:, :, N:N + 128]), reads=[kkey], writes=[kkey])
            S.op('pool', lambda e: e.tensor_copy(out=vtok[:, 0, :], in_=vtok[:, nt, :]), reads=[vkey], writes=[vkey])
        else:
            bSn = ph.T("bSn", [128, NH, 128], F32)
            bSc = ph.T("bSc", [128, KVH, 2, 256], F32)
            S.dma('act', bSn[:], I['c_biasSn'], writes=['bSn'])
            S.dma('act', bSc[:], I['c_biasSc'], writes=['bSc'])
            ck32 = ph.T("ck32", [128, NSEQ_S, 256], F32)
            ckd = ph.T("ckd", [128, NSEQ_S, KVH, 2, 64], BF16)
            KcT = ph.T("KcT", [128, NSEQ_S, KVH, 128], BF16)
            Vc = ph.T("Vc", [128, NSEQ_S, 256], BF16)
            ETc = [ph.T("ETc%d" % b, [128, 256], BF16) for b in range(2)]
            t1 = ph.T("sa_t1", [128, 512], F32)
            S.dma('act', ck32[:], I['ck'][l].rearrange("n r c -> r n c"), writes=['ck32'])
            S.dma('act', self.O['sk'][l].rearrange("n r c -> r n c")[0:120], ck32[8:128, :, :], reads=['ck32'], writes=[('o_sk', l)])
            for dup in range(2):
                S.op('dve' if dup else 'pool', lambda e: e.tensor_copy(out=ckd[:, :, :, dup, :], in_=ck32[:].rearrange("p n (c d) -> p n c d", d=64)),
                     reads=['ck32'], writes=['ckd'])
            S.dma('act', ck32[:], I['cv'][l].rearrange("n r c -> r n c"), writes=['ck32'])
            S.dma('act', self.O['sv'][l].rearrange("n r c -> r n c")[0:120], ck32[8:128, :, :], reads=['ck32'], writes=[('o_sv', l)])
            S.op('act', lambda e: e.copy(out=Vc[:], in_=ck32[:]), reads=['ck32'], writes=['Vc'])
            for n0 in range(0, NSEQ_S, 2):
                ps, pk = self.ps()
                psb = ps[:].bitcast(BF16)
                for ni in range(2):
                    for c in range(KVH):
                        S.op('pe', lambda e: e.transpose(out=psb[:, (ni * 4 + c) * 128:(ni * 4 + c + 1) * 128],
                                                         in_=ckd[:, n0 + ni, c, :, :].rearrange("p a d -> p (a d)"), identity=self.identb[:]),
                             reads=['ckd', 'identb'], writes=[pk])
                S.op('act' if (n0 // 2) % 2 else 'dve',
                     (lambda e: e.copy(out=KcT[:, n0:n0 + 2, :, :].rearrange("p n c k -> p (n c k)"), in_=psb)) if (n0 // 2) % 2 else
                     (lambda e: e.tensor_copy(out=KcT[:, n0:n0 + 2, :, :].rearrange("p n c k -> p (n c k)"), in_=psb)),
                     reads=[pk], writes=['KcT'])
            for n in range(NSEQ_S):
                S.dma('act', self.O['sk'][l, n, 120:128, :], kv32[n * 8:(n + 1) * 8, 0:256], reads=['kv32'], writes=[('o_skn', l, n)])
                S.dma('act', self.O['sv'][l, n, 120:128, :], kv32[n * 8:(n + 1) * 8, 256:512], reads=['kv32'], writes=[('o_svn', l, n)])
            for c in range(KVH):
                for par in range(2):
                    hp = par * 64
                    ps, pk = self.ps()
                    for i2 in range(2):
                        h = 4 * c + 2 * i2 + par
                        self.mm(ps[:, i2 * 128:(i2 + 1) * 128], kT[hp:hp + 64, c, 128:256], qT[hp:hp + 64, h // 2, 0:128], True, True,
                                reads=[kkey, 'qT'], writes=[pk])
                    S.op('dve', lambda e: e.tensor_tensor(out=sb[:, 0:256].rearrange("p (i q) -> p i q", i=2),
                                                          in0=ps[:, 0:256].rearrange("p (i q) -> p i q", i=2),
                                                          in1=bSn[:, 4 * c + par:4 * c + 4:2, :], op=ALU.add), reads=[pk, 'bSn'], writes=['sbt'])
                    S.op('act', lambda e: e.activation(out=ET[par][:, 0:256], in_=sb[:, 0:256], func=AF.Exp), reads=['sbt'], writes=[('ET', par)])
                    ps, pk = self.ps()
                    for n in range(NSEQ_S):
                        for i2 in range(2):
                            h = 4 * c + 2 * i2 + par
                            col = (n * 2 + i2) * 8
                            self.mm(ps[:, col:col + 8], KcT[hp:hp + 64, n, c, :], qT[hp:hp + 64, h // 2, n * 8:(n + 1) * 8], True, True,
                                    reads=['KcT', 'qT'], writes=[pk])
                    S.op('dve', lambda e: e.tensor_tensor(out=sb[:, 256:512], in0=ps[:, 0:256], in1=bSc[:, c, par, :], op=ALU.add),
                         reads=[pk, 'bSc'], writes=['sbt'])
                    S.op('act', lambda e: e.activation(out=ETc[par][:], in_=sb[:, 256:512], func=AF.Exp), reads=['sbt'], writes=[('ETc', par)])
                ps1, pk1 = self.ps()
                ps2, pk2 = self.ps()
                for i4 in range(4):
                    par, i2 = i4 % 2, i4 // 2
                    hp = par * 64
                    for part in range(2):
                        lhs_n = vtok[:, 1, c * 64:(c + 1) * 64] if part == 0 else self.onesb[:, :]
                        self.mm(ps1[hp:hp + 64, part * 256 + i2 * 128: part * 256 + (i2 + 1) * 128], lhs_n,
                                ET[par][:, i2 * 128:(i2 + 1) * 128], True, True, reads=[vkey, 'onesb', ('ET', par)], writes=[pk1])
                        for n in range(NSEQ_S):
                            lhs_c = Vc[:, n, c * 64:(c + 1) * 64] if part == 0 else self.onesb[:, :]
                            col = part * 256 + i2 * 128 + n * 8
                            self.mm(ps2[hp:hp + 64, col:col + 8], lhs_c, ETc[par][:, (n * 2 + i2) * 8:(n * 2 + i2) * 8 + 8], True, True,
                                    reads=['Vc', 'onesb', ('ETc', par)], writes=[pk2])
                S.op('act', lambda e: e.copy(out=t1[:], in_=ps1[:, :]), reads=[pk1], writes=['sa_t1'])
                S.op('dve', lambda e: e.tensor_tensor(out=t1[:], in0=t1[:], in1=ps2[:, :], op=ALU.add), reads=['sa_t1', pk2], writes=['sa_t1'])
                for jj in range(2):
                    j = 2 * c + jj
                    S.op('dve', lambda e: e.tensor_scalar(out=den[:, jj * 128:(jj + 1) * 128], in0=t1[:, 256 + jj * 128:256 + (jj + 1) * 128],
                                                          scalar1=self.esink[:, l, j:j + 1], scalar2=None, op0=ALU.add),
                         reads=['sa_t1', 'esink'], writes=['den'])
                S.op('dve', lambda e: e.reciprocal(out=den[:], in_=den[:]), reads=['den'], writes=['den'])
                S.op('dve', lambda e: e.tensor_tensor(out=osb[:], in0=t1[:, 0:256], in1=den[:], op=ALU.mult), reads=['sa_t1', 'den'], writes=['osb'])
                S.op('pool', lambda e: e.tensor_tensor(out=self.mixacc[:, 2 * c:2 * c + 2, 0:128], in0=self.mixacc[:, 2 * c:2 * c + 2, 0:128],
                                                       in1=osb[:].rearrange("p (j q) -> p j q", j=2), op=ALU.mult),
                     reads=['osb', ('mixacc', 2 * c), ('mixacc', 2 * c + 1)], writes=[('mixacc', 2 * c), ('mixacc', 2 * c + 1)])
        self.dump('oa_g' if stream == 'P' else 's_oa_g', self.mixacc[:, :, 0:N], [128, 8, N], [('mixacc', j) for j in range(8)])
        ph.close()


_CACHE = {}


def consts():
    ident = np.eye(128, dtype=np.float32)
    slopes = 2.0 ** (-8.0 * np.arange(1, NH + 1) / NH)
    k = np.arange(128)[:, None]
    q = np.arange(128)[None, :]
    biasP = np.zeros((128, 2, NH, 128), np.float32)
    rel_prev = 128 + q - k
    rel_cur = q - k
    for h in range(NH):
        biasP[:, 0, h, :] = np.where(rel_prev < 128, -slopes[h] * rel_prev, NEG)
        biasP[:, 1, h, :] = np.where(rel_cur >= 0, -slopes[h] * rel_cur, NEG)
    kn, ks = np.arange(128)[:, None] // 8, np.arange(128)[:, None] % 8
    qn, qt = np.arange(128)[None, :] // 8, np.arange(128)[None, :] % 8
    biasSn = np.zeros((128, NH, 128), np.float32)
    for h in range(NH):
        biasSn[:, h, :] = np.where((kn == qn) & (ks <= qt), -slopes[h] * (qt - ks), NEG)
    biasSc = np.zeros((128, KVH, 2, NSEQ_S, 2, T_S), np.float32)
    j = np.arange(128)[:, None]
    t = np.arange(T_S)[None, :]
    for c in range(KVH):
        for i4 in range(4):
            h = 4 * c + i4
            biasSc[:, c, i4 % 2, :, i4 // 2, :] = np.where(j > t, -slopes[h] * (128 + t - j), NEG)[:, None, :]
    biasSc = biasSc.reshape(128, KVH, 2, 256)
    p = np.arange(128)[:, None]
    f = np.arange(128)[None, :]
    strictT = (p < f).astype(np.float32)
    inclT = (p <= f).astype(np.float32)
    strictA = (f < p).astype(np.float32)
    identb = (p == f).astype(np.float32)
    mask = np.stack([np.tile(m, (1, 4)) for m in (strictT, inclT, strictA, identb)], axis=1).astype(np.float32)
    rmask = np.ones((128, 512), np.float32)
    rmask[:, ::128] = 0.0
    blk = np.zeros((128, 128), np.float32)
    blk[:64, :64] = 1.0
    blk[64:, 64:] = 1.0
    return dict(c_ident=ident, c_biasP=biasP, c_biasSn=biasSn, c_biasSc=biasSc, c_mask=mask, c_rmask=rmask, c_blk=blk)


def get_prog(seq, **kw):
    key = (seq, tuple(sorted((k, str(v)) for k, v in kw.items())))
    if key not in _CACHE:
        kb = KB(seq, **kw)
        kb.build()
        _CACHE[key] = kb
    return _CACHE[key]


def make_in_maps(inputs, seq):
    cst = consts()
    maps = []
    f = lambda a: np.ascontiguousarray(a, dtype=np.float32)
    for c in range(NCORES):
        n0 = c * NSEQ_S
        m = {
            'xp': f(inputs['x_prompt'][c % 2, :seq]),
            'xs': f(inputs['x_sample'][n0:n0 + NSEQ_S].reshape(128, D)),
            'ck': f(inputs['cache_k'][:, n0:n0 + NSEQ_S].reshape(DEPTH, NSEQ_S, 128, 256)),
            'cv': f(inputs['cache_v'][:, n0:n0 + NSEQ_S].reshape(DEPTH, NSEQ_S, 128, 256)),
            'sconv': f(inputs['state_conv'][:, n0:n0 + NSEQ_S]),
            'srg': f(inputs['state_rglru'][:, n0:n0 + NSEQ_S]),
            'sshift': f(inputs['state_shift'][:, n0:n0 + NSEQ_S]),
            'swkv': f(inputs['state_wkv'][:, n0:n0 + NSEQ_S]),
        }
        for nm in ['w_in', 'attn_sinks', 'conv_w', 'conv_b', 'rg_wa', 'rg_ba', 'rg_wx', 'rg_bx', 'rg_lambda',
                   'rw_mu', 'rw_w0', 'rw_wup', 'rw_a0', 'rw_aup', 'rw_gup', 'rw_kk', 'rw_ka', 'rw_lnw', 'rw_lnb',
                   'w_out', 'ln1_g', 'ln1_b', 'w_gu', 'w_down', 'ln2_g', 'ln2_b']:
            m[nm] = f(inputs[nm])
        m['rw_rk'] = f(np.asarray(inputs['rw_rk']).reshape(DEPTH, D))
        m.update(cst)
        maps.append(m)
    return maps


def kernel(**inputs):
    seq = inputs['x_prompt'].shape[1]
    kb = get_prog(seq)
    maps = make_in_maps(inputs, seq)
    res = run_bass_kernel_spmd(kb.nc, maps, core_ids=list(range(NCORES)))
    R = res.results
    B = inputs['x_prompt'].shape[0]
    yp = np.stack([R[b]['yp'] for b in range(B)])
    ys = np.concatenate([R[c]['ys'].reshape(NSEQ_S, T_S, D) for c in range(NCORES)])

    def pst(nm, shp):
        return np.stack([R[b][nm] for b in range(B)], axis=1).reshape(shp)

    def sst(nm, shp):
        return np.concatenate([R[c][nm] for c in range(NCORES)], axis=1).reshape(shp)
    NS = NCORES * NSEQ_S
    outs = (yp, ys,
            pst('pk', (DEPTH, B, 128, KVH, HD)), pst('pv', (DEPTH, B, 128, KVH, HD)),
            pst('pconv', (DEPTH, B, 3, D)), pst('prg', (DEPTH, B, D)), pst('pshift', (DEPTH, B, SHW)),
            pst('pwkv', (DEPTH, B, NH, HD, HD)),
            sst('sk', (DEPTH, NS, 128, KVH, HD)), sst('sv', (DEPTH, NS, 128, KVH, HD)),
            sst('sconv_o', (DEPTH, NS, 3, D)), sst('srg_o', (DEPTH, NS, D)), sst('sshift_o', (DEPTH, NS, SHW)),
            sst('swkv_o', (DEPTH, NS, NH, HD, HD)))
    return tuple(np.ascontiguousarray(o, dtype=np.float32) for o in outs)
```

```python
import math
import numpy as np
from contextlib import ExitStack
import concourse.bass as bass
import concourse.mybir as mybir
from concourse.bass_utils import run_bass_kernel_spmd

F32 = mybir.dt.float32
BF16 = mybir.dt.bfloat16
ALU = mybir.AluOpType
AF = mybir.ActivationFunctionType
AX = mybir.AxisListType

D = 1024
NH = 16
HD = 64
KVH = 4
DFF = 2816
INC = 8992
OFF_K, OFF_V, OFF_RNN, OFF_RW, OFF_GATE = 1024, 1280, 1536, 2560, 5920
SHW = 3360
DEPTH = 2
ALPHA = (2 * DEPTH) ** 0.25
LN_EPS = 1e-5
GN_EPS = 64e-5
NEG = -30000.0
NCORES = 8
NSEQ_S = 16
T_S = 8
NWBUF = 2


class Sched:
    def __init__(self, nc, es, n_dma_sems=32):
        self.nc = nc
        self.es = es
        self.engs = {'pe': nc.tensor, 'dve': nc.vector, 'act': nc.scalar, 'pool': nc.gpsimd, 'sp': nc.sync}
        self.csem = {e: es.enter_context(nc.semaphore("cs_" + e)) for e in ('pe', 'dve', 'act', 'pool')}
        self.ccnt = {e: 0 for e in self.csem}
        self.dsem = {q: [es.enter_context(nc.semaphore("ds%s%d" % (q, i))) for i in range(n)]
                     for q, n in (('sp', 16), ('act', 12), ('pool', 4))}
        self.dcnt = {q: [0] * len(v) for q, v in self.dsem.items()}
        self.dnext = {q: 0 for q in self.dsem}
        self.known = {e: {} for e in self.engs}
        self.lastw = {}
        self.reads = {}
        self.ninst = 0
        self.psn = 0

    def _wait(self, e, tok):
        if tok is None:
            return
        sem, val, owner = tok
        if owner == 'pe' and e == 'pe':
            return
        if self.known[e].get(sem.name, 0) >= val:
            return
        self.engs[e].wait_ge(sem, val)
        self.known[e][sem.name] = val
        self.ninst += 1

    def _deps(self, e, reads, writes):
        for k in reads:
            self._wait(e, self.lastw.get(k))
        for k in writes:
            self._wait(e, self.lastw.get(k))
            for t in self.reads.get(k, ()):
                self._wait(e, t)

    def _commit(self, tok, reads, writes):
        for k in writes:
            self.lastw[k] = tok
            self.reads[k] = []
        for k in reads:
            if k in writes:
                continue
            lst = self.reads.setdefault(k, [])
            lst[:] = [t for t in lst if t[0].name != tok[0].name]
            lst.append(tok)

    def op(self, e, fn, reads=(), writes=()):
        self._deps(e, reads, writes)
        ins = fn(self.engs[e])
        self.ccnt[e] += 1
        ins.then_inc(self.csem[e], 1)
        tok = (self.csem[e], self.ccnt[e], e)
        self._commit(tok, reads, writes)
        self.ninst += 1
        return tok

    def dma(self, e, out, in_, reads=(), writes=(), **kw):
        i = self.dnext[e]
        self.dnext[e] = (i + 1) % len(self.dsem[e])
        if self.dcnt[e][i] > 0:
            self._wait(e, (self.dsem[e][i], self.dcnt[e][i], 'dma'))
        self._deps(e, reads, writes)
        ins = self.engs[e].dma_start(out=out, in_=in_, **kw)
        self.dcnt[e][i] += 16
        ins.then_inc(self.dsem[e][i], 16)
        tok = (self.dsem[e][i], self.dcnt[e][i], 'dma')
        self._commit(tok, reads, writes)
        self.ninst += 1
        return tok

    def release(self, names, engines=('pe', 'dve', 'act', 'pool', 'sp')):
        names = set(names)
        keys = [k for k in list(self.lastw.keys()) + list(self.reads.keys())
                if (k in names) or (isinstance(k, tuple) and k[0] in names)]
        for e in engines:
            for k in keys:
                self._wait(e, self.lastw.get(k))
                for t in self.reads.get(k, ()):
                    self._wait(e, t)
        for k in set(keys):
            self.lastw.pop(k, None)
            self.reads.pop(k, None)

    def finish(self, e='sp'):
        for k, t in list(self.lastw.items()):
            self._wait(e, t)


class Phase:
    def __init__(self, kb):
        self.kb = kb
        self.es = ExitStack()
        self.names = []
        kb.open_phases.append(self)

    _uid = [0]

    def T(self, name, shape, dt):
        self.names.append(name)
        Phase._uid[0] += 1
        return self.es.enter_context(self.kb.nc.sbuf_tensor("%s_%d" % (name, Phase._uid[0]), list(shape), dt))

    def close(self):
        self.kb.open_phases.remove(self)
        self.kb.S.release(self.names)
        self.es.close()


def w_in_tiles():
    t = []
    t.append(('q0', [(0, 512)]))
    t.append(('q1', [(512, 512)]))
    t.append(('kdup', None))
    t.append(('kv', [(OFF_K, 512)]))
    t.append(('gA0', [(OFF_GATE, 512)]))
    t.append(('gA1', [(OFF_GATE + 512, 512)]))
    return t


class _Stop(Exception):
    pass


class KB:
    def __init__(self, seq, depth=DEPTH, do_sample=True, debug=(), stop_after=None):
        self.seq = seq
        self.depth = depth
        self.do_sample = do_sample
        self.debug = set(debug)
        self.stop_after = stop_after
        self.dbg_outs = {}
        self.open_phases = []
        self.nc = bass.Bass("TRN2", target_bir_lowering=False)
        self.es = ExitStack()

    def chk(self, name):
        if self.stop_after == name:
            raise _Stop()

    def din(self, name, shape, dt=F32):
        return self.nc.dram_tensor(name, list(shape), dt, kind="ExternalInput").ap()

    def dout(self, name, shape, dt=F32):
        return self.nc.dram_tensor(name, list(shape), dt, kind="ExternalOutput").ap()

    def dscr(self, name, shape, dt):
        return self.nc.dram_tensor(name, list(shape), dt, kind="Internal").ap()

    def T(self, name, shape, dt):
        return self.es.enter_context(self.nc.sbuf_tensor(name, list(shape), dt))

    def ps(self):
        i = self.S.psn
        self.S.psn = (i + 1) % 8
        return self.psb[i], 'ps%d' % i

    def dump(self, name, ap, shape, key):
        if name not in self.debug:
            return
        o = self.dout("dbg_" + name, shape, ap.dtype if hasattr(ap, 'dtype') else F32)
        self.dbg_outs[name] = shape
        self.S.dma('act', o, ap, reads=key if isinstance(key, list) else [key], writes=['dbg_' + name])

    def mm(self, out, lhsT, rhs, start, stop, reads, writes):
        return self.S.op('pe', lambda e: e.matmul(out, lhsT=lhsT, rhs=rhs, start=start, stop=stop),
                         reads=reads, writes=writes)

    def evac_copy(self, eng, out, in_, reads, writes, scale=None):
        if eng == 'act':
            if scale is None:
                return self.S.op('act', lambda e: e.copy(out=out, in_=in_), reads=reads, writes=writes)
            return self.S.op('act', lambda e: e.activation(out=out, in_=in_, func=AF.Copy, scale=scale),
                             reads=reads, writes=writes)
        if scale is None:
            return self.S.op(eng, lambda e: e.tensor_copy(out=out, in_=in_), reads=reads, writes=writes)
        return self.S.op(eng, lambda e: e.tensor_scalar(out=out, in0=in_, scalar1=scale, scalar2=None, op0=ALU.mult),
                         reads=reads, writes=writes)

    def build(self):
        nc, es = self.nc, self.es
        self.S = S = Sched(nc, es)
        seq = self.seq
        L = self.depth
        I = self.I = {}
        I['xp'] = self.din('xp', [seq, D])
        I['xs'] = self.din('xs', [128, D])
        I['ck'] = self.din('ck', [DEPTH, NSEQ_S, 128, 256])
        I['cv'] = self.din('cv', [DEPTH, NSEQ_S, 128, 256])
        I['sconv'] = self.din('sconv', [DEPTH, NSEQ_S, 3, D])
        I['srg'] = self.din('srg', [DEPTH, NSEQ_S, D])
        I['sshift'] = self.din('sshift', [DEPTH, NSEQ_S, SHW])
        I['swkv'] = self.din('swkv', [DEPTH, NSEQ_S, NH, HD, HD])
        for nm, shp in [('w_in', [DEPTH, D, INC]), ('attn_sinks', [DEPTH, NH]), ('conv_w', [DEPTH, 4, D]),
                        ('conv_b', [DEPTH, D]), ('rg_wa', [DEPTH, 16, 64, 64]), ('rg_ba', [DEPTH, D]),
                        ('rg_wx', [DEPTH, 16, 64, 64]), ('rg_bx', [DEPTH, D]), ('rg_lambda', [DEPTH, D]),
                        ('rw_mu', [DEPTH, SHW]), ('rw_w0', [DEPTH, D]), ('rw_wup', [DEPTH, 64, D]),
                        ('rw_a0', [DEPTH, D]), ('rw_aup', [DEPTH, 64, D]), ('rw_gup', [DEPTH, 160, D]),
                        ('rw_kk', [DEPTH, D]), ('rw_ka', [DEPTH, D]), ('rw_rk', [DEPTH, D]),
                        ('rw_lnw', [DEPTH, D]), ('rw_lnb', [DEPTH, D]), ('w_out', [DEPTH, D, D]),
                        ('ln1_g', [DEPTH, D]), ('ln1_b', [DEPTH, D]), ('w_gu', [DEPTH, D, 2 * DFF]),
                        ('w_down', [DEPTH, DFF, D]), ('ln2_g', [DEPTH, D]), ('ln2_b', [DEPTH, D])]:
            I[nm] = self.din(nm, shp)
        I['c_ident'] = self.din('c_ident', [128, 128])
        I['c_biasP'] = self.din('c_biasP', [128, 2, NH, 128])
        I['c_biasSn'] = self.din('c_biasSn', [128, NH, 128])
        I['c_biasSc'] = self.din('c_biasSc', [128, KVH, 2, 256])
        I['c_mask'] = self.din('c_mask', [128, 4, 512])
        I['c_rmask'] = self.din('c_rmask', [128, 512])
        I['c_blk'] = self.din('c_blk', [128, 128])
        O = self.O = {}
        O['yp'] = self.dout('yp', [seq, D])
        O['ys'] = self.dout('ys', [128, D])
        O['pk'] = self.dout('pk', [DEPTH, 128, 256])
        O['pv'] = self.dout('pv', [DEPTH, 128, 256])
        O['pconv'] = self.dout('pconv', [DEPTH, 3, D])
        O['prg'] = self.dout('prg', [DEPTH, D])
        O['pshift'] = self.dout('pshift', [DEPTH, SHW])
        O['pwkv'] = self.dout('pwkv', [DEPTH, NH, HD, HD])
        O['sk'] = self.dout('sk', [DEPTH, NSEQ_S, 128, 256])
        O['sv'] = self.dout('sv', [DEPTH, NSEQ_S, 128, 256])
        O['sconv_o'] = self.dout('sconv_o', [DEPTH, NSEQ_S, 3, D])
        O['srg_o'] = self.dout('srg_o', [DEPTH, NSEQ_S, D])
        O['sshift_o'] = self.dout('sshift_o', [DEPTH, NSEQ_S, SHW])
        O['swkv_o'] = self.dout('swkv_o', [DEPTH, NSEQ_S, NH, HD, HD])
        W = self.W = {}
        W['w_in'] = self.dscr('wb_in', [DEPTH, D, INC], BF16)
        W['w_out'] = self.dscr('wb_out', [DEPTH, D, D], BF16)
        W['w_gu'] = self.dscr('wb_gu', [DEPTH, D, 2 * DFF], BF16)
        W['w_down'] = self.dscr('wb_down', [DEPTH, DFF, D], BF16)
        self.scr_wabd = self.dscr('scr_wabd', [128, DEPTH, 2, 8, 128], BF16)
        self.scr_wlora = self.dscr('scr_wlora', [128, DEPTH, D], BF16)
        self.scr_wgup = self.dscr('scr_wgup', [128, DEPTH, 2, D], BF16)
        self.scr_v = self.dscr('scr_v', [DEPTH, NSEQ_S, 8, 2, 6, T_S, 64], F32)
        self.scr_y = self.dscr('scr_y', [DEPTH, NSEQ_S, 8, 2, T_S, 64], F32)

        self.psb = [es.enter_context(nc.psum_tensor("psb%d" % i, [128, 512], F32)) for i in range(8)]
        self.wbuf = [self.T("wbuf%d" % i, [128, 8, 512], BF16) for i in range(NWBUF)]
        self.wn = 0
        self.xres = self.T("xres", [128, 4, D], F32)
        self.xT = self.T("xT", [128, 8, 512], BF16)
        self.mixacc = self.T("mixacc", [128, 8, 512], F32)
        self.identf = self.T("identf", [128, 128], F32)
        self.xb_tmp = [self.T("xb_tmp%d" % i, [128, D], BF16) for i in range(1)]
        self.identb = self.T("identb", [128, 128], BF16)
        self.onesb = self.T("onesb", [128, 64], BF16)
        self.esink = self.T("esink", [128, DEPTH, 8], F32)
        self.epsln = self.T("epsln", [128, 2], F32)
        self.p_cw = self.T("p_cw", [128, DEPTH, 4, 8], F32)
        self.p_cb = self.T("p_cb", [128, DEPTH, 8], F32)
        self.p_ba = self.T("p_ba", [128, DEPTH, 8], F32)
        self.p_bx = self.T("p_bx", [128, DEPTH, 8], F32)
        self.p_c8 = self.T("p_c8", [128, DEPTH, 8], F32)
        self.p_mu = self.T("p_mu", [128, DEPTH, 27], F32)
        self.p_w0 = self.T("p_w0", [128, DEPTH, 8], F32)
        self.p_a0 = self.T("p_a0", [128, DEPTH, 8], F32)
        self.p_kk = self.T("p_kk", [128, DEPTH, 8], F32)
        self.p_ka = self.T("p_ka", [128, DEPTH, 8], F32)
        self.p_rk = self.T("p_rk", [128, DEPTH, 8], F32)
        self.p_omka = self.T("p_omka", [128, DEPTH, 8], F32)
        self.blkb = self.T("blkb", [128, 128], BF16)
        self.blkf = self.T("blkf", [128, 256], F32)
        self.convst = {'P': self.T("convstP", [128, DEPTH, 8, 1, 3], F32), 'S': self.T("convstS", [128, DEPTH, 8, NSEQ_S, 3], F32)}
        self.hst = {'P': self.T("hstP", [128, DEPTH, 8, 1], F32), 'S': self.T("hstS", [128, DEPTH, 8, NSEQ_S], F32)}
        self.shst = {'P': self.T("shstP", [128, DEPTH, 27, 1], F32), 'S': self.T("shstS", [128, DEPTH, 27, NSEQ_S], F32)}

        self.phP = Phase(self)
        self.kT_p = [self.phP.T("kTp%d" % l, [128, KVH, 128 + 512], BF16) for l in range(L)]
        self.vtok_p = [self.phP.T("vtokp%d" % l, [128, 5, 256], BF16) for l in range(L)]
        self.H32 = self.phP.T("H32", [128, DEPTH, 8, 128], F32)
        self.Hbf = self.phP.T("Hbf", [128, DEPTH, 8, 128], BF16)
        self.maskb = self.phP.T("maskb", [128, 4, 512], BF16)
        self.rmaskP = self.phP.T("rmaskP", [128, 512], F32)
        self.phP.names = ['kTp', 'vtokp', 'H32', 'Hbf', 'maskb', 'rmaskP']
        try:
            self.main_body()
        except _Stop:
            for ph in reversed(list(self.open_phases)):
                ph.close()
        S.finish('sp')
        S.finish('act')
        self.es.close()
        return nc

    def main_body(self):
        S, I, O, L, seq = self.S, self.I, self.O, self.depth, self.seq
        self.setup()
        self.chk('setup')
        ngroups = seq // 512
        for g in range(ngroups):
            self.load_x(I['xp'][g * 512:(g + 1) * 512, :], 4)
            self.chk('loadx')
            for l in range(L):
                self.layer(l, 'P', g, ngroups)
            self.store_y(O['yp'][g * 512:(g + 1) * 512, :], 4)
        self.chk('pgroups')
        for l in range(L):
            self.prompt_state_out(l)
        self.chk('pstates')
        self.phP.close()
        if self.do_sample:
            self.load_x(I['xs'], 1)
            for l in range(L):
                self.sample_state_in(l)
            self.chk('S_in')
            for l in range(L):
                self.layer(l, 'S', 0, 1)
                self.sample_state_out(l)
            self.store_y(O['ys'], 1)

    def setup(self):
        S, I, W = self.S, self.I, self.W
        ph = Phase(self)
        st32 = [ph.T("wst32_%d" % i, [128, 2048], F32) for i in range(3)]
        st16 = [ph.T("wst16_%d" % i, [128, 2048], BF16) for i in range(3)]
        pc = 0
        for nm in ['w_in', 'w_out', 'w_gu', 'w_down']:
            src = I[nm]
            dst = W[nm]
            tot = 1
            for s_ in src.shape:
                tot *= s_
            sf = src.rearrange("l a b -> (l a b)").rearrange("(r c) -> r c", c=2048)
            df = dst.rearrange("l a b -> (l a b)").rearrange("(r c) -> r c", c=2048)
            R = tot // 2048
            r0 = 0
            while r0 < R:
                rr = min(128, R - r0)
                b = pc % 3
                S.dma('sp', st32[b][0:rr, :], sf[r0:r0 + rr, :], writes=[('wst32', b)])
                ce = ('dve', 'act', 'pool')[pc % 3]
                if ce == 'act':
                    S.op('act', lambda e: e.copy(out=st16[b][0:rr, :], in_=st32[b][0:rr, :]),
                         reads=[('wst32', b)], writes=[('wst16', b)])
                else:
                    S.op(ce, lambda e: e.tensor_copy(out=st16[b][0:rr, :], in_=st32[b][0:rr, :]),
                         reads=[('wst32', b)], writes=[('wst16', b)])
                S.dma('act', df[r0:r0 + rr, :], st16[b][0:rr, :], reads=[('wst16', b)], writes=[('W', nm, pc)])
                r0 += rr
                pc += 1
        ph.names = ['wst32', 'wst16']
        ph.close()
        for k, t in list(S.lastw.items()):
            if isinstance(k, tuple) and k[0] == 'W':
                S._wait('sp', t)
        S.dma('sp', self.identf[:], I['c_ident'], writes=['identf'])
        S.op('dve', lambda e: e.tensor_copy(out=self.identb[:], in_=self.identf[:]), reads=['identf'], writes=['identb'])
        S.op('dve', lambda e: e.memset(self.onesb[:], 1.0), writes=['onesb'])
        S.op('dve', lambda e: e.memset(self.epsln[:, 0:1], LN_EPS), writes=['epsln'])
        S.op('dve', lambda e: e.memset(self.epsln[:, 1:2], GN_EPS), writes=['epsln'])
        sk = I['attn_sinks'].rearrange("l (j two) -> two l j", two=2)
        for half in range(2):
            S.dma('sp', self.esink[half * 64:(half + 1) * 64, :, :],
                  sk[half:half + 1, :, :].broadcast_to([64, DEPTH, 8]), writes=['esink'],
                  allow_slow_non_contiguous=True)
        S.op('act', lambda e: e.activation(out=self.esink[:], in_=self.esink[:], func=AF.Exp),
             reads=['esink'], writes=['esink'])
        self.setup_params()
        for l in range(self.depth):
            S.op('dve', lambda e: e.memset(self.kT_p[l][:], 0.0), writes=[('kTp', l)])
            S.op('pool', lambda e: e.memset(self.vtok_p[l][:], 0.0), writes=[('vtokp', l)])


    def colload(self, dst, src_vec, key, nblk=8):
        self.S.dma('act', dst, src_vec.rearrange("(j p) -> p j", p=128), writes=[key], allow_slow_non_contiguous=True)

    def setup_params(self):
        S, I = self.S, self.I
        ph = Phase(self)
        for l in range(DEPTH):
            for i in range(4):
                self.colload(self.p_cw[:, l, i, :], I['conv_w'][l, i], 'p_cw')
            self.colload(self.p_cb[:, l, :], I['conv_b'][l], 'p_cb')
            self.colload(self.p_ba[:, l, :], I['rg_ba'][l], 'p_ba')
            self.colload(self.p_bx[:, l, :], I['rg_bx'][l], 'p_bx')
            self.colload(self.p_c8[:, l, :], I['rg_lambda'][l], 'p_c8')
            self.colload(self.p_mu[:, l, 0:26], I['rw_mu'][l, 0:3328], 'p_mu', 26)
            S.dma('act', self.p_mu[0:32, l, 26:27], I['rw_mu'][l, 3328:3360].rearrange("(p o) -> p o", o=1), writes=['p_mu'],
                  allow_slow_non_contiguous=True)
            for nm, t in (('rw_w0', self.p_w0), ('rw_a0', self.p_a0), ('rw_kk', self.p_kk), ('rw_ka', self.p_ka),
                          ('rw_rk', self.p_rk)):
                self.colload(t[:, l, :], I[nm][l], nm)
        c8 = self.p_c8[:].rearrange("p l j -> p (l j)")
        x = ph.T("sp_x", [128, 16], F32)
        z = ph.T("sp_z", [128, 16], F32)
        z2 = ph.T("sp_z2", [128, 16], F32)
        acc = ph.T("sp_acc", [128, 16], F32)
        S.op('act', lambda e: e.activation(out=x[:], in_=c8, func=AF.Exp, scale=-1.0), reads=['p_c8'], writes=['sp_x'])
        S.op('dve', lambda e: e.tensor_scalar(out=z[:], in0=x[:], scalar1=2.0, scalar2=None, op0=ALU.add), reads=['sp_x'], writes=['sp_z'])
        S.op('dve', lambda e: e.reciprocal(out=z[:], in_=z[:]), reads=['sp_z'], writes=['sp_z'])
        S.op('dve', lambda e: e.tensor_tensor(out=z[:], in0=z[:], in1=x[:], op=ALU.mult), reads=['sp_z', 'sp_x'], writes=['sp_z'])
        S.op('dve', lambda e: e.tensor_tensor(out=z2[:], in0=z[:], in1=z[:], op=ALU.mult), reads=['sp_z'], writes=['sp_z2'])
        S.op('dve', lambda e: e.tensor_scalar(out=acc[:], in0=z2[:], scalar1=1.0 / 9, scalar2=1.0 / 7, op0=ALU.mult, op1=ALU.add),
             reads=['sp_z2'], writes=['sp_acc'])
        for cst in (1.0 / 5, 1.0 / 3, 1.0):
            S.op('dve', lambda e: e.tensor_tensor(out=acc[:], in0=acc[:], in1=z2[:], op=ALU.mult), reads=['sp_acc', 'sp_z2'], writes=['sp_acc'])
            S.op('dve', lambda e: e.tensor_scalar(out=acc[:], in0=acc[:], scalar1=cst, scalar2=None, op0=ALU.add), reads=['sp_acc'], writes=['sp_acc'])
        S.op('dve', lambda e: e.tensor_tensor(out=acc[:], in0=acc[:], in1=z[:], op=ALU.mult), reads=['sp_acc', 'sp_z'], writes=['sp_acc'])
        S.op('dve', lambda e: e.tensor_scalar(out=c8, in0=acc[:], scalar1=-16.0, scalar2=None, op0=ALU.mult), reads=['sp_acc'], writes=['p_c8'])
        ph.close()
        ph = Phase(self)
        st = ph.T("wbd_st", [128, DEPTH, 2, 8, 128], F32)
        S.op('pool', lambda e: e.memset(st[:], 0.0), writes=['wbd_st'])
        for l in range(DEPTH):
            for gi, nm in enumerate(('rg_wa', 'rg_wx')):
                src = I[nm][l].rearrange("(j hh) i c -> hh i j c", hh=2)
                for hh in range(2):
                    S.dma('act', st[hh * 64:(hh + 1) * 64, l, gi, :, hh * 64:(hh + 1) * 64], src[hh], writes=['wbd_st'])
        wabd_t = ph.T("wabd_t", [128, DEPTH, 2, 8, 128], BF16)
        S.op('dve', lambda e: e.tensor_copy(out=wabd_t[:], in_=st[:]), reads=['wbd_st'], writes=['wabd_t'])
        S.dma('act', self.scr_wabd, wabd_t[:], reads=['wabd_t'], writes=['scr_wabd'])
        for st_ in ('P', 'S'):
            S.op('pool', lambda e: e.memset(self.convst[st_][:], 0.0), writes=[('convst', st_)])
            S.op('pool', lambda e: e.memset(self.hst[st_][:], 0.0), writes=[('hst', st_)])
            S.op('pool', lambda e: e.memset(self.shst[st_][:], 0.0), writes=[('shst', st_)])
        ph.close()
        ph = Phase(self)
        st2 = ph.T("lora_st", [128, DEPTH, D], F32)
        st3 = ph.T("gup_st", [128, DEPTH, 2, D], F32)
        S.op('pool', lambda e: e.memset(st3[:], 0.0), writes=['gup_st'])
        for l in range(DEPTH):
            S.dma('act', st2[0:64, l, :], I['rw_wup'][l], writes=['lora_st'])
            S.dma('act', st2[64:128, l, :], I['rw_aup'][l], writes=['lora_st'])
            S.dma('act', st3[:, l, 0, :], I['rw_gup'][l, 0:128, :], writes=['gup_st'])
            S.dma('act', st3[0:32, l, 1, :], I['rw_gup'][l, 128:160, :], writes=['gup_st'])
        wlora_t = ph.T("wlora_t", [128, DEPTH, D], BF16)
        wgup_t = ph.T("wgup_t", [128, DEPTH, 2, D], BF16)
        S.op('dve', lambda e: e.tensor_copy(out=wlora_t[:], in_=st2[:]), reads=['lora_st'], writes=['wlora_t'])
        S.op('dve', lambda e: e.tensor_copy(out=wgup_t[:], in_=st3[:]), reads=['gup_st'], writes=['wgup_t'])
        S.dma('act', self.scr_wlora, wlora_t[:], reads=['wlora_t'], writes=['scr_wlora'])
        S.dma('act', self.scr_wgup, wgup_t[:], reads=['wgup_t'], writes=['scr_wgup'])
        S.op('dve', lambda e: e.tensor_scalar(out=self.p_omka[:], in0=self.p_ka[:], scalar1=-1.0, scalar2=1.0, op0=ALU.mult, op1=ALU.add),
             reads=['rw_ka'], writes=['p_omka'])
        S.op('pool', lambda e: e.memset(self.H32[:], 0.0), writes=['H32'])
        S.op('pool', lambda e: e.memset(self.Hbf[:], 0.0), writes=['Hbf'])
        ph.close()
        ph = Phase(self)
        mk = ph.T("mask_st", [128, 4, 512], F32)
        S.dma('act', mk[:], I['c_mask'], writes=['mask_st'])
        S.op('dve', lambda e: e.tensor_copy(out=self.maskb[:], in_=mk[:]), reads=['mask_st'], writes=['maskb'])
        bk = ph.T("blk_st", [128, 128], F32)
        S.dma('act', bk[:], I['c_blk'], writes=['blk_st'])
        S.op('dve', lambda e: e.tensor_copy(out=self.blkb[:], in_=bk[:]), reads=['blk_st'], writes=['blkb'])
        for r2 in range(2):
            S.op('dve', lambda e: e.tensor_copy(out=self.blkf[:, r2 * 128:(r2 + 1) * 128], in_=bk[:]), reads=['blk_st'], writes=['blkf'])
        S.dma('act', self.rmaskP[:], I['c_rmask'], writes=['rmaskP'])
        ph.close()


    def prompt_state_out(self, l):
        S, O = self.S, self.O
        ph = Phase(self)
        cv, hs, sh = self.convst['P'], self.hst['P'], self.shst['P']
        self.fm_to_rows(ph, lambda blk, w: cv[0:w, l, blk, 0, :], 3, D, O['pconv'][l], [('convst', 'P')], ('o_pconv', l))
        self.fm_to_rows(ph, lambda blk, w: hs[0:w, l, blk, :], 1, D, O['prg'][l:l + 1, :], [('hst', 'P')], ('o_prg', l))
        self.fm_to_rows(ph, lambda blk, w: sh[0:w, l, blk, :], 1, SHW, O['pshift'][l:l + 1, :], [('shst', 'P')], ('o_pshift', l))
        for j in range(8):
            ps, pk = self.ps()
            S.op('pe', lambda e: e.transpose(out=ps[:, 0:128], in_=self.H32[:, l, j, :], identity=self.identf[:]),
                 reads=[('H32', l, j), 'identf'], writes=[pk])
            tmp = ph.T("pw_tmp%d" % j, [128, 128], F32)
            S.op('act', lambda e: e.copy(out=tmp[:], in_=ps[:, 0:128]), reads=[pk], writes=[('pw_tmp', j)])
            for hh in range(2):
                S.dma('act', O['pwkv'][l, 2 * j + hh], tmp[hh * 64:(hh + 1) * 64, hh * 64:(hh + 1) * 64],
                      reads=[('pw_tmp', j)], writes=[('o_pwkv', l, j, hh)])
        ph.names.append('pw_tmp')
        ph.close()

    def sample_state_in(self, l):
        S, I = self.S, self.I
        ph = Phase(self)
        cv, hs, sh = self.convst['S'], self.hst['S'], self.shst['S']
        self.rows_to_fm(ph, I['sconv'][l].rearrange("n i c -> (n i) c"), 48, D,
                        lambda b0, nb, w: cv[0:w, l, b0:b0 + nb, :, :].rearrange("p b n i -> p b (n i)"), [('convst', 'S')])
        ph.close()
        ph = Phase(self)
        self.rows_to_fm(ph, I['srg'][l], NSEQ_S, D, lambda b0, nb, w: hs[0:w, l, b0:b0 + nb, :], [('hst', 'S')])
        ph.close()
        ph = Phase(self)
        self.rows_to_fm(ph, I['sshift'][l], NSEQ_S, SHW, lambda b0, nb, w: sh[0:w, l, b0:b0 + nb, :], [('shst', 'S')])
        ph.close()

    def sample_state_out(self, l):
        S, O = self.S, self.O
        cv, hs, sh = self.convst['S'], self.hst['S'], self.shst['S']
        ph = Phase(self)
        self.fm_to_rows(ph, lambda blk, w: cv[0:w, l, blk, :, :].rearrange("p n i -> p (n i)"), 48, D,
                        O['sconv_o'][l].rearrange("n i c -> (n i) c"), [('convst', 'S')], ('o_sconv', l))
        ph.close()
        ph = Phase(self)
        self.fm_to_rows(ph, lambda blk, w: hs[0:w, l, blk, :], NSEQ_S, D, O['srg_o'][l], [('hst', 'S')], ('o_srg', l))
        ph.close()
        ph = Phase(self)
        self.fm_to_rows(ph, lambda blk, w: sh[0:w, l, blk, :], NSEQ_S, SHW, O['sshift_o'][l], [('shst', 'S')], ('o_sshift', l))
        ph.close()

    def rows_to_fm(self, ph, src_rows, R, F, dst_fn, keys_w):
        S = self.S
        tmp = ph.T("r2f_tmp", [128, F], F32)
        S.dma('act', tmp[0:R, :], src_rows, writes=['r2f_tmp'])
        nblk = (F + 127) // 128
        per = max(1, 512 // R)
        b0 = 0
        while b0 < nblk:
            nb = min(per, nblk - b0)
            ps, pk = self.ps()
            wlast = 128
            for b in range(nb):
                w = min(128, F - (b0 + b) * 128)
                wlast = w
                S.op('pe', lambda e: e.transpose(out=ps[0:w, b * R:(b + 1) * R], in_=tmp[0:R, (b0 + b) * 128:(b0 + b) * 128 + w],
                                                 identity=self.identf[0:R, 0:R]), reads=['r2f_tmp', 'identf'], writes=[pk])
            nfull = nb if wlast == 128 else nb - 1
            if nfull > 0:
                S.op('act', lambda e: e.copy(out=dst_fn(b0, nfull, 128), in_=ps[:, 0:nfull * R].rearrange("p (b r) -> p b r", r=R)),
                     reads=[pk], writes=keys_w)
            if nfull < nb:
                S.op('act', lambda e: e.copy(out=dst_fn(b0 + nfull, 1, wlast),
                                             in_=ps[0:wlast, nfull * R:(nfull + 1) * R].rearrange("p (b r) -> p b r", r=R)),
                     reads=[pk], writes=keys_w)
            b0 += nb

    def fm_to_rows(self, ph, src_fn, R, F, dst_rows, keys_r, okey):
        S = self.S
        tmp = ph.T("f2r_tmp", [128, F], F32)
        nblk = (F + 127) // 128
        b0 = 0
        while b0 < nblk:
            nb = min(4, nblk - b0)
            ps, pk = self.ps()
            tot = 0
            for b in range(nb):
                w = min(128, F - (b0 + b) * 128)
                S.op('pe', lambda e: e.transpose(out=ps[0:R, b * 128:b * 128 + w], in_=src_fn(b0 + b, w),
                                                 identity=self.identf[0:w, 0:w]), reads=keys_r + ['identf'], writes=[pk])
                tot += w
            S.op('act', lambda e: e.copy(out=tmp[0:R, b0 * 128:b0 * 128 + tot], in_=ps[0:R, 0:tot]), reads=[pk], writes=['f2r_tmp'])
            b0 += nb
        S.dma('act', dst_rows, tmp[0:R, :], reads=['f2r_tmp'], writes=[okey])

    def load_x(self, src, ntile):
        self.S.dma('act', self.xres[:, 0:ntile, :], src.rearrange("(t p) d -> p t d", p=128),
                   writes=[('xres', t) for t in range(ntile)])
        for t in range(ntile):
            self.make_xT(t)

    def store_y(self, dst, ntile):
        self.S.dma('act', dst.rearrange("(t p) d -> p t d", p=128), self.xres[:, 0:ntile, :],
                   reads=[('xres', t) for t in range(ntile)], writes=[('yout', id(dst))])

    def make_xT(self, t):
        S = self.S
        self.xbn = 0
        xb = self.xb_tmp[self.xbn]
        xbk = ('xb_tmp', self.xbn)
        S.op('act', lambda e: e.copy(out=xb[:], in_=self.xres[:, t, :]), reads=[('xres', t)], writes=[xbk])
        ps, pk = self.ps()
        psb = ps[:].bitcast(BF16)
        for k in range(8):
            S.op('pe', lambda e: e.transpose(out=psb[:, k * 128:(k + 1) * 128], in_=xb[:, k * 128:(k + 1) * 128],
                                             identity=self.identb[:]),
                 reads=[xbk, 'identb'], writes=[pk])
        S.op('dve', lambda e: e.tensor_copy(out=self.xT[:, :, t * 128:(t + 1) * 128],
                                            in_=psb.rearrange("p (k c) -> p k c", k=8)),
             reads=[pk], writes=[('xT', t)])

    def wtile(self, wname, l, cols, rows=(0, 8)):
        S = self.S
        i = self.wn
        self.wn = (self.wn + 1) % NWBUF
        buf = self.wbuf[i]
        src = self.W[wname][l]
        k0, k1 = rows
        off = 0
        keys = []
        for ci, (c0, n) in enumerate(cols):
            key = ('wbuf', i) if ci == 0 else ('wbufx', i, ci)
            S.dma('sp', buf[:, 0:k1 - k0, off:off + n],
                  src[k0 * 128:k1 * 128, c0:c0 + n].rearrange("(k p) c -> p k c", p=128), writes=[key])
            keys.append(key)
            off += n
        return buf, keys

    def layer(self, l, stream, g, ngroups):
        S = self.S
        nt = 4 if stream == 'P' else 1
        N = nt * 128
        self.attention(l, stream, g, ngroups, nt, N)
        self.chk('attn')
        self.chk(stream + '_attn')
        self.rglru(l, stream, g, ngroups, nt, N)
        self.chk('rglru')
        self.chk(stream + '_rglru')
        self.rwkv(l, stream, g, ngroups, nt, N)
        self.chk('rwkv')
        self.chk(stream + '_rwkv')
        self.dense_tail(l, stream, nt, N)
        self.chk('dense')

    def load_ln(self, l):
        S, I = self.S, self.I
        for i, nm in enumerate(['ln1_g', 'ln1_b', 'ln2_g', 'ln2_b']):
            S.dma('act', self.lnbuf[:, i, :], I[nm][l:l + 1, :].broadcast_to([128, D]), writes=[('lnbuf', i)])

    def layer_norm_tile(self, ph, t, which):
        S = self.S
        x = self.xres[:, t, :]
        key = ('xres', t)
        st = ph.T("lnst", [128, 2, 6], F32)
        mv = ph.T("lnmv", [128, 2], F32)
        rs = ph.T("lnrs", [128, 1], F32)
        for hf in range(2):
            S.op('dve', lambda e: e.bn_stats(out=st[:, hf, :], in_=self.xres[:, t, hf * 512:(hf + 1) * 512]),
                 reads=[key], writes=['lnst'])
        S.op('dve', lambda e: e.bn_aggr(out=mv[:], in_=st[:].rearrange("p a b -> p (a b)")), reads=['lnst'], writes=['lnmv'])
        S.op('act', lambda e: e.activation(out=rs[:], in_=mv[:, 1:2], func=AF.Sqrt, bias=self.epsln[:, 0:1], scale=1.0),
             reads=['lnmv', 'epsln'], writes=['lnrs'])
        S.op('dve', lambda e: e.reciprocal(out=rs[:], in_=rs[:]), reads=['lnrs'], writes=['lnrs'])
        S.op('dve', lambda e: e.tensor_scalar(out=x, in0=x, scalar1=mv[:, 0:1], scalar2=rs[:, 0:1],
                                              op0=ALU.subtract, op1=ALU.mult), reads=[key, 'lnmv', 'lnrs'], writes=[key])
        S.op('pool', lambda e: e.tensor_tensor(out=x, in0=x, in1=self.lnbuf[:, 2 * which, :], op=ALU.mult),
             reads=[key, ('lnbuf', 2 * which)], writes=[key])
        S.op('pool', lambda e: e.tensor_tensor(out=x, in0=x, in1=self.lnbuf[:, 2 * which + 1, :], op=ALU.add),
             reads=[key, ('lnbuf', 2 * which + 1)], writes=[key])

    def dense_tail(self, l, stream, nt, N):
        S = self.S
        ph = Phase(self)
        self.lnbuf = ph.T("lnbuf", [128, 4, D], F32)
        self.load_ln(l)
        self.mixT = ph.T("mixT", [128, 8, N], BF16)
        for j in range(8):
            S.op('act', lambda e: e.copy(out=self.mixT[:, j, 0:N], in_=self.mixacc[:, j, 0:N]),
                 reads=[('mixacc', j)], writes=[('mixT', j)])
        mkeys = [('mixT', j) for j in range(8)]
        wbs = [self.wtile('w_out', l, [(half * 512, 512)]) for half in range(2)]
        for t in range(nt):
            for half in range(2):
                wb, wk = wbs[half]
                ps, pk = self.ps()
                for k in range(8):
                    self.mm(ps[:, :], self.mixT[:, k, t * 128:(t + 1) * 128], wb[:, k, :], k == 0, k == 7,
                            reads=wk + [('mixT', k)], writes=[pk])
                xs = self.xres[:, t, half * 512:(half + 1) * 512]
                S.op('dve', lambda e: e.scalar_tensor_tensor(out=xs, in0=xs, scalar=ALPHA, in1=ps[:, :],
                                                             op0=ALU.mult, op1=ALU.add),
                     reads=[pk, ('xres', t)], writes=[('xres', t)])
            self.layer_norm_tile(ph, t, 0)
            self.make_xT(t)
        self.dump('x1', self.xres[:, 0:nt, :], [128, nt, D], [('xres', t) for t in range(nt)])
        hT = ph.T("hT", [128, 22, N], BF16)
        sg = ph.T("sgt", [128, N], F32)
        xkeys = [('xT', t) for t in range(nt)]
        NB = DFF // 128
        for j0 in range(0, NB, 4):
            nb = min(4, NB - j0)
            wg, wgk = self.wtile('w_gu', l, [(j0 * 128, nb * 128)])
            wu, wuk = self.wtile('w_gu', l, [(DFF + j0 * 128, nb * 128)])
            for b in range(nb):
                j = j0 + b
                psg, pgk = self.ps()
                for k in range(8):
                    self.mm(psg[:, 0:N], wg[:, k, b * 128:(b + 1) * 128], self.xT[:, k, 0:N], k == 0, k == 7,
                            reads=wgk + xkeys, writes=[pgk])
                psu, puk = self.ps()
                for k in range(8):
                    self.mm(psu[:, 0:N], wu[:, k, b * 128:(b + 1) * 128], self.xT[:, k, 0:N], k == 0, k == 7,
                            reads=wuk + xkeys, writes=[puk])
                S.op('act', lambda e: e.activation(out=sg[:], in_=psg[:, 0:N], func=AF.Silu), reads=[pgk], writes=['sgt'])
                S.op('dve', lambda e: e.tensor_tensor(out=hT[:, j, :], in0=sg[:], in1=psu[:, 0:N], op=ALU.mult),
                     reads=['sgt', puk], writes=[('hT', j)])
        for half in range(2):
            accs = [self.ps() for _ in range(nt)]
            pieces = [(0, 8), (8, 16), (16, 22)]
            for pi, (k0, k1) in enumerate(pieces):
                wb, wk = self.wtile('w_down', l, [(half * 512, 512)], rows=(k0, k1))
                for t in range(nt):
                    ps, pk = accs[t]
                    for k in range(k0, k1):
                        self.mm(ps[:, :], hT[:, k, t * 128:(t + 1) * 128], wb[:, k - k0, :], k == 0, k == NB - 1,
                                reads=wk + [('hT', k)], writes=[pk])
            for t in range(nt):
                ps, pk = accs[t]
                xs = self.xres[:, t, half * 512:(half + 1) * 512]
                S.op('dve', lambda e: e.scalar_tensor_tensor(out=xs, in0=xs, scalar=ALPHA, in1=ps[:, :],
                                                             op0=ALU.mult, op1=ALU.add),
                     reads=[pk, ('xres', t)], writes=[('xres', t)])
        for t in range(nt):
            self.layer_norm_tile(ph, t, 1)
            if l < self.depth - 1:
                self.make_xT(t)
        self.dump('x2', self.xres[:, 0:nt, :], [128, nt, D], [('xres', t) for t in range(nt)])
        ph.close()

    def interleave(self, gens):
        gens = list(gens)
        while gens:
            for g_ in list(gens):
                try:
                    next(g_)
                except StopIteration:
                    gens.remove(g_)

    def interleave_staged(self, gens):
        gens = list(gens)
        waiting = []
        active = list(gens)
        while active:
            for g_ in list(active):
                try:
                    r = next(g_)
                except StopIteration:
                    active.remove(g_)
                    continue
                if r == 'CHAIN':
                    active.remove(g_)
                    waiting.append(g_)
        waiting.sort(key=lambda g_: gens.index(g_))
        post = []
        for g_ in waiting:
            while True:
                try:
                    r = next(g_)
                except StopIteration:
                    break
                if r == 'POST':
                    post.append(g_)
                    break
        self.interleave(post)

    def rglru(self, l, stream, g, ngroups, nt, N):
        S, I = self.S, self.I
        ph = Phase(self)
        nseq, T = (1, 512) if stream == 'P' else (NSEQ_S, T_S)
        convst, hst = self.convst[stream], self.hst[stream]
        ckey, hkey = ('convst', stream), ('hst', stream)
        xkeys = [('xT', t) for t in range(nt)]
        NW = 4
        xrh = [ph.T("xrh%d" % i, [128, nseq, 3 + T], F32) for i in range(NW)]
        xc = [ph.T("xc%d" % i, [128, nseq, T], F32) for i in range(NW)]
        xcb = [ph.T("xcb%d" % i, [128, N], BF16) for i in range(NW)]
        rr = [ph.T("rg_r%d" % i, [128, N], F32) for i in range(NW)]
        ii = [ph.T("rg_i%d" % i, [128, N], F32) for i in range(NW)]
        aa = [ph.T("rg_a%d" % i, [128, nseq, T], F32) for i in range(NW)]
        uu = [ph.T("rg_u%d" % i, [128, nseq, T], F32) for i in range(NW)]
        hh = ph.T("rg_h", [128, 4, nseq, T], F32)
        ph.names += ['xrh', 'xc', 'xcb', 'rg_r', 'rg_i', 'rg_a', 'rg_u']
        wa_l = ph.T("wa_bd", [128, 2, 8, 128], BF16)
        S.dma('act', wa_l[:], self.scr_wabd[:, l], reads=['scr_wabd'], writes=['wa_bd'])
        flat = lambda t3: t3[:].rearrange("p n t -> p (n t)")

        def block(half, b, wb, wk, wg, wgk):
            j = half * 4 + b
            K_ = lambda nm: (nm, b)
            ps, pk = self.ps()
            for k in range(8):
                self.mm(ps[:, 0:N], wb[:, k, b * 128:(b + 1) * 128], self.xT[:, k, 0:N], k == 0, k == 7,
                        reads=wk + xkeys, writes=[pk])
            yield
            S.op('act', lambda e: e.copy(out=xrh[b][:, :, 3:3 + T], in_=ps[:, 0:N].rearrange("p (n t) -> p n t", n=nseq)),
                 reads=[pk], writes=[K_('xrh')])
            S.op('pool', lambda e: e.tensor_copy(out=xrh[b][:, :, 0:3], in_=convst[:, l, j, :, :]), reads=[ckey], writes=[K_('xrh')])
            yield
            cw = self.p_cw
            S.op('dve', lambda e: e.tensor_scalar(out=xc[b][:], in0=xrh[b][:, :, 0:T], scalar1=cw[:, l, 0, j:j + 1],
                                                  scalar2=self.p_cb[:, l, j:j + 1], op0=ALU.mult, op1=ALU.add),
                 reads=[K_('xrh'), 'p_cw', 'p_cb'], writes=[K_('xc')])
            yield
            for i in range(1, 4):
                S.op('dve', lambda e: e.scalar_tensor_tensor(out=xc[b][:], in0=xrh[b][:, :, i:i + T], scalar=cw[:, l, i, j:j + 1],
                                                             in1=xc[b][:], op0=ALU.mult, op1=ALU.add),
                     reads=[K_('xrh'), 'p_cw', K_('xc')], writes=[K_('xc')])
                yield
            S.op('pool', lambda e: e.tensor_copy(out=convst[:, l, j, :, :], in_=xrh[b][:, :, T:T + 3]), reads=[K_('xrh')], writes=[ckey])
            S.op('act', lambda e: e.copy(out=xcb[b][:], in_=flat(xc[b])), reads=[K_('xc')], writes=[K_('xcb')])
            yield
            pss = []
            for gi in range(2):
                ps2, pk2 = self.ps()
                self.mm(ps2[:, 0:N], wa_l[:, gi, j, :], xcb[b][:], True, True, reads=['wa_bd', K_('xcb')], writes=[pk2])
                pss.append((ps2, pk2))
            yield
            for gi, (dst, dk, bias) in enumerate(((rr[b], K_('rg_r'), self.p_ba), (ii[b], K_('rg_i'), self.p_bx))):
                ps2, pk2 = pss[gi]
                S.op('act', lambda e: e.activation(out=dst[:], in_=ps2[:, 0:N], func=AF.Sigmoid, bias=bias[:, l, j:j + 1], scale=1.0),
                     reads=[pk2, 'p_ba', 'p_bx'], writes=[dk])
            yield
            S.op('act', lambda e: e.activation(out=flat(aa[b]), in_=rr[b][:], func=AF.Exp, scale=self.p_c8[:, l, j:j + 1]),
                 reads=[K_('rg_r'), 'p_c8'], writes=[K_('rg_a')])
            S.op('pool', lambda e: e.tensor_tensor(out=ii[b][:], in0=ii[b][:], in1=flat(xc[b]), op=ALU.mult), reads=[K_('rg_i'), K_('xc')], writes=[K_('rg_i')])
            yield
            S.op('dve', lambda e: e.tensor_tensor(out=rr[b][:], in0=flat(aa[b]), in1=flat(aa[b]), op=ALU.mult), reads=[K_('rg_a')], writes=[K_('rg_r')])
            yield
            S.op('dve', lambda e: e.tensor_scalar(out=rr[b][:], in0=rr[b][:], scalar1=-1.0, scalar2=1.0, op0=ALU.mult, op1=ALU.add),
                 reads=[K_('rg_r')], writes=[K_('rg_r')])
            yield
            S.op('dve', lambda e: e.tensor_scalar(out=rr[b][:], in0=rr[b][:], scalar1=1e-30, scalar2=None, op0=ALU.max), reads=[K_('rg_r')], writes=[K_('rg_r')])
            yield
            S.op('act', lambda e: e.activation(out=rr[b][:], in_=rr[b][:], func=AF.Sqrt), reads=[K_('rg_r')], writes=[K_('rg_r')])
            yield
            S.op('dve', lambda e: e.tensor_tensor(out=flat(uu[b]), in0=ii[b][:], in1=rr[b][:], op=ALU.mult), reads=[K_('rg_i'), K_('rg_r')], writes=[K_('rg_u')])
            yield
            for n in range(nseq):
                S.op('dve', lambda e: e.tensor_tensor_scan(out=hh[:, b, n, :], data0=aa[b][:, n, :], data1=uu[b][:, n, :],
                                                           initial=hst[:, l, j, n:n + 1], op0=ALU.mult, op1=ALU.add),
                     reads=[K_('rg_a'), K_('rg_u'), hkey], writes=[('rg_h', b)])
            yield
            S.op('pool', lambda e: e.tensor_copy(out=hst[:, l, j, :], in_=hh[:, b, :, T - 1]), reads=[('rg_h', b)], writes=[hkey])
            ps, pk = self.ps()
            for k in range(8):
                self.mm(ps[:, 0:N], wg[:, k, b * 128:(b + 1) * 128], self.xT[:, k, 0:N], k == 0, k == 7,
                        reads=wgk + xkeys, writes=[pk])
            yield
            gbt = ii[b]
            S.op('act', lambda e: e.activation(out=gbt[:], in_=ps[:, 0:N], func=AF.Sigmoid), reads=[pk], writes=[K_('rg_i')])
            yield
            S.op('dve', lambda e: e.tensor_tensor(out=gbt[:], in0=gbt[:], in1=hh[:, b, :, :].rearrange("p n t -> p (n t)"), op=ALU.mult),
                 reads=[K_('rg_i'), ('rg_h', b)], writes=[K_('rg_i')])
            yield
            S.op('pool', lambda e: e.tensor_tensor(out=self.mixacc[:, j, 0:N], in0=self.mixacc[:, j, 0:N], in1=gbt[:], op=ALU.add),
                 reads=[K_('rg_i'), ('mixacc', j)], writes=[('mixacc', j)])
            yield

        for half in range(2):
            wb, wk = self.wtile('w_in', l, [(OFF_RNN + half * 512, 512)])
            wg, wgk = self.wtile('w_in', l, [(OFF_GATE + D + half * 512, 512)])
            self.interleave([block(half, b, wb, wk, wg, wgk) for b in range(4)])
        self.dump('mixAB' if stream == 'P' else 's_mixAB', self.mixacc[:, :, 0:N], [128, 8, N], [('mixacc', j) for j in range(8)])
        ph.close()

    def rwkv(self, l, stream, g, ngroups, nt, N):
        S, I = self.S, self.I
        ph = Phase(self)
        is_s = stream == 'S'
        nseq, T = (NSEQ_S, T_S) if is_s else (1, 512)
        shst = self.shst[stream]
        skey = ('shst', stream)
        xkeys = [('xT', t) for t in range(nt)]
        pch = ph.T("pch", [128, nseq, 1 + T], F32)
        dd = ph.T("rw_d", [128, N], F32)
        lwla = ph.T("lwla", [128, N], BF16)
        siglg = ph.T("siglg", [128, 2, N], BF16)
        self.wlora_l = ph.T("wlora", [128, D], BF16)
        self.wgup_l = ph.T("wgup", [128, 2, D], BF16)
        S.dma('act', self.wlora_l[:], self.scr_wlora[:, l], reads=['scr_wlora'], writes=['wlora'])
        S.dma('act', self.wgup_l[:], self.scr_wgup[:, l], reads=['scr_wgup'], writes=['wgup'])
        lnw = ph.T("lnw_t", [128, D], F32)
        lnb = ph.T("lnb_t", [128, D], F32)
        S.dma('act', lnw[:], I['rw_lnw'][l:l + 1, :].broadcast_to([128, D]), writes=['lnw_t'])
        S.dma('act', lnb[:], I['rw_lnb'][l:l + 1, :].broadcast_to([128, D]), writes=['lnb_t'])
        flat = lambda t3: t3[:].rearrange("p n t -> p (n t)")

        def shifted(ps, pk, blk, w, out_ap, out_key, func=None):
            S.op('act', lambda e: e.copy(out=pch[0:w, :, 1:1 + T], in_=ps[0:w, 0:N].rearrange("p (n t) -> p n t", n=nseq)),
                 reads=[pk], writes=['pch'])
            S.op('pool', lambda e: e.tensor_copy(out=pch[0:w, :, 0], in_=shst[0:w, l, blk, :]), reads=[skey], writes=['pch'])
            S.op('pool', lambda e: e.tensor_copy(out=shst[0:w, l, blk, :], in_=pch[0:w, :, T]), reads=['pch'], writes=[skey])
            S.op('dve', lambda e: e.tensor_tensor(out=dd[0:w, :].rearrange("p (n t) -> p n t", n=nseq), in0=pch[0:w, :, 0:T],
                                                  in1=pch[0:w, :, 1:1 + T], op=ALU.subtract), reads=['pch'], writes=['rw_d'])
            if func is None:
                S.op('dve', lambda e: e.scalar_tensor_tensor(out=out_ap.rearrange("p (n t) -> p n t", n=nseq), in0=dd[0:w, :].rearrange("p (n t) -> p n t", n=nseq),
                                                             scalar=self.p_mu[0:w, l, blk:blk + 1], in1=pch[0:w, :, 1:1 + T],
                                                             op0=ALU.mult, op1=ALU.add), reads=['rw_d', 'pch', 'p_mu'], writes=[out_key])
            else:
                S.op('dve', lambda e: e.scalar_tensor_tensor(out=dd[0:w, :].rearrange("p (n t) -> p n t", n=nseq), in0=dd[0:w, :].rearrange("p (n t) -> p n t", n=nseq),
                                                             scalar=self.p_mu[0:w, l, blk:blk + 1], in1=pch[0:w, :, 1:1 + T],
                                                             op0=ALU.mult, op1=ALU.add), reads=['rw_d', 'pch', 'p_mu'], writes=['rw_d'])
                func(dd)

        wb, wk = self.wtile('w_in', l, [(OFF_RW + 3072, 288)])
        for bi, (c0, w) in enumerate(((0, 128), (128, 128), (256, 32))):
            ps, pk = self.ps()
            for k in range(8):
                self.mm(ps[0:w, 0:N], wb[:, k, c0:c0 + w], self.xT[:, k, 0:N], k == 0, k == 7, reads=wk + xkeys, writes=[pk])
            if bi == 0:
                def f0(d_):
                    S.op('act', lambda e: e.activation(out=lwla[0:64, :], in_=d_[0:64, :], func=AF.Tanh), reads=['rw_d'], writes=['lwla'])
                    S.op('act', lambda e: e.copy(out=lwla[64:128, :], in_=d_[64:128, :]), reads=['rw_d'], writes=['lwla'])
                shifted(ps, pk, 24, 128, None, None, f0)
            else:
                def f1(d_, bi=bi, w=w):
                    S.op('act', lambda e: e.activation(out=siglg[0:w, bi - 1, :], in_=d_[0:w, :], func=AF.Sigmoid), reads=['rw_d'], writes=['siglg'])
                shifted(ps, pk, 24 + bi, w, None, None, f1)

        if not is_s:
            for rnd in range(4):
                self.rwkv_round(ph, l, stream, g, nt, N, rnd, shifted, lwla, siglg, lnw, lnb, xkeys, None)
        else:
            ytok = ph.T("s2_ytok", [128, D], F32)
            nats = []
            self.pre_nat = [(ph.T("w_vT%d" % r_, [128, 2, N], BF16), ph.T("w_pT%d" % r_, [128, 2, N], BF16),
                             ph.T("w_ggT%d" % r_, [128, 2, N], F32)) for r_ in range(4)]
            phA = Phase(self)
            V6 = phA.T("s2_V6", [128, 6, D], F32)
            for rnd in range(4):
                nats.append(self.rwkv_round(ph, l, stream, g, nt, N, rnd, shifted, lwla, siglg, lnw, lnb, xkeys, V6))
            for n in range(NSEQ_S):
                dstn = self.scr_v[l, n].rearrange("j hh q t d -> q t (j hh) d")
                for q in range(6):
                    S.dma('act', dstn[q], V6[n * 8:(n + 1) * 8, q, :].rearrange("t (jh d) -> t jh d", d=64),
                          reads=[('s2_V6', jx) for jx in range(8)], writes=[('scr_v', l, n, q)])
            phA.names.append('s2_V6')
            phA.close()
            svkeys = [('scr_v', l, n, q) for n in range(NSEQ_S) for q in range(6)]
            for hh in range(2):
                phB = Phase(self)
                Sst = phB.T("s2_S", [128, 64, 64], F32)
                tmp = phB.T("s2_tmp", [128, 64, 64], F32)
                vec = phB.T("s2_vec", [128, 6, T_S, 64], F32)
                yS = phB.T("s2_y", [128, T_S, 64], F32)
                sa = phB.T("s2_sa", [128, 64], F32)
                S.dma('act', Sst[:].rearrange("p v k -> p (v k)"),
                      self.I['swkv'][l].rearrange("n (j hh) v k -> hh (n j) (v k)", hh=2)[hh], writes=['s2_S'])
                S.dma('act', vec[:].rearrange("p q t d -> p (q t d)"),
                      self.scr_v[l].rearrange("n j hh q t d -> hh (n j) (q t d)")[hh], reads=svkeys, writes=['s2_vec'])
                bv = lambda q, t: vec[:, q, t, :].unsqueeze(1).to_broadcast([128, 64, 64])
                bo = lambda ap: ap.unsqueeze(2).to_broadcast([128, 64, 64])
                for t in range(T_S):
                    S.op('dve', lambda e: e.tensor_tensor(out=tmp[:], in0=Sst[:], in1=bv(4, t), op=ALU.mult), reads=['s2_S', 's2_vec'], writes=['s2_tmp'])
                    S.op('dve', lambda e: e.tensor_reduce(out=sa[:], in_=tmp[:], axis=AX.X, op=ALU.add), reads=['s2_tmp'], writes=['s2_sa'])
                    S.op('pool', lambda e: e.tensor_tensor(out=Sst[:], in0=Sst[:], in1=bv(1, t), op=ALU.mult), reads=['s2_S', 's2_vec'], writes=['s2_S'])
                    S.op('dve', lambda e: e.tensor_tensor(out=tmp[:], in0=bo(sa[:]), in1=bv(5, t), op=ALU.mult), reads=['s2_sa', 's2_vec'], writes=['s2_tmp'])
                    S.op('pool', lambda e: e.tensor_tensor(out=Sst[:], in0=Sst[:], in1=tmp[:], op=ALU.subtract), reads=['s2_S', 's2_tmp'], writes=['s2_S'])
                    S.op('dve', lambda e: e.tensor_tensor(out=tmp[:], in0=bo(vec[:, 3, t, :]), in1=bv(2, t), op=ALU.mult), reads=['s2_vec'], writes=['s2_tmp'])
                    S.op('pool', lambda e: e.tensor_tensor(out=Sst[:], in0=Sst[:], in1=tmp[:], op=ALU.add), reads=['s2_S', 's2_tmp'], writes=['s2_S'])
                    S.op('dve', lambda e: e.tensor_tensor(out=tmp[:], in0=Sst[:], in1=bv(0, t), op=ALU.mult), reads=['s2_S', 's2_vec'], writes=['s2_tmp'])
                    S.op('dve', lambda e: e.tensor_reduce(out=yS[:, t, :], in_=tmp[:], axis=AX.X, op=ALU.add), reads=['s2_tmp'], writes=['s2_y'])
                S.dma('act', self.O['swkv_o'][l].rearrange("n (j hh) v k -> hh (n j) (v k)", hh=2)[hh], Sst[:].rearrange("p v k -> p (v k)"),
                      reads=['s2_S'], writes=[('o_swkv', l, hh)])
                S.dma('act', self.scr_y[l].rearrange("n j hh t v -> hh (n j) (t v)")[hh], yS[:].rearrange("p t v -> p (t v)"),
                      reads=['s2_y'], writes=[('scr_y', l, hh)])
                phB.close()
            for n in range(NSEQ_S):
                S.dma('act', ytok[n * 8:(n + 1) * 8, :].rearrange("t (jh v) -> t jh v", v=64),
                      self.scr_y[l, n].rearrange("j hh t v -> t (j hh) v"), reads=[('scr_y', l, 0), ('scr_y', l, 1)], writes=[('s2_ytok', n)])
            for rnd in range(4):
                for _ in self.wkv_tile(ph, l, 2 * rnd, 0, nats[rnd], lnw, lnb,
                                       s2_y=(ytok[:, rnd * 256:(rnd + 1) * 256], [('s2_ytok', n) for n in range(NSEQ_S)])):
                    pass
        self.dump('mixed' if not is_s else 's_mixed', self.mixacc[:, :, 0:N], [128, 8, N], [('mixacc', j) for j in range(8)])
        ph.names.append('s2_ytok')
        ph.close()

    def rwkv_round(self, ph0, l, stream, g, nt, N, rnd, shifted, lwla, siglg, lnw, lnb, xkeys, V6):
        S, I = self.S, self.I
        is_s = stream == 'S'
        ph = Phase(self)
        j0 = 2 * rnd
        sf = str(rnd) if is_s else ''
        pho = ph0 if is_s else ph
        aT = ph.T("w_aT", [128, 2, N], BF16)
        rT = ph.T("w_rT", [128, 2, N], BF16)
        bT = ph.T("w_bT", [128, 2, N], BF16)
        kT = ph.T("w_kT", [128, 2, N], BF16)
        if is_s:
            vT, pT, ggT = self.pre_nat[rnd]
        else:
            vT = ph.T("w_vT", [128, 2, N], BF16)
            pT = ph.T("w_pT", [128, 2, N], BF16)
            ggT = ph.T("w_ggT", [128, 2, N], F32)
        gC = ph.T("w_gC", [128, 2, 4], F32)
        r32 = ph.T("w_r32", [128, N], F32)
        k32 = ph.T("w_k32", [128, N], F32)
        v32 = ph.T("w_v32", [128, N], F32)
        lw32 = ph.T("w_lw32", [128, N], F32)
        cl = ph.T("w_cl", [128, N], F32)
        a32 = ph.T("w_a32", [128, N], F32)
        kk32 = ph.T("w_kk32", [128, N], F32)
        t32 = ph.T("w_t32", [128, N], F32)
        tb = ph.T("w_tb", [128, N], BF16)
        eI = ph.T("w_eI", [128, N], F32)
        eN = ph.T("w_eN", [128, N], F32)
        c0 = OFF_RW + j0 * 128
        wb, wk = self.wtile('w_in', l, [(c0, 256), (c0 + D, 256)])
        wb2, wk2 = self.wtile('w_in', l, [(c0 + 2 * D, 256), (OFF_GATE + 2 * D + j0 * 128, 256)])
        for jj in range(2):
            j = j0 + jj
            for qi, (dst, dkey, wbx, wkx, coff, blk) in enumerate(((r32, 'w_r32', wb, wk, jj * 128, j),
                                                                  (k32, 'w_k32', wb, wk, 256 + jj * 128, 8 + j),
                                                                  (v32, 'w_v32', wb2, wk2, jj * 128, 16 + j))):
                ps, pk = self.ps()
                for k in range(8):
                    self.mm(ps[:, 0:N], wbx[:, k, coff:coff + 128], self.xT[:, k, 0:N], k == 0, k == 7, reads=wkx + xkeys, writes=[pk])
                shifted(ps, pk, blk, 128, dst[:], dkey)
            ps, pk = self.ps()
            self.mm(ps[:, 0:N], self.wlora_l[0:64, j * 128:(j + 1) * 128], lwla[0:64, :], True, True, reads=['wlora', 'lwla'], writes=[pk])
            S.op('act', lambda e: e.activation(out=lw32[:], in_=ps[:, 0:N], func=AF.Sigmoid, bias=self.p_w0[:, l, j:j + 1], scale=1.0),
                 reads=[pk, 'rw_w0'], writes=['w_lw32'])
            S.op('pool', lambda e: e.tensor_scalar(out=lw32[:], in0=lw32[:], scalar1=-math.exp(-0.5), scalar2=None, op0=ALU.mult),
                 reads=['w_lw32'], writes=['w_lw32'])
            ps, pk = self.ps()
            self.mm(ps[:, 0:N], self.wlora_l[64:128, j * 128:(j + 1) * 128], lwla[64:128, :], True, True, reads=['wlora', 'lwla'], writes=[pk])
            S.op('act', lambda e: e.activation(out=a32[:], in_=ps[:, 0:N], func=AF.Sigmoid, bias=self.p_a0[:, l, j:j + 1], scale=1.0),
                 reads=[pk, 'rw_a0'], writes=['w_a32'])
            S.op('dve', lambda e: e.tensor_scalar(out=kk32[:], in0=k32[:], scalar1=self.p_kk[:, l, j:j + 1], scalar2=None, op0=ALU.mult),
                 reads=['w_k32', 'rw_kk'], writes=['w_kk32'])
            S.op('act', lambda e: e.activation(out=tb[:], in_=kk32[:], func=AF.Square), reads=['w_kk32'], writes=['w_tb'])
            ps, pk = self.ps()
            self.mm(ps[:, 0:N], self.blkb[:, :], tb[:], True, True, reads=['blkb', 'w_tb'], writes=[pk])
            S.op('act', lambda e: e.activation(out=t32[:], in_=ps[:, 0:N], func=AF.Sqrt), reads=[pk], writes=['w_t32'])
            S.op('dve', lambda e: e.tensor_scalar(out=t32[:], in0=t32[:], scalar1=1e-12, scalar2=None, op0=ALU.max), reads=['w_t32'], writes=['w_t32'])
            S.op('dve', lambda e: e.reciprocal(out=t32[:], in_=t32[:]), reads=['w_t32'], writes=['w_t32'])
            S.op('dve', lambda e: e.tensor_tensor(out=kk32[:], in0=kk32[:], in1=t32[:], op=ALU.mult), reads=['w_kk32', 'w_t32'], writes=['w_kk32'])
            S.op('dve', lambda e: e.tensor_scalar(out=t32[:], in0=a32[:], scalar1=self.p_ka[:, l, j:j + 1], scalar2=self.p_omka[:, l, j:j + 1],
                                                  op0=ALU.mult, op1=ALU.add), reads=['w_a32', 'rw_ka', 'p_omka'], writes=['w_t32'])
            S.op('pool', lambda e: e.tensor_tensor(out=k32[:], in0=k32[:], in1=t32[:], op=ALU.mult), reads=['w_k32', 'w_t32'], writes=['w_k32'])
            if is_s:
                S.op('pool', lambda e: e.tensor_tensor(out=t32[:], in0=r32[:], in1=k32[:], op=ALU.mult), reads=['w_r32', 'w_k32'], writes=['w_t32'])
                S.op('dve', lambda e: e.tensor_scalar(out=pT[:, jj, :], in0=t32[:], scalar1=self.p_rk[:, l, j:j + 1], scalar2=None, op0=ALU.mult),
                     reads=['w_t32', 'rw_rk'], writes=['w_pT' + sf])
                S.op('act', lambda e: e.copy(out=vT[:, jj, :], in_=v32[:]), reads=['w_v32'], writes=['w_vT' + sf])
                S.op('act', lambda e: e.activation(out=eI[:], in_=lw32[:], func=AF.Exp), reads=['w_lw32'], writes=['w_eI'])
                S.op('dve', lambda e: e.tensor_tensor(out=eN[:], in0=kk32[:], in1=a32[:], op=ALU.mult), reads=['w_kk32', 'w_a32'], writes=['w_eN'])
                psa, pka = self.ps()
                for q, (src, skey) in enumerate(((r32, 'w_r32'), (eI, 'w_eI'), (k32, 'w_k32'), (v32, 'w_v32'))):
                    S.op('pe', lambda e: e.transpose(out=psa[:, q * 128:(q + 1) * 128], in_=src[:, :], identity=self.identf[:]),
                         reads=[skey, 'identf'], writes=[pka])
                psb_, pkb = self.ps()
                for q, (src, skey) in enumerate(((kk32, 'w_kk32'), (eN, 'w_eN'))):
                    S.op('pe', lambda e: e.transpose(out=psb_[:, q * 128:(q + 1) * 128], in_=src[:, :], identity=self.identf[:]),
                         reads=[skey, 'identf'], writes=[pkb])
                S.op('act', lambda e: e.copy(out=V6[:, 0:4, j * 128:(j + 1) * 128], in_=psa[:, :].rearrange("p (q c) -> p q c", q=4)),
                     reads=[pka], writes=[('s2_V6', j)])
                S.op('dve', lambda e: e.tensor_copy(out=V6[:, 4:6, j * 128:(j + 1) * 128], in_=psb_[:, 0:256].rearrange("p (q c) -> p q c", q=2)),
                     reads=[pkb], writes=[('s2_V6', j)])
            if not is_s:
                S.op('dve', lambda e: e.tensor_tensor_scan(out=cl[:], data0=self.rmaskP[:, 0:N], data1=lw32[:], initial=0.0, op0=ALU.mult, op1=ALU.add),
                     reads=['rmaskP', 'w_lw32'], writes=['w_cl'])
                S.op('act', lambda e: e.activation(out=eI[:], in_=cl[:], func=AF.Exp), reads=['w_cl'], writes=['w_eI'])
                S.op('act', lambda e: e.activation(out=eN[:], in_=cl[:], func=AF.Exp, scale=-1.0), reads=['w_cl'], writes=['w_eN'])
                S.op('pool', lambda e: e.tensor_copy(out=gC[:, jj, :], in_=eI[:, 127:N:128]), reads=['w_eI'], writes=['w_gC'])
                S.op('dve', lambda e: e.tensor_tensor(out=rT[:, jj, :], in0=r32[:], in1=eI[:], op=ALU.mult), reads=['w_r32', 'w_eI'], writes=['w_rT'])
                S.op('pool', lambda e: e.tensor_tensor(out=t32[:], in0=r32[:], in1=k32[:], op=ALU.mult), reads=['w_r32', 'w_k32'], writes=['w_t32'])
                S.op('dve', lambda e: e.tensor_scalar(out=pT[:, jj, :], in0=t32[:], scalar1=self.p_rk[:, l, j:j + 1], scalar2=None, op0=ALU.mult),
                     reads=['w_t32', 'rw_rk'], writes=['w_pT' + sf])
                S.op('dve', lambda e: e.tensor_tensor(out=kT[:, jj, :], in0=k32[:], in1=eN[:], op=ALU.mult), reads=['w_k32', 'w_eN'], writes=['w_kT'])
                S.op('pool', lambda e: e.tensor_tensor(out=t32[:], in0=kk32[:], in1=a32[:], op=ALU.mult), reads=['w_kk32', 'w_a32'], writes=['w_t32'])
                S.op('dve', lambda e: e.tensor_tensor(out=bT[:, jj, :], in0=t32[:], in1=eN[:], op=ALU.mult), reads=['w_t32', 'w_eN'], writes=['w_bT'])
                S.op('pool', lambda e: e.tensor_tensor(out=t32[:], in0=cl[:], in1=lw32[:], op=ALU.subtract), reads=['w_cl', 'w_lw32'], writes=['w_t32'])
                S.op('act', lambda e: e.activation(out=t32[:], in_=t32[:], func=AF.Exp), reads=['w_t32'], writes=['w_t32'])
                S.op('dve', lambda e: e.scalar_tensor_tensor(out=aT[:, jj, :], in0=kk32[:], scalar=-1.0, in1=t32[:], op0=ALU.mult, op1=ALU.mult),
                     reads=['w_kk32', 'w_t32'], writes=['w_aT'])
                S.op('act', lambda e: e.copy(out=vT[:, jj, :], in_=v32[:]), reads=['w_v32'], writes=['w_vT' + sf])
            ps, pk = self.ps()
            self.mm(ps[:, 0:N], self.wgup_l[:, 0, j * 128:(j + 1) * 128], siglg[:, 0, :], True, False, reads=['wgup', 'siglg'], writes=[pk])
            self.mm(ps[:, 0:N], self.wgup_l[0:32, 1, j * 128:(j + 1) * 128], siglg[0:32, 1, :], False, True, reads=['wgup', 'siglg'], writes=[pk])
            ps2, pk2 = self.ps()
            for k in range(8):
                self.mm(ps2[:, 0:N], wb2[:, k, 256 + jj * 128:256 + (jj + 1) * 128], self.xT[:, k, 0:N], k == 0, k == 7, reads=wk2 + xkeys, writes=[pk2])
            S.op('act', lambda e: e.activation(out=t32[:], in_=ps2[:, 0:N], func=AF.Sigmoid), reads=[pk2], writes=['w_t32'])
            S.op('dve', lambda e: e.tensor_tensor(out=ggT[:, jj, :], in0=ps[:, 0:N], in1=t32[:], op=ALU.mult), reads=[pk, 'w_t32'], writes=['w_ggT' + sf])
        nat = dict(aT=aT, rT=rT, bT=bT, kT=kT, vT=vT, pT=pT, ggT=ggT, gC=gC, sf=sf)
        if not is_s:
            for t0_ in range(0, nt, 2):
                self.interleave_staged([self.wkv_tile(ph, l, j0, t0_ + i_, nat, lnw, lnb, slot=i_) for i_ in range(2)])
        ph.close()
        return nat

    def wkv_tile(self, ph0, l, j0, t, nat, lnw, lnb, s2_y=None, slot=0):
        S0 = self.S
        sl = '_s%d' % slot
        km = lambda ks: [((k + sl) if isinstance(k, str) and k.startswith('k_') else k) for k in ks]

        class _S:
            @staticmethod
            def op(e, fn_, reads=(), writes=()):
                return S0.op(e, fn_, reads=km(reads), writes=km(writes))
        S = _S()
        mm0 = self.mm

        def MM(out, lhsT, rhs, start, stop, reads, writes):
            return mm0(out, lhsT, rhs, start, stop, reads=km(reads), writes=km(writes))

        class _Reuse:
            def T(_s, name, shape, dt):
                name = name + sl
                cache = ph0.__dict__.setdefault('_tile_cache', {})
                if name not in cache:
                    cache[name] = ph0.T(name, shape, dt)
                return cache[name]

            def close(_s):
                pass
        ph = _Reuse()
        aT, rT, bT, kT, vT, pT, ggT, gC = (nat[k] for k in ('aT', 'rT', 'bT', 'kT', 'vT', 'pT', 'ggT', 'gC'))
        sf = nat['sf']
        cs = slice(t * 128, (t + 1) * 128)
        tok = ph.T("k_tok", [128, 4, 2, 128], BF16)
        ps, pk = self.ps()
        psb = ps[:].bitcast(BF16)
        for qi, (src, skey) in enumerate(((vT, 'w_vT' + sf), (bT, 'w_bT'), (kT, 'w_kT'), (pT, 'w_pT' + sf))):
            if s2_y is not None and qi in (1, 2):
                src, skey = vT, 'w_vT' + sf
            for jj in range(2):
                S.op('pe', lambda e: e.transpose(out=psb[:, (qi * 2 + jj) * 128:(qi * 2 + jj + 1) * 128], in_=src[:, jj, cs],
                                                 identity=self.identb[:]), reads=[skey, 'identb'], writes=[pk])
                yield
        S.op('act', lambda e: e.copy(out=tok[:].rearrange("p q j c -> p (q j c)"), in_=psb), reads=[pk], writes=['k_tok'])
        yield
        if s2_y is None:
            blks = {}
            for nm, (src, skey) in (('a', (aT, 'w_aT')), ('r', (rT, 'w_rT')), ('b', (bT, 'w_bT'))):
                isnew = ("k_blk" + nm + sl) not in ph0.__dict__.get('_tile_cache', {})
                bl = ph.T("k_blk" + nm, [128, 2, 2, 128], BF16)
                if isnew:
                    S.op('pool', lambda e: e.memset(bl[:], 0.0), writes=['k_blk' + nm])
                    yield
                for hh in range(2):
                    eng = 'act' if hh else 'pool'
                    if eng == 'act':
                        S.op('act', lambda e: e.copy(out=bl[hh * 64:(hh + 1) * 64, :, hh, :], in_=src[hh * 64:(hh + 1) * 64, :, cs]),
                             reads=[skey, 'k_blk' + nm], writes=['k_blk' + nm])
                        yield
                    else:
                        S.op('pool', lambda e: e.tensor_copy(out=bl[hh * 64:(hh + 1) * 64, :, hh, :], in_=src[hh * 64:(hh + 1) * 64, :, cs]),
                             reads=[skey, 'k_blk' + nm], writes=['k_blk' + nm])
                        yield
                blks[nm] = bl
            mats = {}
            specs = (('AabT', bT, 'w_bT', 'a', 0), ('ArbT', bT, 'w_bT', 'r', 1), ('AakT', kT, 'w_kT', 'a', 0),
                     ('ArkT', kT, 'w_kT', 'r', 1), ('Aab', aT, 'w_aT', 'b', 2))
            for mi, (nm, lsrc, lkey, rb, mk) in enumerate(specs):
                ps, pk = self.ps()
                for jj in range(2):
                    MM(ps[:, jj * 256:(jj + 1) * 256], lsrc[:, jj, cs], blks[rb][:, jj, :, :].rearrange("p h c -> p (h c)"),
                            True, True, reads=[lkey, 'k_blk' + rb], writes=[pk])
                m = ph.T("k_" + nm, [128, 4, 128], BF16)
                S.op('dve', lambda e: e.tensor_tensor(out=m[:].rearrange("p h c -> p (h c)"), in0=ps[:, :], in1=self.maskb[:, mk, :], op=ALU.mult),
                     reads=[pk, 'maskb'], writes=['k_' + nm])
                yield
                mats[nm] = m
            Pm = [ph.T("k_P%d" % i, [128, 4, 128], BF16) for i in range(2)]
            Am = [ph.T("k_A%d" % i, [128, 4, 128], BF16) for i in range(2)]
            Bm = [ph.T("k_B%d" % i, [128, 4, 128], BF16) for i in range(2)]
            fl = lambda m_: m_[:].rearrange("p h c -> p (h c)")
            S.op('pool', lambda e: e.tensor_tensor(out=fl(Pm[0]), in0=fl(mats['AabT']), in1=self.maskb[:, 3, :], op=ALU.add),
                 reads=['k_AabT', 'maskb'], writes=['k_P0'])
            yield
            Acur, Akey, Bcur, Bkey = mats['Aab'], 'k_Aab', mats['AabT'], 'k_AabT'
            pcur = 0
            pend = None

            def p_update(Ap, Apk, pcur_):
                ps_, pk_ = self.ps()
                for h in range(4):
                    MM(ps_[:, h * 128:(h + 1) * 128], Ap[:, h, :], Pm[pcur_][:, h, :], True, True, reads=[Apk, 'k_P%d' % pcur_], writes=[pk_])
                S.op('dve', lambda e: e.tensor_tensor(out=fl(Pm[1 - pcur_]), in0=ps_[:, :], in1=fl(Pm[pcur_]), op=ALU.add),
                     reads=[pk_, 'k_P%d' % pcur_], writes=['k_P%d' % (1 - pcur_)])
                return 1 - pcur_
            Am3 = Am + [ph.T("k_A2", [128, 4, 128], BF16)]
            for step in range(6):
                An, Ank = Am3[step % 3], 'k_A%d' % (step % 3)
                psA, pkA = self.ps()
                for h in range(4):
                    MM(psA[:, h * 128:(h + 1) * 128], Bcur[:, h, :], Acur[:, h, :], True, True, reads=[Akey, Bkey], writes=[pkA])
                if step < 5:
                    Bn, Bnk = Bm[step % 2], 'k_B%d' % (step % 2)
                    psB, pkB = self.ps()
                    for h in range(4):
                        MM(psB[:, h * 128:(h + 1) * 128], Acur[:, h, :], Bcur[:, h, :], True, True, reads=[Akey, Bkey], writes=[pkB])
                S.op('act', lambda e: e.copy(out=fl(An), in_=psA[:, :]), reads=[pkA], writes=[Ank])
                yield
                if step < 5:
                    S.op('dve', lambda e: e.tensor_copy(out=fl(Bn), in_=psB[:, :]), reads=[pkB], writes=[Bnk])
                    yield
                if pend is not None:
                    pcur = p_update(pend[0], pend[1], pcur)
                pend = (An, Ank)
                Acur, Akey = An, Ank
                if step < 5:
                    Bcur, Bkey = Bn, Bnk
            pcur = p_update(pend[0], pend[1], pcur)
            Pf, Pkey = Pm[pcur], 'k_P%d' % pcur
            yield 'CHAIN'
            Hb = self.Hbf
            Xs = ph.T("k_Xs", [128, 4, 64], BF16)
            Us = ph.T("k_Us", [128, 4, 64], BF16)
            psX, pkX = self.ps()
            for jj in range(2):
                j = j0 + jj
                MM(psX[:, jj * 128:(jj + 1) * 128], aT[:, jj, cs], Hb[:, l, j, :], True, False, reads=['w_aT', ('Hbf', l, j)], writes=[pkX])
                for hh in range(2):
                    h = jj * 2 + hh
                    MM(psX[:, h * 64:(h + 1) * 64], mats['AakT'][:, h, :], tok[:, 0, jj, hh * 64:(hh + 1) * 64], False, hh == 1,
                            reads=['k_AakT', 'k_tok'], writes=[pkX])
            S.op('act', lambda e: e.copy(out=Xs[:].rearrange("p h c -> p (h c)"), in_=psX[:, 0:256]), reads=[pkX], writes=['k_Xs'])
            yield
            psU, pkU = self.ps()
            for h in range(4):
                MM(psU[:, h * 64:(h + 1) * 64], Pf[:, h, :], Xs[:, h, :], True, True, reads=[Pkey, 'k_Xs'], writes=[pkU])
            S.op('dve', lambda e: e.tensor_copy(out=Us[:].rearrange("p h c -> p (h c)"), in_=psU[:, 0:256]), reads=[pkU], writes=['k_Us'])
            yield
            psY, pkY = self.ps()
            for jj in range(2):
                j = j0 + jj
                MM(psY[:, jj * 128:(jj + 1) * 128], rT[:, jj, cs], Hb[:, l, j, :], True, False, reads=['w_rT', ('Hbf', l, j)], writes=[pkY])
                for hh in range(2):
                    h = jj * 2 + hh
                    MM(psY[:, h * 64:(h + 1) * 64], mats['ArbT'][:, h, :], Us[:, h, :], False, False, reads=['k_ArbT', 'k_Us'], writes=[pkY])
                    MM(psY[:, h * 64:(h + 1) * 64], mats['ArkT'][:, h, :], tok[:, 0, jj, hh * 64:(hh + 1) * 64], False, hh == 1,
                            reads=['k_ArkT', 'k_tok'], writes=[pkY])
            psH, pkH = self.ps()
            for jj in range(2):
                MM(psH[:, jj * 128:(jj + 1) * 128], tok[:, 1, jj, :], Us[:, 2 * jj:2 * jj + 2, :].rearrange("p h c -> p (h c)"), True, False,
                        reads=['k_tok', 'k_Us'], writes=[pkH])
                MM(psH[:, jj * 128:(jj + 1) * 128], tok[:, 2, jj, :], tok[:, 0, jj, :], False, True, reads=['k_tok'], writes=[pkH])
            hd = ph.T("k_hd", [128, 256], F32)
            S.op('dve', lambda e: e.tensor_tensor(out=hd[:], in0=psH[:, 0:256], in1=self.blkf[:], op=ALU.mult), reads=[pkH, 'blkf'], writes=['k_hd'])
            yield
            for jj in range(2):
                j = j0 + jj
                S.op('pool', lambda e: e.tensor_tensor(out=hd[:, jj * 128:(jj + 1) * 128], in0=hd[:, jj * 128:(jj + 1) * 128], in1=self.H32[:, l, j, :], op=ALU.add),
                     reads=['k_hd', ('H32', l, j)], writes=['k_hd'])
                yield
                S.op('dve', lambda e: e.tensor_scalar(out=self.H32[:, l, j, :], in0=hd[:, jj * 128:(jj + 1) * 128], scalar1=gC[:, jj, t:t + 1], scalar2=None, op0=ALU.mult),
                     reads=['k_hd', 'w_gC'], writes=[('H32', l, j)])
                yield
                S.op('act', lambda e: e.copy(out=Hb[:, l, j, :], in_=self.H32[:, l, j, :]), reads=[('H32', l, j)], writes=[('Hbf', l, j)])
                yield
        yield 'POST'
        y = ph.T("k_y", [128, 4, 64], F32)
        sq = ph.T("k_sq", [128, 4, 64], F32)
        st = ph.T("k_st", [128, 4], F32)
        st2 = ph.T("k_st2", [128, 4], F32)
        if s2_y is None:
            S.op('act', lambda e: e.copy(out=y[:].rearrange("p h c -> p (h c)"), in_=psY[:, 0:256]), reads=[pkY], writes=['k_y'])
            yield
        else:
            S.op('act', lambda e: e.copy(out=y[:].rearrange("p h c -> p (h c)"), in_=s2_y[0]), reads=s2_y[1], writes=['k_y'])
            yield
        S.op('dve', lambda e: e.tensor_reduce(out=st[:], in_=y[:], axis=AX.X, op=ALU.add), reads=['k_y'], writes=['k_st'])
        yield
        S.op('dve', lambda e: e.tensor_scalar(out=st[:], in0=st[:], scalar1=-1.0 / 64, scalar2=None, op0=ALU.mult), reads=['k_st'], writes=['k_st'])
        yield
        S.op('dve', lambda e: e.tensor_tensor(out=y[:], in0=y[:], in1=st[:].unsqueeze(2).to_broadcast([128, 4, 64]), op=ALU.add),
             reads=['k_y', 'k_st'], writes=['k_y'])
        yield
        S.op('pool', lambda e: e.tensor_tensor(out=sq[:], in0=y[:], in1=y[:], op=ALU.mult), reads=['k_y'], writes=['k_sq'])
        yield
        S.op('dve', lambda e: e.tensor_reduce(out=st2[:], in_=sq[:], axis=AX.X, op=ALU.add), reads=['k_sq'], writes=['k_st2'])
        yield
        S.op('act', lambda e: e.activation(out=st2[:], in_=st2[:], func=AF.Sqrt, bias=self.epsln[:, 1:2], scale=1.0 / 64),
             reads=['k_st2', 'epsln'], writes=['k_st2'])
        yield
        S.op('dve', lambda e: e.reciprocal(out=st2[:], in_=st2[:]), reads=['k_st2'], writes=['k_st2'])
        yield
        S.op('dve', lambda e: e.tensor_tensor(out=y[:], in0=y[:], in1=st2[:].unsqueeze(2).to_broadcast([128, 4, 64]), op=ALU.mult),
             reads=['k_y', 'k_st2'], writes=['k_y'])
        yield
        yf = y[:].rearrange("p h c -> p (h c)")
        fsl = slice(j0 * 128, j0 * 128 + 256)
        S.op('pool', lambda e: e.tensor_tensor(out=yf, in0=yf, in1=lnw[:, fsl], op=ALU.mult), reads=['k_y', 'lnw_t'], writes=['k_y'])
        yield
        S.op('pool', lambda e: e.tensor_tensor(out=yf, in0=yf, in1=lnb[:, fsl], op=ALU.add), reads=['k_y', 'lnb_t'], writes=['k_y'])
        yield
        S.op('dve', lambda e: e.tensor_reduce(out=st[:], in_=tok[:, 3, :, :].rearrange("p j (h c) -> p (j h) c", h=2), axis=AX.X, op=ALU.add),
             reads=['k_tok'], writes=['k_st'])
        yield
        S.op('dve', lambda e: e.tensor_tensor(out=sq[:], in0=tok[:, 0, :, :].rearrange("p j (h c) -> p (j h) c", h=2),
                                              in1=st[:].unsqueeze(2).to_broadcast([128, 4, 64]), op=ALU.mult), reads=['k_tok', 'k_st'], writes=['k_sq'])
        yield
        S.op('pool', lambda e: e.tensor_tensor(out=y[:], in0=y[:], in1=sq[:], op=ALU.add), reads=['k_y', 'k_sq'], writes=['k_y'])
        yield
        psT, pkT = self.ps()
        for jj in range(2):
            S.op('pe', lambda e: e.transpose(out=psT[:, jj * 128:(jj + 1) * 128], in_=yf[:, jj * 128:(jj + 1) * 128], identity=self.identf[:]),
                 reads=['k_y', 'identf'], writes=[pkT])
            yield
        og = ph.T("k_og", [128, 2, 128], F32)
        S.op('dve', lambda e: e.tensor_tensor(out=og[:], in0=psT[:, 0:256].rearrange("p (j c) -> p j c", j=2), in1=ggT[:, :, cs], op=ALU.mult),
             reads=[pkT, 'w_ggT' + sf], writes=['k_og'])
        yield
        S.op('pool', lambda e: e.tensor_tensor(out=self.mixacc[:, j0:j0 + 2, cs], in0=self.mixacc[:, j0:j0 + 2, cs], in1=og[:], op=ALU.add),
             reads=['k_og', ('mixacc', j0), ('mixacc', j0 + 1)], writes=[('mixacc', j0), ('mixacc', j0 + 1)])
        yield
        ph.close()

    def attention(self, l, stream, g, ngroups, nt, N):
        S, I = self.S, self.I
        ph = Phase(self)
        qT = ph.T("qT", [128, 8, N], BF16)
        ET = [ph.T("ET%d" % b, [128, 512], BF16) for b in range(2)]
        sb = ph.T("sbt", [128, 512], F32)
        den = ph.T("den", [128, 256], F32)
        osb = ph.T("osb", [128, 256], F32)
        if stream == 'P':
            self.biasP = ph.T("biasP", [128, 2, NH, 128], F32)
            S.dma('act', self.biasP[:], I['c_biasP'], writes=['biasP'])
        kv32_pre = ph.T("kv32", [128, 512], F32)
        if stream == 'P':
            kT = self.kT_p[l]
            vtok = self.vtok_p[l]
            kkey, vkey = ('kTp', l), ('vtokp', l)
        else:
            kT = ph.T("kTs", [128, KVH, 128 + 128], BF16)
            vtok = ph.T("vtoks", [128, 2, 256], BF16)
            kkey, vkey = 'kTs', 'vtoks'
        xkeys = [('xT', t) for t in range(nt)]
        for half in range(2):
            wb, wk = self.wtile('w_in', l, [(half * 512, 512)])
            for b in range(4):
                ps, pk = self.ps()
                for k in range(8):
                    self.mm(ps[:, 0:N], wb[:, k, b * 128:(b + 1) * 128], self.xT[:, k, 0:N], k == 0, k == 7,
                            reads=wk + xkeys, writes=[pk])
                self.evac_copy('act' if b % 2 else 'dve', qT[:, half * 4 + b, :], ps[:, 0:N], [pk], ['qT'], scale=0.125)
        self.chk('q')
        i = self.wn
        self.wn = (self.wn + 1) % NWBUF
        wb = self.wbuf[i]
        wk = [('wbuf', i)]
        srcw = self.W['w_in'][l]
        kv = srcw[:, OFF_K:OFF_K + 256].rearrange("(k p) (c d) -> p k c d", p=128, d=64)
        wbv = wb[:, :, :].rearrange("p k (c two d) -> p k c two d", two=2, d=64)
        for dup in range(2):
            for k in range(8):
                S.dma('sp', wbv[:, k, :, dup, :], kv[:, k, :, :],
                      writes=wk if (dup == 0 and k == 0) else [('wbufx', i, dup, k)])
        wk_all = wk + [('wbufx', i, dup, k) for dup in range(2) for k in range(8) if not (dup == 0 and k == 0)]
        for c in range(KVH):
            ps, pk = self.ps()
            for k in range(8):
                self.mm(ps[:, 0:N], wb[:, k, c * 128:(c + 1) * 128], self.xT[:, k, 0:N], k == 0, k == 7,
                        reads=wk_all + xkeys, writes=[pk])
            self.evac_copy('act' if c % 2 else 'dve', kT[:, c, 128:128 + N], ps[:, 0:N], [pk], [kkey])
        self.chk('kdup')
        wb, wk = self.wtile('w_in', l, [(OFF_K, 512)])
        last_tile_needs_k = (stream == 'S') or (g == ngroups - 1)
        kv32 = None
        import os
        kvvar = os.environ.get('KVVAR', '')
        for t in range(nt):
            if kvvar == 'dmaonly':
                break
            ps, pk = self.ps()
            for k in range(8):
                self.mm(ps[:, :], self.xT[:, k, t * 128:(t + 1) * 128], wb[:, k, :], k == 0, k == 7,
                        reads=wk + [('xT', t)], writes=[pk])
            if kvvar == 'mmonly':
                continue
            if kvvar != 'noact' and not (kvvar == 'noact3' and t == 3):
                self.evac_copy('act', vtok[:, 1 + t, :], ps[:, 256:512], [pk], [vkey])
            if last_tile_needs_k and t == nt - 1 and kvvar != 'nokv32':
                kv32 = kv32_pre
                self.evac_copy('act', kv32[:], ps[:, :], [pk], ['kv32'])
        import os
        if kv32 is not None and stream == 'P' and not os.environ.get('NOKVOUT'):
            S.dma('act', self.O['pk'][l], kv32[:, 0:256], reads=['kv32'], writes=[('o_pk', l)])
            S.dma('act', self.O['pv'][l], kv32[:, 256:512], reads=['kv32'], writes=[('o_pv', l)])
        self.chk('kvtok')
        for half in range(2):
            wb, wk = self.wtile('w_in', l, [(OFF_GATE + half * 512, 512)])
            for b in range(4):
                ps, pk = self.ps()
                for k in range(8):
                    self.mm(ps[:, 0:N], wb[:, k, b * 128:(b + 1) * 128], self.xT[:, k, 0:N], k == 0, k == 7,
                            reads=wk + xkeys, writes=[pk])
                j = half * 4 + b
                S.op('act', lambda e: e.activation(out=self.mixacc[:, j, 0:N], in_=ps[:, 0:N], func=AF.Sigmoid),
                     reads=[pk], writes=[('mixacc', j)])
        self.chk('proj')
        if stream == 'P':
            for t in range(nt):
                first = (g == 0 and t == 0)
                blks = [1] if first else [0, 1]
                nb = len(blks)
                for c in range(KVH):
                    for par in range(2):
                        ps, pk = self.ps()
                        hp = par * 64
                        for bi, blk in enumerate(blks):
                            koff = 128 * t if blk == 0 else 128 * (t + 1)
                            for i2 in range(2):
                                h = 4 * c + 2 * i2 + par
                                col = (bi * 2 + i2) * 128
                                self.mm(ps[:, col:col + 128], kT[hp:hp + 64, c, koff:koff + 128],
                                        qT[hp:hp + 64, h // 2, t * 128:(t + 1) * 128], True, True,
                                        reads=[kkey, 'qT'], writes=[pk])
                        W_ = nb * 256
                        bias_ap = self.biasP[:, blks[0]:blks[0] + nb, 4 * c + par:4 * c + 4:2, :]
                        S.op('dve', lambda e: e.tensor_tensor(out=sb[:, 0:W_].rearrange("p (b i q) -> p b i q", b=nb, i=2),
                                                              in0=ps[:, 0:W_].rearrange("p (b i q) -> p b i q", b=nb, i=2),
                                                              in1=bias_ap, op=ALU.add),
                             reads=[pk, 'biasP'], writes=['sbt'])
                        S.op('act', lambda e: e.activation(out=ET[par][:, 0:W_], in_=sb[:, 0:W_], func=AF.Exp),
                             reads=['sbt'], writes=[('ET', par)])
                    ps, pk = self.ps()
                    for i4 in range(4):
                        par = i4 % 2
                        hp = par * 64
                        i2 = i4 // 2
                        jj = i4 // 2
                        for part in range(2):
                            for bi, blk in enumerate(blks):
                                slot = t if blk == 0 else t + 1
                                lhsT = vtok[:, slot, c * 64:(c + 1) * 64] if part == 0 else self.onesb[:, :]
                                col = (bi * 2 + i2) * 128
                                self.mm(ps[hp:hp + 64, part * 256 + jj * 128: part * 256 + (jj + 1) * 128], lhsT,
                                        ET[par][:, col:col + 128], bi == 0, bi == nb - 1,
                                        reads=[vkey, 'onesb', ('ET', par)], writes=[pk])
                    for jj in range(2):
                        j = 2 * c + jj
                        S.op('dve', lambda e: e.tensor_scalar(out=den[:, jj * 128:(jj + 1) * 128],
                                                              in0=ps[:, 256 + jj * 128:256 + (jj + 1) * 128],
                                                              scalar1=self.esink[:, l, j:j + 1], scalar2=None, op0=ALU.add),
                             reads=[pk, 'esink'], writes=['den'])
                    S.op('dve', lambda e: e.reciprocal(out=den[:], in_=den[:]), reads=['den'], writes=['den'])
                    S.op('dve', lambda e: e.tensor_tensor(out=osb[:], in0=ps[:, 0:256], in1=den[:], op=ALU.mult),
                         reads=[pk, 'den'], writes=['osb'])
                    S.op('pool', lambda e: e.tensor_tensor(out=self.mixacc[:, 2 * c:2 * c + 2, t * 128:(t + 1) * 128],
                                                           in0=self.mixacc[:, 2 * c:2 * c + 2, t * 128:(t + 1) * 128],
                                                           in1=osb[:].rearrange("p (j q) -> p j q", j=2), op=ALU.mult),
                         reads=['osb', ('mixacc', 2 * c), ('mixacc', 2 * c + 1)],
                         writes=[('mixacc', 2 * c), ('mixacc', 2 * c + 1)])
            S.op('pool', lambda e: e.tensor_copy(out=kT[:, :, 0:128], in_=kT[:, :, N:N + 128]), reads=[kkey], writes=[kkey])
            S.op('pool', lambda e: e.tensor_copy(out=vtok[:, 0, :], in_=vtok[:, nt, :]), reads=[vkey], writes=[vkey])
        else:
            bSn = ph.T("bSn", [128, NH, 128], F32)
            bSc = ph.T("bSc", [128, KVH, 2, 256], F32)
            S.dma('act', bSn[:], I['c_biasSn'], writes=['bSn'])
            S.dma('act', bSc[:], I['c_biasSc'], writes=['bSc'])
            ck32 = ph.T("ck32", [128, NSEQ_S, 256], F32)
            ckd = ph.T("ckd", [128, NSEQ_S, KVH, 2, 64], BF16)
            KcT = ph.T("KcT", [128, NSEQ_S, KVH, 128], BF16)
            Vc = ph.T("Vc", [128, NSEQ_S, 256], BF16)
            ETc = [ph.T("ETc%d" % b, [128, 256], BF16) for b in range(2)]
            t1 = ph.T("sa_t1", [128, 512], F32)
            S.dma('act', ck32[:], I['ck'][l].rearrange("n r c -> r n c"), writes=['ck32'])
            S.dma('act', self.O['sk'][l].rearrange("n r c -> r n c")[0:120], ck32[8:128, :, :], reads=['ck32'], writes=[('o_sk', l)])
            for dup in range(2):
                S.op('dve' if dup else 'pool', lambda e: e.tensor_copy(out=ckd[:, :, :, dup, :], in_=ck32[:].rearrange("p n (c d) -> p n c d", d=64)),
                     reads=['ck32'], writes=['ckd'])
            S.dma('act', ck32[:], I['cv'][l].rearrange("n r c -> r n c"), writes=['ck32'])
            S.dma('act', self.O['sv'][l].rearrange("n r c -> r n c")[0:120], ck32[8:128, :, :], reads=['ck32'], writes=[('o_sv', l)])
            S.op('act', lambda e: e.copy(out=Vc[:], in_=ck32[:]), reads=['ck32'], writes=['Vc'])
            for n0 in range(0, NSEQ_S, 2):
                ps, pk = self.ps()
                psb = ps[:].bitcast(BF16)
                for ni in range(2):
                    for c in range(KVH):
                        S.op('pe', lambda e: e.transpose(out=psb[:, (ni * 4 + c) * 128:(ni * 4 + c + 1) * 128],
                                                         in_=ckd[:, n0 + ni, c, :, :].rearrange("p a d -> p (a d)"), identity=self.identb[:]),
                             reads=['ckd', 'identb'], writes=[pk])
                S.op('act' if (n0 // 2) % 2 else 'dve',
                     (lambda e: e.copy(out=KcT[:, n0:n0 + 2, :, :].rearrange("p n c k -> p (n c k)"), in_=psb)) if (n0 // 2) % 2 else
                     (lambda e: e.tensor_copy(out=KcT[:, n0:n0 + 2, :, :].rearrange("p n c k -> p (n c k)"), in_=psb)),
                     reads=[pk], writes=['KcT'])
            for n in range(NSEQ_S):
                S.dma('act', self.O['sk'][l, n, 120:128, :], kv32[n * 8:(n + 1) * 8, 0:256], reads=['kv32'], writes=[('o_skn', l, n)])
                S.dma('act', self.O['sv'][l, n, 120:128, :], kv32[n * 8:(n + 1) * 8, 256:512], reads=['kv32'], writes=[('o_svn', l, n)])
            for c in range(KVH):
                for par in range(2):
                    hp = par * 64
                    ps, pk = self.ps()
                    for i2 in range(2):
                        h = 4 * c + 2 * i2 + par
                        self.mm(ps[:, i2 * 128:(i2 + 1) * 128], kT[hp:hp + 64, c, 128:256], qT[hp:hp + 64, h // 2, 0:128], True, True,
                                reads=[kkey, 'qT'], writes=[pk])
                    S.op('dve', lambda e: e.tensor_tensor(out=sb[:, 0:256].rearrange("p (i q) -> p i q", i=2),
                                                          in0=ps[:, 0:256].rearrange("p (i q) -> p i q", i=2),
                                                          in1=bSn[:, 4 * c + par:4 * c + 4:2, :], op=ALU.add), reads=[pk, 'bSn'], writes=['sbt'])
                    S.op('act', lambda e: e.activation(out=ET[par][:, 0:256], in_=sb[:, 0:256], func=AF.Exp), reads=['sbt'], writes=[('ET', par)])
                    ps, pk = self.ps()
                    for n in range(NSEQ_S):
                        for i2 in range(2):
                            h = 4 * c + 2 * i2 + par
                            col = (n * 2 + i2) * 8
                            self.mm(ps[:, col:col + 8], KcT[hp:hp + 64, n, c, :], qT[hp:hp + 64, h // 2, n * 8:(n + 1) * 8], True, True,
                                    reads=['KcT', 'qT'], writes=[pk])
                    S.op('dve', lambda e: e.tensor_tensor(out=sb[:, 256:512], in0=ps[:, 0:256], in1=bSc[:, c, par, :], op=ALU.add),
                         reads=[pk, 'bSc'], writes=['sbt'])
                    S.op('act', lambda e: e.activation(out=ETc[par][:], in_=sb[:, 256:512], func=AF.Exp), reads=['sbt'], writes=[('ETc', par)])
                ps1, pk1 = self.ps()
                ps2, pk2 = self.ps()
                for i4 in range(4):
                    par, i2 = i4 % 2, i4 // 2
                    hp = par * 64
                    for part in range(2):
                        lhs_n = vtok[:, 1, c * 64:(c + 1) * 64] if part == 0 else self.onesb[:, :]
                        self.mm(ps1[hp:hp + 64, part * 256 + i2 * 128: part * 256 + (i2 + 1) * 128], lhs_n,
                                ET[par][:, i2 * 128:(i2 + 1) * 128], True, True, reads=[vkey, 'onesb', ('ET', par)], writes=[pk1])
                        for n in range(NSEQ_S):
                            lhs_c = Vc[:, n, c * 64:(c + 1) * 64] if part == 0 else self.onesb[:, :]
                            col = part * 256 + i2 * 128 + n * 8
                            self.mm(ps2[hp:hp + 64, col:col + 8], lhs_c, ETc[par][:, (n * 2 + i2) * 8:(n * 2 + i2) * 8 + 8], True, True,
                                    reads=['Vc', 'onesb', ('ETc', par)], writes=[pk2])
                S.op('act', lambda e: e.copy(out=t1[:], in_=ps1[:, :]), reads=[pk1], writes=['sa_t1'])
                S.op('dve', lambda e: e.tensor_tensor(out=t1[:], in0=t1[:], in1=ps2[:, :], op=ALU.add), reads=['sa_t1', pk2], writes=['sa_t1'])
                for jj in range(2):
                    j = 2 * c + jj
                    S.op('dve', lambda e: e.tensor_scalar(out=den[:, jj * 128:(jj + 1) * 128], in0=t1[:, 256 + jj * 128:256 + (jj + 1) * 128],
                                                          scalar1=self.esink[:, l, j:j + 1], scalar2=None, op0=ALU.add),
                         reads=['sa_t1', 'esink'], writes=['den'])
                S.op('dve', lambda e: e.reciprocal(out=den[:], in_=den[:]), reads=['den'], writes=['den'])
                S.op('dve', lambda e: e.tensor_tensor(out=osb[:], in0=t1[:, 0:256], in1=den[:], op=ALU.mult), reads=['sa_t1', 'den'], writes=['osb'])
                S.op('pool', lambda e: e.tensor_tensor(out=self.mixacc[:, 2 * c:2 * c + 2, 0:128], in0=self.mixacc[:, 2 * c:2 * c + 2, 0:128],
                                                       in1=osb[:].rearrange("p (j q) -> p j q", j=2), op=ALU.mult),
                     reads=['osb', ('mixacc', 2 * c), ('mixacc', 2 * c + 1)], writes=[('mixacc', 2 * c), ('mixacc', 2 * c + 1)])
        self.dump('oa_g' if stream == 'P' else 's_oa_g', self.mixacc[:, :, 0:N], [128, 8, N], [('mixacc', j) for j in range(8)])
        ph.close()


_CACHE = {}


def consts():
    ident = np.eye(128, dtype=np.float32)
    slopes = 2.0 ** (-8.0 * np.arange(1, NH + 1) / NH)
    k = np.arange(128)[:, None]
    q = np.arange(128)[None, :]
    biasP = np.zeros((128, 2, NH, 128), np.float32)
    rel_prev = 128 + q - k
    rel_cur = q - k
    for h in range(NH):
        biasP[:, 0, h, :] = np.where(rel_prev < 128, -slopes[h] * rel_prev, NEG)
        biasP[:, 1, h, :] = np.where(rel_cur >= 0, -slopes[h] * rel_cur, NEG)
    kn, ks = np.arange(128)[:, None] // 8, np.arange(128)[:, None] % 8
    qn, qt = np.arange(128)[None, :] // 8, np.arange(128)[None, :] % 8
    biasSn = np.zeros((128, NH, 128), np.float32)
    for h in range(NH):
        biasSn[:, h, :] = np.where((kn == qn) & (ks <= qt), -slopes[h] * (qt - ks), NEG)
    biasSc = np.zeros((128, KVH, 2, NSEQ_S, 2, T_S), np.float32)
    j = np.arange(128)[:, None]
    t = np.arange(T_S)[None, :]
    for c in range(KVH):
        for i4 in range(4):
            h = 4 * c + i4
            biasSc[:, c, i4 % 2, :, i4 // 2, :] = np.where(j > t, -slopes[h] * (128 + t - j), NEG)[:, None, :]
    biasSc = biasSc.reshape(128, KVH, 2, 256)
    p = np.arange(128)[:, None]
    f = np.arange(128)[None, :]
    strictT = (p < f).astype(np.float32)
    inclT = (p <= f).astype(np.float32)
    strictA = (f < p).astype(np.float32)
    identb = (p == f).astype(np.float32)
    mask = np.stack([np.tile(m, (1, 4)) for m in (strictT, inclT, strictA, identb)], axis=1).astype(np.float32)
    rmask = np.ones((128, 512), np.float32)
    rmask[:, ::128] = 0.0
    blk = np.zeros((128, 128), np.float32)
    blk[:64, :64] = 1.0
    blk[64:, 64:] = 1.0
    return dict(c_ident=ident, c_biasP=biasP, c_biasSn=biasSn, c_biasSc=biasSc, c_mask=mask, c_rmask=rmask, c_blk=blk)


def get_prog(seq, **kw):
    key = (seq, tuple(sorted((k, str(v)) for k, v in kw.items())))
    if key not in _CACHE:
        kb = KB(seq, **kw)
        kb.build()
        _CACHE[key] = kb
    return _CACHE[key]


def make_in_maps(inputs, seq):
    cst = consts()
    maps = []
    f = lambda a: np.ascontiguousarray(a, dtype=np.float32)
    for c in range(NCORES):
        n0 = c * NSEQ_S
        m = {
            'xp': f(inputs['x_prompt'][c % 2, :seq]),
            'xs': f(inputs['x_sample'][n0:n0 + NSEQ_S].reshape(128, D)),
            'ck': f(inputs['cache_k'][:, n0:n0 + NSEQ_S].reshape(DEPTH, NSEQ_S, 128, 256)),
            'cv': f(inputs['cache_v'][:, n0:n0 + NSEQ_S].reshape(DEPTH, NSEQ_S, 128, 256)),
            'sconv': f(inputs['state_conv'][:, n0:n0 + NSEQ_S]),
            'srg': f(inputs['state_rglru'][:, n0:n0 + NSEQ_S]),
            'sshift': f(inputs['state_shift'][:, n0:n0 + NSEQ_S]),
            'swkv': f(inputs['state_wkv'][:, n0:n0 + NSEQ_S]),
        }
        for nm in ['w_in', 'attn_sinks', 'conv_w', 'conv_b', 'rg_wa', 'rg_ba', 'rg_wx', 'rg_bx', 'rg_lambda',
                   'rw_mu', 'rw_w0', 'rw_wup', 'rw_a0', 'rw_aup', 'rw_gup', 'rw_kk', 'rw_ka', 'rw_lnw', 'rw_lnb',
                   'w_out', 'ln1_g', 'ln1_b', 'w_gu', 'w_down', 'ln2_g', 'ln2_b']:
            m[nm] = f(inputs[nm])
        m['rw_rk'] = f(np.asarray(inputs['rw_rk']).reshape(DEPTH, D))
        m.update(cst)
        maps.append(m)
    return maps


def kernel(**inputs):
    seq = inputs['x_prompt'].shape[1]
    kb = get_prog(seq)
    maps = make_in_maps(inputs, seq)
    res = run_bass_kernel_spmd(kb.nc, maps, core_ids=list(range(NCORES)))
    R = res.results
    B = inputs['x_prompt'].shape[0]
    yp = np.stack([R[b]['yp'] for b in range(B)])
    ys = np.concatenate([R[c]['ys'].reshape(NSEQ_S, T_S, D) for c in range(NCORES)])

    def pst(nm, shp):
        return np.stack([R[b][nm] for b in range(B)], axis=1).reshape(shp)

    def sst(nm, shp):
        return np.concatenate([R[c][nm] for c in range(NCORES)], axis=1).reshape(shp)
    NS = NCORES * NSEQ_S
    outs = (yp, ys,
            pst('pk', (DEPTH, B, 128, KVH, HD)), pst('pv', (DEPTH, B, 128, KVH, HD)),
            pst('pconv', (DEPTH, B, 3, D)), pst('prg', (DEPTH, B, D)), pst('pshift', (DEPTH, B, SHW)),
            pst('pwkv', (DEPTH, B, NH, HD, HD)),
            sst('sk', (DEPTH, NS, 128, KVH, HD)), sst('sv', (DEPTH, NS, 128, KVH, HD)),
            sst('sconv_o', (DEPTH, NS, 3, D)), sst('srg_o', (DEPTH, NS, D)), sst('sshift_o', (DEPTH, NS, SHW)),
            sst('swkv_o', (DEPTH, NS, NH, HD, HD)))
    return tuple(np.ascontiguousarray(o, dtype=np.float32) for o in outs)
```

```python
import math
import numpy as np
from contextlib import ExitStack
import concourse.bass as bass
import concourse.mybir as mybir
from concourse.bass_utils import run_bass_kernel_spmd

F32 = mybir.dt.float32
BF16 = mybir.dt.bfloat16
ALU = mybir.AluOpType
AF = mybir.ActivationFunctionType
AX = mybir.AxisListType

D = 1024
NH = 16
HD = 64
KVH = 4
DFF = 2816
INC = 8992
OFF_K, OFF_V, OFF_RNN, OFF_RW, OFF_GATE = 1024, 1280, 1536, 2560, 5920
SHW = 3360
DEPTH = 2
ALPHA = (2 * DEPTH) ** 0.25
LN_EPS = 1e-5
GN_EPS = 64e-5
NEG = -30000.0
NCORES = 8
NSEQ_S = 16
T_S = 8
NWBUF = 2


class Sched:
    def __init__(self, nc, es, n_dma_sems=32):
        self.nc = nc
        self.es = es
        self.engs = {'pe': nc.tensor, 'dve': nc.vector, 'act': nc.scalar, 'pool': nc.gpsimd, 'sp': nc.sync}
        self.csem = {e: es.enter_context(nc.semaphore("cs_" + e)) for e in ('pe', 'dve', 'act', 'pool')}
        self.ccnt = {e: 0 for e in self.csem}
        self.dsem = {q: [es.enter_context(nc.semaphore("ds%s%d" % (q, i))) for i in range(n)]
                     for q, n in (('sp', 16), ('act', 12), ('pool', 4))}
        self.dcnt = {q: [0] * len(v) for q, v in self.dsem.items()}
        self.dnext = {q: 0 for q in self.dsem}
        self.known = {e: {} for e in self.engs}
        self.lastw = {}
        self.reads = {}
        self.ninst = 0
        self.psn = 0

    def _wait(self, e, tok):
        if tok is None:
            return
        sem, val, owner = tok
        if owner == 'pe' and e == 'pe':
            return
        if self.known[e].get(sem.name, 0) >= val:
            return
        self.engs[e].wait_ge(sem, val)
        self.known[e][sem.name] = val
        self.ninst += 1

    def _deps(self, e, reads, writes):
        for k in reads:
            self._wait(e, self.lastw.get(k))
        for k in writes:
            self._wait(e, self.lastw.get(k))
            for t in self.reads.get(k, ()):
                self._wait(e, t)

    def _commit(self, tok, reads, writes):
        for k in writes:
            self.lastw[k] = tok
            self.reads[k] = []
        for k in reads:
            if k in writes:
                continue
            lst = self.reads.setdefault(k, [])
            lst[:] = [t for t in lst if t[0].name != tok[0].name]
            lst.append(tok)

    def op(self, e, fn, reads=(), writes=()):
        self._deps(e, reads, writes)
        ins = fn(self.engs[e])
        self.ccnt[e] += 1
        ins.then_inc(self.csem[e], 1)
        tok = (self.csem[e], self.ccnt[e], e)
        self._commit(tok, reads, writes)
        self.ninst += 1
        return tok

    def dma(self, e, out, in_, reads=(), writes=(), **kw):
        i = self.dnext[e]
        self.dnext[e] = (i + 1) % len(self.dsem[e])
        if self.dcnt[e][i] > 0:
            self._wait(e, (self.dsem[e][i], self.dcnt[e][i], 'dma'))
        self._deps(e, reads, writes)
        ins = self.engs[e].dma_start(out=out, in_=in_, **kw)
        self.dcnt[e][i] += 16
        ins.then_inc(self.dsem[e][i], 16)
        tok = (self.dsem[e][i], self.dcnt[e][i], 'dma')
        self._commit(tok, reads, writes)
        self.ninst += 1
        return tok

    def release(self, names, engines=('pe', 'dve', 'act', 'pool', 'sp')):
        names = set(names)
        keys = [k for k in list(self.lastw.keys()) + list(self.reads.keys())
                if (k in names) or (isinstance(k, tuple) and k[0] in names)]
        for e in engines:
            for k in keys:
                self._wait(e, self.lastw.get(k))
                for t in self.reads.get(k, ()):
                    self._wait(e, t)
        for k in set(keys):
            self.lastw.pop(k, None)
            self.reads.pop(k, None)

    def finish(self, e='sp'):
        for k, t in list(self.lastw.items()):
            self._wait(e, t)


class Phase:
    def __init__(self, kb):
        self.kb = kb
        self.es = ExitStack()
        self.names = []
        kb.open_phases.append(self)

    _uid = [0]

    def T(self, name, shape, dt):
        self.names.append(name)
        Phase._uid[0] += 1
        return self.es.enter_context(self.kb.nc.sbuf_tensor("%s_%d" % (name, Phase._uid[0]), list(shape), dt))

    def close(self):
        self.kb.open_phases.remove(self)
        self.kb.S.release(self.names)
        self.es.close()


def w_in_tiles():
    t = []
    t.append(('q0', [(0, 512)]))
    t.append(('q1', [(512, 512)]))
    t.append(('kdup', None))
    t.append(('kv', [(OFF_K, 512)]))
    t.append(('gA0', [(OFF_GATE, 512)]))
    t.append(('gA1', [(OFF_GATE + 512, 512)]))
    return t


class _Stop(Exception):
    pass


class KB:
    def __init__(self, seq, depth=DEPTH, do_sample=True, debug=(), stop_after=None):
        self.seq = seq
        self.depth = depth
        self.do_sample = do_sample
        self.debug = set(debug)
        self.stop_after = stop_after
        self.dbg_outs = {}
        self.open_phases = []
        self.nc = bass.Bass("TRN2", target_bir_lowering=False)
        self.es = ExitStack()

    def chk(self, name):
        if self.stop_after == name:
            raise _Stop()

    def din(self, name, shape, dt=F32):
        return self.nc.dram_tensor(name, list(shape), dt, kind="ExternalInput").ap()

    def dout(self, name, shape, dt=F32):
        return self.nc.dram_tensor(name, list(shape), dt, kind="ExternalOutput").ap()

    def dscr(self, name, shape, dt):
        return self.nc.dram_tensor(name, list(shape), dt, kind="Internal").ap()

    def T(self, name, shape, dt):
        return self.es.enter_context(self.nc.sbuf_tensor(name, list(shape), dt))

    def ps(self):
        i = self.S.psn
        self.S.psn = (i + 1) % 8
        return self.psb[i], 'ps%d' % i

    def dump(self, name, ap, shape, key):
        if name not in self.debug:
            return
        o = self.dout("dbg_" + name, shape, ap.dtype if hasattr(ap, 'dtype') else F32)
        self.dbg_outs[name] = shape
        self.S.dma('act', o, ap, reads=key if isinstance(key, list) else [key], writes=['dbg_' + name])

    def mm(self, out, lhsT, rhs, start, stop, reads, writes):
        return self.S.op('pe', lambda e: e.matmul(out, lhsT=lhsT, rhs=rhs, start=start, stop=stop),
                         reads=reads, writes=writes)

    def evac_copy(self, eng, out, in_, reads, writes, scale=None):
        if eng == 'act':
            if scale is None:
                return self.S.op('act', lambda e: e.copy(out=out, in_=in_), reads=reads, writes=writes)
            return self.S.op('act', lambda e: e.activation(out=out, in_=in_, func=AF.Copy, scale=scale),
                             reads=reads, writes=writes)
        if scale is None:
            return self.S.op(eng, lambda e: e.tensor_copy(out=out, in_=in_), reads=reads, writes=writes)
        return self.S.op(eng, lambda e: e.tensor_scalar(out=out, in0=in_, scalar1=scale, scalar2=None, op0=ALU.mult),
                         reads=reads, writes=writes)

    def build(self):
        nc, es = self.nc, self.es
        self.S = S = Sched(nc, es)
        seq = self.seq
        L = self.depth
        I = self.I = {}
        I['xp'] = self.din('xp', [seq, D])
        I['xs'] = self.din('xs', [128, D])
        I['ck'] = self.din('ck', [DEPTH, NSEQ_S, 128, 256])
        I['cv'] = self.din('cv', [DEPTH, NSEQ_S, 128, 256])
        I['sconv'] = self.din('sconv', [DEPTH, NSEQ_S, 3, D])
        I['srg'] = self.din('srg', [DEPTH, NSEQ_S, D])
        I['sshift'] = self.din('sshift', [DEPTH, NSEQ_S, SHW])
        I['swkv'] = self.din('swkv', [DEPTH, NSEQ_S, NH, HD, HD])
        for nm, shp in [('w_in', [DEPTH, D, INC]), ('attn_sinks', [DEPTH, NH]), ('conv_w', [DEPTH, 4, D]),
                        ('conv_b', [DEPTH, D]), ('rg_wa', [DEPTH, 16, 64, 64]), ('rg_ba', [DEPTH, D]),
                        ('rg_wx', [DEPTH, 16, 64, 64]), ('rg_bx', [DEPTH, D]), ('rg_lambda', [DEPTH, D]),
                        ('rw_mu', [DEPTH, SHW]), ('rw_w0', [DEPTH, D]), ('rw_wup', [DEPTH, 64, D]),
                        ('rw_a0', [DEPTH, D]), ('rw_aup', [DEPTH, 64, D]), ('rw_gup', [DEPTH, 160, D]),
                        ('rw_kk', [DEPTH, D]), ('rw_ka', [DEPTH, D]), ('rw_rk', [DEPTH, D]),
                        ('rw_lnw', [DEPTH, D]), ('rw_lnb', [DEPTH, D]), ('w_out', [DEPTH, D, D]),
                        ('ln1_g', [DEPTH, D]), ('ln1_b', [DEPTH, D]), ('w_gu', [DEPTH, D, 2 * DFF]),
                        ('w_down', [DEPTH, DFF, D]), ('ln2_g', [DEPTH, D]), ('ln2_b', [DEPTH, D])]:
            I[nm] = self.din(nm, shp)
        I['c_ident'] = self.din('c_ident', [128, 128])
        I['c_biasP'] = self.din('c_biasP', [128, 2, NH, 128])
        I['c_biasSn'] = self.din('c_biasSn', [128, NH, 128])
        I['c_biasSc'] = self.din('c_biasSc', [128, KVH, 2, 256])
        I['c_mask'] = self.din('c_mask', [128, 4, 512])
        I['c_rmask'] = self.din('c_rmask', [128, 512])
        I['c_blk'] = self.din('c_blk', [128, 128])
        O = self.O = {}
        O['yp'] = self.dout('yp', [seq, D])
        O['ys'] = self.dout('ys', [128, D])
        O['pk'] = self.dout('pk', [DEPTH, 128, 256])
        O['pv'] = self.dout('pv', [DEPTH, 128, 256])
        O['pconv'] = self.dout('pconv', [DEPTH, 3, D])
        O['prg'] = self.dout('prg', [DEPTH, D])
        O['pshift'] = self.dout('pshift', [DEPTH, SHW])
        O['pwkv'] = self.dout('pwkv', [DEPTH, NH, HD, HD])
        O['sk'] = self.dout('sk', [DEPTH, NSEQ_S, 128, 256])
        O['sv'] = self.dout('sv', [DEPTH, NSEQ_S, 128, 256])
        O['sconv_o'] = self.dout('sconv_o', [DEPTH, NSEQ_S, 3, D])
        O['srg_o'] = self.dout('srg_o', [DEPTH, NSEQ_S, D])
        O['sshift_o'] = self.dout('sshift_o', [DEPTH, NSEQ_S, SHW])
        O['swkv_o'] = self.dout('swkv_o', [DEPTH, NSEQ_S, NH, HD, HD])
        W = self.W = {}
        W['w_in'] = self.dscr('wb_in', [DEPTH, D, INC], BF16)
        W['w_out'] = self.dscr('wb_out', [DEPTH, D, D], BF16)
        W['w_gu'] = self.dscr('wb_gu', [DEPTH, D, 2 * DFF], BF16)
        W['w_down'] = self.dscr('wb_down', [DEPTH, DFF, D], BF16)
        self.scr_wabd = self.dscr('scr_wabd', [128, DEPTH, 2, 8, 128], BF16)
        self.scr_wlora = self.dscr('scr_wlora', [128, DEPTH, D], BF16)
        self.scr_wgup = self.dscr('scr_wgup', [128, DEPTH, 2, D], BF16)
        self.scr_v = self.dscr('scr_v', [DEPTH, NSEQ_S, 8, 2, 6, T_S, 64], F32)
        self.scr_y = self.dscr('scr_y', [DEPTH, NSEQ_S, 8, 2, T_S, 64], F32)

        self.psb = [es.enter_context(nc.psum_tensor("psb%d" % i, [128, 512], F32)) for i in range(8)]
        self.wbuf = [self.T("wbuf%d" % i, [128, 8, 512], BF16) for i in range(NWBUF)]
        self.wn = 0
        self.xres = self.T("xres", [128, 4, D], F32)
        self.xT = self.T("xT", [128, 8, 512], BF16)
        self.mixacc = self.T("mixacc", [128, 8, 512], F32)
        self.identf = self.T("identf", [128, 128], F32)
        self.xb_tmp = [self.T("xb_tmp%d" % i, [128, D], BF16) for i in range(1)]
        self.identb = self.T("identb", [128, 128], BF16)
        self.onesb = self.T("onesb", [128, 64], BF16)
        self.esink = self.T("esink", [128, DEPTH, 8], F32)
        self.epsln = self.T("epsln", [128, 2], F32)
        self.p_cw = self.T("p_cw", [128, DEPTH, 4, 8], F32)
        self.p_cb = self.T("p_cb", [128, DEPTH, 8], F32)
        self.p_ba = self.T("p_ba", [128, DEPTH, 8], F32)
        self.p_bx = self.T("p_bx", [128, DEPTH, 8], F32)
        self.p_c8 = self.T("p_c8", [128, DEPTH, 8], F32)
        self.p_mu = self.T("p_mu", [128, DEPTH, 27], F32)
        self.p_w0 = self.T("p_w0", [128, DEPTH, 8], F32)
        self.p_a0 = self.T("p_a0", [128, DEPTH, 8], F32)
        self.p_kk = self.T("p_kk", [128, DEPTH, 8], F32)
        self.p_ka = self.T("p_ka", [128, DEPTH, 8], F32)
        self.p_rk = self.T("p_rk", [128, DEPTH, 8], F32)
        self.p_omka = self.T("p_omka", [128, DEPTH, 8], F32)
        self.blkb = self.T("blkb", [128, 128], BF16)
        self.blkf = self.T("blkf", [128, 256], F32)
        self.convst = {'P': self.T("convstP", [128, DEPTH, 8, 1, 3], F32), 'S': self.T("convstS", [128, DEPTH, 8, NSEQ_S, 3], F32)}
        self.hst = {'P': self.T("hstP", [128, DEPTH, 8, 1], F32), 'S': self.T("hstS", [128, DEPTH, 8, NSEQ_S], F32)}
        self.shst = {'P': self.T("shstP", [128, DEPTH, 27, 1], F32), 'S': self.T("shstS", [128, DEPTH, 27, NSEQ_S], F32)}

        self.phP = Phase(self)
        self.kT_p = [self.phP.T("kTp%d" % l, [128, KVH, 128 + 512], BF16) for l in range(L)]
        self.vtok_p = [self.phP.T("vtokp%d" % l, [128, 5, 256], BF16) for l in range(L)]
        self.H32 = self.phP.T("H32", [128, DEPTH, 8, 128], F32)
        self.Hbf = self.phP.T("Hbf", [128, DEPTH, 8, 128], BF16)
        self.maskb = self.phP.T("maskb", [128, 4, 512], BF16)
        self.rmaskP = self.phP.T("rmaskP", [128, 512], F32)
        self.phP.names = ['kTp', 'vtokp', 'H32', 'Hbf', 'maskb', 'rmaskP']
        try:
            self.main_body()
        except _Stop:
            for ph in reversed(list(self.open_phases)):
                ph.close()
        S.finish('sp')
        S.finish('act')
        self.es.close()
        return nc

    def main_body(self):
        S, I, O, L, seq = self.S, self.I, self.O, self.depth, self.seq
        self.setup()
        self.chk('setup')
        ngroups = seq // 512
        for g in range(ngroups):
            self.load_x(I['xp'][g * 512:(g + 1) * 512, :], 4)
            self.chk('loadx')
            for l in range(L):
                self.layer(l, 'P', g, ngroups)
            self.store_y(O['yp'][g * 512:(g + 1) * 512, :], 4)
        self.chk('pgroups')
        for l in range(L):
            self.prompt_state_out(l)
        self.chk('pstates')
        self.phP.close()
        if self.do_sample:
            self.load_x(I['xs'], 1)
            for l in range(L):
                self.sample_state_in(l)
            self.chk('S_in')
            for l in range(L):
                self.layer(l, 'S', 0, 1)
                self.sample_state_out(l)
            self.store_y(O['ys'], 1)

    def setup(self):
        S, I, W = self.S, self.I, self.W
        ph = Phase(self)
        st32 = [ph.T("wst32_%d" % i, [128, 2048], F32) for i in range(3)]
        st16 = [ph.T("wst16_%d" % i, [128, 2048], BF16) for i in range(3)]
        pc = 0
        for nm in ['w_in', 'w_out', 'w_gu', 'w_down']:
            src = I[nm]
            dst = W[nm]
            tot = 1
            for s_ in src.shape:
                tot *= s_
            sf = src.rearrange("l a b -> (l a b)").rearrange("(r c) -> r c", c=2048)
            df = dst.rearrange("l a b -> (l a b)").rearrange("(r c) -> r c", c=2048)
            R = tot // 2048
            r0 = 0
            while r0 < R:
                rr = min(128, R - r0)
                b = pc % 3
                S.dma('sp', st32[b][0:rr, :], sf[r0:r0 + rr, :], writes=[('wst32', b)])
                ce = ('dve', 'act', 'pool')[pc % 3]
                if ce == 'act':
                    S.op('act', lambda e: e.copy(out=st16[b][0:rr, :], in_=st32[b][0:rr, :]),
                         reads=[('wst32', b)], writes=[('wst16', b)])
                else:
                    S.op(ce, lambda e: e.tensor_copy(out=st16[b][0:rr, :], in_=st32[b][0:rr, :]),
                         reads=[('wst32', b)], writes=[('wst16', b)])
                S.dma('act', df[r0:r0 + rr, :], st16[b][0:rr, :], reads=[('wst16', b)], writes=[('W', nm, pc)])
                r0 += rr
                pc += 1
        ph.names = ['wst32', 'wst16']
        ph.close()
        for k, t in list(S.lastw.items()):
            if isinstance(k, tuple) and k[0] == 'W':
                S._wait('sp', t)
        S.dma('sp', self.identf[:], I['c_ident'], writes=['identf'])
        S.op('dve', lambda e: e.tensor_copy(out=self.identb[:], in_=self.identf[:]), reads=['identf'], writes=['identb'])
        S.op('dve', lambda e: e.memset(self.onesb[:], 1.0), writes=['onesb'])
        S.op('dve', lambda e: e.memset(self.epsln[:, 0:1], LN_EPS), writes=['epsln'])
        S.op('dve', lambda e: e.memset(self.epsln[:, 1:2], GN_EPS), writes=['epsln'])
        sk = I['attn_sinks'].rearrange("l (j two) -> two l j", two=2)
        for half in range(2):
            S.dma('sp', self.esink[half * 64:(half + 1) * 64, :, :],
                  sk[half:half + 1, :, :].broadcast_to([64, DEPTH, 8]), writes=['esink'],
                  allow_slow_non_contiguous=True)
        S.op('act', lambda e: e.activation(out=self.esink[:], in_=self.esink[:], func=AF.Exp),
             reads=['esink'], writes=['esink'])
        self.setup_params()
        for l in range(self.depth):
            S.op('dve', lambda e: e.memset(self.kT_p[l][:], 0.0), writes=[('kTp', l)])
            S.op('pool', lambda e: e.memset(self.vtok_p[l][:], 0.0), writes=[('vtokp', l)])


    def colload(self, dst, src_vec, key, nblk=8):
        self.S.dma('act', dst, src_vec.rearrange("(j p) -> p j", p=128), writes=[key], allow_slow_non_contiguous=True)

    def setup_params(self):
        S, I = self.S, self.I
        ph = Phase(self)
        for l in range(DEPTH):
            for i in range(4):
                self.colload(self.p_cw[:, l, i, :], I['conv_w'][l, i], 'p_cw')
            self.colload(self.p_cb[:, l, :], I['conv_b'][l], 'p_cb')
            self.colload(self.p_ba[:, l, :], I['rg_ba'][l], 'p_ba')
            self.colload(self.p_bx[:, l, :], I['rg_bx'][l], 'p_bx')
            self.colload(self.p_c8[:, l, :], I['rg_lambda'][l], 'p_c8')
            self.colload(self.p_mu[:, l, 0:26], I['rw_mu'][l, 0:3328], 'p_mu', 26)
            S.dma('act', self.p_mu[0:32, l, 26:27], I['rw_mu'][l, 3328:3360].rearrange("(p o) -> p o", o=1), writes=['p_mu'],
                  allow_slow_non_contiguous=True)
            for nm, t in (('rw_w0', self.p_w0), ('rw_a0', self.p_a0), ('rw_kk', self.p_kk), ('rw_ka', self.p_ka),
                          ('rw_rk', self.p_rk)):
                self.colload(t[:, l, :], I[nm][l], nm)
        c8 = self.p_c8[:].rearrange("p l j -> p (l j)")
        x = ph.T("sp_x", [128, 16], F32)
        z = ph.T("sp_z", [128, 16], F32)
        z2 = ph.T("sp_z2", [128, 16], F32)
        acc = ph.T("sp_acc", [128, 16], F32)
        S.op('act', lambda e: e.activation(out=x[:], in_=c8, func=AF.Exp, scale=-1.0), reads=['p_c8'], writes=['sp_x'])
        S.op('dve', lambda e: e.tensor_scalar(out=z[:], in0=x[:], scalar1=2.0, scalar2=None, op0=ALU.add), reads=['sp_x'], writes=['sp_z'])
        S.op('dve', lambda e: e.reciprocal(out=z[:], in_=z[:]), reads=['sp_z'], writes=['sp_z'])
        S.op('dve', lambda e: e.tensor_tensor(out=z[:], in0=z[:], in1=x[:], op=ALU.mult), reads=['sp_z', 'sp_x'], writes=['sp_z'])
        S.op('dve', lambda e: e.tensor_tensor(out=z2[:], in0=z[:], in1=z[:], op=ALU.mult), reads=['sp_z'], writes=['sp_z2'])
        S.op('dve', lambda e: e.tensor_scalar(out=acc[:], in0=z2[:], scalar1=1.0 / 9, scalar2=1.0 / 7, op0=ALU.mult, op1=ALU.add),
             reads=['sp_z2'], writes=['sp_acc'])
        for cst in (1.0 / 5, 1.0 / 3, 1.0):
            S.op('dve', lambda e: e.tensor_tensor(out=acc[:], in0=acc[:], in1=z2[:], op=ALU.mult), reads=['sp_acc', 'sp_z2'], writes=['sp_acc'])
            S.op('dve', lambda e: e.tensor_scalar(out=acc[:], in0=acc[:], scalar1=cst, scalar2=None, op0=ALU.add), reads=['sp_acc'], writes=['sp_acc'])
        S.op('dve', lambda e: e.tensor_tensor(out=acc[:], in0=acc[:], in1=z[:], op=ALU.mult), reads=['sp_acc', 'sp_z'], writes=['sp_acc'])
        S.op('dve', lambda e: e.tensor_scalar(out=c8, in0=acc[:], scalar1=-16.0, scalar2=None, op0=ALU.mult), reads=['sp_acc'], writes=['p_c8'])
        ph.close()
        ph = Phase(self)
        st = ph.T("wbd_st", [128, DEPTH, 2, 8, 128], F32)
        S.op('pool', lambda e: e.memset(st[:], 0.0), writes=['wbd_st'])
        for l in range(DEPTH):
            for gi, nm in enumerate(('rg_wa', 'rg_wx')):
                src = I[nm][l].rearrange("(j hh) i c -> hh i j c", hh=2)
                for hh in range(2):
                    S.dma('act', st[hh * 64:(hh + 1) * 64, l, gi, :, hh * 64:(hh + 1) * 64], src[hh], writes=['wbd_st'])
        wabd_t = ph.T("wabd_t", [128, DEPTH, 2, 8, 128], BF16)
        S.op('dve', lambda e: e.tensor_copy(out=wabd_t[:], in_=st[:]), reads=['wbd_st'], writes=['wabd_t'])
        S.dma('act', self.scr_wabd, wabd_t[:], reads=['wabd_t'], writes=['scr_wabd'])
        for st_ in ('P', 'S'):
            S.op('pool', lambda e: e.memset(self.convst[st_][:], 0.0), writes=[('convst', st_)])
            S.op('pool', lambda e: e.memset(self.hst[st_][:], 0.0), writes=[('hst', st_)])
            S.op('pool', lambda e: e.memset(self.shst[st_][:], 0.0), writes=[('shst', st_)])
        ph.close()
        ph = Phase(self)
        st2 = ph.T("lora_st", [128, DEPTH, D], F32)
        st3 = ph.T("gup_st", [128, DEPTH, 2, D], F32)
        S.op('pool', lambda e: e.memset(st3[:], 0.0), writes=['gup_st'])
        for l in range(DEPTH):
            S.dma('act', st2[0:64, l, :], I['rw_wup'][l], writes=['lora_st'])
            S.dma('act', st2[64:128, l, :], I['rw_aup'][l], writes=['lora_st'])
            S.dma('act', st3[:, l, 0, :], I['rw_gup'][l, 0:128, :], writes=['gup_st'])
            S.dma('act', st3[0:32, l, 1, :], I['rw_gup'][l, 128:160, :], writes=['gup_st'])
        wlora_t = ph.T("wlora_t", [128, DEPTH, D], BF16)
        wgup_t = ph.T("wgup_t", [128, DEPTH, 2, D], BF16)
        S.op('dve', lambda e: e.tensor_copy(out=wlora_t[:], in_=st2[:]), reads=['lora_st'], writes=['wlora_t'])
        S.op('dve', lambda e: e.tensor_copy(out=wgup_t[:], in_=st3[:]), reads=['gup_st'], writes=['wgup_t'])
        S.dma('act', self.scr_wlora, wlora_t[:], reads=['wlora_t'], writes=['scr_wlora'])
        S.dma('act', self.scr_wgup, wgup_t[:], reads=['wgup_t'], writes=['scr_wgup'])
        S.op('dve', lambda e: e.tensor_scalar(out=self.p_omka[:], in0=self.p_ka[:], scalar1=-1.0, scalar2=1.0, op0=ALU.mult, op1=ALU.add),
             reads=['rw_ka'], writes=['p_omka'])
        S.op('pool', lambda e: e.memset(self.H32[:], 0.0), writes=['H32'])
        S.op('pool', lambda e: e.memset(self.Hbf[:], 0.0), writes=['Hbf'])
        ph.close()
        ph = Phase(self)
        mk = ph.T("mask_st", [128, 4, 512], F32)
        S.dma('act', mk[:], I['c_mask'], writes=['mask_st'])
        S.op('dve', lambda e: e.tensor_copy(out=self.maskb[:], in_=mk[:]), reads=['mask_st'], writes=['maskb'])
        bk = ph.T("blk_st", [128, 128], F32)
        S.dma('act', bk[:], I['c_blk'], writes=['blk_st'])
        S.op('dve', lambda e: e.tensor_copy(out=self.blkb[:], in_=bk[:]), reads=['blk_st'], writes=['blkb'])
        for r2 in range(2):
            S.op('dve', lambda e: e.tensor_copy(out=self.blkf[:, r2 * 128:(r2 + 1) * 128], in_=bk[:]), reads=['blk_st'], writes=['blkf'])
        S.dma('act', self.rmaskP[:], I['c_rmask'], writes=['rmaskP'])
        ph.close()


    def prompt_state_out(self, l):
        S, O = self.S, self.O
        ph = Phase(self)
        cv, hs, sh = self.convst['P'], self.hst['P'], self.shst['P']
        self.fm_to_rows(ph, lambda blk, w: cv[0:w, l, blk, 0, :], 3, D, O['pconv'][l], [('convst', 'P')], ('o_pconv', l))
        self.fm_to_rows(ph, lambda blk, w: hs[0:w, l, blk, :], 1, D, O['prg'][l:l + 1, :], [('hst', 'P')], ('o_prg', l))
        self.fm_to_rows(ph, lambda blk, w: sh[0:w, l, blk, :], 1, SHW, O['pshift'][l:l + 1, :], [('shst', 'P')], ('o_pshift', l))
        for j in range(8):
            ps, pk = self.ps()
            S.op('pe', lambda e: e.transpose(out=ps[:, 0:128], in_=self.H32[:, l, j, :], identity=self.identf[:]),
                 reads=[('H32', l, j), 'identf'], writes=[pk])
            tmp = ph.T("pw_tmp%d" % j, [128, 128], F32)
            S.op('act', lambda e: e.copy(out=tmp[:], in_=ps[:, 0:128]), reads=[pk], writes=[('pw_tmp', j)])
            for hh in range(2):
                S.dma('act', O['pwkv'][l, 2 * j + hh], tmp[hh * 64:(hh + 1) * 64, hh * 64:(hh + 1) * 64],
                      reads=[('pw_tmp', j)], writes=[('o_pwkv', l, j, hh)])
        ph.names.append('pw_tmp')
        ph.close()

    def sample_state_in(self, l):
        S, I = self.S, self.I
        ph = Phase(self)
        cv, hs, sh = self.convst['S'], self.hst['S'], self.shst['S']
        self.rows_to_fm(ph, I['sconv'][l].rearrange("n i c -> (n i) c"), 48, D,
                        lambda b0, nb, w: cv[0:w, l, b0:b0 + nb, :, :].rearrange("p b n i -> p b (n i)"), [('convst', 'S')])
        ph.close()
        ph = Phase(self)
        self.rows_to_fm(ph, I['srg'][l], NSEQ_S, D, lambda b0, nb, w: hs[0:w, l, b0:b0 + nb, :], [('hst', 'S')])
        ph.close()
        ph = Phase(self)
        self.rows_to_fm(ph, I['sshift'][l], NSEQ_S, SHW, lambda b0, nb, w: sh[0:w, l, b0:b0 + nb, :], [('shst', 'S')])
        ph.close()

    def sample_state_out(self, l):
        S, O = self.S, self.O
        cv, hs, sh = self.convst['S'], self.hst['S'], self.shst['S']
        ph = Phase(self)
        self.fm_to_rows(ph, lambda blk, w: cv[0:w, l, blk, :, :].rearrange("p n i -> p (n i)"), 48, D,
                        O['sconv_o'][l].rearrange("n i c -> (n i) c"), [('convst', 'S')], ('o_sconv', l))
        ph.close()
        ph = Phase(self)
        self.fm_to_rows(ph, lambda blk, w: hs[0:w, l, blk, :], NSEQ_S, D, O['srg_o'][l], [('hst', 'S')], ('o_srg', l))
        ph.close()
        ph = Phase(self)
        self.fm_to_rows(ph, lambda blk, w: sh[0:w, l, blk, :], NSEQ_S, SHW, O['sshift_o'][l], [('shst', 'S')], ('o_sshift', l))
        ph.close()

    def rows_to_fm(self, ph, src_rows, R, F, dst_fn, keys_w):
        S = self.S
        tmp = ph.T("r2f_tmp", [128, F], F32)
        S.dma('act', tmp[0:R, :], src_rows, writes=['r2f_tmp'])
        nblk = (F + 127) // 128
        per = max(1, 512 // R)
        b0 = 0
        while b0 < nblk:
            nb = min(per, nblk - b0)
            ps, pk = self.ps()
            wlast = 128
            for b in range(nb):
                w = min(128, F - (b0 + b) * 128)
                wlast = w
                S.op('pe', lambda e: e.transpose(out=ps[0:w, b * R:(b + 1) * R], in_=tmp[0:R, (b0 + b) * 128:(b0 + b) * 128 + w],
                                                 identity=self.identf[0:R, 0:R]), reads=['r2f_tmp', 'identf'], writes=[pk])
            nfull = nb if wlast == 128 else nb - 1
            if nfull > 0:
                S.op('act', lambda e: e.copy(out=dst_fn(b0, nfull, 128), in_=ps[:, 0:nfull * R].rearrange("p (b r) -> p b r", r=R)),
                     reads=[pk], writes=keys_w)
            if nfull < nb:
                S.op('act', lambda e: e.copy(out=dst_fn(b0 + nfull, 1, wlast),
                                             in_=ps[0:wlast, nfull * R:(nfull + 1) * R].rearrange("p (b r) -> p b r", r=R)),
                     reads=[pk], writes=keys_w)
            b0 += nb

    def fm_to_rows(self, ph, src_fn, R, F, dst_rows, keys_r, okey):
        S = self.S
        tmp = ph.T("f2r_tmp", [128, F], F32)
        nblk = (F + 127) // 128
        b0 = 0
        while b0 < nblk:
            nb = min(4, nblk - b0)
            ps, pk = self.ps()
            tot = 0
            for b in range(nb):
                w = min(128, F - (b0 + b) * 128)
                S.op('pe', lambda e: e.transpose(out=ps[0:R, b * 128:b * 128 + w], in_=src_fn(b0 + b, w),
                                                 identity=self.identf[0:w, 0:w]), reads=keys_r + ['identf'], writes=[pk])
                tot += w
            S.op('act', lambda e: e.copy(out=tmp[0:R, b0 * 128:b0 * 128 + tot], in_=ps[0:R, 0:tot]), reads=[pk], writes=['f2r_tmp'])
            b0 += nb
        S.dma('act', dst_rows, tmp[0:R, :], reads=['f2r_tmp'], writes=[okey])

    def load_x(self, src, ntile):
        self.S.dma('act', self.xres[:, 0:ntile, :], src.rearrange("(t p) d -> p t d", p=128),
                   writes=[('xres', t) for t in range(ntile)])
        for t in range(ntile):
            self.make_xT(t)

    def store_y(self, dst, ntile):
        self.S.dma('act', dst.rearrange("(t p) d -> p t d", p=128), self.xres[:, 0:ntile, :],
                   reads=[('xres', t) for t in range(ntile)], writes=[('yout', id(dst))])

    def make_xT(self, t):
        S = self.S
        self.xbn = 0
        xb = self.xb_tmp[self.xbn]
        xbk = ('xb_tmp', self.xbn)
        S.op('act', lambda e: e.copy(out=xb[:], in_=self.xres[:, t, :]), reads=[('xres', t)], writes=[xbk])
        ps, pk = self.ps()
        psb = ps[:].bitcast(BF16)
        for k in range(8):
            S.op('pe', lambda e: e.transpose(out=psb[:, k * 128:(k + 1) * 128], in_=xb[:, k * 128:(k + 1) * 128],
                                             identity=self.identb[:]),
                 reads=[xbk, 'identb'], writes=[pk])
        S.op('dve', lambda e: e.tensor_copy(out=self.xT[:, :, t * 128:(t + 1) * 128],
                                            in_=psb.rearrange("p (k c) -> p k c", k=8)),
             reads=[pk], writes=[('xT', t)])

    def wtile(self, wname, l, cols, rows=(0, 8)):
        S = self.S
        i = self.wn
        self.wn = (self.wn + 1) % NWBUF
        buf = self.wbuf[i]
        src = self.W[wname][l]
        k0, k1 = rows
        off = 0
        keys = []
        for ci, (c0, n) in enumerate(cols):
            key = ('wbuf', i) if ci == 0 else ('wbufx', i, ci)
            S.dma('sp', buf[:, 0:k1 - k0, off:off + n],
                  src[k0 * 128:k1 * 128, c0:c0 + n].rearrange("(k p) c -> p k c", p=128), writes=[key])
            keys.append(key)
            off += n
        return buf, keys

    def layer(self, l, stream, g, ngroups):
        S = self.S
        nt = 4 if stream == 'P' else 1
        N = nt * 128
        self.attention(l, stream, g, ngroups, nt, N)
        self.chk('attn')
        self.chk(stream + '_attn')
        self.rglru(l, stream, g, ngroups, nt, N)
        self.chk('rglru')
        self.chk(stream + '_rglru')
        self.rwkv(l, stream, g, ngroups, nt, N)
        self.chk('rwkv')
        self.chk(stream + '_rwkv')
        self.dense_tail(l, stream, nt, N)
        self.chk('dense')

    def load_ln(self, l):
        S, I = self.S, self.I
        for i, nm in enumerate(['ln1_g', 'ln1_b', 'ln2_g', 'ln2_b']):
            S.dma('act', self.lnbuf[:, i, :], I[nm][l:l + 1, :].broadcast_to([128, D]), writes=[('lnbuf', i)])

    def layer_norm_tile(self, ph, t, which):
        S = self.S
        x = self.xres[:, t, :]
        key = ('xres', t)
        cache = ph.__dict__.setdefault('_ln_cache', {})
        if t not in cache:
            cache[t] = (ph.T("lnst%d" % t, [128, 2, 6], F32), ph.T("lnmv%d" % t, [128, 2], F32), ph.T("lnrs%d" % t, [128, 1], F32))
            ph.names += ['lnst', 'lnmv', 'lnrs']
        st, mv, rs = cache[t]
        kst, kmv, krs = ('lnst', t), ('lnmv', t), ('lnrs', t)
        for hf in range(2):
            S.op('dve', lambda e: e.bn_stats(out=st[:, hf, :], in_=self.xres[:, t, hf * 512:(hf + 1) * 512]),
                 reads=[key], writes=[kst])
        yield
        S.op('dve', lambda e: e.bn_aggr(out=mv[:], in_=st[:].rearrange("p a b -> p (a b)")), reads=[kst], writes=[kmv])
        yield
        S.op('act', lambda e: e.activation(out=rs[:], in_=mv[:, 1:2], func=AF.Sqrt, bias=self.epsln[:, 0:1], scale=1.0),
             reads=[kmv, 'epsln'], writes=[krs])
        yield
        S.op('dve', lambda e: e.reciprocal(out=rs[:], in_=rs[:]), reads=[krs], writes=[krs])
        yield
        S.op('dve', lambda e: e.tensor_scalar(out=x, in0=x, scalar1=mv[:, 0:1], scalar2=rs[:, 0:1],
                                              op0=ALU.subtract, op1=ALU.mult), reads=[key, kmv, krs], writes=[key])
        yield
        S.op('pool', lambda e: e.tensor_tensor(out=x, in0=x, in1=self.lnbuf[:, 2 * which, :], op=ALU.mult),
             reads=[key, ('lnbuf', 2 * which)], writes=[key])
        yield
        S.op('pool', lambda e: e.tensor_tensor(out=x, in0=x, in1=self.lnbuf[:, 2 * which + 1, :], op=ALU.add),
             reads=[key, ('lnbuf', 2 * which + 1)], writes=[key])
        yield

    def dense_tail(self, l, stream, nt, N):
        S = self.S
        ph = Phase(self)
        self.lnbuf = ph.T("lnbuf", [128, 4, D], F32)
        self.load_ln(l)
        self.mixT = ph.T("mixT", [128, 8, N], BF16)
        for j in range(8):
            S.op('act', lambda e: e.copy(out=self.mixT[:, j, 0:N], in_=self.mixacc[:, j, 0:N]),
                 reads=[('mixacc', j)], writes=[('mixT', j)])
        mkeys = [('mixT', j) for j in range(8)]
        wbs = [self.wtile('w_out', l, [(half * 512, 512)]) for half in range(2)]
        for t in range(nt):
            for half in range(2):
                wb, wk = wbs[half]
                ps, pk = self.ps()
                for k in range(8):
                    self.mm(ps[:, :], self.mixT[:, k, t * 128:(t + 1) * 128], wb[:, k, :], k == 0, k == 7,
                            reads=wk + [('mixT', k)], writes=[pk])
                xs = self.xres[:, t, half * 512:(half + 1) * 512]
                S.op('dve', lambda e: e.scalar_tensor_tensor(out=xs, in0=xs, scalar=ALPHA, in1=ps[:, :],
                                                             op0=ALU.mult, op1=ALU.add),
                     reads=[pk, ('xres', t)], writes=[('xres', t)])
        self.interleave([self.layer_norm_tile(ph, t, 0) for t in range(nt)])
        for t in range(nt):
            self.make_xT(t)
        self.dump('x1', self.xres[:, 0:nt, :], [128, nt, D], [('xres', t) for t in range(nt)])
        hT = ph.T("hT", [128, 22, N], BF16)
        sg = ph.T("sgt", [128, N], F32)
        xkeys = [('xT', t) for t in range(nt)]
        NB = DFF // 128
        for j0 in range(0, NB, 4):
            nb = min(4, NB - j0)
            wg, wgk = self.wtile('w_gu', l, [(j0 * 128, nb * 128)])
            wu, wuk = self.wtile('w_gu', l, [(DFF + j0 * 128, nb * 128)])
            for b in range(nb):
                j = j0 + b
                psg, pgk = self.ps()
                for k in range(8):
                    self.mm(psg[:, 0:N], wg[:, k, b * 128:(b + 1) * 128], self.xT[:, k, 0:N], k == 0, k == 7,
                            reads=wgk + xkeys, writes=[pgk])
                psu, puk = self.ps()
                for k in range(8):
                    self.mm(psu[:, 0:N], wu[:, k, b * 128:(b + 1) * 128], self.xT[:, k, 0:N], k == 0, k == 7,
                            reads=wuk + xkeys, writes=[puk])
                S.op('act', lambda e: e.activation(out=sg[:], in_=psg[:, 0:N], func=AF.Silu), reads=[pgk], writes=['sgt'])
                S.op('dve', lambda e: e.tensor_tensor(out=hT[:, j, :], in0=sg[:], in1=psu[:, 0:N], op=ALU.mult),
                     reads=['sgt', puk], writes=[('hT', j)])
        for half in range(2):
            accs = [self.ps() for _ in range(nt)]
            pieces = [(0, 8), (8, 16), (16, 22)]
            for pi, (k0, k1) in enumerate(pieces):
                wb, wk = self.wtile('w_down', l, [(half * 512, 512)], rows=(k0, k1))
                for t in range(nt):
                    ps, pk = accs[t]
                    for k in range(k0, k1):
                        self.mm(ps[:, :], hT[:, k, t * 128:(t + 1) * 128], wb[:, k - k0, :], k == 0, k == NB - 1,
                                reads=wk + [('hT', k)], writes=[pk])
            for t in range(nt):
                ps, pk = accs[t]
                xs = self.xres[:, t, half * 512:(half + 1) * 512]
                S.op('dve', lambda e: e.scalar_tensor_tensor(out=xs, in0=xs, scalar=ALPHA, in1=ps[:, :],
                                                             op0=ALU.mult, op1=ALU.add),
                     reads=[pk, ('xres', t)], writes=[('xres', t)])
        self.interleave([self.layer_norm_tile(ph, t, 1) for t in range(nt)])
        for t in range(nt):
            if l < self.depth - 1:
                self.make_xT(t)
        self.dump('x2', self.xres[:, 0:nt, :], [128, nt, D], [('xres', t) for t in range(nt)])
        ph.close()

    def interleave(self, gens):
        gens = list(gens)
        while gens:
            for g_ in list(gens):
                try:
                    next(g_)
                except StopIteration:
                    gens.remove(g_)

    def interleave_staged(self, gens):
        gens = list(gens)
        waiting = []
        active = list(gens)
        while active:
            for g_ in list(active):
                try:
                    r = next(g_)
                except StopIteration:
                    active.remove(g_)
                    continue
                if r == 'CHAIN':
                    active.remove(g_)
                    waiting.append(g_)
        waiting.sort(key=lambda g_: gens.index(g_))
        post = []
        for g_ in waiting:
            while True:
                try:
                    r = next(g_)
                except StopIteration:
                    break
                if r == 'POST':
                    post.append(g_)
                    break
        self.interleave(post)

    def rglru(self, l, stream, g, ngroups, nt, N):
        S, I = self.S, self.I
        ph = Phase(self)
        nseq, T = (1, 512) if stream == 'P' else (NSEQ_S, T_S)
        convst, hst = self.convst[stream], self.hst[stream]
        ckey, hkey = ('convst', stream), ('hst', stream)
        xkeys = [('xT', t) for t in range(nt)]
        NW = 4
        xrh = [ph.T("xrh%d" % i, [128, nseq, 3 + T], F32) for i in range(NW)]
        xc = [ph.T("xc%d" % i, [128, nseq, T], F32) for i in range(NW)]
        xcb = [ph.T("xcb%d" % i, [128, N], BF16) for i in range(NW)]
        rr = [ph.T("rg_r%d" % i, [128, N], F32) for i in range(NW)]
        ii = [ph.T("rg_i%d" % i, [128, N], F32) for i in range(NW)]
        aa = [ph.T("rg_a%d" % i, [128, nseq, T], F32) for i in range(NW)]
        uu = [ph.T("rg_u%d" % i, [128, nseq, T], F32) for i in range(NW)]
        hh = ph.T("rg_h", [128, 4, nseq, T], F32)
        ph.names += ['xrh', 'xc', 'xcb', 'rg_r', 'rg_i', 'rg_a', 'rg_u']
        wa_l = ph.T("wa_bd", [128, 2, 8, 128], BF16)
        S.dma('act', wa_l[:], self.scr_wabd[:, l], reads=['scr_wabd'], writes=['wa_bd'])
        flat = lambda t3: t3[:].rearrange("p n t -> p (n t)")

        def block(half, b, wb, wk, wg, wgk):
            j = half * 4 + b
            K_ = lambda nm: (nm, b)
            ps, pk = self.ps()
            for k in range(8):
                self.mm(ps[:, 0:N], wb[:, k, b * 128:(b + 1) * 128], self.xT[:, k, 0:N], k == 0, k == 7,
                        reads=wk + xkeys, writes=[pk])
            yield
            S.op('act', lambda e: e.copy(out=xrh[b][:, :, 3:3 + T], in_=ps[:, 0:N].rearrange("p (n t) -> p n t", n=nseq)),
                 reads=[pk], writes=[K_('xrh')])
            S.op('pool', lambda e: e.tensor_copy(out=xrh[b][:, :, 0:3], in_=convst[:, l, j, :, :]), reads=[ckey], writes=[K_('xrh')])
            yield
            cw = self.p_cw
            S.op('dve', lambda e: e.tensor_scalar(out=xc[b][:], in0=xrh[b][:, :, 0:T], scalar1=cw[:, l, 0, j:j + 1],
                                                  scalar2=self.p_cb[:, l, j:j + 1], op0=ALU.mult, op1=ALU.add),
                 reads=[K_('xrh'), 'p_cw', 'p_cb'], writes=[K_('xc')])
            yield
            for i in range(1, 4):
                S.op('dve', lambda e: e.scalar_tensor_tensor(out=xc[b][:], in0=xrh[b][:, :, i:i + T], scalar=cw[:, l, i, j:j + 1],
                                                             in1=xc[b][:], op0=ALU.mult, op1=ALU.add),
                     reads=[K_('xrh'), 'p_cw', K_('xc')], writes=[K_('xc')])
                yield
            S.op('pool', lambda e: e.tensor_copy(out=convst[:, l, j, :, :], in_=xrh[b][:, :, T:T + 3]), reads=[K_('xrh')], writes=[ckey])
            S.op('act', lambda e: e.copy(out=xcb[b][:], in_=flat(xc[b])), reads=[K_('xc')], writes=[K_('xcb')])
            yield
            pss = []
            for gi in range(2):
                ps2, pk2 = self.ps()
                self.mm(ps2[:, 0:N], wa_l[:, gi, j, :], xcb[b][:], True, True, reads=['wa_bd', K_('xcb')], writes=[pk2])
                pss.append((ps2, pk2))
            yield
            for gi, (dst, dk, bias) in enumerate(((rr[b], K_('rg_r'), self.p_ba), (ii[b], K_('rg_i'), self.p_bx))):
                ps2, pk2 = pss[gi]
                S.op('act', lambda e: e.activation(out=dst[:], in_=ps2[:, 0:N], func=AF.Sigmoid, bias=bias[:, l, j:j + 1], scale=1.0),
                     reads=[pk2, 'p_ba', 'p_bx'], writes=[dk])
            yield
            S.op('act', lambda e: e.activation(out=flat(aa[b]), in_=rr[b][:], func=AF.Exp, scale=self.p_c8[:, l, j:j + 1]),
                 reads=[K_('rg_r'), 'p_c8'], writes=[K_('rg_a')])
            S.op('pool', lambda e: e.tensor_tensor(out=ii[b][:], in0=ii[b][:], in1=flat(xc[b]), op=ALU.mult), reads=[K_('rg_i'), K_('xc')], writes=[K_('rg_i')])
            yield
            S.op('dve', lambda e: e.tensor_tensor(out=rr[b][:], in0=flat(aa[b]), in1=flat(aa[b]), op=ALU.mult), reads=[K_('rg_a')], writes=[K_('rg_r')])
            yield
            S.op('dve', lambda e: e.tensor_scalar(out=rr[b][:], in0=rr[b][:], scalar1=-1.0, scalar2=1.0, op0=ALU.mult, op1=ALU.add),
                 reads=[K_('rg_r')], writes=[K_('rg_r')])
            yield
            S.op('dve', lambda e: e.tensor_scalar(out=rr[b][:], in0=rr[b][:], scalar1=1e-30, scalar2=None, op0=ALU.max), reads=[K_('rg_r')], writes=[K_('rg_r')])
            yield
            S.op('act', lambda e: e.activation(out=rr[b][:], in_=rr[b][:], func=AF.Sqrt), reads=[K_('rg_r')], writes=[K_('rg_r')])
            yield
            S.op('dve', lambda e: e.tensor_tensor(out=flat(uu[b]), in0=ii[b][:], in1=rr[b][:], op=ALU.mult), reads=[K_('rg_i'), K_('rg_r')], writes=[K_('rg_u')])
            yield
            for n in range(nseq):
                S.op('dve', lambda e: e.tensor_tensor_scan(out=hh[:, b, n, :], data0=aa[b][:, n, :], data1=uu[b][:, n, :],
                                                           initial=hst[:, l, j, n:n + 1], op0=ALU.mult, op1=ALU.add),
                     reads=[K_('rg_a'), K_('rg_u'), hkey], writes=[('rg_h', b)])
            yield
            S.op('pool', lambda e: e.tensor_copy(out=hst[:, l, j, :], in_=hh[:, b, :, T - 1]), reads=[('rg_h', b)], writes=[hkey])
            ps, pk = self.ps()
            for k in range(8):
                self.mm(ps[:, 0:N], wg[:, k, b * 128:(b + 1) * 128], self.xT[:, k, 0:N], k == 0, k == 7,
                        reads=wgk + xkeys, writes=[pk])
            yield
            gbt = ii[b]
            S.op('act', lambda e: e.activation(out=gbt[:], in_=ps[:, 0:N], func=AF.Sigmoid), reads=[pk], writes=[K_('rg_i')])
            yield
            S.op('dve', lambda e: e.tensor_tensor(out=gbt[:], in0=gbt[:], in1=hh[:, b, :, :].rearrange("p n t -> p (n t)"), op=ALU.mult),
                 reads=[K_('rg_i'), ('rg_h', b)], writes=[K_('rg_i')])
            yield
            S.op('pool', lambda e: e.tensor_tensor(out=self.mixacc[:, j, 0:N], in0=self.mixacc[:, j, 0:N], in1=gbt[:], op=ALU.add),
                 reads=[K_('rg_i'), ('mixacc', j)], writes=[('mixacc', j)])
            yield

        for half in range(2):
            wb, wk = self.wtile('w_in', l, [(OFF_RNN + half * 512, 512)])
            wg, wgk = self.wtile('w_in', l, [(OFF_GATE + D + half * 512, 512)])
            self.interleave([block(half, b, wb, wk, wg, wgk) for b in range(4)])
        self.dump('mixAB' if stream == 'P' else 's_mixAB', self.mixacc[:, :, 0:N], [128, 8, N], [('mixacc', j) for j in range(8)])
        ph.close()

    def rwkv(self, l, stream, g, ngroups, nt, N):
        S, I = self.S, self.I
        ph = Phase(self)
        is_s = stream == 'S'
        nseq, T = (NSEQ_S, T_S) if is_s else (1, 512)
        shst = self.shst[stream]
        skey = ('shst', stream)
        xkeys = [('xT', t) for t in range(nt)]
        pch = ph.T("pch", [128, nseq, 1 + T], F32)
        dd = ph.T("rw_d", [128, N], F32)
        lwla = ph.T("lwla", [128, N], BF16)
        siglg = ph.T("siglg", [128, 2, N], BF16)
        self.wlora_l = ph.T("wlora", [128, D], BF16)
        self.wgup_l = ph.T("wgup", [128, 2, D], BF16)
        S.dma('act', self.wlora_l[:], self.scr_wlora[:, l], reads=['scr_wlora'], writes=['wlora'])
        S.dma('act', self.wgup_l[:], self.scr_wgup[:, l], reads=['scr_wgup'], writes=['wgup'])
        lnw = ph.T("lnw_t", [128, D], F32)
        lnb = ph.T("lnb_t", [128, D], F32)
        S.dma('act', lnw[:], I['rw_lnw'][l:l + 1, :].broadcast_to([128, D]), writes=['lnw_t'])
        S.dma('act', lnb[:], I['rw_lnb'][l:l + 1, :].broadcast_to([128, D]), writes=['lnb_t'])
        flat = lambda t3: t3[:].rearrange("p n t -> p (n t)")

        def shifted(ps, pk, blk, w, out_ap, out_key, func=None, bufs=None, S_=None):
            pch_, dd_ = bufs if bufs is not None else (pch, dd)
            So = S_ if S_ is not None else S
            So.op('act', lambda e: e.copy(out=pch_[0:w, :, 1:1 + T], in_=ps[0:w, 0:N].rearrange("p (n t) -> p n t", n=nseq)),
                 reads=[pk], writes=['pch'])
            yield
            So.op('pool', lambda e: e.tensor_copy(out=pch_[0:w, :, 0], in_=shst[0:w, l, blk, :]), reads=[skey], writes=['pch'])
            yield
            So.op('pool', lambda e: e.tensor_copy(out=shst[0:w, l, blk, :], in_=pch_[0:w, :, T]), reads=['pch'], writes=[skey])
            yield
            So.op('dve', lambda e: e.tensor_tensor(out=dd_[0:w, :].rearrange("p (n t) -> p n t", n=nseq), in0=pch_[0:w, :, 0:T],
                                                  in1=pch_[0:w, :, 1:1 + T], op=ALU.subtract), reads=['pch'], writes=['rw_d'])
            yield
            if func is None:
                So.op('dve', lambda e: e.scalar_tensor_tensor(out=out_ap.rearrange("p (n t) -> p n t", n=nseq), in0=dd_[0:w, :].rearrange("p (n t) -> p n t", n=nseq),
                                                             scalar=self.p_mu[0:w, l, blk:blk + 1], in1=pch_[0:w, :, 1:1 + T],
                                                             op0=ALU.mult, op1=ALU.add), reads=['rw_d', 'pch', 'p_mu'], writes=[out_key])
                yield
            else:
                So.op('dve', lambda e: e.scalar_tensor_tensor(out=dd_[0:w, :].rearrange("p (n t) -> p n t", n=nseq), in0=dd_[0:w, :].rearrange("p (n t) -> p n t", n=nseq),
                                                             scalar=self.p_mu[0:w, l, blk:blk + 1], in1=pch_[0:w, :, 1:1 + T],
                                                             op0=ALU.mult, op1=ALU.add), reads=['rw_d', 'pch', 'p_mu'], writes=['rw_d'])
                yield
                func(dd_)

        wb, wk = self.wtile('w_in', l, [(OFF_RW + 3072, 288)])
        for bi, (c0, w) in enumerate(((0, 128), (128, 128), (256, 32))):
            ps, pk = self.ps()
            for k in range(8):
                self.mm(ps[0:w, 0:N], wb[:, k, c0:c0 + w], self.xT[:, k, 0:N], k == 0, k == 7, reads=wk + xkeys, writes=[pk])
            if bi == 0:
                def f0(d_):
                    S.op('act', lambda e: e.activation(out=lwla[0:64, :], in_=d_[0:64, :], func=AF.Tanh), reads=['rw_d'], writes=['lwla'])
                    S.op('act', lambda e: e.copy(out=lwla[64:128, :], in_=d_[64:128, :]), reads=['rw_d'], writes=['lwla'])
                for _ in shifted(ps, pk, 24, 128, None, None, f0):
                    pass
            else:
                def f1(d_, bi=bi, w=w):
                    S.op('act', lambda e: e.activation(out=siglg[0:w, bi - 1, :], in_=d_[0:w, :], func=AF.Sigmoid), reads=['rw_d'], writes=['siglg'])
                for _ in shifted(ps, pk, 24 + bi, w, None, None, f1):
                    pass

        if not is_s:
            for rnd in range(4):
                self.rwkv_round(ph, l, stream, g, nt, N, rnd, shifted, lwla, siglg, lnw, lnb, xkeys, None)
        else:
            ytok = ph.T("s2_ytok", [128, D], F32)
            nats = []
            self.pre_nat = [(ph.T("w_vT%d" % r_, [128, 2, N], BF16), ph.T("w_pT%d" % r_, [128, 2, N], BF16),
                             ph.T("w_ggT%d" % r_, [128, 2, N], F32)) for r_ in range(4)]
            phA = Phase(self)
            V6 = phA.T("s2_V6", [128, 6, D], F32)
            for rnd in range(4):
                nats.append(self.rwkv_round(ph, l, stream, g, nt, N, rnd, shifted, lwla, siglg, lnw, lnb, xkeys, V6))
            for n in range(NSEQ_S):
                dstn = self.scr_v[l, n].rearrange("j hh q t d -> q t (j hh) d")
                for q in range(6):
                    S.dma('act', dstn[q], V6[n * 8:(n + 1) * 8, q, :].rearrange("t (jh d) -> t jh d", d=64),
                          reads=[('s2_V6', jx) for jx in range(8)], writes=[('scr_v', l, n, q)])
            phA.names.append('s2_V6')
            phA.close()
            svkeys = [('scr_v', l, n, q) for n in range(NSEQ_S) for q in range(6)]
            for hh in range(2):
                phB = Phase(self)
                Sst = phB.T("s2_S", [128, 64, 64], F32)
                tmp = phB.T("s2_tmp", [128, 64, 64], F32)
                vec = phB.T("s2_vec", [128, 6, T_S, 64], F32)
                yS = phB.T("s2_y", [128, T_S, 64], F32)
                sa = phB.T("s2_sa", [128, 64], F32)
                S.dma('act', Sst[:].rearrange("p v k -> p (v k)"),
                      self.I['swkv'][l].rearrange("n (j hh) v k -> hh (n j) (v k)", hh=2)[hh], writes=['s2_S'])
                S.dma('act', vec[:].rearrange("p q t d -> p (q t d)"),
                      self.scr_v[l].rearrange("n j hh q t d -> hh (n j) (q t d)")[hh], reads=svkeys, writes=['s2_vec'])
                bv = lambda q, t: vec[:, q, t, :].unsqueeze(1).to_broadcast([128, 64, 64])
                bo = lambda ap: ap.unsqueeze(2).to_broadcast([128, 64, 64])
                for t in range(T_S):
                    S.op('dve', lambda e: e.tensor_tensor(out=tmp[:], in0=Sst[:], in1=bv(4, t), op=ALU.mult), reads=['s2_S', 's2_vec'], writes=['s2_tmp'])
                    S.op('dve', lambda e: e.tensor_reduce(out=sa[:], in_=tmp[:], axis=AX.X, op=ALU.add), reads=['s2_tmp'], writes=['s2_sa'])
                    S.op('pool', lambda e: e.tensor_tensor(out=Sst[:], in0=Sst[:], in1=bv(1, t), op=ALU.mult), reads=['s2_S', 's2_vec'], writes=['s2_S'])
                    S.op('dve', lambda e: e.tensor_tensor(out=tmp[:], in0=bo(sa[:]), in1=bv(5, t), op=ALU.mult), reads=['s2_sa', 's2_vec'], writes=['s2_tmp'])
                    S.op('pool', lambda e: e.tensor_tensor(out=Sst[:], in0=Sst[:], in1=tmp[:], op=ALU.subtract), reads=['s2_S', 's2_tmp'], writes=['s2_S'])
                    S.op('dve', lambda e: e.tensor_tensor(out=tmp[:], in0=bo(vec[:, 3, t, :]), in1=bv(2, t), op=ALU.mult), reads=['s2_vec'], writes=['s2_tmp'])
                    S.op('pool', lambda e: e.tensor_tensor(out=Sst[:], in0=Sst[:], in1=tmp[:], op=ALU.add), reads=['s2_S', 's2_tmp'], writes=['s2_S'])
                    S.op('dve', lambda e: e.tensor_tensor(out=tmp[:], in0=Sst[:], in1=bv(0, t), op=ALU.mult), reads=['s2_S', 's2_vec'], writes=['s2_tmp'])
                    S.op('dve', lambda e: e.tensor_reduce(out=yS[:, t, :], in_=tmp[:], axis=AX.X, op=ALU.add), reads=['s2_tmp'], writes=['s2_y'])
                S.dma('act', self.O['swkv_o'][l].rearrange("n (j hh) v k -> hh (n j) (v k)", hh=2)[hh], Sst[:].rearrange("p v k -> p (v k)"),
                      reads=['s2_S'], writes=[('o_swkv', l, hh)])
                S.dma('act', self.scr_y[l].rearrange("n j hh t v -> hh (n j) (t v)")[hh], yS[:].rearrange("p t v -> p (t v)"),
                      reads=['s2_y'], writes=[('scr_y', l, hh)])
                phB.close()
            for n in range(NSEQ_S):
                S.dma('act', ytok[n * 8:(n + 1) * 8, :].rearrange("t (jh v) -> t jh v", v=64),
                      self.scr_y[l, n].rearrange("j hh t v -> t (j hh) v"), reads=[('scr_y', l, 0), ('scr_y', l, 1)], writes=[('s2_ytok', n)])
            for rnd in range(4):
                for _ in self.wkv_tile(ph, l, 2 * rnd, 0, nats[rnd], lnw, lnb,
                                       s2_y=(ytok[:, rnd * 256:(rnd + 1) * 256], [('s2_ytok', n) for n in range(NSEQ_S)])):
                    pass
        self.dump('mixed' if not is_s else 's_mixed', self.mixacc[:, :, 0:N], [128, 8, N], [('mixacc', j) for j in range(8)])
        ph.names.append('s2_ytok')
        ph.close()

    def rwkv_round(self, ph0, l, stream, g, nt, N, rnd, shifted, lwla, siglg, lnw, lnb, xkeys, V6):
        S, I = self.S, self.I
        is_s = stream == 'S'
        ph = Phase(self)
        j0 = 2 * rnd
        sf = str(rnd) if is_s else ''
        pho = ph0 if is_s else ph
        aT = ph.T("w_aT", [128, 2, N], BF16)
        rT = ph.T("w_rT", [128, 2, N], BF16)
        bT = ph.T("w_bT", [128, 2, N], BF16)
        kT = ph.T("w_kT", [128, 2, N], BF16)
        if is_s:
            vT, pT, ggT = self.pre_nat[rnd]
        else:
            vT = ph.T("w_vT", [128, 2, N], BF16)
            pT = ph.T("w_pT", [128, 2, N], BF16)
            ggT = ph.T("w_ggT", [128, 2, N], F32)
        gC = ph.T("w_gC", [128, 2, 4], F32)
        php = Phase(self)
        TK = ('w_r32', 'w_k32', 'w_v32', 'w_lw32', 'w_cl', 'w_a32', 'w_kk32', 'w_t32', 'w_tb', 'w_eI', 'w_eN', 'pch', 'rw_d')
        nseq_, T_ = (NSEQ_S, T_S) if is_s else (1, 512)
        pbufs = []
        for jj_ in range(2):
            d_ = {}
            for nm_ in ('r32', 'k32', 'v32', 'lw32', 'cl', 'a32', 'kk32', 't32', 'eI', 'eN'):
                d_[nm_] = php.T("w_%s_%d" % (nm_, jj_), [128, N], F32)
            d_['tb'] = php.T("w_tb_%d" % jj_, [128, N], BF16)
            d_['pch'] = php.T("pch_%d" % jj_, [128, nseq_, 1 + T_], F32)
            d_['dd'] = php.T("rw_d_%d" % jj_, [128, N], F32)
            pbufs.append(d_)
        php.names += [k_ + '_0' for k_ in TK] + [k_ + '_1' for k_ in TK]
        c0 = OFF_RW + j0 * 128
        wb, wk = self.wtile('w_in', l, [(c0, 256), (c0 + D, 256)])
        wb2, wk2 = self.wtile('w_in', l, [(c0 + 2 * D, 256), (OFF_GATE + 2 * D + j0 * 128, 256)])
        S0 = S

        def pair(jj):
            j = j0 + jj
            B_ = pbufs[jj]
            r32, k32, v32, lw32, cl, a32, kk32, t32, eI, eN, tb = (B_[k_] for k_ in ('r32', 'k32', 'v32', 'lw32', 'cl', 'a32', 'kk32', 't32', 'eI', 'eN', 'tb'))
            km = lambda ks: [((k_ + '_%d' % jj) if (isinstance(k_, str) and k_ in TK) else k_) for k_ in ks]

            class _S:
                @staticmethod
                def op(e, fn_, reads=(), writes=()):
                    return S0.op(e, fn_, reads=km(reads), writes=km(writes))
            S = _S()
            mm0 = self.mm

            def MM(out, lhsT, rhs, start, stop, reads, writes):
                return mm0(out, lhsT, rhs, start, stop, reads=km(reads), writes=km(writes))
            for qi, (dst, dkey, wbx, wkx, coff, blk) in enumerate(((r32, 'w_r32', wb, wk, jj * 128, j),
                                                                  (k32, 'w_k32', wb, wk, 256 + jj * 128, 8 + j),
                                                                  (v32, 'w_v32', wb2, wk2, jj * 128, 16 + j))):
                ps, pk = self.ps()
                for k in range(8):
                    MM(ps[:, 0:N], wbx[:, k, coff:coff + 128], self.xT[:, k, 0:N], k == 0, k == 7, reads=wkx + xkeys, writes=[pk])
                yield from shifted(ps, pk, blk, 128, dst[:], dkey, bufs=(B_['pch'], B_['dd']), S_=S)
            ps, pk = self.ps()
            MM(ps[:, 0:N], self.wlora_l[0:64, j * 128:(j + 1) * 128], lwla[0:64, :], True, True, reads=['wlora', 'lwla'], writes=[pk])
            S.op('act', lambda e: e.activation(out=lw32[:], in_=ps[:, 0:N], func=AF.Sigmoid, bias=self.p_w0[:, l, j:j + 1], scale=1.0),
                 reads=[pk, 'rw_w0'], writes=['w_lw32'])
            yield
            S.op('pool', lambda e: e.tensor_scalar(out=lw32[:], in0=lw32[:], scalar1=-math.exp(-0.5), scalar2=None, op0=ALU.mult),
                 reads=['w_lw32'], writes=['w_lw32'])
            yield
            ps, pk = self.ps()
            MM(ps[:, 0:N], self.wlora_l[64:128, j * 128:(j + 1) * 128], lwla[64:128, :], True, True, reads=['wlora', 'lwla'], writes=[pk])
            S.op('act', lambda e: e.activation(out=a32[:], in_=ps[:, 0:N], func=AF.Sigmoid, bias=self.p_a0[:, l, j:j + 1], scale=1.0),
                 reads=[pk, 'rw_a0'], writes=['w_a32'])
            yield
            S.op('dve', lambda e: e.tensor_scalar(out=kk32[:], in0=k32[:], scalar1=self.p_kk[:, l, j:j + 1], scalar2=None, op0=ALU.mult),
                 reads=['w_k32', 'rw_kk'], writes=['w_kk32'])
            yield
            S.op('act', lambda e: e.activation(out=tb[:], in_=kk32[:], func=AF.Square), reads=['w_kk32'], writes=['w_tb'])
            yield
            ps, pk = self.ps()
            MM(ps[:, 0:N], self.blkb[:, :], tb[:], True, True, reads=['blkb', 'w_tb'], writes=[pk])
            S.op('act', lambda e: e.activation(out=t32[:], in_=ps[:, 0:N], func=AF.Sqrt), reads=[pk], writes=['w_t32'])
            yield
            S.op('dve', lambda e: e.tensor_scalar(out=t32[:], in0=t32[:], scalar1=1e-12, scalar2=None, op0=ALU.max), reads=['w_t32'], writes=['w_t32'])
            yield
            S.op('dve', lambda e: e.reciprocal(out=t32[:], in_=t32[:]), reads=['w_t32'], writes=['w_t32'])
            yield
            S.op('dve', lambda e: e.tensor_tensor(out=kk32[:], in0=kk32[:], in1=t32[:], op=ALU.mult), reads=['w_kk32', 'w_t32'], writes=['w_kk32'])
            yield
            S.op('dve', lambda e: e.tensor_scalar(out=t32[:], in0=a32[:], scalar1=self.p_ka[:, l, j:j + 1], scalar2=self.p_omka[:, l, j:j + 1],
                                                  op0=ALU.mult, op1=ALU.add), reads=['w_a32', 'rw_ka', 'p_omka'], writes=['w_t32'])
            yield
            S.op('pool', lambda e: e.tensor_tensor(out=k32[:], in0=k32[:], in1=t32[:], op=ALU.mult), reads=['w_k32', 'w_t32'], writes=['w_k32'])
            yield
            if is_s:
                S.op('pool', lambda e: e.tensor_tensor(out=t32[:], in0=r32[:], in1=k32[:], op=ALU.mult), reads=['w_r32', 'w_k32'], writes=['w_t32'])
                yield
                S.op('dve', lambda e: e.tensor_scalar(out=pT[:, jj, :], in0=t32[:], scalar1=self.p_rk[:, l, j:j + 1], scalar2=None, op0=ALU.mult),
                     reads=['w_t32', 'rw_rk'], writes=['w_pT' + sf])
                yield
                S.op('act', lambda e: e.copy(out=vT[:, jj, :], in_=v32[:]), reads=['w_v32'], writes=['w_vT' + sf])
                yield
                S.op('act', lambda e: e.activation(out=eI[:], in_=lw32[:], func=AF.Exp), reads=['w_lw32'], writes=['w_eI'])
                yield
                S.op('dve', lambda e: e.tensor_tensor(out=eN[:], in0=kk32[:], in1=a32[:], op=ALU.mult), reads=['w_kk32', 'w_a32'], writes=['w_eN'])
                yield
                psa, pka = self.ps()
                for q, (src, skey) in enumerate(((r32, 'w_r32'), (eI, 'w_eI'), (k32, 'w_k32'), (v32, 'w_v32'))):
                    S.op('pe', lambda e: e.transpose(out=psa[:, q * 128:(q + 1) * 128], in_=src[:, :], identity=self.identf[:]),
                         reads=[skey, 'identf'], writes=[pka])
                    yield
                psb_, pkb = self.ps()
                for q, (src, skey) in enumerate(((kk32, 'w_kk32'), (eN, 'w_eN'))):
                    S.op('pe', lambda e: e.transpose(out=psb_[:, q * 128:(q + 1) * 128], in_=src[:, :], identity=self.identf[:]),
                         reads=[skey, 'identf'], writes=[pkb])
                    yield
                S.op('act', lambda e: e.copy(out=V6[:, 0:4, j * 128:(j + 1) * 128], in_=psa[:, :].rearrange("p (q c) -> p q c", q=4)),
                     reads=[pka], writes=[('s2_V6', j)])
                yield
                S.op('dve', lambda e: e.tensor_copy(out=V6[:, 4:6, j * 128:(j + 1) * 128], in_=psb_[:, 0:256].rearrange("p (q c) -> p q c", q=2)),
                     reads=[pkb], writes=[('s2_V6', j)])
                yield
            if not is_s:
                S.op('dve', lambda e: e.tensor_tensor_scan(out=cl[:], data0=self.rmaskP[:, 0:N], data1=lw32[:], initial=0.0, op0=ALU.mult, op1=ALU.add),
                     reads=['rmaskP', 'w_lw32'], writes=['w_cl'])
                yield
                S.op('act', lambda e: e.activation(out=eI[:], in_=cl[:], func=AF.Exp), reads=['w_cl'], writes=['w_eI'])
                yield
                S.op('act', lambda e: e.activation(out=eN[:], in_=cl[:], func=AF.Exp, scale=-1.0), reads=['w_cl'], writes=['w_eN'])
                yield
                S.op('pool', lambda e: e.tensor_copy(out=gC[:, jj, :], in_=eI[:, 127:N:128]), reads=['w_eI'], writes=['w_gC'])
                yield
                S.op('dve', lambda e: e.tensor_tensor(out=rT[:, jj, :], in0=r32[:], in1=eI[:], op=ALU.mult), reads=['w_r32', 'w_eI'], writes=['w_rT'])
                yield
                S.op('pool', lambda e: e.tensor_tensor(out=t32[:], in0=r32[:], in1=k32[:], op=ALU.mult), reads=['w_r32', 'w_k32'], writes=['w_t32'])
                yield
                S.op('dve', lambda e: e.tensor_scalar(out=pT[:, jj, :], in0=t32[:], scalar1=self.p_rk[:, l, j:j + 1], scalar2=None, op0=ALU.mult),
                     reads=['w_t32', 'rw_rk'], writes=['w_pT' + sf])
                yield
                S.op('dve', lambda e: e.tensor_tensor(out=kT[:, jj, :], in0=k32[:], in1=eN[:], op=ALU.mult), reads=['w_k32', 'w_eN'], writes=['w_kT'])
                yield
                S.op('pool', lambda e: e.tensor_tensor(out=t32[:], in0=kk32[:], in1=a32[:], op=ALU.mult), reads=['w_kk32', 'w_a32'], writes=['w_t32'])
                yield
                S.op('dve', lambda e: e.tensor_tensor(out=bT[:, jj, :], in0=t32[:], in1=eN[:], op=ALU.mult), reads=['w_t32', 'w_eN'], writes=['w_bT'])
                yield
                S.op('pool', lambda e: e.tensor_tensor(out=t32[:], in0=cl[:], in1=lw32[:], op=ALU.subtract), reads=['w_cl', 'w_lw32'], writes=['w_t32'])
                yield
                S.op('act', lambda e: e.activation(out=t32[:], in_=t32[:], func=AF.Exp), reads=['w_t32'], writes=['w_t32'])
                yield
                S.op('dve', lambda e: e.scalar_tensor_tensor(out=aT[:, jj, :], in0=kk32[:], scalar=-1.0, in1=t32[:], op0=ALU.mult, op1=ALU.mult),
                     reads=['w_kk32', 'w_t32'], writes=['w_aT'])
                yield
                S.op('act', lambda e: e.copy(out=vT[:, jj, :], in_=v32[:]), reads=['w_v32'], writes=['w_vT' + sf])
                yield
            ps, pk = self.ps()
            MM(ps[:, 0:N], self.wgup_l[:, 0, j * 128:(j + 1) * 128], siglg[:, 0, :], True, False, reads=['wgup', 'siglg'], writes=[pk])
            MM(ps[:, 0:N], self.wgup_l[0:32, 1, j * 128:(j + 1) * 128], siglg[0:32, 1, :], False, True, reads=['wgup', 'siglg'], writes=[pk])
            ps2, pk2 = self.ps()
            for k in range(8):
                MM(ps2[:, 0:N], wb2[:, k, 256 + jj * 128:256 + (jj + 1) * 128], self.xT[:, k, 0:N], k == 0, k == 7, reads=wk2 + xkeys, writes=[pk2])
            S.op('act', lambda e: e.activation(out=t32[:], in_=ps2[:, 0:N], func=AF.Sigmoid), reads=[pk2], writes=['w_t32'])
            yield
            S.op('dve', lambda e: e.tensor_tensor(out=ggT[:, jj, :], in0=ps[:, 0:N], in1=t32[:], op=ALU.mult), reads=[pk, 'w_t32'], writes=['w_ggT' + sf])
            yield

        gens_ = [pair(0), pair(1)]
        if is_s:
            for g_ in gens_:
                for _ in g_:
                    pass
        else:
            self.interleave(gens_)
        php.close()
        nat = dict(aT=aT, rT=rT, bT=bT, kT=kT, vT=vT, pT=pT, ggT=ggT, gC=gC, sf=sf)
        if not is_s:
            for t0_ in range(0, nt, 2):
                self.interleave_staged([self.wkv_tile(ph, l, j0, t0_ + i_, nat, lnw, lnb, slot=i_) for i_ in range(2)])
        ph.close()
        return nat

    def wkv_tile(self, ph0, l, j0, t, nat, lnw, lnb, s2_y=None, slot=0):
        S0 = self.S
        sl = '_s%d' % slot
        km = lambda ks: [((k + sl) if isinstance(k, str) and k.startswith('k_') else k) for k in ks]

        class _S:
            @staticmethod
            def op(e, fn_, reads=(), writes=()):
                return S0.op(e, fn_, reads=km(reads), writes=km(writes))
        S = _S()
        mm0 = self.mm

        def MM(out, lhsT, rhs, start, stop, reads, writes):
            return mm0(out, lhsT, rhs, start, stop, reads=km(reads), writes=km(writes))

        class _Reuse:
            def T(_s, name, shape, dt):
                name = name + sl
                cache = ph0.__dict__.setdefault('_tile_cache', {})
                if name not in cache:
                    cache[name] = ph0.T(name, shape, dt)
                return cache[name]

            def close(_s):
                pass
        ph = _Reuse()
        aT, rT, bT, kT, vT, pT, ggT, gC = (nat[k] for k in ('aT', 'rT', 'bT', 'kT', 'vT', 'pT', 'ggT', 'gC'))
        sf = nat['sf']
        cs = slice(t * 128, (t + 1) * 128)
        tok = ph.T("k_tok", [128, 4, 2, 128], BF16)
        ps, pk = self.ps()
        psb = ps[:].bitcast(BF16)
        for qi, (src, skey) in enumerate(((vT, 'w_vT' + sf), (bT, 'w_bT'), (kT, 'w_kT'), (pT, 'w_pT' + sf))):
            if s2_y is not None and qi in (1, 2):
                src, skey = vT, 'w_vT' + sf
            for jj in range(2):
                S.op('pe', lambda e: e.transpose(out=psb[:, (qi * 2 + jj) * 128:(qi * 2 + jj + 1) * 128], in_=src[:, jj, cs],
                                                 identity=self.identb[:]), reads=[skey, 'identb'], writes=[pk])
                yield
        S.op('act', lambda e: e.copy(out=tok[:].rearrange("p q j c -> p (q j c)"), in_=psb), reads=[pk], writes=['k_tok'])
        yield
        if s2_y is None:
            blks = {}
            for nm, (src, skey) in (('a', (aT, 'w_aT')), ('r', (rT, 'w_rT')), ('b', (bT, 'w_bT'))):
                isnew = ("k_blk" + nm + sl) not in ph0.__dict__.get('_tile_cache', {})
                bl = ph.T("k_blk" + nm, [128, 2, 2, 128], BF16)
                if isnew:
                    S.op('pool', lambda e: e.memset(bl[:], 0.0), writes=['k_blk' + nm])
                    yield
                for hh in range(2):
                    eng = 'act' if hh else 'pool'
                    if eng == 'act':
                        S.op('act', lambda e: e.copy(out=bl[hh * 64:(hh + 1) * 64, :, hh, :], in_=src[hh * 64:(hh + 1) * 64, :, cs]),
                             reads=[skey, 'k_blk' + nm], writes=['k_blk' + nm])
                        yield
                    else:
                        S.op('pool', lambda e: e.tensor_copy(out=bl[hh * 64:(hh + 1) * 64, :, hh, :], in_=src[hh * 64:(hh + 1) * 64, :, cs]),
                             reads=[skey, 'k_blk' + nm], writes=['k_blk' + nm])
                        yield
                blks[nm] = bl
            mats = {}
            specs = (('AabT', bT, 'w_bT', 'a', 0), ('ArbT', bT, 'w_bT', 'r', 1), ('AakT', kT, 'w_kT', 'a', 0),
                     ('ArkT', kT, 'w_kT', 'r', 1), ('Aab', aT, 'w_aT', 'b', 2))
            for mi, (nm, lsrc, lkey, rb, mk) in enumerate(specs):
                ps, pk = self.ps()
                for jj in range(2):
                    MM(ps[:, jj * 256:(jj + 1) * 256], lsrc[:, jj, cs], blks[rb][:, jj, :, :].rearrange("p h c -> p (h c)"),
                            True, True, reads=[lkey, 'k_blk' + rb], writes=[pk])
                m = ph.T("k_" + nm, [128, 4, 128], BF16)
                S.op('dve', lambda e: e.tensor_tensor(out=m[:].rearrange("p h c -> p (h c)"), in0=ps[:, :], in1=self.maskb[:, mk, :], op=ALU.mult),
                     reads=[pk, 'maskb'], writes=['k_' + nm])
                yield
                mats[nm] = m
            Pm = [ph.T("k_P%d" % i, [128, 4, 128], BF16) for i in range(2)]
            Am = [ph.T("k_A%d" % i, [128, 4, 128], BF16) for i in range(2)]
            Bm = [ph.T("k_B%d" % i, [128, 4, 128], BF16) for i in range(2)]
            fl = lambda m_: m_[:].rearrange("p h c -> p (h c)")
            S.op('pool', lambda e: e.tensor_tensor(out=fl(Pm[0]), in0=fl(mats['AabT']), in1=self.maskb[:, 3, :], op=ALU.add),
                 reads=['k_AabT', 'maskb'], writes=['k_P0'])
            yield
            Acur, Akey, Bcur, Bkey = mats['Aab'], 'k_Aab', mats['AabT'], 'k_AabT'
            pcur = 0
            pend = None

            def p_update(Ap, Apk, pcur_):
                ps_, pk_ = self.ps()
                for h in range(4):
                    MM(ps_[:, h * 128:(h + 1) * 128], Ap[:, h, :], Pm[pcur_][:, h, :], True, True, reads=[Apk, 'k_P%d' % pcur_], writes=[pk_])
                S.op('dve', lambda e: e.tensor_tensor(out=fl(Pm[1 - pcur_]), in0=ps_[:, :], in1=fl(Pm[pcur_]), op=ALU.add),
                     reads=[pk_, 'k_P%d' % pcur_], writes=['k_P%d' % (1 - pcur_)])
                return 1 - pcur_
            Am3 = Am + [ph.T("k_A2", [128, 4, 128], BF16)]
            for step in range(6):
                An, Ank = Am3[step % 3], 'k_A%d' % (step % 3)
                psA, pkA = self.ps()
                for h in range(4):
                    MM(psA[:, h * 128:(h + 1) * 128], Bcur[:, h, :], Acur[:, h, :], True, True, reads=[Akey, Bkey], writes=[pkA])
                if step < 5:
                    Bn, Bnk = Bm[step % 2], 'k_B%d' % (step % 2)
                    psB, pkB = self.ps()
                    for h in range(4):
                        MM(psB[:, h * 128:(h + 1) * 128], Acur[:, h, :], Bcur[:, h, :], True, True, reads=[Akey, Bkey], writes=[pkB])
                S.op('act', lambda e: e.copy(out=fl(An), in_=psA[:, :]), reads=[pkA], writes=[Ank])
                yield
                if step < 5:
                    S.op('dve', lambda e: e.tensor_copy(out=fl(Bn), in_=psB[:, :]), reads=[pkB], writes=[Bnk])
                    yield
                if pend is not None:
                    pcur = p_update(pend[0], pend[1], pcur)
                pend = (An, Ank)
                Acur, Akey = An, Ank
                if step < 5:
                    Bcur, Bkey = Bn, Bnk
            pcur = p_update(pend[0], pend[1], pcur)
            Pf, Pkey = Pm[pcur], 'k_P%d' % pcur
            yield 'CHAIN'
            Hb = self.Hbf
            Xs = ph.T("k_Xs", [128, 4, 64], BF16)
            Us = ph.T("k_Us", [128, 4, 64], BF16)
            psX, pkX = self.ps()
            for jj in range(2):
                j = j0 + jj
                MM(psX[:, jj * 128:(jj + 1) * 128], aT[:, jj, cs], Hb[:, l, j, :], True, False, reads=['w_aT', ('Hbf', l, j)], writes=[pkX])
                for hh in range(2):
                    h = jj * 2 + hh
                    MM(psX[:, h * 64:(h + 1) * 64], mats['AakT'][:, h, :], tok[:, 0, jj, hh * 64:(hh + 1) * 64], False, hh == 1,
                            reads=['k_AakT', 'k_tok'], writes=[pkX])
            S.op('act', lambda e: e.copy(out=Xs[:].rearrange("p h c -> p (h c)"), in_=psX[:, 0:256]), reads=[pkX], writes=['k_Xs'])
            yield
            psU, pkU = self.ps()
            for h in range(4):
                MM(psU[:, h * 64:(h + 1) * 64], Pf[:, h, :], Xs[:, h, :], True, True, reads=[Pkey, 'k_Xs'], writes=[pkU])
            S.op('dve', lambda e: e.tensor_copy(out=Us[:].rearrange("p h c -> p (h c)"), in_=psU[:, 0:256]), reads=[pkU], writes=['k_Us'])
            yield
            psY, pkY = self.ps()
            for jj in range(2):
                j = j0 + jj
                MM(psY[:, jj * 128:(jj + 1) * 128], rT[:, jj, cs], Hb[:, l, j, :], True, False, reads=['w_rT', ('Hbf', l, j)], writes=[pkY])
                for hh in range(2):
                    h = jj * 2 + hh
                    MM(psY[:, h * 64:(h + 1) * 64], mats['ArbT'][:, h, :], Us[:, h, :], False, False, reads=['k_ArbT', 'k_Us'], writes=[pkY])
                    MM(psY[:, h * 64:(h + 1) * 64], mats['ArkT'][:, h, :], tok[:, 0, jj, hh * 64:(hh + 1) * 64], False, hh == 1,
                            reads=['k_ArkT', 'k_tok'], writes=[pkY])
            psH, pkH = self.ps()
            for jj in range(2):
                MM(psH[:, jj * 128:(jj + 1) * 128], tok[:, 1, jj, :], Us[:, 2 * jj:2 * jj + 2, :].rearrange("p h c -> p (h c)"), True, False,
                        reads=['k_tok', 'k_Us'], writes=[pkH])
                MM(psH[:, jj * 128:(jj + 1) * 128], tok[:, 2, jj, :], tok[:, 0, jj, :], False, True, reads=['k_tok'], writes=[pkH])
            hd = ph.T("k_hd", [128, 256], F32)
            S.op('dve', lambda e: e.tensor_tensor(out=hd[:], in0=psH[:, 0:256], in1=self.blkf[:], op=ALU.mult), reads=[pkH, 'blkf'], writes=['k_hd'])
            yield
            for jj in range(2):
                j = j0 + jj
                S.op('pool', lambda e: e.tensor_tensor(out=hd[:, jj * 128:(jj + 1) * 128], in0=hd[:, jj * 128:(jj + 1) * 128], in1=self.H32[:, l, j, :], op=ALU.add),
                     reads=['k_hd', ('H32', l, j)], writes=['k_hd'])
                yield
                S.op('dve', lambda e: e.tensor_scalar(out=self.H32[:, l, j, :], in0=hd[:, jj * 128:(jj + 1) * 128], scalar1=gC[:, jj, t:t + 1], scalar2=None, op0=ALU.mult),
                     reads=['k_hd', 'w_gC'], writes=[('H32', l, j)])
                yield
                S.op('act', lambda e: e.copy(out=Hb[:, l, j, :], in_=self.H32[:, l, j, :]), reads=[('H32', l, j)], writes=[('Hbf', l, j)])
                yield
        yield 'POST'
        y = ph.T("k_y", [128, 4, 64], F32)
        sq = ph.T("k_sq", [128, 4, 64], F32)
        st = ph.T("k_st", [128, 4], F32)
        st2 = ph.T("k_st2", [128, 4], F32)
        if s2_y is None:
            S.op('act', lambda e: e.copy(out=y[:].rearrange("p h c -> p (h c)"), in_=psY[:, 0:256]), reads=[pkY], writes=['k_y'])
            yield
        else:
            S.op('act', lambda e: e.copy(out=y[:].rearrange("p h c -> p (h c)"), in_=s2_y[0]), reads=s2_y[1], writes=['k_y'])
            yield
        S.op('dve', lambda e: e.tensor_reduce(out=st[:], in_=y[:], axis=AX.X, op=ALU.add), reads=['k_y'], writes=['k_st'])
        yield
        S.op('dve', lambda e: e.tensor_scalar(out=st[:], in0=st[:], scalar1=-1.0 / 64, scalar2=None, op0=ALU.mult), reads=['k_st'], writes=['k_st'])
        yield
        S.op('dve', lambda e: e.tensor_tensor(out=y[:], in0=y[:], in1=st[:].unsqueeze(2).to_broadcast([128, 4, 64]), op=ALU.add),
             reads=['k_y', 'k_st'], writes=['k_y'])
        yield
        S.op('pool', lambda e: e.tensor_tensor(out=sq[:], in0=y[:], in1=y[:], op=ALU.mult), reads=['k_y'], writes=['k_sq'])
        yield
        S.op('dve', lambda e: e.tensor_reduce(out=st2[:], in_=sq[:], axis=AX.X, op=ALU.add), reads=['k_sq'], writes=['k_st2'])
        yield
        S.op('act', lambda e: e.activation(out=st2[:], in_=st2[:], func=AF.Sqrt, bias=self.epsln[:, 1:2], scale=1.0 / 64),
             reads=['k_st2', 'epsln'], writes=['k_st2'])
        yield
        S.op('dve', lambda e: e.reciprocal(out=st2[:], in_=st2[:]), reads=['k_st2'], writes=['k_st2'])
        yield
        S.op('dve', lambda e: e.tensor_tensor(out=y[:], in0=y[:], in1=st2[:].unsqueeze(2).to_broadcast([128, 4, 64]), op=ALU.mult),
             reads=['k_y', 'k_st2'], writes=['k_y'])
        yield
        yf = y[:].rearrange("p h c -> p (h c)")
        fsl = slice(j0 * 128, j0 * 128 + 256)
        S.op('pool', lambda e: e.tensor_tensor(out=yf, in0=yf, in1=lnw[:, fsl], op=ALU.mult), reads=['k_y', 'lnw_t'], writes=['k_y'])
        yield
        S.op('pool', lambda e: e.tensor_tensor(out=yf, in0=yf, in1=lnb[:, fsl], op=ALU.add), reads=['k_y', 'lnb_t'], writes=['k_y'])
        yield
        S.op('dve', lambda e: e.tensor_reduce(out=st[:], in_=tok[:, 3, :, :].rearrange("p j (h c) -> p (j h) c", h=2), axis=AX.X, op=ALU.add),
             reads=['k_tok'], writes=['k_st'])
        yield
        S.op('dve', lambda e: e.tensor_tensor(out=sq[:], in0=tok[:, 0, :, :].rearrange("p j (h c) -> p (j h) c", h=2),
                                              in1=st[:].unsqueeze(2).to_broadcast([128, 4, 64]), op=ALU.mult), reads=['k_tok', 'k_st'], writes=['k_sq'])
        yield
        S.op('pool', lambda e: e.tensor_tensor(out=y[:], in0=y[:], in1=sq[:], op=ALU.add), reads=['k_y', 'k_sq'], writes=['k_y'])
        yield
        psT, pkT = self.ps()
        for jj in range(2):
            S.op('pe', lambda e: e.transpose(out=psT[:, jj * 128:(jj + 1) * 128], in_=yf[:, jj * 128:(jj + 1) * 128], identity=self.identf[:]),
                 reads=['k_y', 'identf'], writes=[pkT])
            yield
        og = ph.T("k_og", [128, 2, 128], F32)
        S.op('dve', lambda e: e.tensor_tensor(out=og[:], in0=psT[:, 0:256].rearrange("p (j c) -> p j c", j=2), in1=ggT[:, :, cs], op=ALU.mult),
             reads=[pkT, 'w_ggT' + sf], writes=['k_og'])
        yield
        S.op('pool', lambda e: e.tensor_tensor(out=self.mixacc[:, j0:j0 + 2, cs], in0=self.mixacc[:, j0:j0 + 2, cs], in1=og[:], op=ALU.add),
             reads=['k_og', ('mixacc', j0), ('mixacc', j0 + 1)], writes=[('mixacc', j0), ('mixacc', j0 + 1)])
        yield
        ph.close()

    def attention(self, l, stream, g, ngroups, nt, N):
        S, I = self.S, self.I
        ph = Phase(self)
        qT = ph.T("qT", [128, 8, N], BF16)
        ET = [ph.T("ET%d" % b, [128, 512], BF16) for b in range(2)]
        sb = ph.T("sbt", [128, 512], F32)
        den = ph.T("den", [128, 256], F32)
        osb = ph.T("osb", [128, 256], F32)
        if stream == 'P':
            self.biasP = ph.T("biasP", [128, 2, NH, 128], F32)
            S.dma('act', self.biasP[:], I['c_biasP'], writes=['biasP'])
        kv32_pre = ph.T("kv32", [128, 512], F32)
        if stream == 'P':
            kT = self.kT_p[l]
            vtok = self.vtok_p[l]
            kkey, vkey = ('kTp', l), ('vtokp', l)
        else:
            kT = ph.T("kTs", [128, KVH, 128 + 128], BF16)
            vtok = ph.T("vtoks", [128, 2, 256], BF16)
            kkey, vkey = 'kTs', 'vtoks'
        xkeys = [('xT', t) for t in range(nt)]
        for half in range(2):
            wb, wk = self.wtile('w_in', l, [(half * 512, 512)])
            for b in range(4):
                ps, pk = self.ps()
                for k in range(8):
                    self.mm(ps[:, 0:N], wb[:, k, b * 128:(b + 1) * 128], self.xT[:, k, 0:N], k == 0, k == 7,
                            reads=wk + xkeys, writes=[pk])
                self.evac_copy('act' if b % 2 else 'dve', qT[:, half * 4 + b, :], ps[:, 0:N], [pk], ['qT'], scale=0.125)
        self.chk('q')
        i = self.wn
        self.wn = (self.wn + 1) % NWBUF
        wb = self.wbuf[i]
        wk = [('wbuf', i)]
        srcw = self.W['w_in'][l]
        kv = srcw[:, OFF_K:OFF_K + 256].rearrange("(k p) (c d) -> p k c d", p=128, d=64)
        wbv = wb[:, :, :].rearrange("p k (c two d) -> p k c two d", two=2, d=64)
        for dup in range(2):
            for k in range(8):
                S.dma('sp', wbv[:, k, :, dup, :], kv[:, k, :, :],
                      writes=wk if (dup == 0 and k == 0) else [('wbufx', i, dup, k)])
        wk_all = wk + [('wbufx', i, dup, k) for dup in range(2) for k in range(8) if not (dup == 0 and k == 0)]
        for c in range(KVH):
            ps, pk = self.ps()
            for k in range(8):
                self.mm(ps[:, 0:N], wb[:, k, c * 128:(c + 1) * 128], self.xT[:, k, 0:N], k == 0, k == 7,
                        reads=wk_all + xkeys, writes=[pk])
            self.evac_copy('act' if c % 2 else 'dve', kT[:, c, 128:128 + N], ps[:, 0:N], [pk], [kkey])
        self.chk('kdup')
        wb, wk = self.wtile('w_in', l, [(OFF_K, 512)])
        last_tile_needs_k = (stream == 'S') or (g == ngroups - 1)
        kv32 = None
        import os
        kvvar = os.environ.get('KVVAR', '')
        for t in range(nt):
            if kvvar == 'dmaonly':
                break
            ps, pk = self.ps()
            for k in range(8):
                self.mm(ps[:, :], self.xT[:, k, t * 128:(t + 1) * 128], wb[:, k, :], k == 0, k == 7,
                        reads=wk + [('xT', t)], writes=[pk])
            if kvvar == 'mmonly':
                continue
            if kvvar != 'noact' and not (kvvar == 'noact3' and t == 3):
                self.evac_copy('act', vtok[:, 1 + t, :], ps[:, 256:512], [pk], [vkey])
            if last_tile_needs_k and t == nt - 1 and kvvar != 'nokv32':
                kv32 = kv32_pre
                self.evac_copy('act', kv32[:], ps[:, :], [pk], ['kv32'])
        import os
        if kv32 is not None and stream == 'P' and not os.environ.get('NOKVOUT'):
            S.dma('act', self.O['pk'][l], kv32[:, 0:256], reads=['kv32'], writes=[('o_pk', l)])
            S.dma('act', self.O['pv'][l], kv32[:, 256:512], reads=['kv32'], writes=[('o_pv', l)])
        self.chk('kvtok')
        for half in range(2):
            wb, wk = self.wtile('w_in', l, [(OFF_GATE + half * 512, 512)])
            for b in range(4):
                ps, pk = self.ps()
                for k in range(8):
                    self.mm(ps[:, 0:N], wb[:, k, b * 128:(b + 1) * 128], self.xT[:, k, 0:N], k == 0, k == 7,
                            reads=wk + xkeys, writes=[pk])
                j = half * 4 + b
                S.op('act', lambda e: e.activation(out=self.mixacc[:, j, 0:N], in_=ps[:, 0:N], func=AF.Sigmoid),
                     reads=[pk], writes=[('mixacc', j)])
        self.chk('proj')
        if stream == 'P':
            sbc = [ph.T("sbt_c%d" % c, [128, 512], F32) for c in range(KVH)]
            ETcp = [[ph.T("ET_c%d_%d" % (c, p_), [128, 512], BF16) for p_ in range(2)] for c in range(KVH)]
            denc = [ph.T("den_c%d" % c, [128, 256], F32) for c in range(KVH)]
            osbc = [ph.T("osb_c%d" % c, [128, 256], F32) for c in range(KVH)]
            ph.names += ['sbt_c', 'ET_c', 'den_c', 'osb_c']

            def attn_c(t, c, blks):
                nb = len(blks)
                sb_, den_, osb_ = sbc[c], denc[c], osbc[c]
                for par in range(2):
                    ps, pk = self.ps()
                    hp = par * 64
                    for bi, blk in enumerate(blks):
                        koff = 128 * t if blk == 0 else 128 * (t + 1)
                        for i2 in range(2):
                            h = 4 * c + 2 * i2 + par
                            col = (bi * 2 + i2) * 128
                            self.mm(ps[:, col:col + 128], kT[hp:hp + 64, c, koff:koff + 128],
                                    qT[hp:hp + 64, h // 2, t * 128:(t + 1) * 128], True, True,
                                    reads=[kkey, 'qT'], writes=[pk])
                    yield
                    W_ = nb * 256
                    bias_ap = self.biasP[:, blks[0]:blks[0] + nb, 4 * c + par:4 * c + 4:2, :]
                    S.op('dve', lambda e: e.tensor_tensor(out=sb_[:, 0:W_].rearrange("p (b i q) -> p b i q", b=nb, i=2),
                                                          in0=ps[:, 0:W_].rearrange("p (b i q) -> p b i q", b=nb, i=2),
                                                          in1=bias_ap, op=ALU.add),
                         reads=[pk, 'biasP'], writes=[('sbt_c', c)])
                    yield
                    S.op('act', lambda e: e.activation(out=ETcp[c][par][:, 0:W_], in_=sb_[:, 0:W_], func=AF.Exp),
                         reads=[('sbt_c', c)], writes=[('ET_c', c, par)])
                    yield
                ps, pk = self.ps()
                for i4 in range(4):
                    par = i4 % 2
                    hp = par * 64
                    i2 = i4 // 2
                    jj = i4 // 2
                    for part in range(2):
                        for bi, blk in enumerate(blks):
                            slot = t if blk == 0 else t + 1
                            lhsT = vtok[:, slot, c * 64:(c + 1) * 64] if part == 0 else self.onesb[:, :]
                            col = (bi * 2 + i2) * 128
                            self.mm(ps[hp:hp + 64, part * 256 + jj * 128: part * 256 + (jj + 1) * 128], lhsT,
                                    ETcp[c][par][:, col:col + 128], bi == 0, bi == nb - 1,
                                    reads=[vkey, 'onesb', ('ET_c', c, par)], writes=[pk])
                yield
                for jj in range(2):
                    j = 2 * c + jj
                    S.op('dve', lambda e: e.tensor_scalar(out=den_[:, jj * 128:(jj + 1) * 128],
                                                          in0=ps[:, 256 + jj * 128:256 + (jj + 1) * 128],
                                                          scalar1=self.esink[:, l, j:j + 1], scalar2=None, op0=ALU.add),
                         reads=[pk, 'esink'], writes=[('den_c', c)])
                yield
                S.op('dve', lambda e: e.reciprocal(out=den_[:], in_=den_[:]), reads=[('den_c', c)], writes=[('den_c', c)])
                yield
                S.op('dve', lambda e: e.tensor_tensor(out=osb_[:], in0=ps[:, 0:256], in1=den_[:], op=ALU.mult),
                     reads=[pk, ('den_c', c)], writes=[('osb_c', c)])
                yield
                S.op('pool', lambda e: e.tensor_tensor(out=self.mixacc[:, 2 * c:2 * c + 2, t * 128:(t + 1) * 128],
                                                       in0=self.mixacc[:, 2 * c:2 * c + 2, t * 128:(t + 1) * 128],
                                                       in1=osb_[:].rearrange("p (j q) -> p j q", j=2), op=ALU.mult),
                     reads=[('osb_c', c), ('mixacc', 2 * c), ('mixacc', 2 * c + 1)],
                     writes=[('mixacc', 2 * c), ('mixacc', 2 * c + 1)])
                yield

            for t in range(nt):
                first = (g == 0 and t == 0)
                blks = [1] if first else [0, 1]
                self.interleave([attn_c(t, c, blks) for c in range(KVH)])
            S.op('pool', lambda e: e.tensor_copy(out=kT[:, :, 0:128], in_=kT[:, :, N:N + 128]), reads=[kkey], writes=[kkey])
            S.op('pool', lambda e: e.tensor_copy(out=vtok[:, 0, :], in_=vtok[:, nt, :]), reads=[vkey], writes=[vkey])
        else:
            bSn = ph.T("bSn", [128, NH, 128], F32)
            bSc = ph.T("bSc", [128, KVH, 2, 256], F32)
            S.dma('act', bSn[:], I['c_biasSn'], writes=['bSn'])
            S.dma('act', bSc[:], I['c_biasSc'], writes=['bSc'])
            ck32 = ph.T("ck32", [128, NSEQ_S, 256], F32)
            ckd = ph.T("ckd", [128, NSEQ_S, KVH, 2, 64], BF16)
            KcT = ph.T("KcT", [128, NSEQ_S, KVH, 128], BF16)
            Vc = ph.T("Vc", [128, NSEQ_S, 256], BF16)
            ETc = [ph.T("ETc%d" % b, [128, 256], BF16) for b in range(2)]
            t1 = ph.T("sa_t1", [128, 512], F32)
            S.dma('act', ck32[:], I['ck'][l].rearrange("n r c -> r n c"), writes=['ck32'])
            S.dma('act', self.O['sk'][l].rearrange("n r c -> r n c")[0:120], ck32[8:128, :, :], reads=['ck32'], writes=[('o_sk', l)])
            for dup in range(2):
                S.op('dve' if dup else 'pool', lambda e: e.tensor_copy(out=ckd[:, :, :, dup, :], in_=ck32[:].rearrange("p n (c d) -> p n c d", d=64)),
                     reads=['ck32'], writes=['ckd'])
            S.dma('act', ck32[:], I['cv'][l].rearrange("n r c -> r n c"), writes=['ck32'])
            S.dma('act', self.O['sv'][l].rearrange("n r c -> r n c")[0:120], ck32[8:128, :, :], reads=['ck32'], writes=[('o_sv', l)])
            S.op('act', lambda e: e.copy(out=Vc[:], in_=ck32[:]), reads=['ck32'], writes=['Vc'])
            for n0 in range(0, NSEQ_S, 2):
                ps, pk = self.ps()
                psb = ps[:].bitcast(BF16)
                for ni in range(2):
                    for c in range(KVH):
                        S.op('pe', lambda e: e.transpose(out=psb[:, (ni * 4 + c) * 128:(ni * 4 + c + 1) * 128],
                                                         in_=ckd[:, n0 + ni, c, :, :].rearrange("p a d -> p (a d)"), identity=self.identb[:]),
                             reads=['ckd', 'identb'], writes=[pk])
                S.op('act' if (n0 // 2) % 2 else 'dve',
                     (lambda e: e.copy(out=KcT[:, n0:n0 + 2, :, :].rearrange("p n c k -> p (n c k)"), in_=psb)) if (n0 // 2) % 2 else
                     (lambda e: e.tensor_copy(out=KcT[:, n0:n0 + 2, :, :].rearrange("p n c k -> p (n c k)"), in_=psb)),
                     reads=[pk], writes=['KcT'])
            for n in range(NSEQ_S):
                S.dma('act', self.O['sk'][l, n, 120:128, :], kv32[n * 8:(n + 1) * 8, 0:256], reads=['kv32'], writes=[('o_skn', l, n)])
                S.dma('act', self.O['sv'][l, n, 120:128, :], kv32[n * 8:(n + 1) * 8, 256:512], reads=['kv32'], writes=[('o_svn', l, n)])
            for c in range(KVH):
                for par in range(2):
                    hp = par * 64
                    ps, pk = self.ps()
                    for i2 in range(2):
                        h = 4 * c + 2 * i2 + par
                        self.mm(ps[:, i2 * 128:(i2 + 1) * 128], kT[hp:hp + 64, c, 128:256], qT[hp:hp + 64, h // 2, 0:128], True, True,
                                reads=[kkey, 'qT'], writes=[pk])
                    S.op('dve', lambda e: e.tensor_tensor(out=sb[:, 0:256].rearrange("p (i q) -> p i q", i=2),
                                                          in0=ps[:, 0:256].rearrange("p (i q) -> p i q", i=2),
                                                          in1=bSn[:, 4 * c + par:4 * c + 4:2, :], op=ALU.add), reads=[pk, 'bSn'], writes=['sbt'])
                    S.op('act', lambda e: e.activation(out=ET[par][:, 0:256], in_=sb[:, 0:256], func=AF.Exp), reads=['sbt'], writes=[('ET', par)])
                    ps, pk = self.ps()
                    for n in range(NSEQ_S):
                        for i2 in range(2):
                            h = 4 * c + 2 * i2 + par
                            col = (n * 2 + i2) * 8
                            self.mm(ps[:, col:col + 8], KcT[hp:hp + 64, n, c, :], qT[hp:hp + 64, h // 2, n * 8:(n + 1) * 8], True, True,
                                    reads=['KcT', 'qT'], writes=[pk])
                    S.op('dve', lambda e: e.tensor_tensor(out=sb[:, 256:512], in0=ps[:, 0:256], in1=bSc[:, c, par, :], op=ALU.add),
                         reads=[pk, 'bSc'], writes=['sbt'])
                    S.op('act', lambda e: e.activation(out=ETc[par][:], in_=sb[:, 256:512], func=AF.Exp), reads=['sbt'], writes=[('ETc', par)])
                ps1, pk1 = self.ps()
                ps2, pk2 = self.ps()
                for i4 in range(4):
                    par, i2 = i4 % 2, i4 // 2
                    hp = par * 64
                    for part in range(2):
                        lhs_n = vtok[:, 1, c * 64:(c + 1) * 64] if part == 0 else self.onesb[:, :]
                        self.mm(ps1[hp:hp + 64, part * 256 + i2 * 128: part * 256 + (i2 + 1) * 128], lhs_n,
                                ET[par][:, i2 * 128:(i2 + 1) * 128], True, True, reads=[vkey, 'onesb', ('ET', par)], writes=[pk1])
                        for n in range(NSEQ_S):
                            lhs_c = Vc[:, n, c * 64:(c + 1) * 64] if part == 0 else self.onesb[:, :]
                            col = part * 256 + i2 * 128 + n * 8
                            self.mm(ps2[hp:hp + 64, col:col + 8], lhs_c, ETc[par][:, (n * 2 + i2) * 8:(n * 2 + i2) * 8 + 8], True, True,
                                    reads=['Vc', 'onesb', ('ETc', par)], writes=[pk2])
                S.op('act', lambda e: e.copy(out=t1[:], in_=ps1[:, :]), reads=[pk1], writes=['sa_t1'])
                S.op('dve', lambda e: e.tensor_tensor(out=t1[:], in0=t1[:], in1=ps2[:, :], op=ALU.add), reads=['sa_t1', pk2], writes=['sa_t1'])
                for jj in range(2):
                    j = 2 * c + jj
                    S.op('dve', lambda e: e.tensor_scalar(out=den[:, jj * 128:(jj + 1) * 128], in0=t1[:, 256 + jj * 128:256 + (jj + 1) * 128],
                                                          scalar1=self.esink[:, l, j:j + 1], scalar2=None, op0=ALU.add),
                         reads=['sa_t1', 'esink'], writes=['den'])
                S.op('dve', lambda e: e.reciprocal(out=den[:], in_=den[:]), reads=['den'], writes=['den'])
                S.op('dve', lambda e: e.tensor_tensor(out=osb[:], in0=t1[:, 0:256], in1=den[:], op=ALU.mult), reads=['sa_t1', 'den'], writes=['osb'])
                S.op('pool', lambda e: e.tensor_tensor(out=self.mixacc[:, 2 * c:2 * c + 2, 0:128], in0=self.mixacc[:, 2 * c:2 * c + 2, 0:128],
                                                       in1=osb[:].rearrange("p (j q) -> p j q", j=2), op=ALU.mult),
                     reads=['osb', ('mixacc', 2 * c), ('mixacc', 2 * c + 1)], writes=[('mixacc', 2 * c), ('mixacc', 2 * c + 1)])
        self.dump('oa_g' if stream == 'P' else 's_oa_g', self.mixacc[:, :, 0:N], [128, 8, N], [('mixacc', j) for j in range(8)])
        ph.close()


_CACHE = {}


def consts():
    ident = np.eye(128, dtype=np.float32)
    slopes = 2.0 ** (-8.0 * np.arange(1, NH + 1) / NH)
    k = np.arange(128)[:, None]
    q = np.arange(128)[None, :]
    biasP = np.zeros((128, 2, NH, 128), np.float32)
    rel_prev = 128 + q - k
    rel_cur = q - k
    for h in range(NH):
        biasP[:, 0, h, :] = np.where(rel_prev < 128, -slopes[h] * rel_prev, NEG)
        biasP[:, 1, h, :] = np.where(rel_cur >= 0, -slopes[h] * rel_cur, NEG)
    kn, ks = np.arange(128)[:, None] // 8, np.arange(128)[:, None] % 8
    qn, qt = np.arange(128)[None, :] // 8, np.arange(128)[None, :] % 8
    biasSn = np.zeros((128, NH, 128), np.float32)
    for h in range(NH):
        biasSn[:, h, :] = np.where((kn == qn) & (ks <= qt), -slopes[h] * (qt - ks), NEG)
    biasSc = np.zeros((128, KVH, 2, NSEQ_S, 2, T_S), np.float32)
    j = np.arange(128)[:, None]
    t = np.arange(T_S)[None, :]
    for c in range(KVH):
        for i4 in range(4):
            h = 4 * c + i4
            biasSc[:, c, i4 % 2, :, i4 // 2, :] = np.where(j > t, -slopes[h] * (128 + t - j), NEG)[:, None, :]
    biasSc = biasSc.reshape(128, KVH, 2, 256)
    p = np.arange(128)[:, None]
    f = np.arange(128)[None, :]
    strictT = (p < f).astype(np.float32)
    inclT = (p <= f).astype(np.float32)
    strictA = (f < p).astype(np.float32)
    identb = (p == f).astype(np.float32)
    mask = np.stack([np.tile(m, (1, 4)) for m in (strictT, inclT, strictA, identb)], axis=1).astype(np.float32)
    rmask = np.ones((128, 512), np.float32)
    rmask[:, ::128] = 0.0
    blk = np.zeros((128, 128), np.float32)
    blk[:64, :64] = 1.0
    blk[64:, 64:] = 1.0
    return dict(c_ident=ident, c_biasP=biasP, c_biasSn=biasSn, c_biasSc=biasSc, c_mask=mask, c_rmask=rmask, c_blk=blk)


def get_prog(seq, **kw):
    key = (seq, tuple(sorted((k, str(v)) for k, v in kw.items())))
    if key not in _CACHE:
        kb = KB(seq, **kw)
        kb.build()
        _CACHE[key] = kb
    return _CACHE[key]


def make_in_maps(inputs, seq):
    cst = consts()
    maps = []
    f = lambda a: np.ascontiguousarray(a, dtype=np.float32)
    for c in range(NCORES):
        n0 = c * NSEQ_S
        m = {
            'xp': f(inputs['x_prompt'][c % 2, :seq]),
            'xs': f(inputs['x_sample'][n0:n0 + NSEQ_S].reshape(128, D)),
            'ck': f(inputs['cache_k'][:, n0:n0 + NSEQ_S].reshape(DEPTH, NSEQ_S, 128, 256)),
            'cv': f(inputs['cache_v'][:, n0:n0 + NSEQ_S].reshape(DEPTH, NSEQ_S, 128, 256)),
            'sconv': f(inputs['state_conv'][:, n0:n0 + NSEQ_S]),
            'srg': f(inputs['state_rglru'][:, n0:n0 + NSEQ_S]),
            'sshift': f(inputs['state_shift'][:, n0:n0 + NSEQ_S]),
            'swkv': f(inputs['state_wkv'][:, n0:n0 + NSEQ_S]),
        }
        for nm in ['w_in', 'attn_sinks', 'conv_w', 'conv_b', 'rg_wa', 'rg_ba', 'rg_wx', 'rg_bx', 'rg_lambda',
                   'rw_mu', 'rw_w0', 'rw_wup', 'rw_a0', 'rw_aup', 'rw_gup', 'rw_kk', 'rw_ka', 'rw_lnw', 'rw_lnb',
                   'w_out', 'ln1_g', 'ln1_b', 'w_gu', 'w_down', 'ln2_g', 'ln2_b']:
            m[nm] = f(inputs[nm])
        m['rw_rk'] = f(np.asarray(inputs['rw_rk']).reshape(DEPTH, D))
        m.update(cst)
        maps.append(m)
    return maps


def kernel(**inputs):
    seq = inputs['x_prompt'].shape[1]
    kb = get_prog(seq)
    maps = make_in_maps(inputs, seq)
    res = run_bass_kernel_spmd(kb.nc, maps, core_ids=list(range(NCORES)))
    R = res.results
    B = inputs['x_prompt'].shape[0]
    yp = np.stack([R[b]['yp'] for b in range(B)])
    ys = np.concatenate([R[c]['ys'].reshape(NSEQ_S, T_S, D) for c in range(NCORES)])

    def pst(nm, shp):
        return np.stack([R[b][nm] for b in range(B)], axis=1).reshape(shp)

    def sst(nm, shp):
        return np.concatenate([R[c][nm] for c in range(NCORES)], axis=1).reshape(shp)
    NS = NCORES * NSEQ_S
    outs = (yp, ys,
            pst('pk', (DEPTH, B, 128, KVH, HD)), pst('pv', (DEPTH, B, 128, KVH, HD)),
            pst('pconv', (DEPTH, B, 3, D)), pst('prg', (DEPTH, B, D)), pst('pshift', (DEPTH, B, SHW)),
            pst('pwkv', (DEPTH, B, NH, HD, HD)),
            sst('sk', (DEPTH, NS, 128, KVH, HD)), sst('sv', (DEPTH, NS, 128, KVH, HD)),
            sst('sconv_o', (DEPTH, NS, 3, D)), sst('srg_o', (DEPTH, NS, D)), sst('sshift_o', (DEPTH, NS, SHW)),
            sst('swkv_o', (DEPTH, NS, NH, HD, HD)))
    return tuple(np.ascontiguousarray(o, dtype=np.float32) for o in outs)
```

```python
import math
import numpy as np
from contextlib import ExitStack
import concourse.bass as bass
import concourse.mybir as mybir
from concourse.bass_utils import run_bass_kernel_spmd

F32 = mybir.dt.float32
BF16 = mybir.dt.bfloat16
ALU = mybir.AluOpType
AF = mybir.ActivationFunctionType
AX = mybir.AxisListType

D = 1024
NH = 16
HD = 64
KVH = 4
DFF = 2816
INC = 8992
OFF_K, OFF_V, OFF_RNN, OFF_RW, OFF_GATE = 1024, 1280, 1536, 2560, 5920
SHW = 3360
DEPTH = 2
ALPHA = (2 * DEPTH) ** 0.25
LN_EPS = 1e-5
GN_EPS = 64e-5
NEG = -30000.0
NCORES = 8
NSEQ_S = 16
T_S = 8
NWBUF = 3


class Sched:
    def __init__(self, nc, es, n_dma_sems=32):
        self.nc = nc
        self.es = es
        self.engs = {'pe': nc.tensor, 'dve': nc.vector, 'act': nc.scalar, 'pool': nc.gpsimd, 'sp': nc.sync}
        self.csem = {e: es.enter_context(nc.semaphore("cs_" + e)) for e in ('pe', 'dve', 'act', 'pool')}
        self.ccnt = {e: 0 for e in self.csem}
        self.dsem = {q: [es.enter_context(nc.semaphore("ds%s%d" % (q, i))) for i in range(n)]
                     for q, n in (('sp', 16), ('act', 12), ('pool', 4))}
        self.dcnt = {q: [0] * len(v) for q, v in self.dsem.items()}
        self.dnext = {q: 0 for q in self.dsem}
        self.known = {e: {} for e in self.engs}
        self.lastw = {}
        self.reads = {}
        self.ninst = 0
        self.psn = 0

    def _wait(self, e, tok):
        if tok is None:
            return
        sem, val, owner = tok
        if owner == 'pe' and e == 'pe':
            return
        if self.known[e].get(sem.name, 0) >= val:
            return
        self.engs[e].wait_ge(sem, val)
        self.known[e][sem.name] = val
        self.ninst += 1

    def _deps(self, e, reads, writes):
        for k in reads:
            self._wait(e, self.lastw.get(k))
        for k in writes:
            self._wait(e, self.lastw.get(k))
            for t in self.reads.get(k, ()):
                self._wait(e, t)

    def _commit(self, tok, reads, writes):
        for k in writes:
            self.lastw[k] = tok
            self.reads[k] = []
        for k in reads:
            if k in writes:
                continue
            lst = self.reads.setdefault(k, [])
            lst[:] = [t for t in lst if t[0].name != tok[0].name]
            lst.append(tok)

    def op(self, e, fn, reads=(), writes=()):
        self._deps(e, reads, writes)
        ins = fn(self.engs[e])
        self.ccnt[e] += 1
        ins.then_inc(self.csem[e], 1)
        tok = (self.csem[e], self.ccnt[e], e)
        self._commit(tok, reads, writes)
        self.ninst += 1
        return tok

    def dma(self, e, out, in_, reads=(), writes=(), **kw):
        i = self.dnext[e]
        self.dnext[e] = (i + 1) % len(self.dsem[e])
        if self.dcnt[e][i] > 0:
            self._wait(e, (self.dsem[e][i], self.dcnt[e][i], 'dma'))
        self._deps(e, reads, writes)
        ins = self.engs[e].dma_start(out=out, in_=in_, **kw)
        self.dcnt[e][i] += 16
        ins.then_inc(self.dsem[e][i], 16)
        tok = (self.dsem[e][i], self.dcnt[e][i], 'dma')
        self._commit(tok, reads, writes)
        self.ninst += 1
        return tok

    def release(self, names, engines=('pe', 'dve', 'act', 'pool', 'sp')):
        names = set(names)
        keys = [k for k in list(self.lastw.keys()) + list(self.reads.keys())
                if (k in names) or (isinstance(k, tuple) and k[0] in names)]
        for e in engines:
            for k in keys:
                self._wait(e, self.lastw.get(k))
                for t in self.reads.get(k, ()):
                    self._wait(e, t)
        for k in set(keys):
            self.lastw.pop(k, None)
            self.reads.pop(k, None)

    def finish(self, e='sp'):
        for k, t in list(self.lastw.items()):
            self._wait(e, t)


class Phase:
    def __init__(self, kb):
        self.kb = kb
        self.es = ExitStack()
        self.names = []
        kb.open_phases.append(self)

    _uid = [0]

    def T(self, name, shape, dt):
        self.names.append(name)
        Phase._uid[0] += 1
        return self.es.enter_context(self.kb.nc.sbuf_tensor("%s_%d" % (name, Phase._uid[0]), list(shape), dt))

    def close(self):
        self.kb.open_phases.remove(self)
        self.kb.S.release(self.names)
        self.es.close()


def w_in_tiles():
    t = []
    t.append(('q0', [(0, 512)]))
    t.append(('q1', [(512, 512)]))
    t.append(('kdup', None))
    t.append(('kv', [(OFF_K, 512)]))
    t.append(('gA0', [(OFF_GATE, 512)]))
    t.append(('gA1', [(OFF_GATE + 512, 512)]))
    return t


class _Stop(Exception):
    pass


class KB:
    def __init__(self, seq, depth=DEPTH, do_sample=True, debug=(), stop_after=None):
        self.seq = seq
        self.depth = depth
        self.do_sample = do_sample
        self.debug = set(debug)
        self.stop_after = stop_after
        self.dbg_outs = {}
        self.open_phases = []
        self.nc = bass.Bass("TRN2", target_bir_lowering=False)
        self.es = ExitStack()

    def chk(self, name):
        if self.stop_after == name:
            raise _Stop()

    def din(self, name, shape, dt=F32):
        return self.nc.dram_tensor(name, list(shape), dt, kind="ExternalInput").ap()

    def dout(self, name, shape, dt=F32):
        return self.nc.dram_tensor(name, list(shape), dt, kind="ExternalOutput").ap()

    def dscr(self, name, shape, dt):
        return self.nc.dram_tensor(name, list(shape), dt, kind="Internal").ap()

    def T(self, name, shape, dt):
        return self.es.enter_context(self.nc.sbuf_tensor(name, list(shape), dt))

    def ps(self):
        i = self.S.psn
        self.S.psn = (i + 1) % 8
        return self.psb[i], 'ps%d' % i

    def dump(self, name, ap, shape, key):
        if name not in self.debug:
            return
        o = self.dout("dbg_" + name, shape, ap.dtype if hasattr(ap, 'dtype') else F32)
        self.dbg_outs[name] = shape
        self.S.dma('act', o, ap, reads=key if isinstance(key, list) else [key], writes=['dbg_' + name])

    def mm(self, out, lhsT, rhs, start, stop, reads, writes):
        return self.S.op('pe', lambda e: e.matmul(out, lhsT=lhsT, rhs=rhs, start=start, stop=stop),
                         reads=reads, writes=writes)

    def evac_copy(self, eng, out, in_, reads, writes, scale=None):
        if eng == 'act':
            if scale is None:
                return self.S.op('act', lambda e: e.copy(out=out, in_=in_), reads=reads, writes=writes)
            return self.S.op('act', lambda e: e.activation(out=out, in_=in_, func=AF.Copy, scale=scale),
                             reads=reads, writes=writes)
        if scale is None:
            return self.S.op(eng, lambda e: e.tensor_copy(out=out, in_=in_), reads=reads, writes=writes)
        return self.S.op(eng, lambda e: e.tensor_scalar(out=out, in0=in_, scalar1=scale, scalar2=None, op0=ALU.mult),
                         reads=reads, writes=writes)

    def build(self):
        nc, es = self.nc, self.es
        self.S = S = Sched(nc, es)
        seq = self.seq
        L = self.depth
        I = self.I = {}
        I['xp'] = self.din('xp', [seq, D])
        I['xs'] = self.din('xs', [128, D])
        I['ck'] = self.din('ck', [DEPTH, NSEQ_S, 128, 256])
        I['cv'] = self.din('cv', [DEPTH, NSEQ_S, 128, 256])
        I['sconv'] = self.din('sconv', [DEPTH, NSEQ_S, 3, D])
        I['srg'] = self.din('srg', [DEPTH, NSEQ_S, D])
        I['sshift'] = self.din('sshift', [DEPTH, NSEQ_S, SHW])
        I['swkv'] = self.din('swkv', [DEPTH, NSEQ_S, NH, HD, HD])
        for nm, shp in [('w_in', [DEPTH, D, INC]), ('attn_sinks', [DEPTH, NH]), ('conv_w', [DEPTH, 4, D]),
                        ('conv_b', [DEPTH, D]), ('rg_wa', [DEPTH, 16, 64, 64]), ('rg_ba', [DEPTH, D]),
                        ('rg_wx', [DEPTH, 16, 64, 64]), ('rg_bx', [DEPTH, D]), ('rg_lambda', [DEPTH, D]),
                        ('rw_mu', [DEPTH, SHW]), ('rw_w0', [DEPTH, D]), ('rw_wup', [DEPTH, 64, D]),
                        ('rw_a0', [DEPTH, D]), ('rw_aup', [DEPTH, 64, D]), ('rw_gup', [DEPTH, 160, D]),
                        ('rw_kk', [DEPTH, D]), ('rw_ka', [DEPTH, D]), ('rw_rk', [DEPTH, D]),
                        ('rw_lnw', [DEPTH, D]), ('rw_lnb', [DEPTH, D]), ('w_out', [DEPTH, D, D]),
                        ('ln1_g', [DEPTH, D]), ('ln1_b', [DEPTH, D]), ('w_gu', [DEPTH, D, 2 * DFF]),
                        ('w_down', [DEPTH, DFF, D]), ('ln2_g', [DEPTH, D]), ('ln2_b', [DEPTH, D])]:
            I[nm] = self.din(nm, shp)
        I['c_ident'] = self.din('c_ident', [128, 128])
        I['c_biasP'] = self.din('c_biasP', [128, 2, NH, 128])
        I['c_biasSn'] = self.din('c_biasSn', [128, NH, 128])
        I['c_biasSc'] = self.din('c_biasSc', [128, KVH, 2, 256])
        I['c_mask'] = self.din('c_mask', [128, 4, 512])
        I['c_rmask'] = self.din('c_rmask', [128, 512])
        I['c_blk'] = self.din('c_blk', [128, 128])
        O = self.O = {}
        O['yp'] = self.dout('yp', [seq, D])
        O['ys'] = self.dout('ys', [128, D])
        O['pk'] = self.dout('pk', [DEPTH, 128, 256])
        O['pv'] = self.dout('pv', [DEPTH, 128, 256])
        O['pconv'] = self.dout('pconv', [DEPTH, 3, D])
        O['prg'] = self.dout('prg', [DEPTH, D])
        O['pshift'] = self.dout('pshift', [DEPTH, SHW])
        O['pwkv'] = self.dout('pwkv', [DEPTH, NH, HD, HD])
        O['sk'] = self.dout('sk', [DEPTH, NSEQ_S, 128, 256])
        O['sv'] = self.dout('sv', [DEPTH, NSEQ_S, 128, 256])
        O['sconv_o'] = self.dout('sconv_o', [DEPTH, NSEQ_S, 3, D])
        O['srg_o'] = self.dout('srg_o', [DEPTH, NSEQ_S, D])
        O['sshift_o'] = self.dout('sshift_o', [DEPTH, NSEQ_S, SHW])
        O['swkv_o'] = self.dout('swkv_o', [DEPTH, NSEQ_S, NH, HD, HD])
        W = self.W = {}
        W['w_in'] = self.dscr('wb_in', [DEPTH, D, INC], BF16)
        W['w_out'] = self.dscr('wb_out', [DEPTH, D, D], BF16)
        W['w_gu'] = self.dscr('wb_gu', [DEPTH, D, 2 * DFF], BF16)
        W['w_down'] = self.dscr('wb_down', [DEPTH, DFF, D], BF16)
        self.scr_wabd = self.dscr('scr_wabd', [128, DEPTH, 2, 8, 128], BF16)
        self.scr_wlora = self.dscr('scr_wlora', [128, DEPTH, D], BF16)
        self.scr_wgup = self.dscr('scr_wgup', [128, DEPTH, 2, D], BF16)
        self.scr_v = self.dscr('scr_v', [DEPTH, NSEQ_S, 8, 2, 6, T_S, 64], F32)
        self.scr_y = self.dscr('scr_y', [DEPTH, NSEQ_S, 8, 2, T_S, 64], F32)

        self.psb = [es.enter_context(nc.psum_tensor("psb%d" % i, [128, 512], F32)) for i in range(8)]
        self.wbuf = [self.T("wbuf%d" % i, [128, 8, 512], BF16) for i in range(NWBUF)]
        self.wn = 0
        self.xres = self.T("xres", [128, 4, D], F32)
        self.xT = self.T("xT", [128, 8, 512], BF16)
        self.mixacc = self.T("mixacc", [128, 8, 512], F32)
        self.identf = self.T("identf", [128, 128], F32)
        self.xb_tmp = [self.T("xb_tmp%d" % i, [128, D], BF16) for i in range(2)]
        self.identb = self.T("identb", [128, 128], BF16)
        self.onesb = self.T("onesb", [128, 64], BF16)
        self.esink = self.T("esink", [128, DEPTH, 8], F32)
        self.epsln = self.T("epsln", [128, 2], F32)
        self.p_cw = self.T("p_cw", [128, DEPTH, 4, 8], F32)
        self.p_cb = self.T("p_cb", [128, DEPTH, 8], F32)
        self.p_ba = self.T("p_ba", [128, DEPTH, 8], F32)
        self.p_bx = self.T("p_bx", [128, DEPTH, 8], F32)
        self.p_c8 = self.T("p_c8", [128, DEPTH, 8], F32)
        self.p_mu = self.T("p_mu", [128, DEPTH, 27], F32)
        self.p_w0 = self.T("p_w0", [128, DEPTH, 8], F32)
        self.p_a0 = self.T("p_a0", [128, DEPTH, 8], F32)
        self.p_kk = self.T("p_kk", [128, DEPTH, 8], F32)
        self.p_ka = self.T("p_ka", [128, DEPTH, 8], F32)
        self.p_rk = self.T("p_rk", [128, DEPTH, 8], F32)
        self.p_omka = self.T("p_omka", [128, DEPTH, 8], F32)
        self.blkb = self.T("blkb", [128, 128], BF16)
        self.blkf = self.T("blkf", [128, 256], F32)
        self.convst = {'P': self.T("convstP", [128, DEPTH, 8, 1, 3], F32), 'S': self.T("convstS", [128, DEPTH, 8, NSEQ_S, 3], F32)}
        self.hst = {'P': self.T("hstP", [128, DEPTH, 8, 1], F32), 'S': self.T("hstS", [128, DEPTH, 8, NSEQ_S], F32)}
        self.shst = {'P': self.T("shstP", [128, DEPTH, 27, 1], F32), 'S': self.T("shstS", [128, DEPTH, 27, NSEQ_S], F32)}

        self.phP = Phase(self)
        self.kT_p = [self.phP.T("kTp%d" % l, [128, KVH, 128 + 512], BF16) for l in range(L)]
        self.vtok_p = [self.phP.T("vtokp%d" % l, [128, 5, 256], BF16) for l in range(L)]
        self.H32 = self.phP.T("H32", [128, DEPTH, 8, 128], F32)
        self.Hbf = self.phP.T("Hbf", [128, DEPTH, 8, 128], BF16)
        self.maskb = self.phP.T("maskb", [128, 4, 512], BF16)
        self.rmaskP = self.phP.T("rmaskP", [128, 512], F32)
        self.phP.names = ['kTp', 'vtokp', 'H32', 'Hbf', 'maskb', 'rmaskP']
        try:
            self.main_body()
        except _Stop:
            for ph in reversed(list(self.open_phases)):
                ph.close()
        S.finish('sp')
        S.finish('act')
        self.es.close()
        return nc

    def main_body(self):
        S, I, O, L, seq = self.S, self.I, self.O, self.depth, self.seq
        self.setup()
        self.chk('setup')
        ngroups = seq // 512
        for g in range(ngroups):
            self.load_x(I['xp'][g * 512:(g + 1) * 512, :], 4)
            self.chk('loadx')
            for l in range(L):
                self.layer(l, 'P', g, ngroups)
            self.store_y(O['yp'][g * 512:(g + 1) * 512, :], 4)
        self.chk('pgroups')
        for l in range(L):
            self.prompt_state_out(l)
        self.chk('pstates')
        self.phP.close()
        if self.do_sample:
            self.load_x(I['xs'], 1)
            for l in range(L):
                self.sample_state_in(l)
            self.chk('S_in')
            for l in range(L):
                self.layer(l, 'S', 0, 1)
                self.sample_state_out(l)
            self.store_y(O['ys'], 1)

    def setup(self):
        S, I, W = self.S, self.I, self.W
        ph = Phase(self)
        st32 = [ph.T("wst32_%d" % i, [128, 2048], F32) for i in range(3)]
        st16 = [ph.T("wst16_%d" % i, [128, 2048], BF16) for i in range(3)]
        pc = 0
        for nm in ['w_in', 'w_out', 'w_gu', 'w_down']:
            src = I[nm]
            dst = W[nm]
            tot = 1
            for s_ in src.shape:
                tot *= s_
            sf = src.rearrange("l a b -> (l a b)").rearrange("(r c) -> r c", c=2048)
            df = dst.rearrange("l a b -> (l a b)").rearrange("(r c) -> r c", c=2048)
            R = tot // 2048
            r0 = 0
            while r0 < R:
                rr = min(128, R - r0)
                b = pc % 3
                S.dma('sp', st32[b][0:rr, :], sf[r0:r0 + rr, :], writes=[('wst32', b)])
                ce = ('dve', 'act', 'pool')[pc % 3]
                if ce == 'act':
                    S.op('act', lambda e: e.copy(out=st16[b][0:rr, :], in_=st32[b][0:rr, :]),
                         reads=[('wst32', b)], writes=[('wst16', b)])
                else:
                    S.op(ce, lambda e: e.tensor_copy(out=st16[b][0:rr, :], in_=st32[b][0:rr, :]),
                         reads=[('wst32', b)], writes=[('wst16', b)])
                S.dma('act', df[r0:r0 + rr, :], st16[b][0:rr, :], reads=[('wst16', b)], writes=[('W', nm, pc)])
                r0 += rr
                pc += 1
        ph.names = ['wst32', 'wst16']
        ph.close()
        for k, t in list(S.lastw.items()):
            if isinstance(k, tuple) and k[0] == 'W':
                S._wait('sp', t)
        S.dma('sp', self.identf[:], I['c_ident'], writes=['identf'])
        S.op('dve', lambda e: e.tensor_copy(out=self.identb[:], in_=self.identf[:]), reads=['identf'], writes=['identb'])
        S.op('dve', lambda e: e.memset(self.onesb[:], 1.0), writes=['onesb'])
        S.op('dve', lambda e: e.memset(self.epsln[:, 0:1], LN_EPS), writes=['epsln'])
        S.op('dve', lambda e: e.memset(self.epsln[:, 1:2], GN_EPS), writes=['epsln'])
        sk = I['attn_sinks'].rearrange("l (j two) -> two l j", two=2)
        for half in range(2):
            S.dma('sp', self.esink[half * 64:(half + 1) * 64, :, :],
                  sk[half:half + 1, :, :].broadcast_to([64, DEPTH, 8]), writes=['esink'],
                  allow_slow_non_contiguous=True)
        S.op('act', lambda e: e.activation(out=self.esink[:], in_=self.esink[:], func=AF.Exp),
             reads=['esink'], writes=['esink'])
        self.setup_params()
        for l in range(self.depth):
            S.op('dve', lambda e: e.memset(self.kT_p[l][:], 0.0), writes=[('kTp', l)])
            S.op('pool', lambda e: e.memset(self.vtok_p[l][:], 0.0), writes=[('vtokp', l)])


    def colload(self, dst, src_vec, key, nblk=8):
        self.S.dma('act', dst, src_vec.rearrange("(j p) -> p j", p=128), writes=[key], allow_slow_non_contiguous=True)

    def setup_params(self):
        S, I = self.S, self.I
        ph = Phase(self)
        for l in range(DEPTH):
            for i in range(4):
                self.colload(self.p_cw[:, l, i, :], I['conv_w'][l, i], 'p_cw')
            self.colload(self.p_cb[:, l, :], I['conv_b'][l], 'p_cb')
            self.colload(self.p_ba[:, l, :], I['rg_ba'][l], 'p_ba')
            self.colload(self.p_bx[:, l, :], I['rg_bx'][l], 'p_bx')
            self.colload(self.p_c8[:, l, :], I['rg_lambda'][l], 'p_c8')
            self.colload(self.p_mu[:, l, 0:26], I['rw_mu'][l, 0:3328], 'p_mu', 26)
            S.dma('act', self.p_mu[0:32, l, 26:27], I['rw_mu'][l, 3328:3360].rearrange("(p o) -> p o", o=1), writes=['p_mu'],
                  allow_slow_non_contiguous=True)
            for nm, t in (('rw_w0', self.p_w0), ('rw_a0', self.p_a0), ('rw_kk', self.p_kk), ('rw_ka', self.p_ka),
                          ('rw_rk', self.p_rk)):
                self.colload(t[:, l, :], I[nm][l], nm)
        c8 = self.p_c8[:].rearrange("p l j -> p (l j)")
        x = ph.T("sp_x", [128, 16], F32)
        z = ph.T("sp_z", [128, 16], F32)
        z2 = ph.T("sp_z2", [128, 16], F32)
        acc = ph.T("sp_acc", [128, 16], F32)
        S.op('act', lambda e: e.activation(out=x[:], in_=c8, func=AF.Exp, scale=-1.0), reads=['p_c8'], writes=['sp_x'])
        S.op('dve', lambda e: e.tensor_scalar(out=z[:], in0=x[:], scalar1=2.0, scalar2=None, op0=ALU.add), reads=['sp_x'], writes=['sp_z'])
        S.op('dve', lambda e: e.reciprocal(out=z[:], in_=z[:]), reads=['sp_z'], writes=['sp_z'])
        S.op('dve', lambda e: e.tensor_tensor(out=z[:], in0=z[:], in1=x[:], op=ALU.mult), reads=['sp_z', 'sp_x'], writes=['sp_z'])
        S.op('dve', lambda e: e.tensor_tensor(out=z2[:], in0=z[:], in1=z[:], op=ALU.mult), reads=['sp_z'], writes=['sp_z2'])
        S.op('dve', lambda e: e.tensor_scalar(out=acc[:], in0=z2[:], scalar1=1.0 / 9, scalar2=1.0 / 7, op0=ALU.mult, op1=ALU.add),
             reads=['sp_z2'], writes=['sp_acc'])
        for cst in (1.0 / 5, 1.0 / 3, 1.0):
            S.op('dve', lambda e: e.tensor_tensor(out=acc[:], in0=acc[:], in1=z2[:], op=ALU.mult), reads=['sp_acc', 'sp_z2'], writes=['sp_acc'])
            S.op('dve', lambda e: e.tensor_scalar(out=acc[:], in0=acc[:], scalar1=cst, scalar2=None, op0=ALU.add), reads=['sp_acc'], writes=['sp_acc'])
        S.op('dve', lambda e: e.tensor_tensor(out=acc[:], in0=acc[:], in1=z[:], op=ALU.mult), reads=['sp_acc', 'sp_z'], writes=['sp_acc'])
        S.op('dve', lambda e: e.tensor_scalar(out=c8, in0=acc[:], scalar1=-16.0, scalar2=None, op0=ALU.mult), reads=['sp_acc'], writes=['p_c8'])
        ph.close()
        ph = Phase(self)
        st = ph.T("wbd_st", [128, DEPTH, 2, 8, 128], F32)
        S.op('pool', lambda e: e.memset(st[:], 0.0), writes=['wbd_st'])
        for l in range(DEPTH):
            for gi, nm in enumerate(('rg_wa', 'rg_wx')):
                src = I[nm][l].rearrange("(j hh) i c -> hh i j c", hh=2)
                for hh in range(2):
                    S.dma('act', st[hh * 64:(hh + 1) * 64, l, gi, :, hh * 64:(hh + 1) * 64], src[hh], writes=['wbd_st'])
        wabd_t = ph.T("wabd_t", [128, DEPTH, 2, 8, 128], BF16)
        S.op('dve', lambda e: e.tensor_copy(out=wabd_t[:], in_=st[:]), reads=['wbd_st'], writes=['wabd_t'])
        S.dma('act', self.scr_wabd, wabd_t[:], reads=['wabd_t'], writes=['scr_wabd'])
        for st_ in ('P', 'S'):
            S.op('pool', lambda e: e.memset(self.convst[st_][:], 0.0), writes=[('convst', st_)])
            S.op('pool', lambda e: e.memset(self.hst[st_][:], 0.0), writes=[('hst', st_)])
            S.op('pool', lambda e: e.memset(self.shst[st_][:], 0.0), writes=[('shst', st_)])
        ph.close()
        ph = Phase(self)
        st2 = ph.T("lora_st", [128, DEPTH, D], F32)
        st3 = ph.T("gup_st", [128, DEPTH, 2, D], F32)
        S.op('pool', lambda e: e.memset(st3[:], 0.0), writes=['gup_st'])
        for l in range(DEPTH):
            S.dma('act', st2[0:64, l, :], I['rw_wup'][l], writes=['lora_st'])
            S.dma('act', st2[64:128, l, :], I['rw_aup'][l], writes=['lora_st'])
            S.dma('act', st3[:, l, 0, :], I['rw_gup'][l, 0:128, :], writes=['gup_st'])
            S.dma('act', st3[0:32, l, 1, :], I['rw_gup'][l, 128:160, :], writes=['gup_st'])
        wlora_t = ph.T("wlora_t", [128, DEPTH, D], BF16)
        wgup_t = ph.T("wgup_t", [128, DEPTH, 2, D], BF16)
        S.op('dve', lambda e: e.tensor_copy(out=wlora_t[:], in_=st2[:]), reads=['lora_st'], writes=['wlora_t'])
        S.op('dve', lambda e: e.tensor_copy(out=wgup_t[:], in_=st3[:]), reads=['gup_st'], writes=['wgup_t'])
        S.dma('act', self.scr_wlora, wlora_t[:], reads=['wlora_t'], writes=['scr_wlora'])
        S.dma('act', self.scr_wgup, wgup_t[:], reads=['wgup_t'], writes=['scr_wgup'])
        S.op('dve', lambda e: e.tensor_scalar(out=self.p_omka[:], in0=self.p_ka[:], scalar1=-1.0, scalar2=1.0, op0=ALU.mult, op1=ALU.add),
             reads=['rw_ka'], writes=['p_omka'])
        S.op('pool', lambda e: e.memset(self.H32[:], 0.0), writes=['H32'])
        S.op('pool', lambda e: e.memset(self.Hbf[:], 0.0), writes=['Hbf'])
        ph.close()
        ph = Phase(self)
        mk = ph.T("mask_st", [128, 4, 512], F32)
        S.dma('act', mk[:], I['c_mask'], writes=['mask_st'])
        S.op('dve', lambda e: e.tensor_copy(out=self.maskb[:], in_=mk[:]), reads=['mask_st'], writes=['maskb'])
        bk = ph.T("blk_st", [128, 128], F32)
        S.dma('act', bk[:], I['c_blk'], writes=['blk_st'])
        S.op('dve', lambda e: e.tensor_copy(out=self.blkb[:], in_=bk[:]), reads=['blk_st'], writes=['blkb'])
        for r2 in range(2):
            S.op('dve', lambda e: e.tensor_copy(out=self.blkf[:, r2 * 128:(r2 + 1) * 128], in_=bk[:]), reads=['blk_st'], writes=['blkf'])
        S.dma('act', self.rmaskP[:], I['c_rmask'], writes=['rmaskP'])
        ph.close()


    def prompt_state_out(self, l):
        S, O = self.S, self.O
        ph = Phase(self)
        cv, hs, sh = self.convst['P'], self.hst['P'], self.shst['P']
        self.fm_to_rows(ph, lambda blk, w: cv[0:w, l, blk, 0, :], 3, D, O['pconv'][l], [('convst', 'P')], ('o_pconv', l))
        self.fm_to_rows(ph, lambda blk, w: hs[0:w, l, blk, :], 1, D, O['prg'][l:l + 1, :], [('hst', 'P')], ('o_prg', l))
        self.fm_to_rows(ph, lambda blk, w: sh[0:w, l, blk, :], 1, SHW, O['pshift'][l:l + 1, :], [('shst', 'P')], ('o_pshift', l))
        for j in range(8):
            ps, pk = self.ps()
            S.op('pe', lambda e: e.transpose(out=ps[:, 0:128], in_=self.H32[:, l, j, :], identity=self.identf[:]),
                 reads=[('H32', l, j), 'identf'], writes=[pk])
            tmp = ph.T("pw_tmp%d" % j, [128, 128], F32)
            S.op('act', lambda e: e.copy(out=tmp[:], in_=ps[:, 0:128]), reads=[pk], writes=[('pw_tmp', j)])
            for hh in range(2):
                S.dma('act', O['pwkv'][l, 2 * j + hh], tmp[hh * 64:(hh + 1) * 64, hh * 64:(hh + 1) * 64],
                      reads=[('pw_tmp', j)], writes=[('o_pwkv', l, j, hh)])
        ph.names.append('pw_tmp')
        ph.close()

    def sample_state_in(self, l):
        S, I = self.S, self.I
        ph = Phase(self)
        cv, hs, sh = self.convst['S'], self.hst['S'], self.shst['S']
        self.rows_to_fm(ph, I['sconv'][l].rearrange("n i c -> (n i) c"), 48, D,
                        lambda b0, nb, w: cv[0:w, l, b0:b0 + nb, :, :].rearrange("p b n i -> p b (n i)"), [('convst', 'S')])
        ph.close()
        ph = Phase(self)
        self.rows_to_fm(ph, I['srg'][l], NSEQ_S, D, lambda b0, nb, w: hs[0:w, l, b0:b0 + nb, :], [('hst', 'S')])
        ph.close()
        ph = Phase(self)
        self.rows_to_fm(ph, I['sshift'][l], NSEQ_S, SHW, lambda b0, nb, w: sh[0:w, l, b0:b0 + nb, :], [('shst', 'S')])
        ph.close()

    def sample_state_out(self, l):
        S, O = self.S, self.O
        cv, hs, sh = self.convst['S'], self.hst['S'], self.shst['S']
        ph = Phase(self)
        self.fm_to_rows(ph, lambda blk, w: cv[0:w, l, blk, :, :].rearrange("p n i -> p (n i)"), 48, D,
                        O['sconv_o'][l].rearrange("n i c -> (n i) c"), [('convst', 'S')], ('o_sconv', l))
        ph.close()
        ph = Phase(self)
        self.fm_to_rows(ph, lambda blk, w: hs[0:w, l, blk, :], NSEQ_S, D, O['srg_o'][l], [('hst', 'S')], ('o_srg', l))
        ph.close()
        ph = Phase(self)
        self.fm_to_rows(ph, lambda blk, w: sh[0:w, l, blk, :], NSEQ_S, SHW, O['sshift_o'][l], [('shst', 'S')], ('o_sshift', l))
        ph.close()

    def rows_to_fm(self, ph, src_rows, R, F, dst_fn, keys_w):
        S = self.S
        tmp = ph.T("r2f_tmp", [128, F], F32)
        S.dma('act', tmp[0:R, :], src_rows, writes=['r2f_tmp'])
        nblk = (F + 127) // 128
        per = max(1, 512 // R)
        b0 = 0
        while b0 < nblk:
            nb = min(per, nblk - b0)
            ps, pk = self.ps()
            wlast = 128
            for b in range(nb):
                w = min(128, F - (b0 + b) * 128)
                wlast = w
                S.op('pe', lambda e: e.transpose(out=ps[0:w, b * R:(b + 1) * R], in_=tmp[0:R, (b0 + b) * 128:(b0 + b) * 128 + w],
                                                 identity=self.identf[0:R, 0:R]), reads=['r2f_tmp', 'identf'], writes=[pk])
            nfull = nb if wlast == 128 else nb - 1
            if nfull > 0:
                S.op('act', lambda e: e.copy(out=dst_fn(b0, nfull, 128), in_=ps[:, 0:nfull * R].rearrange("p (b r) -> p b r", r=R)),
                     reads=[pk], writes=keys_w)
            if nfull < nb:
                S.op('act', lambda e: e.copy(out=dst_fn(b0 + nfull, 1, wlast),
                                             in_=ps[0:wlast, nfull * R:(nfull + 1) * R].rearrange("p (b r) -> p b r", r=R)),
                     reads=[pk], writes=keys_w)
            b0 += nb

    def fm_to_rows(self, ph, src_fn, R, F, dst_rows, keys_r, okey):
        S = self.S
        tmp = ph.T("f2r_tmp", [128, F], F32)
        nblk = (F + 127) // 128
        b0 = 0
        while b0 < nblk:
            nb = min(4, nblk - b0)
            ps, pk = self.ps()
            tot = 0
            for b in range(nb):
                w = min(128, F - (b0 + b) * 128)
                S.op('pe', lambda e: e.transpose(out=ps[0:R, b * 128:b * 128 + w], in_=src_fn(b0 + b, w),
                                                 identity=self.identf[0:w, 0:w]), reads=keys_r + ['identf'], writes=[pk])
                tot += w
            S.op('act', lambda e: e.copy(out=tmp[0:R, b0 * 128:b0 * 128 + tot], in_=ps[0:R, 0:tot]), reads=[pk], writes=['f2r_tmp'])
            b0 += nb
        S.dma('act', dst_rows, tmp[0:R, :], reads=['f2r_tmp'], writes=[okey])

    def load_x(self, src, ntile):
        self.S.dma('act', self.xres[:, 0:ntile, :], src.rearrange("(t p) d -> p t d", p=128),
                   writes=[('xres', t) for t in range(ntile)])
        for t in range(ntile):
            self.make_xT(t)

    def store_y(self, dst, ntile):
        self.S.dma('act', dst.rearrange("(t p) d -> p t d", p=128), self.xres[:, 0:ntile, :],
                   reads=[('xres', t) for t in range(ntile)], writes=[('yout', id(dst))])

    def make_xT(self, t):
        S = self.S
        self.xbn = (getattr(self, 'xbn', 0) + 1) % 2
        xb = self.xb_tmp[self.xbn]
        xbk = ('xb_tmp', self.xbn)
        S.op('act', lambda e: e.copy(out=xb[:], in_=self.xres[:, t, :]), reads=[('xres', t)], writes=[xbk])
        ps, pk = self.ps()
        psb = ps[:].bitcast(BF16)
        for k in range(8):
            S.op('pe', lambda e: e.transpose(out=psb[:, k * 128:(k + 1) * 128], in_=xb[:, k * 128:(k + 1) * 128],
                                             identity=self.identb[:]),
                 reads=[xbk, 'identb'], writes=[pk])
        S.op('dve', lambda e: e.tensor_copy(out=self.xT[:, :, t * 128:(t + 1) * 128],
                                            in_=psb.rearrange("p (k c) -> p k c", k=8)),
             reads=[pk], writes=[('xT', t)])

    def wtile(self, wname, l, cols, rows=(0, 8)):
        S = self.S
        i = self.wn
        self.wn = (self.wn + 1) % NWBUF
        buf = self.wbuf[i]
        src = self.W[wname][l]
        k0, k1 = rows
        off = 0
        keys = []
        for ci, (c0, n) in enumerate(cols):
            key = ('wbuf', i) if ci == 0 else ('wbufx', i, ci)
            S.dma('sp', buf[:, 0:k1 - k0, off:off + n],
                  src[k0 * 128:k1 * 128, c0:c0 + n].rearrange("(k p) c -> p k c", p=128), writes=[key])
            keys.append(key)
            off += n
        return buf, keys

    def layer(self, l, stream, g, ngroups):
        S = self.S
        nt = 4 if stream == 'P' else 1
        N = nt * 128
        self.attention(l, stream, g, ngroups, nt, N)
        self.chk('attn')
        self.chk(stream + '_attn')
        self.rglru(l, stream, g, ngroups, nt, N)
        self.chk('rglru')
        self.chk(stream + '_rglru')
        self.rwkv(l, stream, g, ngroups, nt, N)
        self.chk('rwkv')
        self.chk(stream + '_rwkv')
        self.dense_tail(l, stream, nt, N)
        self.chk('dense')

    def load_ln(self, l):
        S, I = self.S, self.I
        for i, nm in enumerate(['ln1_g', 'ln1_b', 'ln2_g', 'ln2_b']):
            S.dma('act', self.lnbuf[:, i, :], I[nm][l:l + 1, :].broadcast_to([128, D]), writes=[('lnbuf', i)])

    def layer_norm_tile(self, ph, t, which):
        S = self.S
        x = self.xres[:, t, :]
        key = ('xres', t)
        cache = ph.__dict__.setdefault('_ln_cache', {})
        if t not in cache:
            cache[t] = (ph.T("lnst%d" % t, [128, 2, 6], F32), ph.T("lnmv%d" % t, [128, 2], F32), ph.T("lnrs%d" % t, [128, 1], F32))
            ph.names += ['lnst', 'lnmv', 'lnrs']
        st, mv, rs = cache[t]
        kst, kmv, krs = ('lnst', t), ('lnmv', t), ('lnrs', t)
        for hf in range(2):
            S.op('dve', lambda e: e.bn_stats(out=st[:, hf, :], in_=self.xres[:, t, hf * 512:(hf + 1) * 512]),
                 reads=[key], writes=[kst])
        yield
        S.op('dve', lambda e: e.bn_aggr(out=mv[:], in_=st[:].rearrange("p a b -> p (a b)")), reads=[kst], writes=[kmv])
        yield
        S.op('act', lambda e: e.activation(out=rs[:], in_=mv[:, 1:2], func=AF.Sqrt, bias=self.epsln[:, 0:1], scale=1.0),
             reads=[kmv, 'epsln'], writes=[krs])
        yield
        S.op('dve', lambda e: e.reciprocal(out=rs[:], in_=rs[:]), reads=[krs], writes=[krs])
        yield
        S.op('dve', lambda e: e.tensor_scalar(out=x, in0=x, scalar1=mv[:, 0:1], scalar2=rs[:, 0:1],
                                              op0=ALU.subtract, op1=ALU.mult), reads=[key, kmv, krs], writes=[key])
        yield
        S.op('pool', lambda e: e.tensor_tensor(out=x, in0=x, in1=self.lnbuf[:, 2 * which, :], op=ALU.mult),
             reads=[key, ('lnbuf', 2 * which)], writes=[key])
        yield
        S.op('pool', lambda e: e.tensor_tensor(out=x, in0=x, in1=self.lnbuf[:, 2 * which + 1, :], op=ALU.add),
             reads=[key, ('lnbuf', 2 * which + 1)], writes=[key])
        yield

    def dense_tail(self, l, stream, nt, N):
        S = self.S
        ph = Phase(self)
        self.lnbuf = ph.T("lnbuf", [128, 4, D], F32)
        self.load_ln(l)
        self.mixT = ph.T("mixT", [128, 8, N], BF16)
        for j in range(8):
            S.op('act', lambda e: e.copy(out=self.mixT[:, j, 0:N], in_=self.mixacc[:, j, 0:N]),
                 reads=[('mixacc', j)], writes=[('mixT', j)])
        mkeys = [('mixT', j) for j in range(8)]
        wbs = [self.wtile('w_out', l, [(half * 512, 512)]) for half in range(2)]
        for t in range(nt):
            for half in range(2):
                wb, wk = wbs[half]
                ps, pk = self.ps()
                for k in range(8):
                    self.mm(ps[:, :], self.mixT[:, k, t * 128:(t + 1) * 128], wb[:, k, :], k == 0, k == 7,
                            reads=wk + [('mixT', k)], writes=[pk])
                xs = self.xres[:, t, half * 512:(half + 1) * 512]
                S.op('dve', lambda e: e.scalar_tensor_tensor(out=xs, in0=xs, scalar=ALPHA, in1=ps[:, :],
                                                             op0=ALU.mult, op1=ALU.add),
                     reads=[pk, ('xres', t)], writes=[('xres', t)])
        self.interleave([self.layer_norm_tile(ph, t, 0) for t in range(nt)])
        for t in range(nt):
            self.make_xT(t)
        self.dump('x1', self.xres[:, 0:nt, :], [128, nt, D], [('xres', t) for t in range(nt)])
        hT = ph.T("hT", [128, 22, N], BF16)
        sg = ph.T("sgt", [128, N], F32)
        xkeys = [('xT', t) for t in range(nt)]
        NB = DFF // 128
        for j0 in range(0, NB, 4):
            nb = min(4, NB - j0)
            wg, wgk = self.wtile('w_gu', l, [(j0 * 128, nb * 128)])
            wu, wuk = self.wtile('w_gu', l, [(DFF + j0 * 128, nb * 128)])
            for b in range(nb):
                j = j0 + b
                psg, pgk = self.ps()
                for k in range(8):
                    self.mm(psg[:, 0:N], wg[:, k, b * 128:(b + 1) * 128], self.xT[:, k, 0:N], k == 0, k == 7,
                            reads=wgk + xkeys, writes=[pgk])
                psu, puk = self.ps()
                for k in range(8):
                    self.mm(psu[:, 0:N], wu[:, k, b * 128:(b + 1) * 128], self.xT[:, k, 0:N], k == 0, k == 7,
                            reads=wuk + xkeys, writes=[puk])
                S.op('act', lambda e: e.activation(out=sg[:], in_=psg[:, 0:N], func=AF.Silu), reads=[pgk], writes=['sgt'])
                S.op('dve', lambda e: e.tensor_tensor(out=hT[:, j, :], in0=sg[:], in1=psu[:, 0:N], op=ALU.mult),
                     reads=['sgt', puk], writes=[('hT', j)])
        for half in range(2):
            accs = [self.ps() for _ in range(nt)]
            pieces = [(0, 8), (8, 16), (16, 22)]
            for pi, (k0, k1) in enumerate(pieces):
                wb, wk = self.wtile('w_down', l, [(half * 512, 512)], rows=(k0, k1))
                for t in range(nt):
                    ps, pk = accs[t]
                    for k in range(k0, k1):
                        self.mm(ps[:, :], hT[:, k, t * 128:(t + 1) * 128], wb[:, k - k0, :], k == 0, k == NB - 1,
                                reads=wk + [('hT', k)], writes=[pk])
            for t in range(nt):
                ps, pk = accs[t]
                xs = self.xres[:, t, half * 512:(half + 1) * 512]
                S.op('dve', lambda e: e.scalar_tensor_tensor(out=xs, in0=xs, scalar=ALPHA, in1=ps[:, :],
                                                             op0=ALU.mult, op1=ALU.add),
                     reads=[pk, ('xres', t)], writes=[('xres', t)])
        self.interleave([self.layer_norm_tile(ph, t, 1) for t in range(nt)])
        for t in range(nt):
            if l < self.depth - 1:
                self.make_xT(t)
        self.dump('x2', self.xres[:, 0:nt, :], [128, nt, D], [('xres', t) for t in range(nt)])
        ph.close()

    def interleave(self, gens):
        gens = list(gens)
        while gens:
            for g_ in list(gens):
                try:
                    next(g_)
                except StopIteration:
                    gens.remove(g_)

    def interleave_staged(self, gens):
        gens = list(gens)
        waiting = []
        active = list(gens)
        while active:
            for g_ in list(active):
                try:
                    r = next(g_)
                except StopIteration:
                    active.remove(g_)
                    continue
                if r == 'CHAIN':
                    active.remove(g_)
                    waiting.append(g_)
        waiting.sort(key=lambda g_: gens.index(g_))
        post = []
        for g_ in waiting:
            while True:
                try:
                    r = next(g_)
                except StopIteration:
                    break
                if r == 'POST':
                    post.append(g_)
                    break
        self.interleave(post)

    def rglru(self, l, stream, g, ngroups, nt, N):
        S, I = self.S, self.I
        ph = Phase(self)
        nseq, T = (1, 512) if stream == 'P' else (NSEQ_S, T_S)
        convst, hst = self.convst[stream], self.hst[stream]
        ckey, hkey = ('convst', stream), ('hst', stream)
        xkeys = [('xT', t) for t in range(nt)]
        NW = 4
        xrh = [ph.T("xrh%d" % i, [128, nseq, 3 + T], F32) for i in range(NW)]
        xc = [ph.T("xc%d" % i, [128, nseq, T], F32) for i in range(NW)]
        xcb = [ph.T("xcb%d" % i, [128, N], BF16) for i in range(NW)]
        rr = [ph.T("rg_r%d" % i, [128, N], F32) for i in range(NW)]
        ii = [ph.T("rg_i%d" % i, [128, N], F32) for i in range(NW)]
        aa = [ph.T("rg_a%d" % i, [128, nseq, T], F32) for i in range(NW)]
        uu = [ph.T("rg_u%d" % i, [128, nseq, T], F32) for i in range(NW)]
        hh = ph.T("rg_h", [128, 4, nseq, T], F32)
        ph.names += ['xrh', 'xc', 'xcb', 'rg_r', 'rg_i', 'rg_a', 'rg_u']
        wa_l = ph.T("wa_bd", [128, 2, 8, 128], BF16)
        S.dma('act', wa_l[:], self.scr_wabd[:, l], reads=['scr_wabd'], writes=['wa_bd'])
        flat = lambda t3: t3[:].rearrange("p n t -> p (n t)")

        def block(half, b, wb, wk, wg, wgk):
            j = half * 4 + b
            K_ = lambda nm: (nm, b)
            ps, pk = self.ps()
            for k in range(8):
                self.mm(ps[:, 0:N], wb[:, k, b * 128:(b + 1) * 128], self.xT[:, k, 0:N], k == 0, k == 7,
                        reads=wk + xkeys, writes=[pk])
            yield
            S.op('act', lambda e: e.copy(out=xrh[b][:, :, 3:3 + T], in_=ps[:, 0:N].rearrange("p (n t) -> p n t", n=nseq)),
                 reads=[pk], writes=[K_('xrh')])
            S.op('pool', lambda e: e.tensor_copy(out=xrh[b][:, :, 0:3], in_=convst[:, l, j, :, :]), reads=[ckey], writes=[K_('xrh')])
            yield
            cw = self.p_cw
            S.op('dve', lambda e: e.tensor_scalar(out=xc[b][:], in0=xrh[b][:, :, 0:T], scalar1=cw[:, l, 0, j:j + 1],
                                                  scalar2=self.p_cb[:, l, j:j + 1], op0=ALU.mult, op1=ALU.add),
                 reads=[K_('xrh'), 'p_cw', 'p_cb'], writes=[K_('xc')])
            yield
            for i in range(1, 4):
                S.op('dve', lambda e: e.scalar_tensor_tensor(out=xc[b][:], in0=xrh[b][:, :, i:i + T], scalar=cw[:, l, i, j:j + 1],
                                                             in1=xc[b][:], op0=ALU.mult, op1=ALU.add),
                     reads=[K_('xrh'), 'p_cw', K_('xc')], writes=[K_('xc')])
                yield
            S.op('pool', lambda e: e.tensor_copy(out=convst[:, l, j, :, :], in_=xrh[b][:, :, T:T + 3]), reads=[K_('xrh')], writes=[ckey])
            S.op('act', lambda e: e.copy(out=xcb[b][:], in_=flat(xc[b])), reads=[K_('xc')], writes=[K_('xcb')])
            yield
            pss = []
            for gi in range(2):
                ps2, pk2 = self.ps()
                self.mm(ps2[:, 0:N], wa_l[:, gi, j, :], xcb[b][:], True, True, reads=['wa_bd', K_('xcb')], writes=[pk2])
                pss.append((ps2, pk2))
            yield
            for gi, (dst, dk, bias) in enumerate(((rr[b], K_('rg_r'), self.p_ba), (ii[b], K_('rg_i'), self.p_bx))):
                ps2, pk2 = pss[gi]
                S.op('act', lambda e: e.activation(out=dst[:], in_=ps2[:, 0:N], func=AF.Sigmoid, bias=bias[:, l, j:j + 1], scale=1.0),
                     reads=[pk2, 'p_ba', 'p_bx'], writes=[dk])
            yield
            S.op('act', lambda e: e.activation(out=flat(aa[b]), in_=rr[b][:], func=AF.Exp, scale=self.p_c8[:, l, j:j + 1]),
                 reads=[K_('rg_r'), 'p_c8'], writes=[K_('rg_a')])
            S.op('pool', lambda e: e.tensor_tensor(out=ii[b][:], in0=ii[b][:], in1=flat(xc[b]), op=ALU.mult), reads=[K_('rg_i'), K_('xc')], writes=[K_('rg_i')])
            yield
            S.op('dve', lambda e: e.tensor_tensor(out=rr[b][:], in0=flat(aa[b]), in1=flat(aa[b]), op=ALU.mult), reads=[K_('rg_a')], writes=[K_('rg_r')])
            yield
            S.op('dve', lambda e: e.tensor_scalar(out=rr[b][:], in0=rr[b][:], scalar1=-1.0, scalar2=1.0, op0=ALU.mult, op1=ALU.add),
                 reads=[K_('rg_r')], writes=[K_('rg_r')])
            yield
            S.op('dve', lambda e: e.tensor_scalar(out=rr[b][:], in0=rr[b][:], scalar1=1e-30, scalar2=None, op0=ALU.max), reads=[K_('rg_r')], writes=[K_('rg_r')])
            yield
            S.op('act', lambda e: e.activation(out=rr[b][:], in_=rr[b][:], func=AF.Sqrt), reads=[K_('rg_r')], writes=[K_('rg_r')])
            yield
            S.op('dve', lambda e: e.tensor_tensor(out=flat(uu[b]), in0=ii[b][:], in1=rr[b][:], op=ALU.mult), reads=[K_('rg_i'), K_('rg_r')], writes=[K_('rg_u')])
            yield
            for n in range(nseq):
                S.op('dve', lambda e: e.tensor_tensor_scan(out=hh[:, b, n, :], data0=aa[b][:, n, :], data1=uu[b][:, n, :],
                                                           initial=hst[:, l, j, n:n + 1], op0=ALU.mult, op1=ALU.add),
                     reads=[K_('rg_a'), K_('rg_u'), hkey], writes=[('rg_h', b)])
            yield
            S.op('pool', lambda e: e.tensor_copy(out=hst[:, l, j, :], in_=hh[:, b, :, T - 1]), reads=[('rg_h', b)], writes=[hkey])
            ps, pk = self.ps()
            for k in range(8):
                self.mm(ps[:, 0:N], wg[:, k, b * 128:(b + 1) * 128], self.xT[:, k, 0:N], k == 0, k == 7,
                        reads=wgk + xkeys, writes=[pk])
            yield
            gbt = ii[b]
            S.op('act', lambda e: e.activation(out=gbt[:], in_=ps[:, 0:N], func=AF.Sigmoid), reads=[pk], writes=[K_('rg_i')])
            yield
            S.op('dve', lambda e: e.tensor_tensor(out=gbt[:], in0=gbt[:], in1=hh[:, b, :, :].rearrange("p n t -> p (n t)"), op=ALU.mult),
                 reads=[K_('rg_i'), ('rg_h', b)], writes=[K_('rg_i')])
            yield
            S.op('pool', lambda e: e.tensor_tensor(out=self.mixacc[:, j, 0:N], in0=self.mixacc[:, j, 0:N], in1=gbt[:], op=ALU.add),
                 reads=[K_('rg_i'), ('mixacc', j)], writes=[('mixacc', j)])
            yield

        for half in range(2):
            wb, wk = self.wtile('w_in', l, [(OFF_RNN + half * 512, 512)])
            wg, wgk = self.wtile('w_in', l, [(OFF_GATE + D + half * 512, 512)])
            self.interleave([block(half, b, wb, wk, wg, wgk) for b in range(4)])
        self.dump('mixAB' if stream == 'P' else 's_mixAB', self.mixacc[:, :, 0:N], [128, 8, N], [('mixacc', j) for j in range(8)])
        ph.close()

    def rwkv(self, l, stream, g, ngroups, nt, N):
        S, I = self.S, self.I
        ph = Phase(self)
        is_s = stream == 'S'
        nseq, T = (NSEQ_S, T_S) if is_s else (1, 512)
        shst = self.shst[stream]
        skey = ('shst', stream)
        xkeys = [('xT', t) for t in range(nt)]
        pch = ph.T("pch", [128, nseq, 1 + T], F32)
        dd = ph.T("rw_d", [128, N], F32)
        lwla = ph.T("lwla", [128, N], BF16)
        siglg = ph.T("siglg", [128, 2, N], BF16)
        self.wlora_l = ph.T("wlora", [128, D], BF16)
        self.wgup_l = ph.T("wgup", [128, 2, D], BF16)
        S.dma('act', self.wlora_l[:], self.scr_wlora[:, l], reads=['scr_wlora'], writes=['wlora'])
        S.dma('act', self.wgup_l[:], self.scr_wgup[:, l], reads=['scr_wgup'], writes=['wgup'])
        lnw = ph.T("lnw_t", [128, D], F32)
        lnb = ph.T("lnb_t", [128, D], F32)
        S.dma('act', lnw[:], I['rw_lnw'][l:l + 1, :].broadcast_to([128, D]), writes=['lnw_t'])
        S.dma('act', lnb[:], I['rw_lnb'][l:l + 1, :].broadcast_to([128, D]), writes=['lnb_t'])
        flat = lambda t3: t3[:].rearrange("p n t -> p (n t)")

        def shifted(ps, pk, blk, w, out_ap, out_key, func=None, bufs=None, S_=None):
            pch_, dd_ = bufs if bufs is not None else (pch, dd)
            So = S_ if S_ is not None else S
            So.op('act', lambda e: e.copy(out=pch_[0:w, :, 1:1 + T], in_=ps[0:w, 0:N].rearrange("p (n t) -> p n t", n=nseq)),
                 reads=[pk], writes=['pch'])
            yield
            So.op('pool', lambda e: e.tensor_copy(out=pch_[0:w, :, 0], in_=shst[0:w, l, blk, :]), reads=[skey], writes=['pch'])
            yield
            So.op('pool', lambda e: e.tensor_copy(out=shst[0:w, l, blk, :], in_=pch_[0:w, :, T]), reads=['pch'], writes=[skey])
            yield
            So.op('pool', lambda e: e.tensor_tensor(out=dd_[0:w, :].rearrange("p (n t) -> p n t", n=nseq), in0=pch_[0:w, :, 0:T],
                                                  in1=pch_[0:w, :, 1:1 + T], op=ALU.subtract), reads=['pch'], writes=['rw_d'])
            yield
            if func is None:
                So.op('dve', lambda e: e.scalar_tensor_tensor(out=out_ap.rearrange("p (n t) -> p n t", n=nseq), in0=dd_[0:w, :].rearrange("p (n t) -> p n t", n=nseq),
                                                             scalar=self.p_mu[0:w, l, blk:blk + 1], in1=pch_[0:w, :, 1:1 + T],
                                                             op0=ALU.mult, op1=ALU.add), reads=['rw_d', 'pch', 'p_mu'], writes=[out_key])
                yield
            else:
                So.op('dve', lambda e: e.scalar_tensor_tensor(out=dd_[0:w, :].rearrange("p (n t) -> p n t", n=nseq), in0=dd_[0:w, :].rearrange("p (n t) -> p n t", n=nseq),
                                                             scalar=self.p_mu[0:w, l, blk:blk + 1], in1=pch_[0:w, :, 1:1 + T],
                                                             op0=ALU.mult, op1=ALU.add), reads=['rw_d', 'pch', 'p_mu'], writes=['rw_d'])
                yield
                func(dd_)

        wb, wk = self.wtile('w_in', l, [(OFF_RW + 3072, 288)])
        for bi, (c0, w) in enumerate(((0, 128), (128, 128), (256, 32))):
            ps, pk = self.ps()
            for k in range(8):
                self.mm(ps[0:w, 0:N], wb[:, k, c0:c0 + w], self.xT[:, k, 0:N], k == 0, k == 7, reads=wk + xkeys, writes=[pk])
            if bi == 0:
                def f0(d_):
                    S.op('act', lambda e: e.activation(out=lwla[0:64, :], in_=d_[0:64, :], func=AF.Tanh), reads=['rw_d'], writes=['lwla'])
                    S.op('act', lambda e: e.copy(out=lwla[64:128, :], in_=d_[64:128, :]), reads=['rw_d'], writes=['lwla'])
                for _ in shifted(ps, pk, 24, 128, None, None, f0):
                    pass
            else:
                def f1(d_, bi=bi, w=w):
                    S.op('act', lambda e: e.activation(out=siglg[0:w, bi - 1, :], in_=d_[0:w, :], func=AF.Sigmoid), reads=['rw_d'], writes=['siglg'])
                for _ in shifted(ps, pk, 24 + bi, w, None, None, f1):
                    pass

        if not is_s:
            for rnd in range(4):
                self.rwkv_round(ph, l, stream, g, nt, N, rnd, shifted, lwla, siglg, lnw, lnb, xkeys, None)
        else:
            ytok = ph.T("s2_ytok", [128, D], F32)
            nats = []
            self.pre_nat = [(ph.T("w_vT%d" % r_, [128, 2, N], BF16), ph.T("w_pT%d" % r_, [128, 2, N], BF16),
                             ph.T("w_ggT%d" % r_, [128, 2, N], F32)) for r_ in range(4)]
            phA = Phase(self)
            V6 = phA.T("s2_V6", [128, 6, D], F32)
            for rnd in range(4):
                nats.append(self.rwkv_round(ph, l, stream, g, nt, N, rnd, shifted, lwla, siglg, lnw, lnb, xkeys, V6))
            for n in range(NSEQ_S):
                dstn = self.scr_v[l, n].rearrange("j hh q t d -> q t (j hh) d")
                for q in range(6):
                    S.dma('act', dstn[q], V6[n * 8:(n + 1) * 8, q, :].rearrange("t (jh d) -> t jh d", d=64),
                          reads=[('s2_V6', jx) for jx in range(8)], writes=[('scr_v', l, n, q)])
            phA.names.append('s2_V6')
            phA.close()
            svkeys = [('scr_v', l, n, q) for n in range(NSEQ_S) for q in range(6)]
            for hh in range(2):
                phB = Phase(self)
                Sst = phB.T("s2_S", [128, 64, 64], F32)
                tmp = phB.T("s2_tmp", [128, 64, 64], F32)
                vec = phB.T("s2_vec", [128, 6, T_S, 64], F32)
                yS = phB.T("s2_y", [128, T_S, 64], F32)
                sa = phB.T("s2_sa", [128, 64], F32)
                S.dma('act', Sst[:].rearrange("p v k -> p (v k)"),
                      self.I['swkv'][l].rearrange("n (j hh) v k -> hh (n j) (v k)", hh=2)[hh], writes=['s2_S'])
                S.dma('act', vec[:].rearrange("p q t d -> p (q t d)"),
                      self.scr_v[l].rearrange("n j hh q t d -> hh (n j) (q t d)")[hh], reads=svkeys, writes=['s2_vec'])
                bv = lambda q, t: vec[:, q, t, :].unsqueeze(1).to_broadcast([128, 64, 64])
                bo = lambda ap: ap.unsqueeze(2).to_broadcast([128, 64, 64])
                for t in range(T_S):
                    S.op('dve', lambda e: e.tensor_tensor(out=tmp[:], in0=Sst[:], in1=bv(4, t), op=ALU.mult), reads=['s2_S', 's2_vec'], writes=['s2_tmp'])
                    S.op('dve', lambda e: e.tensor_reduce(out=sa[:], in_=tmp[:], axis=AX.X, op=ALU.add), reads=['s2_tmp'], writes=['s2_sa'])
                    S.op('pool', lambda e: e.tensor_tensor(out=Sst[:], in0=Sst[:], in1=bv(1, t), op=ALU.mult), reads=['s2_S', 's2_vec'], writes=['s2_S'])
                    S.op('dve', lambda e: e.tensor_tensor(out=tmp[:], in0=bo(sa[:]), in1=bv(5, t), op=ALU.mult), reads=['s2_sa', 's2_vec'], writes=['s2_tmp'])
                    S.op('pool', lambda e: e.tensor_tensor(out=Sst[:], in0=Sst[:], in1=tmp[:], op=ALU.subtract), reads=['s2_S', 's2_tmp'], writes=['s2_S'])
                    S.op('dve', lambda e: e.tensor_tensor(out=tmp[:], in0=bo(vec[:, 3, t, :]), in1=bv(2, t), op=ALU.mult), reads=['s2_vec'], writes=['s2_tmp'])
                    S.op('pool', lambda e: e.tensor_tensor(out=Sst[:], in0=Sst[:], in1=tmp[:], op=ALU.add), reads=['s2_S', 's2_tmp'], writes=['s2_S'])
                    S.op('dve', lambda e: e.tensor_tensor(out=tmp[:], in0=Sst[:], in1=bv(0, t), op=ALU.mult), reads=['s2_S', 's2_vec'], writes=['s2_tmp'])
                    S.op('dve', lambda e: e.tensor_reduce(out=yS[:, t, :], in_=tmp[:], axis=AX.X, op=ALU.add), reads=['s2_tmp'], writes=['s2_y'])
                S.dma('act', self.O['swkv_o'][l].rearrange("n (j hh) v k -> hh (n j) (v k)", hh=2)[hh], Sst[:].rearrange("p v k -> p (v k)"),
                      reads=['s2_S'], writes=[('o_swkv', l, hh)])
                S.dma('act', self.scr_y[l].rearrange("n j hh t v -> hh (n j) (t v)")[hh], yS[:].rearrange("p t v -> p (t v)"),
                      reads=['s2_y'], writes=[('scr_y', l, hh)])
                phB.close()
            for n in range(NSEQ_S):
                S.dma('act', ytok[n * 8:(n + 1) * 8, :].rearrange("t (jh v) -> t jh v", v=64),
                      self.scr_y[l, n].rearrange("j hh t v -> t (j hh) v"), reads=[('scr_y', l, 0), ('scr_y', l, 1)], writes=[('s2_ytok', n)])
            for rnd in range(4):
                for _ in self.wkv_tile(ph, l, 2 * rnd, 0, nats[rnd], lnw, lnb,
                                       s2_y=(ytok[:, rnd * 256:(rnd + 1) * 256], [('s2_ytok', n) for n in range(NSEQ_S)])):
                    pass
        self.dump('mixed' if not is_s else 's_mixed', self.mixacc[:, :, 0:N], [128, 8, N], [('mixacc', j) for j in range(8)])
        ph.names.append('s2_ytok')
        ph.close()

    def rwkv_round(self, ph0, l, stream, g, nt, N, rnd, shifted, lwla, siglg, lnw, lnb, xkeys, V6):
        S, I = self.S, self.I
        is_s = stream == 'S'
        ph = Phase(self)
        j0 = 2 * rnd
        sf = str(rnd) if is_s else ''
        pho = ph0 if is_s else ph
        aT = ph.T("w_aT", [128, 2, N], BF16)
        rT = ph.T("w_rT", [128, 2, N], BF16)
        bT = ph.T("w_bT", [128, 2, N], BF16)
        kT = ph.T("w_kT", [128, 2, N], BF16)
        if is_s:
            vT, pT, ggT = self.pre_nat[rnd]
        else:
            vT = ph.T("w_vT", [128, 2, N], BF16)
            pT = ph.T("w_pT", [128, 2, N], BF16)
            ggT = ph.T("w_ggT", [128, 2, N], F32)
        gC = ph.T("w_gC", [128, 2, 4], F32)
        php = Phase(self)
        TK = ('w_r32', 'w_k32', 'w_v32', 'w_lw32', 'w_cl', 'w_a32', 'w_kk32', 'w_t32', 'w_tb', 'w_eI', 'w_eN', 'pch', 'rw_d')
        nseq_, T_ = (NSEQ_S, T_S) if is_s else (1, 512)
        pbufs = []
        for jj_ in range(2):
            d_ = {}
            for nm_ in ('r32', 'k32', 'v32', 'lw32', 'cl', 'a32', 'kk32', 't32', 'eI', 'eN'):
                d_[nm_] = php.T("w_%s_%d" % (nm_, jj_), [128, N], F32)
            d_['tb'] = php.T("w_tb_%d" % jj_, [128, N], BF16)
            d_['pch'] = php.T("pch_%d" % jj_, [128, nseq_, 1 + T_], F32)
            d_['dd'] = php.T("rw_d_%d" % jj_, [128, N], F32)
            pbufs.append(d_)
        php.names += [k_ + '_0' for k_ in TK] + [k_ + '_1' for k_ in TK]
        c0 = OFF_RW + j0 * 128
        wb, wk = self.wtile('w_in', l, [(c0, 256), (c0 + D, 256)])
        wb2, wk2 = self.wtile('w_in', l, [(c0 + 2 * D, 256), (OFF_GATE + 2 * D + j0 * 128, 256)])
        S0 = S

        def pair(jj):
            j = j0 + jj
            B_ = pbufs[jj]
            r32, k32, v32, lw32, cl, a32, kk32, t32, eI, eN, tb = (B_[k_] for k_ in ('r32', 'k32', 'v32', 'lw32', 'cl', 'a32', 'kk32', 't32', 'eI', 'eN', 'tb'))
            km = lambda ks: [((k_ + '_%d' % jj) if (isinstance(k_, str) and k_ in TK) else k_) for k_ in ks]

            class _S:
                @staticmethod
                def op(e, fn_, reads=(), writes=()):
                    return S0.op(e, fn_, reads=km(reads), writes=km(writes))
            S = _S()
            mm0 = self.mm

            def MM(out, lhsT, rhs, start, stop, reads, writes):
                return mm0(out, lhsT, rhs, start, stop, reads=km(reads), writes=km(writes))
            for qi, (dst, dkey, wbx, wkx, coff, blk) in enumerate(((r32, 'w_r32', wb, wk, jj * 128, j),
                                                                  (k32, 'w_k32', wb, wk, 256 + jj * 128, 8 + j),
                                                                  (v32, 'w_v32', wb2, wk2, jj * 128, 16 + j))):
                ps, pk = self.ps()
                for k in range(8):
                    MM(ps[:, 0:N], wbx[:, k, coff:coff + 128], self.xT[:, k, 0:N], k == 0, k == 7, reads=wkx + xkeys, writes=[pk])
                yield from shifted(ps, pk, blk, 128, dst[:], dkey, bufs=(B_['pch'], B_['dd']), S_=S)
            ps, pk = self.ps()
            MM(ps[:, 0:N], self.wlora_l[0:64, j * 128:(j + 1) * 128], lwla[0:64, :], True, True, reads=['wlora', 'lwla'], writes=[pk])
            S.op('act', lambda e: e.activation(out=lw32[:], in_=ps[:, 0:N], func=AF.Sigmoid, bias=self.p_w0[:, l, j:j + 1], scale=1.0),
                 reads=[pk, 'rw_w0'], writes=['w_lw32'])
            yield
            S.op('pool', lambda e: e.tensor_scalar(out=lw32[:], in0=lw32[:], scalar1=-math.exp(-0.5), scalar2=None, op0=ALU.mult),
                 reads=['w_lw32'], writes=['w_lw32'])
            yield
            ps, pk = self.ps()
            MM(ps[:, 0:N], self.wlora_l[64:128, j * 128:(j + 1) * 128], lwla[64:128, :], True, True, reads=['wlora', 'lwla'], writes=[pk])
            S.op('act', lambda e: e.activation(out=a32[:], in_=ps[:, 0:N], func=AF.Sigmoid, bias=self.p_a0[:, l, j:j + 1], scale=1.0),
                 reads=[pk, 'rw_a0'], writes=['w_a32'])
            yield
            S.op('dve', lambda e: e.tensor_scalar(out=kk32[:], in0=k32[:], scalar1=self.p_kk[:, l, j:j + 1], scalar2=None, op0=ALU.mult),
                 reads=['w_k32', 'rw_kk'], writes=['w_kk32'])
            yield
            S.op('act', lambda e: e.activation(out=tb[:], in_=kk32[:], func=AF.Square), reads=['w_kk32'], writes=['w_tb'])
            yield
            ps, pk = self.ps()
            MM(ps[:, 0:N], self.blkb[:, :], tb[:], True, True, reads=['blkb', 'w_tb'], writes=[pk])
            S.op('act', lambda e: e.activation(out=t32[:], in_=ps[:, 0:N], func=AF.Sqrt), reads=[pk], writes=['w_t32'])
            yield
            S.op('dve', lambda e: e.tensor_scalar(out=t32[:], in0=t32[:], scalar1=1e-12, scalar2=None, op0=ALU.max), reads=['w_t32'], writes=['w_t32'])
            yield
            S.op('dve', lambda e: e.reciprocal(out=t32[:], in_=t32[:]), reads=['w_t32'], writes=['w_t32'])
            yield
            S.op('dve', lambda e: e.tensor_tensor(out=kk32[:], in0=kk32[:], in1=t32[:], op=ALU.mult), reads=['w_kk32', 'w_t32'], writes=['w_kk32'])
            yield
            S.op('dve', lambda e: e.tensor_scalar(out=t32[:], in0=a32[:], scalar1=self.p_ka[:, l, j:j + 1], scalar2=self.p_omka[:, l, j:j + 1],
                                                  op0=ALU.mult, op1=ALU.add), reads=['w_a32', 'rw_ka', 'p_omka'], writes=['w_t32'])
            yield
            S.op('pool', lambda e: e.tensor_tensor(out=k32[:], in0=k32[:], in1=t32[:], op=ALU.mult), reads=['w_k32', 'w_t32'], writes=['w_k32'])
            yield
            if is_s:
                S.op('pool', lambda e: e.tensor_tensor(out=t32[:], in0=r32[:], in1=k32[:], op=ALU.mult), reads=['w_r32', 'w_k32'], writes=['w_t32'])
                yield
                S.op('dve', lambda e: e.tensor_scalar(out=pT[:, jj, :], in0=t32[:], scalar1=self.p_rk[:, l, j:j + 1], scalar2=None, op0=ALU.mult),
                     reads=['w_t32', 'rw_rk'], writes=['w_pT' + sf])
                yield
                S.op('act', lambda e: e.copy(out=vT[:, jj, :], in_=v32[:]), reads=['w_v32'], writes=['w_vT' + sf])
                yield
                S.op('act', lambda e: e.activation(out=eI[:], in_=lw32[:], func=AF.Exp), reads=['w_lw32'], writes=['w_eI'])
                yield
                S.op('dve', lambda e: e.tensor_tensor(out=eN[:], in0=kk32[:], in1=a32[:], op=ALU.mult), reads=['w_kk32', 'w_a32'], writes=['w_eN'])
                yield
                psa, pka = self.ps()
                for q, (src, skey) in enumerate(((r32, 'w_r32'), (eI, 'w_eI'), (k32, 'w_k32'), (v32, 'w_v32'))):
                    S.op('pe', lambda e: e.transpose(out=psa[:, q * 128:(q + 1) * 128], in_=src[:, :], identity=self.identf[:]),
                         reads=[skey, 'identf'], writes=[pka])
                    yield
                psb_, pkb = self.ps()
                for q, (src, skey) in enumerate(((kk32, 'w_kk32'), (eN, 'w_eN'))):
                    S.op('pe', lambda e: e.transpose(out=psb_[:, q * 128:(q + 1) * 128], in_=src[:, :], identity=self.identf[:]),
                         reads=[skey, 'identf'], writes=[pkb])
                    yield
                S.op('act', lambda e: e.copy(out=V6[:, 0:4, j * 128:(j + 1) * 128], in_=psa[:, :].rearrange("p (q c) -> p q c", q=4)),
                     reads=[pka], writes=[('s2_V6', j)])
                yield
                S.op('dve', lambda e: e.tensor_copy(out=V6[:, 4:6, j * 128:(j + 1) * 128], in_=psb_[:, 0:256].rearrange("p (q c) -> p q c", q=2)),
                     reads=[pkb], writes=[('s2_V6', j)])
                yield
            if not is_s:
                S.op('dve', lambda e: e.tensor_tensor_scan(out=cl[:], data0=self.rmaskP[:, 0:N], data1=lw32[:], initial=0.0, op0=ALU.mult, op1=ALU.add),
                     reads=['rmaskP', 'w_lw32'], writes=['w_cl'])
                yield
                S.op('act', lambda e: e.activation(out=eI[:], in_=cl[:], func=AF.Exp), reads=['w_cl'], writes=['w_eI'])
                yield
                S.op('act', lambda e: e.activation(out=eN[:], in_=cl[:], func=AF.Exp, scale=-1.0), reads=['w_cl'], writes=['w_eN'])
                yield
                S.op('pool', lambda e: e.tensor_copy(out=gC[:, jj, :], in_=eI[:, 127:N:128]), reads=['w_eI'], writes=['w_gC'])
                yield
                S.op('dve', lambda e: e.tensor_tensor(out=rT[:, jj, :], in0=r32[:], in1=eI[:], op=ALU.mult), reads=['w_r32', 'w_eI'], writes=['w_rT'])
                yield
                S.op('pool', lambda e: e.tensor_tensor(out=t32[:], in0=r32[:], in1=k32[:], op=ALU.mult), reads=['w_r32', 'w_k32'], writes=['w_t32'])
                yield
                S.op('dve', lambda e: e.tensor_scalar(out=pT[:, jj, :], in0=t32[:], scalar1=self.p_rk[:, l, j:j + 1], scalar2=None, op0=ALU.mult),
                     reads=['w_t32', 'rw_rk'], writes=['w_pT' + sf])
                yield
                S.op('dve', lambda e: e.tensor_tensor(out=kT[:, jj, :], in0=k32[:], in1=eN[:], op=ALU.mult), reads=['w_k32', 'w_eN'], writes=['w_kT'])
                yield
                S.op('pool', lambda e: e.tensor_tensor(out=t32[:], in0=kk32[:], in1=a32[:], op=ALU.mult), reads=['w_kk32', 'w_a32'], writes=['w_t32'])
                yield
                S.op('dve', lambda e: e.tensor_tensor(out=bT[:, jj, :], in0=t32[:], in1=eN[:], op=ALU.mult), reads=['w_t32', 'w_eN'], writes=['w_bT'])
                yield
                S.op('pool', lambda e: e.tensor_tensor(out=t32[:], in0=cl[:], in1=lw32[:], op=ALU.subtract), reads=['w_cl', 'w_lw32'], writes=['w_t32'])
                yield
                S.op('act', lambda e: e.activation(out=t32[:], in_=t32[:], func=AF.Exp), reads=['w_t32'], writes=['w_t32'])
                yield
                S.op('dve', lambda e: e.scalar_tensor_tensor(out=aT[:, jj, :], in0=kk32[:], scalar=-1.0, in1=t32[:], op0=ALU.mult, op1=ALU.mult),
                     reads=['w_kk32', 'w_t32'], writes=['w_aT'])
                yield
                S.op('act', lambda e: e.copy(out=vT[:, jj, :], in_=v32[:]), reads=['w_v32'], writes=['w_vT' + sf])
                yield
            ps, pk = self.ps()
            MM(ps[:, 0:N], self.wgup_l[:, 0, j * 128:(j + 1) * 128], siglg[:, 0, :], True, False, reads=['wgup', 'siglg'], writes=[pk])
            MM(ps[:, 0:N], self.wgup_l[0:32, 1, j * 128:(j + 1) * 128], siglg[0:32, 1, :], False, True, reads=['wgup', 'siglg'], writes=[pk])
            ps2, pk2 = self.ps()
            for k in range(8):
                MM(ps2[:, 0:N], wb2[:, k, 256 + jj * 128:256 + (jj + 1) * 128], self.xT[:, k, 0:N], k == 0, k == 7, reads=wk2 + xkeys, writes=[pk2])
            S.op('act', lambda e: e.activation(out=t32[:], in_=ps2[:, 0:N], func=AF.Sigmoid), reads=[pk2], writes=['w_t32'])
            yield
            S.op('dve', lambda e: e.tensor_tensor(out=ggT[:, jj, :], in0=ps[:, 0:N], in1=t32[:], op=ALU.mult), reads=[pk, 'w_t32'], writes=['w_ggT' + sf])
            yield

        gens_ = [pair(0), pair(1)]
        if is_s:
            for g_ in gens_:
                for _ in g_:
                    pass
        else:
            self.interleave(gens_)
        php.close()
        nat = dict(aT=aT, rT=rT, bT=bT, kT=kT, vT=vT, pT=pT, ggT=ggT, gC=gC, sf=sf)
        if not is_s:
            for t0_ in range(0, nt, 2):
                self.interleave_staged([self.wkv_tile(ph, l, j0, t0_ + i_, nat, lnw, lnb, slot=i_) for i_ in range(2)])
        ph.close()
        return nat

    def wkv_tile(self, ph0, l, j0, t, nat, lnw, lnb, s2_y=None, slot=0):
        S0 = self.S
        sl = '_s%d' % slot
        km = lambda ks: [((k + sl) if isinstance(k, str) and k.startswith('k_') else k) for k in ks]

        class _S:
            @staticmethod
            def op(e, fn_, reads=(), writes=()):
                return S0.op(e, fn_, reads=km(reads), writes=km(writes))
        S = _S()
        mm0 = self.mm

        def MM(out, lhsT, rhs, start, stop, reads, writes):
            return mm0(out, lhsT, rhs, start, stop, reads=km(reads), writes=km(writes))

        class _Reuse:
            def T(_s, name, shape, dt):
                name = name + sl
                cache = ph0.__dict__.setdefault('_tile_cache', {})
                if name not in cache:
                    cache[name] = ph0.T(name, shape, dt)
                return cache[name]

            def close(_s):
                pass
        ph = _Reuse()
        aT, rT, bT, kT, vT, pT, ggT, gC = (nat[k] for k in ('aT', 'rT', 'bT', 'kT', 'vT', 'pT', 'ggT', 'gC'))
        sf = nat['sf']
        cs = slice(t * 128, (t + 1) * 128)
        tok = ph.T("k_tok", [128, 4, 2, 128], BF16)
        ps, pk = self.ps()
        psb = ps[:].bitcast(BF16)
        for qi, (src, skey) in enumerate(((vT, 'w_vT' + sf), (bT, 'w_bT'), (kT, 'w_kT'), (pT, 'w_pT' + sf))):
            if s2_y is not None and qi in (1, 2):
                src, skey = vT, 'w_vT' + sf
            for jj in range(2):
                S.op('pe', lambda e: e.transpose(out=psb[:, (qi * 2 + jj) * 128:(qi * 2 + jj + 1) * 128], in_=src[:, jj, cs],
                                                 identity=self.identb[:]), reads=[skey, 'identb'], writes=[pk])
                yield
        S.op('act', lambda e: e.copy(out=tok[:].rearrange("p q j c -> p (q j c)"), in_=psb), reads=[pk], writes=['k_tok'])
        yield
        if s2_y is None:
            blks = {}
            for nm, (src, skey) in (('a', (aT, 'w_aT')), ('r', (rT, 'w_rT')), ('b', (bT, 'w_bT'))):
                isnew = ("k_blk" + nm + sl) not in ph0.__dict__.get('_tile_cache', {})
                bl = ph.T("k_blk" + nm, [128, 2, 2, 128], BF16)
                if isnew:
                    S.op('pool', lambda e: e.memset(bl[:], 0.0), writes=['k_blk' + nm])
                    yield
                for hh in range(2):
                    eng = 'act' if hh else 'pool'
                    if eng == 'act':
                        S.op('act', lambda e: e.copy(out=bl[hh * 64:(hh + 1) * 64, :, hh, :], in_=src[hh * 64:(hh + 1) * 64, :, cs]),
                             reads=[skey, 'k_blk' + nm], writes=['k_blk' + nm])
                        yield
                    else:
                        S.op('pool', lambda e: e.tensor_copy(out=bl[hh * 64:(hh + 1) * 64, :, hh, :], in_=src[hh * 64:(hh + 1) * 64, :, cs]),
                             reads=[skey, 'k_blk' + nm], writes=['k_blk' + nm])
                        yield
                blks[nm] = bl
            mats = {}
            specs = (('AabT', bT, 'w_bT', 'a', 0), ('ArbT', bT, 'w_bT', 'r', 1), ('AakT', kT, 'w_kT', 'a', 0),
                     ('ArkT', kT, 'w_kT', 'r', 1), ('Aab', aT, 'w_aT', 'b', 2))
            for mi, (nm, lsrc, lkey, rb, mk) in enumerate(specs):
                ps, pk = self.ps()
                for jj in range(2):
                    MM(ps[:, jj * 256:(jj + 1) * 256], lsrc[:, jj, cs], blks[rb][:, jj, :, :].rearrange("p h c -> p (h c)"),
                            True, True, reads=[lkey, 'k_blk' + rb], writes=[pk])
                m = ph.T("k_" + nm, [128, 4, 128], BF16)
                S.op('dve', lambda e: e.tensor_tensor(out=m[:].rearrange("p h c -> p (h c)"), in0=ps[:, :], in1=self.maskb[:, mk, :], op=ALU.mult),
                     reads=[pk, 'maskb'], writes=['k_' + nm])
                yield
                mats[nm] = m
            Pm = [ph.T("k_P%d" % i, [128, 4, 128], BF16) for i in range(2)]
            Am = [ph.T("k_A%d" % i, [128, 4, 128], BF16) for i in range(2)]
            Bm = [ph.T("k_B%d" % i, [128, 4, 128], BF16) for i in range(2)]
            fl = lambda m_: m_[:].rearrange("p h c -> p (h c)")
            S.op('pool', lambda e: e.tensor_tensor(out=fl(Pm[0]), in0=fl(mats['AabT']), in1=self.maskb[:, 3, :], op=ALU.add),
                 reads=['k_AabT', 'maskb'], writes=['k_P0'])
            yield
            Acur, Akey, Bcur, Bkey = mats['Aab'], 'k_Aab', mats['AabT'], 'k_AabT'
            pcur = 0
            pend = None

            def p_update(Ap, Apk, pcur_):
                ps_, pk_ = self.ps()
                for h in range(4):
                    MM(ps_[:, h * 128:(h + 1) * 128], Ap[:, h, :], Pm[pcur_][:, h, :], True, True, reads=[Apk, 'k_P%d' % pcur_], writes=[pk_])
                S.op('dve', lambda e: e.tensor_tensor(out=fl(Pm[1 - pcur_]), in0=ps_[:, :], in1=fl(Pm[pcur_]), op=ALU.add),
                     reads=[pk_, 'k_P%d' % pcur_], writes=['k_P%d' % (1 - pcur_)])
                return 1 - pcur_
            Am3 = Am + [ph.T("k_A2", [128, 4, 128], BF16)]
            for step in range(6):
                An, Ank = Am3[step % 3], 'k_A%d' % (step % 3)
                psA, pkA = self.ps()
                for h in range(4):
                    MM(psA[:, h * 128:(h + 1) * 128], Bcur[:, h, :], Acur[:, h, :], True, True, reads=[Akey, Bkey], writes=[pkA])
                if step < 5:
                    Bn, Bnk = Bm[step % 2], 'k_B%d' % (step % 2)
                    psB, pkB = self.ps()
                    for h in range(4):
                        MM(psB[:, h * 128:(h + 1) * 128], Acur[:, h, :], Bcur[:, h, :], True, True, reads=[Akey, Bkey], writes=[pkB])
                S.op('act', lambda e: e.copy(out=fl(An), in_=psA[:, :]), reads=[pkA], writes=[Ank])
                yield
                if step < 5:
                    S.op('act', lambda e: e.copy(out=fl(Bn), in_=psB[:, :]), reads=[pkB], writes=[Bnk])
                    yield
                if pend is not None:
                    pcur = p_update(pend[0], pend[1], pcur)
                pend = (An, Ank)
                Acur, Akey = An, Ank
                if step < 5:
                    Bcur, Bkey = Bn, Bnk
            pcur = p_update(pend[0], pend[1], pcur)
            Pf, Pkey = Pm[pcur], 'k_P%d' % pcur
            yield 'CHAIN'
            Hb = self.Hbf
            Xs = ph.T("k_Xs", [128, 4, 64], BF16)
            Us = ph.T("k_Us", [128, 4, 64], BF16)
            psX, pkX = self.ps()
            for jj in range(2):
                j = j0 + jj
                MM(psX[:, jj * 128:(jj + 1) * 128], aT[:, jj, cs], Hb[:, l, j, :], True, False, reads=['w_aT', ('Hbf', l, j)], writes=[pkX])
                for hh in range(2):
                    h = jj * 2 + hh
                    MM(psX[:, h * 64:(h + 1) * 64], mats['AakT'][:, h, :], tok[:, 0, jj, hh * 64:(hh + 1) * 64], False, hh == 1,
                            reads=['k_AakT', 'k_tok'], writes=[pkX])
            S.op('act', lambda e: e.copy(out=Xs[:].rearrange("p h c -> p (h c)"), in_=psX[:, 0:256]), reads=[pkX], writes=['k_Xs'])
            yield
            psU, pkU = self.ps()
            for h in range(4):
                MM(psU[:, h * 64:(h + 1) * 64], Pf[:, h, :], Xs[:, h, :], True, True, reads=[Pkey, 'k_Xs'], writes=[pkU])
            S.op('act', lambda e: e.copy(out=Us[:].rearrange("p h c -> p (h c)"), in_=psU[:, 0:256]), reads=[pkU], writes=['k_Us'])
            yield
            psY, pkY = self.ps()
            for jj in range(2):
                j = j0 + jj
                MM(psY[:, jj * 128:(jj + 1) * 128], rT[:, jj, cs], Hb[:, l, j, :], True, False, reads=['w_rT', ('Hbf', l, j)], writes=[pkY])
                for hh in range(2):
                    h = jj * 2 + hh
                    MM(psY[:, h * 64:(h + 1) * 64], mats['ArbT'][:, h, :], Us[:, h, :], False, False, reads=['k_ArbT', 'k_Us'], writes=[pkY])
                    MM(psY[:, h * 64:(h + 1) * 64], mats['ArkT'][:, h, :], tok[:, 0, jj, hh * 64:(hh + 1) * 64], False, hh == 1,
                            reads=['k_ArkT', 'k_tok'], writes=[pkY])
            psH, pkH = self.ps()
            for jj in range(2):
                MM(psH[:, jj * 128:(jj + 1) * 128], tok[:, 1, jj, :], Us[:, 2 * jj:2 * jj + 2, :].rearrange("p h c -> p (h c)"), True, False,
                        reads=['k_tok', 'k_Us'], writes=[pkH])
                MM(psH[:, jj * 128:(jj + 1) * 128], tok[:, 2, jj, :], tok[:, 0, jj, :], False, True, reads=['k_tok'], writes=[pkH])
            hd = ph.T("k_hd", [128, 256], F32)
            S.op('dve', lambda e: e.tensor_tensor(out=hd[:], in0=psH[:, 0:256], in1=self.blkf[:], op=ALU.mult), reads=[pkH, 'blkf'], writes=['k_hd'])
            yield
            for jj in range(2):
                j = j0 + jj
                S.op('pool', lambda e: e.tensor_tensor(out=hd[:, jj * 128:(jj + 1) * 128], in0=hd[:, jj * 128:(jj + 1) * 128], in1=self.H32[:, l, j, :], op=ALU.add),
                     reads=['k_hd', ('H32', l, j)], writes=['k_hd'])
                yield
                S.op('dve', lambda e: e.tensor_scalar(out=self.H32[:, l, j, :], in0=hd[:, jj * 128:(jj + 1) * 128], scalar1=gC[:, jj, t:t + 1], scalar2=None, op0=ALU.mult),
                     reads=['k_hd', 'w_gC'], writes=[('H32', l, j)])
                yield
                S.op('act', lambda e: e.copy(out=Hb[:, l, j, :], in_=self.H32[:, l, j, :]), reads=[('H32', l, j)], writes=[('Hbf', l, j)])
                yield
        yield 'POST'
        y = ph.T("k_y", [128, 4, 64], F32)
        sq = ph.T("k_sq", [128, 4, 64], F32)
        st = ph.T("k_st", [128, 4], F32)
        st2 = ph.T("k_st2", [128, 4], F32)
        if s2_y is None:
            S.op('act', lambda e: e.copy(out=y[:].rearrange("p h c -> p (h c)"), in_=psY[:, 0:256]), reads=[pkY], writes=['k_y'])
            yield
        else:
            S.op('act', lambda e: e.copy(out=y[:].rearrange("p h c -> p (h c)"), in_=s2_y[0]), reads=s2_y[1], writes=['k_y'])
            yield
        S.op('dve', lambda e: e.tensor_reduce(out=st[:], in_=y[:], axis=AX.X, op=ALU.add), reads=['k_y'], writes=['k_st'])
        yield
        S.op('dve', lambda e: e.tensor_scalar(out=st[:], in0=st[:], scalar1=-1.0 / 64, scalar2=None, op0=ALU.mult), reads=['k_st'], writes=['k_st'])
        yield
        S.op('dve', lambda e: e.tensor_tensor(out=y[:], in0=y[:], in1=st[:].unsqueeze(2).to_broadcast([128, 4, 64]), op=ALU.add),
             reads=['k_y', 'k_st'], writes=['k_y'])
        yield
        S.op('pool', lambda e: e.tensor_tensor(out=sq[:], in0=y[:], in1=y[:], op=ALU.mult), reads=['k_y'], writes=['k_sq'])
        yield
        S.op('dve', lambda e: e.tensor_reduce(out=st2[:], in_=sq[:], axis=AX.X, op=ALU.add), reads=['k_sq'], writes=['k_st2'])
        yield
        S.op('act', lambda e: e.activation(out=st2[:], in_=st2[:], func=AF.Sqrt, bias=self.epsln[:, 1:2], scale=1.0 / 64),
             reads=['k_st2', 'epsln'], writes=['k_st2'])
        yield
        S.op('dve', lambda e: e.reciprocal(out=st2[:], in_=st2[:]), reads=['k_st2'], writes=['k_st2'])
        yield
        S.op('dve', lambda e: e.tensor_tensor(out=y[:], in0=y[:], in1=st2[:].unsqueeze(2).to_broadcast([128, 4, 64]), op=ALU.mult),
             reads=['k_y', 'k_st2'], writes=['k_y'])
        yield
        yf = y[:].rearrange("p h c -> p (h c)")
        fsl = slice(j0 * 128, j0 * 128 + 256)
        S.op('pool', lambda e: e.tensor_tensor(out=yf, in0=yf, in1=lnw[:, fsl], op=ALU.mult), reads=['k_y', 'lnw_t'], writes=['k_y'])
        yield
        S.op('pool', lambda e: e.tensor_tensor(out=yf, in0=yf, in1=lnb[:, fsl], op=ALU.add), reads=['k_y', 'lnb_t'], writes=['k_y'])
        yield
        S.op('dve', lambda e: e.tensor_reduce(out=st[:], in_=tok[:, 3, :, :].rearrange("p j (h c) -> p (j h) c", h=2), axis=AX.X, op=ALU.add),
             reads=['k_tok'], writes=['k_st'])
        yield
        S.op('dve', lambda e: e.tensor_tensor(out=sq[:], in0=tok[:, 0, :, :].rearrange("p j (h c) -> p (j h) c", h=2),
                                              in1=st[:].unsqueeze(2).to_broadcast([128, 4, 64]), op=ALU.mult), reads=['k_tok', 'k_st'], writes=['k_sq'])
        yield
        S.op('pool', lambda e: e.tensor_tensor(out=y[:], in0=y[:], in1=sq[:], op=ALU.add), reads=['k_y', 'k_sq'], writes=['k_y'])
        yield
        psT, pkT = self.ps()
        for jj in range(2):
            S.op('pe', lambda e: e.transpose(out=psT[:, jj * 128:(jj + 1) * 128], in_=yf[:, jj * 128:(jj + 1) * 128], identity=self.identf[:]),
                 reads=['k_y', 'identf'], writes=[pkT])
            yield
        og = ph.T("k_og", [128, 2, 128], F32)
        S.op('dve', lambda e: e.tensor_tensor(out=og[:], in0=psT[:, 0:256].rearrange("p (j c) -> p j c", j=2), in1=ggT[:, :, cs], op=ALU.mult),
             reads=[pkT, 'w_ggT' + sf], writes=['k_og'])
        yield
        S.op('pool', lambda e: e.tensor_tensor(out=self.mixacc[:, j0:j0 + 2, cs], in0=self.mixacc[:, j0:j0 + 2, cs], in1=og[:], op=ALU.add),
             reads=['k_og', ('mixacc', j0), ('mixacc', j0 + 1)], writes=[('mixacc', j0), ('mixacc', j0 + 1)])
        yield
        ph.close()

    def attention(self, l, stream, g, ngroups, nt, N):
        S, I = self.S, self.I
        ph = Phase(self)
        qT = ph.T("qT", [128, 8, N], BF16)
        ET = [ph.T("ET%d" % b, [128, 512], BF16) for b in range(2)]
        sb = ph.T("sbt", [128, 512], F32)
        den = ph.T("den", [128, 256], F32)
        osb = ph.T("osb", [128, 256], F32)
        if stream == 'P':
            self.biasP = ph.T("biasP", [128, 2, NH, 128], F32)
            S.dma('act', self.biasP[:], I['c_biasP'], writes=['biasP'])
        kv32_pre = ph.T("kv32", [128, 512], F32)
        if stream == 'P':
            kT = self.kT_p[l]
            vtok = self.vtok_p[l]
            kkey, vkey = ('kTp', l), ('vtokp', l)
        else:
            kT = ph.T("kTs", [128, KVH, 128 + 128], BF16)
            vtok = ph.T("vtoks", [128, 2, 256], BF16)
            kkey, vkey = 'kTs', 'vtoks'
        xkeys = [('xT', t) for t in range(nt)]
        for half in range(2):
            wb, wk = self.wtile('w_in', l, [(half * 512, 512)])
            for b in range(4):
                ps, pk = self.ps()
                for k in range(8):
                    self.mm(ps[:, 0:N], wb[:, k, b * 128:(b + 1) * 128], self.xT[:, k, 0:N], k == 0, k == 7,
                            reads=wk + xkeys, writes=[pk])
                self.evac_copy('act' if b % 2 else 'dve', qT[:, half * 4 + b, :], ps[:, 0:N], [pk], ['qT'], scale=0.125)
        self.chk('q')
        i = self.wn
        self.wn = (self.wn + 1) % NWBUF
        wb = self.wbuf[i]
        wk = [('wbuf', i)]
        srcw = self.W['w_in'][l]
        kv = srcw[:, OFF_K:OFF_K + 256].rearrange("(k p) (c d) -> p k c d", p=128, d=64)
        wbv = wb[:, :, :].rearrange("p k (c two d) -> p k c two d", two=2, d=64)
        for dup in range(2):
            for k in range(8):
                S.dma('sp', wbv[:, k, :, dup, :], kv[:, k, :, :],
                      writes=wk if (dup == 0 and k == 0) else [('wbufx', i, dup, k)])
        wk_all = wk + [('wbufx', i, dup, k) for dup in range(2) for k in range(8) if not (dup == 0 and k == 0)]
        for c in range(KVH):
            ps, pk = self.ps()
            for k in range(8):
                self.mm(ps[:, 0:N], wb[:, k, c * 128:(c + 1) * 128], self.xT[:, k, 0:N], k == 0, k == 7,
                        reads=wk_all + xkeys, writes=[pk])
            self.evac_copy('act' if c % 2 else 'dve', kT[:, c, 128:128 + N], ps[:, 0:N], [pk], [kkey])
        self.chk('kdup')
        wb, wk = self.wtile('w_in', l, [(OFF_K, 512)])
        last_tile_needs_k = (stream == 'S') or (g == ngroups - 1)
        kv32 = None
        import os
        kvvar = os.environ.get('KVVAR', '')
        for t in range(nt):
            if kvvar == 'dmaonly':
                break
            ps, pk = self.ps()
            for k in range(8):
                self.mm(ps[:, :], self.xT[:, k, t * 128:(t + 1) * 128], wb[:, k, :], k == 0, k == 7,
                        reads=wk + [('xT', t)], writes=[pk])
            if kvvar == 'mmonly':
                continue
            if kvvar != 'noact' and not (kvvar == 'noact3' and t == 3):
                self.evac_copy('act', vtok[:, 1 + t, :], ps[:, 256:512], [pk], [vkey])
            if last_tile_needs_k and t == nt - 1 and kvvar != 'nokv32':
                kv32 = kv32_pre
                self.evac_copy('act', kv32[:], ps[:, :], [pk], ['kv32'])
        import os
        if kv32 is not None and stream == 'P' and not os.environ.get('NOKVOUT'):
            S.dma('act', self.O['pk'][l], kv32[:, 0:256], reads=['kv32'], writes=[('o_pk', l)])
            S.dma('act', self.O['pv'][l], kv32[:, 256:512], reads=['kv32'], writes=[('o_pv', l)])
        self.chk('kvtok')
        for half in range(2):
            wb, wk = self.wtile('w_in', l, [(OFF_GATE + half * 512, 512)])
            for b in range(4):
                ps, pk = self.ps()
                for k in range(8):
                    self.mm(ps[:, 0:N], wb[:, k, b * 128:(b + 1) * 128], self.xT[:, k, 0:N], k == 0, k == 7,
                            reads=wk + xkeys, writes=[pk])
                j = half * 4 + b
                S.op('act', lambda e: e.activation(out=self.mixacc[:, j, 0:N], in_=ps[:, 0:N], func=AF.Sigmoid),
                     reads=[pk], writes=[('mixacc', j)])
        self.chk('proj')
        if stream == 'P':
            sbc = [ph.T("sbt_c%d" % c, [128, 512], F32) for c in range(KVH)]
            ETcp = [[ph.T("ET_c%d_%d" % (c, p_), [128, 512], BF16) for p_ in range(2)] for c in range(KVH)]
            denc = [ph.T("den_c%d" % c, [128, 256], F32) for c in range(KVH)]
            osbc = [ph.T("osb_c%d" % c, [128, 256], F32) for c in range(KVH)]
            ph.names += ['sbt_c', 'ET_c', 'den_c', 'osb_c']

            def attn_c(t, c, blks):
                nb = len(blks)
                sb_, den_, osb_ = sbc[c], denc[c], osbc[c]
                for par in range(2):
                    ps, pk = self.ps()
                    hp = par * 64
                    for bi, blk in enumerate(blks):
                        koff = 128 * t if blk == 0 else 128 * (t + 1)
                        for i2 in range(2):
                            h = 4 * c + 2 * i2 + par
                            col = (bi * 2 + i2) * 128
                            self.mm(ps[:, col:col + 128], kT[hp:hp + 64, c, koff:koff + 128],
                                    qT[hp:hp + 64, h // 2, t * 128:(t + 1) * 128], True, True,
                                    reads=[kkey, 'qT'], writes=[pk])
                    yield
                    W_ = nb * 256
                    bias_ap = self.biasP[:, blks[0]:blks[0] + nb, 4 * c + par:4 * c + 4:2, :]
                    S.op('dve', lambda e: e.tensor_tensor(out=sb_[:, 0:W_].rearrange("p (b i q) -> p b i q", b=nb, i=2),
                                                          in0=ps[:, 0:W_].rearrange("p (b i q) -> p b i q", b=nb, i=2),
                                                          in1=bias_ap, op=ALU.add),
                         reads=[pk, 'biasP'], writes=[('sbt_c', c)])
                    yield
                    S.op('act', lambda e: e.activation(out=ETcp[c][par][:, 0:W_], in_=sb_[:, 0:W_], func=AF.Exp),
                         reads=[('sbt_c', c)], writes=[('ET_c', c, par)])
                    yield
                ps, pk = self.ps()
                for i4 in range(4):
                    par = i4 % 2
                    hp = par * 64
                    i2 = i4 // 2
                    jj = i4 // 2
                    for part in range(2):
                        for bi, blk in enumerate(blks):
                            slot = t if blk == 0 else t + 1
                            lhsT = vtok[:, slot, c * 64:(c + 1) * 64] if part == 0 else self.onesb[:, :]
                            col = (bi * 2 + i2) * 128
                            self.mm(ps[hp:hp + 64, part * 256 + jj * 128: part * 256 + (jj + 1) * 128], lhsT,
                                    ETcp[c][par][:, col:col + 128], bi == 0, bi == nb - 1,
                                    reads=[vkey, 'onesb', ('ET_c', c, par)], writes=[pk])
                yield
                for jj in range(2):
                    j = 2 * c + jj
                    S.op('dve', lambda e: e.tensor_scalar(out=den_[:, jj * 128:(jj + 1) * 128],
                                                          in0=ps[:, 256 + jj * 128:256 + (jj + 1) * 128],
                                                          scalar1=self.esink[:, l, j:j + 1], scalar2=None, op0=ALU.add),
                         reads=[pk, 'esink'], writes=[('den_c', c)])
                yield
                S.op('dve', lambda e: e.reciprocal(out=den_[:], in_=den_[:]), reads=[('den_c', c)], writes=[('den_c', c)])
                yield
                S.op('dve', lambda e: e.tensor_tensor(out=osb_[:], in0=ps[:, 0:256], in1=den_[:], op=ALU.mult),
                     reads=[pk, ('den_c', c)], writes=[('osb_c', c)])
                yield
                S.op('pool', lambda e: e.tensor_tensor(out=self.mixacc[:, 2 * c:2 * c + 2, t * 128:(t + 1) * 128],
                                                       in0=self.mixacc[:, 2 * c:2 * c + 2, t * 128:(t + 1) * 128],
                                                       in1=osb_[:].rearrange("p (j q) -> p j q", j=2), op=ALU.mult),
                     reads=[('osb_c', c), ('mixacc', 2 * c), ('mixacc', 2 * c + 1)],
                     writes=[('mixacc', 2 * c), ('mixacc', 2 * c + 1)])
                yield

            for t in range(nt):
                first = (g == 0 and t == 0)
                blks = [1] if first else [0, 1]
                self.interleave([attn_c(t, c, blks) for c in range(KVH)])
            S.op('pool', lambda e: e.tensor_copy(out=kT[:, :, 0:128], in_=kT[:, :, N:N + 128]), reads=[kkey], writes=[kkey])
            S.op('pool', lambda e: e.tensor_copy(out=vtok[:, 0, :], in_=vtok[:, nt, :]), reads=[vkey], writes=[vkey])
        else:
            bSn = ph.T("bSn", [128, NH, 128], F32)
            bSc = ph.T("bSc", [128, KVH, 2, 256], F32)
            S.dma('act', bSn[:], I['c_biasSn'], writes=['bSn'])
            S.dma('act', bSc[:], I['c_biasSc'], writes=['bSc'])
            ck32 = ph.T("ck32", [128, NSEQ_S, 256], F32)
            ckd = ph.T("ckd", [128, NSEQ_S, KVH, 2, 64], BF16)
            KcT = ph.T("KcT", [128, NSEQ_S, KVH, 128], BF16)
            Vc = ph.T("Vc", [128, NSEQ_S, 256], BF16)
            ETc = [ph.T("ETc%d" % b, [128, 256], BF16) for b in range(2)]
            t1 = ph.T("sa_t1", [128, 512], F32)
            S.dma('act', ck32[:], I['ck'][l].rearrange("n r c -> r n c"), writes=['ck32'])
            S.dma('act', self.O['sk'][l].rearrange("n r c -> r n c")[0:120], ck32[8:128, :, :], reads=['ck32'], writes=[('o_sk', l)])
            for dup in range(2):
                S.op('dve' if dup else 'pool', lambda e: e.tensor_copy(out=ckd[:, :, :, dup, :], in_=ck32[:].rearrange("p n (c d) -> p n c d", d=64)),
                     reads=['ck32'], writes=['ckd'])
            S.dma('act', ck32[:], I['cv'][l].rearrange("n r c -> r n c"), writes=['ck32'])
            S.dma('act', self.O['sv'][l].rearrange("n r c -> r n c")[0:120], ck32[8:128, :, :], reads=['ck32'], writes=[('o_sv', l)])
            S.op('act', lambda e: e.copy(out=Vc[:], in_=ck32[:]), reads=['ck32'], writes=['Vc'])
            for n0 in range(0, NSEQ_S, 2):
                ps, pk = self.ps()
                psb = ps[:].bitcast(BF16)
                for ni in range(2):
                    for c in range(KVH):
                        S.op('pe', lambda e: e.transpose(out=psb[:, (ni * 4 + c) * 128:(ni * 4 + c + 1) * 128],
                                                         in_=ckd[:, n0 + ni, c, :, :].rearrange("p a d -> p (a d)"), identity=self.identb[:]),
                             reads=['ckd', 'identb'], writes=[pk])
                S.op('act' if (n0 // 2) % 2 else 'dve',
                     (lambda e: e.copy(out=KcT[:, n0:n0 + 2, :, :].rearrange("p n c k -> p (n c k)"), in_=psb)) if (n0 // 2) % 2 else
                     (lambda e: e.tensor_copy(out=KcT[:, n0:n0 + 2, :, :].rearrange("p n c k -> p (n c k)"), in_=psb)),
                     reads=[pk], writes=['KcT'])
            for n in range(NSEQ_S):
                S.dma('act', self.O['sk'][l, n, 120:128, :], kv32[n * 8:(n + 1) * 8, 0:256], reads=['kv32'], writes=[('o_skn', l, n)])
                S.dma('act', self.O['sv'][l, n, 120:128, :], kv32[n * 8:(n + 1) * 8, 256:512], reads=['kv32'], writes=[('o_svn', l, n)])
            for c in range(KVH):
                for par in range(2):
                    hp = par * 64
                    ps, pk = self.ps()
                    for i2 in range(2):
                        h = 4 * c + 2 * i2 + par
                        self.mm(ps[:, i2 * 128:(i2 + 1) * 128], kT[hp:hp + 64, c, 128:256], qT[hp:hp + 64, h // 2, 0:128], True, True,
                                reads=[kkey, 'qT'], writes=[pk])
                    S.op('dve', lambda e: e.tensor_tensor(out=sb[:, 0:256].rearrange("p (i q) -> p i q", i=2),
                                                          in0=ps[:, 0:256].rearrange("p (i q) -> p i q", i=2),
                                                          in1=bSn[:, 4 * c + par:4 * c + 4:2, :], op=ALU.add), reads=[pk, 'bSn'], writes=['sbt'])
                    S.op('act', lambda e: e.activation(out=ET[par][:, 0:256], in_=sb[:, 0:256], func=AF.Exp), reads=['sbt'], writes=[('ET', par)])
                    ps, pk = self.ps()
                    for n in range(NSEQ_S):
                        for i2 in range(2):
                            h = 4 * c + 2 * i2 + par
                            col = (n * 2 + i2) * 8
                            self.mm(ps[:, col:col + 8], KcT[hp:hp + 64, n, c, :], qT[hp:hp + 64, h // 2, n * 8:(n + 1) * 8], True, True,
                                    reads=['KcT', 'qT'], writes=[pk])
                    S.op('dve', lambda e: e.tensor_tensor(out=sb[:, 256:512], in0=ps[:, 0:256], in1=bSc[:, c, par, :], op=ALU.add),
                         reads=[pk, 'bSc'], writes=['sbt'])
                    S.op('act', lambda e: e.activation(out=ETc[par][:], in_=sb[:, 256:512], func=AF.Exp), reads=['sbt'], writes=[('ETc', par)])
                ps1, pk1 = self.ps()
                ps2, pk2 = self.ps()
                for i4 in range(4):
                    par, i2 = i4 % 2, i4 // 2
                    hp = par * 64
                    for part in range(2):
                        lhs_n = vtok[:, 1, c * 64:(c + 1) * 64] if part == 0 else self.onesb[:, :]
                        self.mm(ps1[hp:hp + 64, part * 256 + i2 * 128: part * 256 + (i2 + 1) * 128], lhs_n,
                                ET[par][:, i2 * 128:(i2 + 1) * 128], True, True, reads=[vkey, 'onesb', ('ET', par)], writes=[pk1])
                        for n in range(NSEQ_S):
                            lhs_c = Vc[:, n, c * 64:(c + 1) * 64] if part == 0 else self.onesb[:, :]
                            col = part * 256 + i2 * 128 + n * 8
                            self.mm(ps2[hp:hp + 64, col:col + 8], lhs_c, ETc[par][:, (n * 2 + i2) * 8:(n * 2 + i2) * 8 + 8], True, True,
                                    reads=['Vc', 'onesb', ('ETc', par)], writes=[pk2])
                S.op('act', lambda e: e.copy(out=t1[:], in_=ps1[:, :]), reads=[pk1], writes=['sa_t1'])
                S.op('dve', lambda e: e.tensor_tensor(out=t1[:], in0=t1[:], in1=ps2[:, :], op=ALU.add), reads=['sa_t1', pk2], writes=['sa_t1'])
                for jj in range(2):
                    j = 2 * c + jj
                    S.op('dve', lambda e: e.tensor_scalar(out=den[:, jj * 128:(jj + 1) * 128], in0=t1[:, 256 + jj * 128:256 + (jj + 1) * 128],
                                                          scalar1=self.esink[:, l, j:j + 1], scalar2=None, op0=ALU.add),
                         reads=['sa_t1', 'esink'], writes=['den'])
                S.op('dve', lambda e: e.reciprocal(out=den[:], in_=den[:]), reads=['den'], writes=['den'])
                S.op('dve', lambda e: e.tensor_tensor(out=osb[:], in0=t1[:, 0:256], in1=den[:], op=ALU.mult), reads=['sa_t1', 'den'], writes=['osb'])
                S.op('pool', lambda e: e.tensor_tensor(out=self.mixacc[:, 2 * c:2 * c + 2, 0:128], in0=self.mixacc[:, 2 * c:2 * c + 2, 0:128],
                                                       in1=osb[:].rearrange("p (j q) -> p j q", j=2), op=ALU.mult),
                     reads=['osb', ('mixacc', 2 * c), ('mixacc', 2 * c + 1)], writes=[('mixacc', 2 * c), ('mixacc', 2 * c + 1)])
        self.dump('oa_g' if stream == 'P' else 's_oa_g', self.mixacc[:, :, 0:N], [128, 8, N], [('mixacc', j) for j in range(8)])
        ph.close()


_CACHE = {}


def consts():
    ident = np.eye(128, dtype=np.float32)
    slopes = 2.0 ** (-8.0 * np.arange(1, NH + 1) / NH)
    k = np.arange(128)[:, None]
    q = np.arange(128)[None, :]
    biasP = np.zeros((128, 2, NH, 128), np.float32)
    rel_prev = 128 + q - k
    rel_cur = q - k
    for h in range(NH):
        biasP[:, 0, h, :] = np.where(rel_prev < 128, -slopes[h] * rel_prev, NEG)
        biasP[:, 1, h, :] = np.where(rel_cur >= 0, -slopes[h] * rel_cur, NEG)
    kn, ks = np.arange(128)[:, None] // 8, np.arange(128)[:, None] % 8
    qn, qt = np.arange(128)[None, :] // 8, np.arange(128)[None, :] % 8
    biasSn = np.zeros((128, NH, 128), np.float32)
    for h in range(NH):
        biasSn[:, h, :] = np.where((kn == qn) & (ks <= qt), -slopes[h] * (qt - ks), NEG)
    biasSc = np.zeros((128, KVH, 2, NSEQ_S, 2, T_S), np.float32)
    j = np.arange(128)[:, None]
    t = np.arange(T_S)[None, :]
    for c in range(KVH):
        for i4 in range(4):
            h = 4 * c + i4
            biasSc[:, c, i4 % 2, :, i4 // 2, :] = np.where(j > t, -slopes[h] * (128 + t - j), NEG)[:, None, :]
    biasSc = biasSc.reshape(128, KVH, 2, 256)
    p = np.arange(128)[:, None]
    f = np.arange(128)[None, :]
    strictT = (p < f).astype(np.float32)
    inclT = (p <= f).astype(np.float32)
    strictA = (f < p).astype(np.float32)
    identb = (p == f).astype(np.float32)
    mask = np.stack([np.tile(m, (1, 4)) for m in (strictT, inclT, strictA, identb)], axis=1).astype(np.float32)
    rmask = np.ones((128, 512), np.float32)
    rmask[:, ::128] = 0.0
    blk = np.zeros((128, 128), np.float32)
    blk[:64, :64] = 1.0
    blk[64:, 64:] = 1.0
    return dict(c_ident=ident, c_biasP=biasP, c_biasSn=biasSn, c_biasSc=biasSc, c_mask=mask, c_rmask=rmask, c_blk=blk)


def get_prog(seq, **kw):
    key = (seq, tuple(sorted((k, str(v)) for k, v in kw.items())))
    if key not in _CACHE:
        kb = KB(seq, **kw)
        kb.build()
        _CACHE[key] = kb
    return _CACHE[key]


def make_in_maps(inputs, seq):
    cst = consts()
    maps = []
    f = lambda a: np.ascontiguousarray(a, dtype=np.float32)
    for c in range(NCORES):
        n0 = c * NSEQ_S
        m = {
            'xp': f(inputs['x_prompt'][c % 2, :seq]),
            'xs': f(inputs['x_sample'][n0:n0 + NSEQ_S].reshape(128, D)),
            'ck': f(inputs['cache_k'][:, n0:n0 + NSEQ_S].reshape(DEPTH, NSEQ_S, 128, 256)),
            'cv': f(inputs['cache_v'][:, n0:n0 + NSEQ_S].reshape(DEPTH, NSEQ_S, 128, 256)),
            'sconv': f(inputs['state_conv'][:, n0:n0 + NSEQ_S]),
            'srg': f(inputs['state_rglru'][:, n0:n0 + NSEQ_S]),
            'sshift': f(inputs['state_shift'][:, n0:n0 + NSEQ_S]),
            'swkv': f(inputs['state_wkv'][:, n0:n0 + NSEQ_S]),
        }
        for nm in ['w_in', 'attn_sinks', 'conv_w', 'conv_b', 'rg_wa', 'rg_ba', 'rg_wx', 'rg_bx', 'rg_lambda',
                   'rw_mu', 'rw_w0', 'rw_wup', 'rw_a0', 'rw_aup', 'rw_gup', 'rw_kk', 'rw_ka', 'rw_lnw', 'rw_lnb',
                   'w_out', 'ln1_g', 'ln1_b', 'w_gu', 'w_down', 'ln2_g', 'ln2_b']:
            m[nm] = f(inputs[nm])
        m['rw_rk'] = f(np.asarray(inputs['rw_rk']).reshape(DEPTH, D))
        m.update(cst)
        maps.append(m)
    return maps


def kernel(**inputs):
    seq = inputs['x_prompt'].shape[1]
    kb = get_prog(seq)
    maps = make_in_maps(inputs, seq)
    res = run_bass_kernel_spmd(kb.nc, maps, core_ids=list(range(NCORES)))
    R = res.results
    B = inputs['x_prompt'].shape[0]
    yp = np.stack([R[b]['yp'] for b in range(B)])
    ys = np.concatenate([R[c]['ys'].reshape(NSEQ_S, T_S, D) for c in range(NCORES)])

    def pst(nm, shp):
        return np.stack([R[b][nm] for b in range(B)], axis=1).reshape(shp)

    def sst(nm, shp):
        return np.concatenate([R[c][nm] for c in range(NCORES)], axis=1).reshape(shp)
    NS = NCORES * NSEQ_S
    outs = (yp, ys,
            pst('pk', (DEPTH, B, 128, KVH, HD)), pst('pv', (DEPTH, B, 128, KVH, HD)),
            pst('pconv', (DEPTH, B, 3, D)), pst('prg', (DEPTH, B, D)), pst('pshift', (DEPTH, B, SHW)),
            pst('pwkv', (DEPTH, B, NH, HD, HD)),
            sst('sk', (DEPTH, NS, 128, KVH, HD)), sst('sv', (DEPTH, NS, 128, KVH, HD)),
            sst('sconv_o', (DEPTH, NS, 3, D)), sst('srg_o', (DEPTH, NS, D)), sst('sshift_o', (DEPTH, NS, SHW)),
            sst('swkv_o', (DEPTH, NS, NH, HD, HD)))
    return tuple(np.ascontiguousarray(o, dtype=np.float32) for o in outs)
```

```python
import math
import numpy as np
from contextlib import ExitStack
import concourse.bass as bass
import concourse.mybir as mybir
from concourse.bass_utils import run_bass_kernel_spmd

F32 = mybir.dt.float32
BF16 = mybir.dt.bfloat16
ALU = mybir.AluOpType
AF = mybir.ActivationFunctionType
AX = mybir.AxisListType

D = 1024
NH = 16
HD = 64
KVH = 4
DFF = 2816
INC = 8992
OFF_K, OFF_V, OFF_RNN, OFF_RW, OFF_GATE = 1024, 1280, 1536, 2560, 5920
SHW = 3360
DEPTH = 2
ALPHA = (2 * DEPTH) ** 0.25
LN_EPS = 1e-5
GN_EPS = 64e-5
NEG = -30000.0
NCORES = 8
NSEQ_S = 16
T_S = 8
NWBUF = 3


class Sched:
    def __init__(self, nc, es, n_dma_sems=32):
        self.nc = nc
        self.es = es
        self.engs = {'pe': nc.tensor, 'dve': nc.vector, 'act': nc.scalar, 'pool': nc.gpsimd, 'sp': nc.sync}
        self.csem = {e: es.enter_context(nc.semaphore("cs_" + e)) for e in ('pe', 'dve', 'act', 'pool')}
        self.ccnt = {e: 0 for e in self.csem}
        self.dsem = {q: [es.enter_context(nc.semaphore("ds%s%d" % (q, i))) for i in range(n)]
                     for q, n in (('sp', 16), ('act', 12), ('pool', 4))}
        self.dcnt = {q: [0] * len(v) for q, v in self.dsem.items()}
        self.dnext = {q: 0 for q in self.dsem}
        self.known = {e: {} for e in self.engs}
        self.lastw = {}
        self.reads = {}
        self.ninst = 0
        self.psn = 0

    def _wait(self, e, tok):
        if tok is None:
            return
        sem, val, owner = tok
        if owner == 'pe' and e == 'pe':
            return
        if self.known[e].get(sem.name, 0) >= val:
            return
        self.engs[e].wait_ge(sem, val)
        self.known[e][sem.name] = val
        self.ninst += 1

    def _deps(self, e, reads, writes):
        for k in reads:
            self._wait(e, self.lastw.get(k))
        for k in writes:
            self._wait(e, self.lastw.get(k))
            for t in self.reads.get(k, ()):
                self._wait(e, t)

    def _commit(self, tok, reads, writes):
        for k in writes:
            self.lastw[k] = tok
            self.reads[k] = []
        for k in reads:
            if k in writes:
                continue
            lst = self.reads.setdefault(k, [])
            lst[:] = [t for t in lst if t[0].name != tok[0].name]
            lst.append(tok)

    def op(self, e, fn, reads=(), writes=()):
        self._deps(e, reads, writes)
        ins = fn(self.engs[e])
        self.ccnt[e] += 1
        ins.then_inc(self.csem[e], 1)
        tok = (self.csem[e], self.ccnt[e], e)
        self._commit(tok, reads, writes)
        self.ninst += 1
        return tok

    def dma(self, e, out, in_, reads=(), writes=(), **kw):
        i = self.dnext[e]
        self.dnext[e] = (i + 1) % len(self.dsem[e])
        if self.dcnt[e][i] > 0:
            self._wait(e, (self.dsem[e][i], self.dcnt[e][i], 'dma'))
        self._deps(e, reads, writes)
        ins = self.engs[e].dma_start(out=out, in_=in_, **kw)
        self.dcnt[e][i] += 16
        ins.then_inc(self.dsem[e][i], 16)
        tok = (self.dsem[e][i], self.dcnt[e][i], 'dma')
        self._commit(tok, reads, writes)
        self.ninst += 1
        return tok

    def release(self, names, engines=('pe', 'dve', 'act', 'pool', 'sp')):
        names = set(names)
        keys = [k for k in list(self.lastw.keys()) + list(self.reads.keys())
                if (k in names) or (isinstance(k, tuple) and k[0] in names)]
        for e in engines:
            for k in keys:
                self._wait(e, self.lastw.get(k))
                for t in self.reads.get(k, ()):
                    self._wait(e, t)
        for k in set(keys):
            self.lastw.pop(k, None)
            self.reads.pop(k, None)

    def finish(self, e='sp'):
        for k, t in list(self.lastw.items()):
            self._wait(e, t)


class Phase:
    def __init__(self, kb):
        self.kb = kb
        self.es = ExitStack()
        self.names = []
        kb.open_phases.append(self)

    _uid = [0]

    def T(self, name, shape, dt):
        self.names.append(name)
        Phase._uid[0] += 1
        return self.es.enter_context(self.kb.nc.sbuf_tensor("%s_%d" % (name, Phase._uid[0]), list(shape), dt))

    def close(self):
        self.kb.open_phases.remove(self)
        self.kb.S.release(self.names)
        self.es.close()


def w_in_tiles():
    t = []
    t.append(('q0', [(0, 512)]))
    t.append(('q1', [(512, 512)]))
    t.append(('kdup', None))
    t.append(('kv', [(OFF_K, 512)]))
    t.append(('gA0', [(OFF_GATE, 512)]))
    t.append(('gA1', [(OFF_GATE + 512, 512)]))
    return t


class _Stop(Exception):
    pass


class KB:
    def __init__(self, seq, depth=DEPTH, do_sample=True, debug=(), stop_after=None):
        self.seq = seq
        self.depth = depth
        self.do_sample = do_sample
        self.debug = set(debug)
        self.stop_after = stop_after
        self.dbg_outs = {}
        self.open_phases = []
        self.nc = bass.Bass("TRN2", target_bir_lowering=False)
        self.es = ExitStack()

    def chk(self, name):
        if self.stop_after == name:
            raise _Stop()

    def din(self, name, shape, dt=F32):
        return self.nc.dram_tensor(name, list(shape), dt, kind="ExternalInput").ap()

    def dout(self, name, shape, dt=F32):
        return self.nc.dram_tensor(name, list(shape), dt, kind="ExternalOutput").ap()

    def dscr(self, name, shape, dt):
        return self.nc.dram_tensor(name, list(shape), dt, kind="Internal").ap()

    def T(self, name, shape, dt):
        return self.es.enter_context(self.nc.sbuf_tensor(name, list(shape), dt))

    def ps(self):
        i = self.S.psn
        self.S.psn = (i + 1) % 8
        return self.psb[i], 'ps%d' % i

    def dump(self, name, ap, shape, key):
        if name not in self.debug:
            return
        o = self.dout("dbg_" + name, shape, ap.dtype if hasattr(ap, 'dtype') else F32)
        self.dbg_outs[name] = shape
        self.S.dma('act', o, ap, reads=key if isinstance(key, list) else [key], writes=['dbg_' + name])

    def mm(self, out, lhsT, rhs, start, stop, reads, writes):
        return self.S.op('pe', lambda e: e.matmul(out, lhsT=lhsT, rhs=rhs, start=start, stop=stop),
                         reads=reads, writes=writes)

    def evac_copy(self, eng, out, in_, reads, writes, scale=None):
        if eng == 'act':
            if scale is None:
                return self.S.op('act', lambda e: e.copy(out=out, in_=in_), reads=reads, writes=writes)
            return self.S.op('act', lambda e: e.activation(out=out, in_=in_, func=AF.Copy, scale=scale),
                             reads=reads, writes=writes)
        if scale is None:
            return self.S.op(eng, lambda e: e.tensor_copy(out=out, in_=in_), reads=reads, writes=writes)
        return self.S.op(eng, lambda e: e.tensor_scalar(out=out, in0=in_, scalar1=scale, scalar2=None, op0=ALU.mult),
                         reads=reads, writes=writes)

    def build(self):
        nc, es = self.nc, self.es
        self.S = S = Sched(nc, es)
        seq = self.seq
        L = self.depth
        I = self.I = {}
        I['xp'] = self.din('xp', [seq, D])
        I['xs'] = self.din('xs', [128, D])
        I['ck'] = self.din('ck', [DEPTH, NSEQ_S, 128, 256])
        I['cv'] = self.din('cv', [DEPTH, NSEQ_S, 128, 256])
        I['sconv'] = self.din('sconv', [DEPTH, NSEQ_S, 3, D])
        I['srg'] = self.din('srg', [DEPTH, NSEQ_S, D])
        I['sshift'] = self.din('sshift', [DEPTH, NSEQ_S, SHW])
        I['swkv'] = self.din('swkv', [DEPTH, NSEQ_S, NH, HD, HD])
        for nm, shp in [('w_in', [DEPTH, D, INC]), ('attn_sinks', [DEPTH, NH]), ('conv_w', [DEPTH, 4, D]),
                        ('conv_b', [DEPTH, D]), ('rg_wa', [DEPTH, 16, 64, 64]), ('rg_ba', [DEPTH, D]),
                        ('rg_wx', [DEPTH, 16, 64, 64]), ('rg_bx', [DEPTH, D]), ('rg_lambda', [DEPTH, D]),
                        ('rw_mu', [DEPTH, SHW]), ('rw_w0', [DEPTH, D]), ('rw_wup', [DEPTH, 64, D]),
                        ('rw_a0', [DEPTH, D]), ('rw_aup', [DEPTH, 64, D]), ('rw_gup', [DEPTH, 160, D]),
                        ('rw_kk', [DEPTH, D]), ('rw_ka', [DEPTH, D]), ('rw_rk', [DEPTH, D]),
                        ('rw_lnw', [DEPTH, D]), ('rw_lnb', [DEPTH, D]), ('w_out', [DEPTH, D, D]),
                        ('ln1_g', [DEPTH, D]), ('ln1_b', [DEPTH, D]), ('w_gu', [DEPTH, D, 2 * DFF]),
                        ('w_down', [DEPTH, DFF, D]), ('ln2_g', [DEPTH, D]), ('ln2_b', [DEPTH, D])]:
            I[nm] = self.din(nm, shp)
        I['c_ident'] = self.din('c_ident', [128, 128])
        I['c_biasP'] = self.din('c_biasP', [128, 2, NH, 128])
        I['c_biasSn'] = self.din('c_biasSn', [128, NH, 128])
        I['c_biasSc'] = self.din('c_biasSc', [128, KVH, 2, 256])
        I['c_mask'] = self.din('c_mask', [128, 4, 512])
        I['c_rmask'] = self.din('c_rmask', [128, 512])
        I['c_blk'] = self.din('c_blk', [128, 128])
        O = self.O = {}
        O['yp'] = self.dout('yp', [seq, D])
        O['ys'] = self.dout('ys', [128, D])
        O['pk'] = self.dout('pk', [DEPTH, 128, 256])
        O['pv'] = self.dout('pv', [DEPTH, 128, 256])
        O['pconv'] = self.dout('pconv', [DEPTH, 3, D])
        O['prg'] = self.dout('prg', [DEPTH, D])
        O['pshift'] = self.dout('pshift', [DEPTH, SHW])
        O['pwkv'] = self.dout('pwkv', [DEPTH, NH, HD, HD])
        O['sk'] = self.dout('sk', [DEPTH, NSEQ_S, 128, 256])
        O['sv'] = self.dout('sv', [DEPTH, NSEQ_S, 128, 256])
        O['sconv_o'] = self.dout('sconv_o', [DEPTH, NSEQ_S, 3, D])
        O['srg_o'] = self.dout('srg_o', [DEPTH, NSEQ_S, D])
        O['sshift_o'] = self.dout('sshift_o', [DEPTH, NSEQ_S, SHW])
        O['swkv_o'] = self.dout('swkv_o', [DEPTH, NSEQ_S, NH, HD, HD])
        W = self.W = {}
        W['w_in'] = self.dscr('wb_in', [DEPTH, D, INC], BF16)
        W['w_out'] = self.dscr('wb_out', [DEPTH, D, D], BF16)
        W['w_gu'] = self.dscr('wb_gu', [DEPTH, D, 2 * DFF], BF16)
        W['w_down'] = self.dscr('wb_down', [DEPTH, DFF, D], BF16)
        self.scr_wabd = self.dscr('scr_wabd', [128, DEPTH, 2, 8, 128], BF16)
        self.scr_wlora = self.dscr('scr_wlora', [128, DEPTH, D], BF16)
        self.scr_wgup = self.dscr('scr_wgup', [128, DEPTH, 2, D], BF16)
        self.scr_v = self.dscr('scr_v', [DEPTH, NSEQ_S, 8, 2, 6, T_S, 64], F32)
        self.scr_y = self.dscr('scr_y', [DEPTH, NSEQ_S, 8, 2, T_S, 64], F32)

        self.psb = [es.enter_context(nc.psum_tensor("psb%d" % i, [128, 512], F32)) for i in range(8)]
        self.wbuf = [self.T("wbuf%d" % i, [128, 8, 512], BF16) for i in range(NWBUF)]
        self.wn = 0
        self.xres = self.T("xres", [128, 4, D], F32)
        self.xT = self.T("xT", [128, 8, 512], BF16)
        self.mixacc = self.T("mixacc", [128, 8, 512], F32)
        self.identf = self.T("identf", [128, 128], F32)
        self.xb_tmp = [self.T("xb_tmp%d" % i, [128, D], BF16) for i in range(2)]
        self.identb = self.T("identb", [128, 128], BF16)
        self.onesb = self.T("onesb", [128, 64], BF16)
        self.esink = self.T("esink", [128, DEPTH, 8], F32)
        self.epsln = self.T("epsln", [128, 2], F32)
        self.p_cw = self.T("p_cw", [128, DEPTH, 4, 8], F32)
        self.p_cb = self.T("p_cb", [128, DEPTH, 8], F32)
        self.p_ba = self.T("p_ba", [128, DEPTH, 8], F32)
        self.p_bx = self.T("p_bx", [128, DEPTH, 8], F32)
        self.p_c8 = self.T("p_c8", [128, DEPTH, 8], F32)
        self.p_mu = self.T("p_mu", [128, DEPTH, 27], F32)
        self.p_w0 = self.T("p_w0", [128, DEPTH, 8], F32)
        self.p_a0 = self.T("p_a0", [128, DEPTH, 8], F32)
        self.p_kk = self.T("p_kk", [128, DEPTH, 8], F32)
        self.p_ka = self.T("p_ka", [128, DEPTH, 8], F32)
        self.p_rk = self.T("p_rk", [128, DEPTH, 8], F32)
        self.p_omka = self.T("p_omka", [128, DEPTH, 8], F32)
        self.blkb = self.T("blkb", [128, 128], BF16)
        self.blkf = self.T("blkf", [128, 256], F32)
        self.convst = {'P': self.T("convstP", [128, DEPTH, 8, 1, 3], F32), 'S': self.T("convstS", [128, DEPTH, 8, NSEQ_S, 3], F32)}
        self.hst = {'P': self.T("hstP", [128, DEPTH, 8, 1], F32), 'S': self.T("hstS", [128, DEPTH, 8, NSEQ_S], F32)}
        self.shst = {'P': self.T("shstP", [128, DEPTH, 27, 1], F32), 'S': self.T("shstS", [128, DEPTH, 27, NSEQ_S], F32)}

        self.phP = Phase(self)
        self.kT_p = [self.phP.T("kTp%d" % l, [128, KVH, 128 + 512], BF16) for l in range(L)]
        self.vtok_p = [self.phP.T("vtokp%d" % l, [128, 5, 256], BF16) for l in range(L)]
        self.H32 = self.phP.T("H32", [128, DEPTH, 8, 128], F32)
        self.Hbf = self.phP.T("Hbf", [128, DEPTH, 8, 128], BF16)
        self.maskb = self.phP.T("maskb", [128, 4, 512], BF16)
        self.rmaskP = self.phP.T("rmaskP", [128, 512], F32)
        self.phP.names = ['kTp', 'vtokp', 'H32', 'Hbf', 'maskb', 'rmaskP']
        try:
            self.main_body()
        except _Stop:
            for ph in reversed(list(self.open_phases)):
                ph.close()
        S.finish('sp')
        S.finish('act')
        self.es.close()
        return nc

    def main_body(self):
        S, I, O, L, seq = self.S, self.I, self.O, self.depth, self.seq
        self.setup()
        self.chk('setup')
        ngroups = seq // 512
        for g in range(ngroups):
            self.load_x(I['xp'][g * 512:(g + 1) * 512, :], 4)
            self.chk('loadx')
            for l in range(L):
                self.layer(l, 'P', g, ngroups)
            self.store_y(O['yp'][g * 512:(g + 1) * 512, :], 4)
        self.chk('pgroups')
        for l in range(L):
            self.prompt_state_out(l)
        self.chk('pstates')
        self.phP.close()
        if self.do_sample:
            self.load_x(I['xs'], 1)
            for l in range(L):
                self.sample_state_in(l)
            self.chk('S_in')
            for l in range(L):
                self.layer(l, 'S', 0, 1)
                self.sample_state_out(l)
            self.store_y(O['ys'], 1)

    def setup(self):
        S, I, W = self.S, self.I, self.W
        ph = Phase(self)
        st32 = [ph.T("wst32_%d" % i, [128, 2048], F32) for i in range(3)]
        st16 = [ph.T("wst16_%d" % i, [128, 2048], BF16) for i in range(3)]
        pc = 0
        for nm in ['w_in', 'w_out', 'w_gu', 'w_down']:
            src = I[nm]
            dst = W[nm]
            tot = 1
            for s_ in src.shape:
                tot *= s_
            sf = src.rearrange("l a b -> (l a b)").rearrange("(r c) -> r c", c=2048)
            df = dst.rearrange("l a b -> (l a b)").rearrange("(r c) -> r c", c=2048)
            R = tot // 2048
            r0 = 0
            while r0 < R:
                rr = min(128, R - r0)
                b = pc % 3
                S.dma('sp', st32[b][0:rr, :], sf[r0:r0 + rr, :], writes=[('wst32', b)])
                ce = ('dve', 'act', 'pool')[pc % 3]
                if ce == 'act':
                    S.op('act', lambda e: e.copy(out=st16[b][0:rr, :], in_=st32[b][0:rr, :]),
                         reads=[('wst32', b)], writes=[('wst16', b)])
                else:
                    S.op(ce, lambda e: e.tensor_copy(out=st16[b][0:rr, :], in_=st32[b][0:rr, :]),
                         reads=[('wst32', b)], writes=[('wst16', b)])
                S.dma('act', df[r0:r0 + rr, :], st16[b][0:rr, :], reads=[('wst16', b)], writes=[('W', nm, pc)])
                r0 += rr
                pc += 1
        ph.names = ['wst32', 'wst16']
        ph.close()
        for k, t in list(S.lastw.items()):
            if isinstance(k, tuple) and k[0] == 'W':
                S._wait('sp', t)
        S.dma('sp', self.identf[:], I['c_ident'], writes=['identf'])
        S.op('dve', lambda e: e.tensor_copy(out=self.identb[:], in_=self.identf[:]), reads=['identf'], writes=['identb'])
        S.op('dve', lambda e: e.memset(self.onesb[:], 1.0), writes=['onesb'])
        S.op('dve', lambda e: e.memset(self.epsln[:, 0:1], LN_EPS), writes=['epsln'])
        S.op('dve', lambda e: e.memset(self.epsln[:, 1:2], GN_EPS), writes=['epsln'])
        sk = I['attn_sinks'].rearrange("l (j two) -> two l j", two=2)
        for half in range(2):
            S.dma('sp', self.esink[half * 64:(half + 1) * 64, :, :],
                  sk[half:half + 1, :, :].broadcast_to([64, DEPTH, 8]), writes=['esink'],
                  allow_slow_non_contiguous=True)
        S.op('act', lambda e: e.activation(out=self.esink[:], in_=self.esink[:], func=AF.Exp),
             reads=['esink'], writes=['esink'])
        self.setup_params()
        for l in range(self.depth):
            S.op('dve', lambda e: e.memset(self.kT_p[l][:], 0.0), writes=[('kTp', l)])
            S.op('pool', lambda e: e.memset(self.vtok_p[l][:], 0.0), writes=[('vtokp', l)])


    def colload(self, dst, src_vec, key, nblk=8):
        self.S.dma('act', dst, src_vec.rearrange("(j p) -> p j", p=128), writes=[key], allow_slow_non_contiguous=True)

    def setup_params(self):
        S, I = self.S, self.I
        ph = Phase(self)
        for l in range(DEPTH):
            for i in range(4):
                self.colload(self.p_cw[:, l, i, :], I['conv_w'][l, i], 'p_cw')
            self.colload(self.p_cb[:, l, :], I['conv_b'][l], 'p_cb')
            self.colload(self.p_ba[:, l, :], I['rg_ba'][l], 'p_ba')
            self.colload(self.p_bx[:, l, :], I['rg_bx'][l], 'p_bx')
            self.colload(self.p_c8[:, l, :], I['rg_lambda'][l], 'p_c8')
            self.colload(self.p_mu[:, l, 0:26], I['rw_mu'][l, 0:3328], 'p_mu', 26)
            S.dma('act', self.p_mu[0:32, l, 26:27], I['rw_mu'][l, 3328:3360].rearrange("(p o) -> p o", o=1), writes=['p_mu'],
                  allow_slow_non_contiguous=True)
            for nm, t in (('rw_w0', self.p_w0), ('rw_a0', self.p_a0), ('rw_kk', self.p_kk), ('rw_ka', self.p_ka),
                          ('rw_rk', self.p_rk)):
                self.colload(t[:, l, :], I[nm][l], nm)
        c8 = self.p_c8[:].rearrange("p l j -> p (l j)")
        x = ph.T("sp_x", [128, 16], F32)
        z = ph.T("sp_z", [128, 16], F32)
        z2 = ph.T("sp_z2", [128, 16], F32)
        acc = ph.T("sp_acc", [128, 16], F32)
        S.op('act', lambda e: e.activation(out=x[:], in_=c8, func=AF.Exp, scale=-1.0), reads=['p_c8'], writes=['sp_x'])
        S.op('dve', lambda e: e.tensor_scalar(out=z[:], in0=x[:], scalar1=2.0, scalar2=None, op0=ALU.add), reads=['sp_x'], writes=['sp_z'])
        S.op('dve', lambda e: e.reciprocal(out=z[:], in_=z[:]), reads=['sp_z'], writes=['sp_z'])
        S.op('dve', lambda e: e.tensor_tensor(out=z[:], in0=z[:], in1=x[:], op=ALU.mult), reads=['sp_z', 'sp_x'], writes=['sp_z'])
        S.op('dve', lambda e: e.tensor_tensor(out=z2[:], in0=z[:], in1=z[:], op=ALU.mult), reads=['sp_z'], writes=['sp_z2'])
        S.op('dve', lambda e: e.tensor_scalar(out=acc[:], in0=z2[:], scalar1=1.0 / 9, scalar2=1.0 / 7, op0=ALU.mult, op1=ALU.add),
             reads=['sp_z2'], writes=['sp_acc'])
        for cst in (1.0 / 5, 1.0 / 3, 1.0):
            S.op('dve', lambda e: e.tensor_tensor(out=acc[:], in0=acc[:], in1=z2[:], op=ALU.mult), reads=['sp_acc', 'sp_z2'], writes=['sp_acc'])
            S.op('dve', lambda e: e.tensor_scalar(out=acc[:], in0=acc[:], scalar1=cst, scalar2=None, op0=ALU.add), reads=['sp_acc'], writes=['sp_acc'])
        S.op('dve', lambda e: e.tensor_tensor(out=acc[:], in0=acc[:], in1=z[:], op=ALU.mult), reads=['sp_acc', 'sp_z'], writes=['sp_acc'])
        S.op('dve', lambda e: e.tensor_scalar(out=c8, in0=acc[:], scalar1=-16.0, scalar2=None, op0=ALU.mult), reads=['sp_acc'], writes=['p_c8'])
        ph.close()
        ph = Phase(self)
        st = ph.T("wbd_st", [128, DEPTH, 2, 8, 128], F32)
        S.op('pool', lambda e: e.memset(st[:], 0.0), writes=['wbd_st'])
        for l in range(DEPTH):
            for gi, nm in enumerate(('rg_wa', 'rg_wx')):
                src = I[nm][l].rearrange("(j hh) i c -> hh i j c", hh=2)
                for hh in range(2):
                    S.dma('act', st[hh * 64:(hh + 1) * 64, l, gi, :, hh * 64:(hh + 1) * 64], src[hh], writes=['wbd_st'])
        wabd_t = ph.T("wabd_t", [128, DEPTH, 2, 8, 128], BF16)
        S.op('dve', lambda e: e.tensor_copy(out=wabd_t[:], in_=st[:]), reads=['wbd_st'], writes=['wabd_t'])
        S.dma('act', self.scr_wabd, wabd_t[:], reads=['wabd_t'], writes=['scr_wabd'])
        for st_ in ('P', 'S'):
            S.op('pool', lambda e: e.memset(self.convst[st_][:], 0.0), writes=[('convst', st_)])
            S.op('pool', lambda e: e.memset(self.hst[st_][:], 0.0), writes=[('hst', st_)])
            S.op('pool', lambda e: e.memset(self.shst[st_][:], 0.0), writes=[('shst', st_)])
        ph.close()
        ph = Phase(self)
        st2 = ph.T("lora_st", [128, DEPTH, D], F32)
        st3 = ph.T("gup_st", [128, DEPTH, 2, D], F32)
        S.op('pool', lambda e: e.memset(st3[:], 0.0), writes=['gup_st'])
        for l in range(DEPTH):
            S.dma('act', st2[0:64, l, :], I['rw_wup'][l], writes=['lora_st'])
            S.dma('act', st2[64:128, l, :], I['rw_aup'][l], writes=['lora_st'])
            S.dma('act', st3[:, l, 0, :], I['rw_gup'][l, 0:128, :], writes=['gup_st'])
            S.dma('act', st3[0:32, l, 1, :], I['rw_gup'][l, 128:160, :], writes=['gup_st'])
        wlora_t = ph.T("wlora_t", [128, DEPTH, D], BF16)
        wgup_t = ph.T("wgup_t", [128, DEPTH, 2, D], BF16)
        S.op('dve', lambda e: e.tensor_copy(out=wlora_t[:], in_=st2[:]), reads=['lora_st'], writes=['wlora_t'])
        S.op('dve', lambda e: e.tensor_copy(out=wgup_t[:], in_=st3[:]), reads=['gup_st'], writes=['wgup_t'])
        S.dma('act', self.scr_wlora, wlora_t[:], reads=['wlora_t'], writes=['scr_wlora'])
        S.dma('act', self.scr_wgup, wgup_t[:], reads=['wgup_t'], writes=['scr_wgup'])
        S.op('dve', lambda e: e.tensor_scalar(out=self.p_omka[:], in0=self.p_ka[:], scalar1=-1.0, scalar2=1.0, op0=ALU.mult, op1=ALU.add),
             reads=['rw_ka'], writes=['p_omka'])
        S.op('pool', lambda e: e.memset(self.H32[:], 0.0), writes=['H32'])
        S.op('pool', lambda e: e.memset(self.Hbf[:], 0.0), writes=['Hbf'])
        ph.close()
        ph = Phase(self)
        mk = ph.T("mask_st", [128, 4, 512], F32)
        S.dma('act', mk[:], I['c_mask'], writes=['mask_st'])
        S.op('dve', lambda e: e.tensor_copy(out=self.maskb[:], in_=mk[:]), reads=['mask_st'], writes=['maskb'])
        bk = ph.T("blk_st", [128, 128], F32)
        S.dma('act', bk[:], I['c_blk'], writes=['blk_st'])
        S.op('dve', lambda e: e.tensor_copy(out=self.blkb[:], in_=bk[:]), reads=['blk_st'], writes=['blkb'])
        for r2 in range(2):
            S.op('dve', lambda e: e.tensor_copy(out=self.blkf[:, r2 * 128:(r2 + 1) * 128], in_=bk[:]), reads=['blk_st'], writes=['blkf'])
        S.dma('act', self.rmaskP[:], I['c_rmask'], writes=['rmaskP'])
        ph.close()


    def prompt_state_out(self, l):
        S, O = self.S, self.O
        ph = Phase(self)
        cv, hs, sh = self.convst['P'], self.hst['P'], self.shst['P']
        self.fm_to_rows(ph, lambda blk, w: cv[0:w, l, blk, 0, :], 3, D, O['pconv'][l], [('convst', 'P')], ('o_pconv', l))
        self.fm_to_rows(ph, lambda blk, w: hs[0:w, l, blk, :], 1, D, O['prg'][l:l + 1, :], [('hst', 'P')], ('o_prg', l))
        self.fm_to_rows(ph, lambda blk, w: sh[0:w, l, blk, :], 1, SHW, O['pshift'][l:l + 1, :], [('shst', 'P')], ('o_pshift', l))
        for j in range(8):
            ps, pk = self.ps()
            S.op('pe', lambda e: e.transpose(out=ps[:, 0:128], in_=self.H32[:, l, j, :], identity=self.identf[:]),
                 reads=[('H32', l, j), 'identf'], writes=[pk])
            tmp = ph.T("pw_tmp%d" % j, [128, 128], F32)
            S.op('act', lambda e: e.copy(out=tmp[:], in_=ps[:, 0:128]), reads=[pk], writes=[('pw_tmp', j)])
            for hh in range(2):
                S.dma('act', O['pwkv'][l, 2 * j + hh], tmp[hh * 64:(hh + 1) * 64, hh * 64:(hh + 1) * 64],
                      reads=[('pw_tmp', j)], writes=[('o_pwkv', l, j, hh)])
        ph.names.append('pw_tmp')
        ph.close()

    def sample_state_in(self, l):
        S, I = self.S, self.I
        ph = Phase(self)
        cv, hs, sh = self.convst['S'], self.hst['S'], self.shst['S']
        self.rows_to_fm(ph, I['sconv'][l].rearrange("n i c -> (n i) c"), 48, D,
                        lambda b0, nb, w: cv[0:w, l, b0:b0 + nb, :, :].rearrange("p b n i -> p b (n i)"), [('convst', 'S')])
        ph.close()
        ph = Phase(self)
        self.rows_to_fm(ph, I['srg'][l], NSEQ_S, D, lambda b0, nb, w: hs[0:w, l, b0:b0 + nb, :], [('hst', 'S')])
        ph.close()
        ph = Phase(self)
        self.rows_to_fm(ph, I['sshift'][l], NSEQ_S, SHW, lambda b0, nb, w: sh[0:w, l, b0:b0 + nb, :], [('shst', 'S')])
        ph.close()

    def sample_state_out(self, l):
        S, O = self.S, self.O
        cv, hs, sh = self.convst['S'], self.hst['S'], self.shst['S']
        ph = Phase(self)
        self.fm_to_rows(ph, lambda blk, w: cv[0:w, l, blk, :, :].rearrange("p n i -> p (n i)"), 48, D,
                        O['sconv_o'][l].rearrange("n i c -> (n i) c"), [('convst', 'S')], ('o_sconv', l))
        ph.close()
        ph = Phase(self)
        self.fm_to_rows(ph, lambda blk, w: hs[0:w, l, blk, :], NSEQ_S, D, O['srg_o'][l], [('hst', 'S')], ('o_srg', l))
        ph.close()
        ph = Phase(self)
        self.fm_to_rows(ph, lambda blk, w: sh[0:w, l, blk, :], NSEQ_S, SHW, O['sshift_o'][l], [('shst', 'S')], ('o_sshift', l))
        ph.close()

    def rows_to_fm(self, ph, src_rows, R, F, dst_fn, keys_w):
        S = self.S
        tmp = ph.T("r2f_tmp", [128, F], F32)
        S.dma('act', tmp[0:R, :], src_rows, writes=['r2f_tmp'])
        nblk = (F + 127) // 128
        per = max(1, 512 // R)
        b0 = 0
        while b0 < nblk:
            nb = min(per, nblk - b0)
            ps, pk = self.ps()
            wlast = 128
            for b in range(nb):
                w = min(128, F - (b0 + b) * 128)
                wlast = w
                S.op('pe', lambda e: e.transpose(out=ps[0:w, b * R:(b + 1) * R], in_=tmp[0:R, (b0 + b) * 128:(b0 + b) * 128 + w],
                                                 identity=self.identf[0:R, 0:R]), reads=['r2f_tmp', 'identf'], writes=[pk])
            nfull = nb if wlast == 128 else nb - 1
            if nfull > 0:
                S.op('act', lambda e: e.copy(out=dst_fn(b0, nfull, 128), in_=ps[:, 0:nfull * R].rearrange("p (b r) -> p b r", r=R)),
                     reads=[pk], writes=keys_w)
            if nfull < nb:
                S.op('act', lambda e: e.copy(out=dst_fn(b0 + nfull, 1, wlast),
                                             in_=ps[0:wlast, nfull * R:(nfull + 1) * R].rearrange("p (b r) -> p b r", r=R)),
                     reads=[pk], writes=keys_w)
            b0 += nb

    def fm_to_rows(self, ph, src_fn, R, F, dst_rows, keys_r, okey):
        S = self.S
        tmp = ph.T("f2r_tmp", [128, F], F32)
        nblk = (F + 127) // 128
        b0 = 0
        while b0 < nblk:
            nb = min(4, nblk - b0)
            ps, pk = self.ps()
            tot = 0
            for b in range(nb):
                w = min(128, F - (b0 + b) * 128)
                S.op('pe', lambda e: e.transpose(out=ps[0:R, b * 128:b * 128 + w], in_=src_fn(b0 + b, w),
                                                 identity=self.identf[0:w, 0:w]), reads=keys_r + ['identf'], writes=[pk])
                tot += w
            S.op('act', lambda e: e.copy(out=tmp[0:R, b0 * 128:b0 * 128 + tot], in_=ps[0:R, 0:tot]), reads=[pk], writes=['f2r_tmp'])
            b0 += nb
        S.dma('act', dst_rows, tmp[0:R, :], reads=['f2r_tmp'], writes=[okey])

    def load_x(self, src, ntile):
        self.S.dma('act', self.xres[:, 0:ntile, :], src.rearrange("(t p) d -> p t d", p=128),
                   writes=[('xres', t) for t in range(ntile)])
        for t in range(ntile):
            self.make_xT(t)

    def store_y(self, dst, ntile):
        self.S.dma('act', dst.rearrange("(t p) d -> p t d", p=128), self.xres[:, 0:ntile, :],
                   reads=[('xres', t) for t in range(ntile)], writes=[('yout', id(dst))])

    def make_xT(self, t):
        S = self.S
        self.xbn = (getattr(self, 'xbn', 0) + 1) % 2
        xb = self.xb_tmp[self.xbn]
        xbk = ('xb_tmp', self.xbn)
        S.op('act', lambda e: e.copy(out=xb[:], in_=self.xres[:, t, :]), reads=[('xres', t)], writes=[xbk])
        ps, pk = self.ps()
        psb = ps[:].bitcast(BF16)
        for k in range(8):
            S.op('pe', lambda e: e.transpose(out=psb[:, k * 128:(k + 1) * 128], in_=xb[:, k * 128:(k + 1) * 128],
                                             identity=self.identb[:]),
                 reads=[xbk, 'identb'], writes=[pk])
        S.op('dve', lambda e: e.tensor_copy(out=self.xT[:, :, t * 128:(t + 1) * 128],
                                            in_=psb.rearrange("p (k c) -> p k c", k=8)),
             reads=[pk], writes=[('xT', t)])

    def wtile(self, wname, l, cols, rows=(0, 8)):
        S = self.S
        i = self.wn
        self.wn = (self.wn + 1) % NWBUF
        buf = self.wbuf[i]
        src = self.W[wname][l]
        k0, k1 = rows
        off = 0
        keys = []
        for ci, (c0, n) in enumerate(cols):
            key = ('wbuf', i) if ci == 0 else ('wbufx', i, ci)
            S.dma('sp', buf[:, 0:k1 - k0, off:off + n],
                  src[k0 * 128:k1 * 128, c0:c0 + n].rearrange("(k p) c -> p k c", p=128), writes=[key])
            keys.append(key)
            off += n
        return buf, keys

    def layer(self, l, stream, g, ngroups):
        S = self.S
        nt = 4 if stream == 'P' else 1
        N = nt * 128
        self.attention(l, stream, g, ngroups, nt, N)
        self.chk('attn')
        self.chk(stream + '_attn')
        self.rglru(l, stream, g, ngroups, nt, N)
        self.chk('rglru')
        self.chk(stream + '_rglru')
        self.rwkv(l, stream, g, ngroups, nt, N)
        self.chk('rwkv')
        self.chk(stream + '_rwkv')
        self.dense_tail(l, stream, nt, N)
        self.chk('dense')

    def load_ln(self, l):
        S, I = self.S, self.I
        for i, nm in enumerate(['ln1_g', 'ln1_b', 'ln2_g', 'ln2_b']):
            S.dma('act', self.lnbuf[:, i, :], I[nm][l:l + 1, :].broadcast_to([128, D]), writes=[('lnbuf', i)])

    def layer_norm_tile(self, ph, t, which):
        S = self.S
        x = self.xres[:, t, :]
        key = ('xres', t)
        cache = ph.__dict__.setdefault('_ln_cache', {})
        if t not in cache:
            cache[t] = (ph.T("lnst%d" % t, [128, 2, 6], F32), ph.T("lnmv%d" % t, [128, 2], F32), ph.T("lnrs%d" % t, [128, 1], F32))
            ph.names += ['lnst', 'lnmv', 'lnrs']
        st, mv, rs = cache[t]
        kst, kmv, krs = ('lnst', t), ('lnmv', t), ('lnrs', t)
        for hf in range(2):
            S.op('dve', lambda e: e.bn_stats(out=st[:, hf, :], in_=self.xres[:, t, hf * 512:(hf + 1) * 512]),
                 reads=[key], writes=[kst])
        yield
        S.op('dve', lambda e: e.bn_aggr(out=mv[:], in_=st[:].rearrange("p a b -> p (a b)")), reads=[kst], writes=[kmv])
        yield
        S.op('act', lambda e: e.activation(out=rs[:], in_=mv[:, 1:2], func=AF.Sqrt, bias=self.epsln[:, 0:1], scale=1.0),
             reads=[kmv, 'epsln'], writes=[krs])
        yield
        S.op('dve', lambda e: e.reciprocal(out=rs[:], in_=rs[:]), reads=[krs], writes=[krs])
        yield
        S.op('dve', lambda e: e.tensor_scalar(out=x, in0=x, scalar1=mv[:, 0:1], scalar2=rs[:, 0:1],
                                              op0=ALU.subtract, op1=ALU.mult), reads=[key, kmv, krs], writes=[key])
        yield
        S.op('pool', lambda e: e.tensor_tensor(out=x, in0=x, in1=self.lnbuf[:, 2 * which, :], op=ALU.mult),
             reads=[key, ('lnbuf', 2 * which)], writes=[key])
        yield
        S.op('pool', lambda e: e.tensor_tensor(out=x, in0=x, in1=self.lnbuf[:, 2 * which + 1, :], op=ALU.add),
             reads=[key, ('lnbuf', 2 * which + 1)], writes=[key])
        yield

    def dense_tail(self, l, stream, nt, N):
        S = self.S
        ph = Phase(self)
        self.lnbuf = ph.T("lnbuf", [128, 4, D], F32)
        self.load_ln(l)
        self.mixT = ph.T("mixT", [128, 8, N], BF16)
        for j in range(8):
            S.op('act', lambda e: e.copy(out=self.mixT[:, j, 0:N], in_=self.mixacc[:, j, 0:N]),
                 reads=[('mixacc', j)], writes=[('mixT', j)])
        mkeys = [('mixT', j) for j in range(8)]
        wbs = [self.wtile('w_out', l, [(half * 512, 512)]) for half in range(2)]
        for t in range(nt):
            for half in range(2):
                wb, wk = wbs[half]
                ps, pk = self.ps()
                for k in range(8):
                    self.mm(ps[:, :], self.mixT[:, k, t * 128:(t + 1) * 128], wb[:, k, :], k == 0, k == 7,
                            reads=wk + [('mixT', k)], writes=[pk])
                xs = self.xres[:, t, half * 512:(half + 1) * 512]
                S.op('dve', lambda e: e.scalar_tensor_tensor(out=xs, in0=xs, scalar=ALPHA, in1=ps[:, :],
                                                             op0=ALU.mult, op1=ALU.add),
                     reads=[pk, ('xres', t)], writes=[('xres', t)])
        self.interleave([self.layer_norm_tile(ph, t, 0) for t in range(nt)])
        for t in range(nt):
            self.make_xT(t)
        self.dump('x1', self.xres[:, 0:nt, :], [128, nt, D], [('xres', t) for t in range(nt)])
        hT = ph.T("hT", [128, 22, N], BF16)
        sg = ph.T("sgt", [128, N], F32)
        xkeys = [('xT', t) for t in range(nt)]
        NB = DFF // 128
        for j0 in range(0, NB, 4):
            nb = min(4, NB - j0)
            wg, wgk = self.wtile('w_gu', l, [(j0 * 128, nb * 128)])
            wu, wuk = self.wtile('w_gu', l, [(DFF + j0 * 128, nb * 128)])
            for b in range(nb):
                j = j0 + b
                psg, pgk = self.ps()
                for k in range(8):
                    self.mm(psg[:, 0:N], wg[:, k, b * 128:(b + 1) * 128], self.xT[:, k, 0:N], k == 0, k == 7,
                            reads=wgk + xkeys, writes=[pgk])
                psu, puk = self.ps()
                for k in range(8):
                    self.mm(psu[:, 0:N], wu[:, k, b * 128:(b + 1) * 128], self.xT[:, k, 0:N], k == 0, k == 7,
                            reads=wuk + xkeys, writes=[puk])
                S.op('act', lambda e: e.activation(out=sg[:], in_=psg[:, 0:N], func=AF.Silu), reads=[pgk], writes=['sgt'])
                S.op('dve', lambda e: e.tensor_tensor(out=hT[:, j, :], in0=sg[:], in1=psu[:, 0:N], op=ALU.mult),
                     reads=['sgt', puk], writes=[('hT', j)])
        for half in range(2):
            accs = [self.ps() for _ in range(nt)]
            pieces = [(0, 8), (8, 16), (16, 22)]
            for pi, (k0, k1) in enumerate(pieces):
                wb, wk = self.wtile('w_down', l, [(half * 512, 512)], rows=(k0, k1))
                for t in range(nt):
                    ps, pk = accs[t]
                    for k in range(k0, k1):
                        self.mm(ps[:, :], hT[:, k, t * 128:(t + 1) * 128], wb[:, k - k0, :], k == 0, k == NB - 1,
                                reads=wk + [('hT', k)], writes=[pk])
            for t in range(nt):
                ps, pk = accs[t]
                xs = self.xres[:, t, half * 512:(half + 1) * 512]
                S.op('dve', lambda e: e.scalar_tensor_tensor(out=xs, in0=xs, scalar=ALPHA, in1=ps[:, :],
                                                             op0=ALU.mult, op1=ALU.add),
                     reads=[pk, ('xres', t)], writes=[('xres', t)])
        self.interleave([self.layer_norm_tile(ph, t, 1) for t in range(nt)])
        for t in range(nt):
            if l < self.depth - 1:
                self.make_xT(t)
        self.dump('x2', self.xres[:, 0:nt, :], [128, nt, D], [('xres', t) for t in range(nt)])
        ph.close()

    def interleave(self, gens):
        gens = list(gens)
        while gens:
            for g_ in list(gens):
                try:
                    next(g_)
                except StopIteration:
                    gens.remove(g_)

    def interleave_staged(self, gens):
        gens = list(gens)
        waiting = []
        active = list(gens)
        while active:
            for g_ in list(active):
                try:
                    r = next(g_)
                except StopIteration:
                    active.remove(g_)
                    continue
                if r == 'CHAIN':
                    active.remove(g_)
                    waiting.append(g_)
        waiting.sort(key=lambda g_: gens.index(g_))
        post = []
        for g_ in waiting:
            while True:
                try:
                    r = next(g_)
                except StopIteration:
                    break
                if r == 'POST':
                    post.append(g_)
                    break
        self.interleave(post)

    def rglru(self, l, stream, g, ngroups, nt, N):
        S, I = self.S, self.I
        ph = Phase(self)
        nseq, T = (1, 512) if stream == 'P' else (NSEQ_S, T_S)
        convst, hst = self.convst[stream], self.hst[stream]
        ckey, hkey = ('convst', stream), ('hst', stream)
        xkeys = [('xT', t) for t in range(nt)]
        NW = 4
        xrh = [ph.T("xrh%d" % i, [128, nseq, 3 + T], F32) for i in range(NW)]
        xc = [ph.T("xc%d" % i, [128, nseq, T], F32) for i in range(NW)]
        xcb = [ph.T("xcb%d" % i, [128, N], BF16) for i in range(NW)]
        rr = [ph.T("rg_r%d" % i, [128, N], F32) for i in range(NW)]
        ii = [ph.T("rg_i%d" % i, [128, N], F32) for i in range(NW)]
        aa = [ph.T("rg_a%d" % i, [128, nseq, T], F32) for i in range(NW)]
        uu = [ph.T("rg_u%d" % i, [128, nseq, T], F32) for i in range(NW)]
        hh = ph.T("rg_h", [128, 4, nseq, T], F32)
        ph.names += ['xrh', 'xc', 'xcb', 'rg_r', 'rg_i', 'rg_a', 'rg_u']
        wa_l = ph.T("wa_bd", [128, 2, 8, 128], BF16)
        S.dma('act', wa_l[:], self.scr_wabd[:, l], reads=['scr_wabd'], writes=['wa_bd'])
        flat = lambda t3: t3[:].rearrange("p n t -> p (n t)")

        def block(half, b, wb, wk, wg, wgk):
            j = half * 4 + b
            K_ = lambda nm: (nm, b)
            ps, pk = self.ps()
            for k in range(8):
                self.mm(ps[:, 0:N], wb[:, k, b * 128:(b + 1) * 128], self.xT[:, k, 0:N], k == 0, k == 7,
                        reads=wk + xkeys, writes=[pk])
            yield
            S.op('act', lambda e: e.copy(out=xrh[b][:, :, 3:3 + T], in_=ps[:, 0:N].rearrange("p (n t) -> p n t", n=nseq)),
                 reads=[pk], writes=[K_('xrh')])
            S.op('pool', lambda e: e.tensor_copy(out=xrh[b][:, :, 0:3], in_=convst[:, l, j, :, :]), reads=[ckey], writes=[K_('xrh')])
            yield
            cw = self.p_cw
            S.op('dve', lambda e: e.tensor_scalar(out=xc[b][:], in0=xrh[b][:, :, 0:T], scalar1=cw[:, l, 0, j:j + 1],
                                                  scalar2=self.p_cb[:, l, j:j + 1], op0=ALU.mult, op1=ALU.add),
                 reads=[K_('xrh'), 'p_cw', 'p_cb'], writes=[K_('xc')])
            yield
            for i in range(1, 4):
                S.op('dve', lambda e: e.scalar_tensor_tensor(out=xc[b][:], in0=xrh[b][:, :, i:i + T], scalar=cw[:, l, i, j:j + 1],
                                                             in1=xc[b][:], op0=ALU.mult, op1=ALU.add),
                     reads=[K_('xrh'), 'p_cw', K_('xc')], writes=[K_('xc')])
                yield
            S.op('pool', lambda e: e.tensor_copy(out=convst[:, l, j, :, :], in_=xrh[b][:, :, T:T + 3]), reads=[K_('xrh')], writes=[ckey])
            S.op('act', lambda e: e.copy(out=xcb[b][:], in_=flat(xc[b])), reads=[K_('xc')], writes=[K_('xcb')])
            yield
            pss = []
            for gi in range(2):
                ps2, pk2 = self.ps()
                self.mm(ps2[:, 0:N], wa_l[:, gi, j, :], xcb[b][:], True, True, reads=['wa_bd', K_('xcb')], writes=[pk2])
                pss.append((ps2, pk2))
            yield
            for gi, (dst, dk, bias) in enumerate(((rr[b], K_('rg_r'), self.p_ba), (ii[b], K_('rg_i'), self.p_bx))):
                ps2, pk2 = pss[gi]
                S.op('act', lambda e: e.activation(out=dst[:], in_=ps2[:, 0:N], func=AF.Sigmoid, bias=bias[:, l, j:j + 1], scale=1.0),
                     reads=[pk2, 'p_ba', 'p_bx'], writes=[dk])
            yield
            S.op('act', lambda e: e.activation(out=flat(aa[b]), in_=rr[b][:], func=AF.Exp, scale=self.p_c8[:, l, j:j + 1]),
                 reads=[K_('rg_r'), 'p_c8'], writes=[K_('rg_a')])
            S.op('pool', lambda e: e.tensor_tensor(out=ii[b][:], in0=ii[b][:], in1=flat(xc[b]), op=ALU.mult), reads=[K_('rg_i'), K_('xc')], writes=[K_('rg_i')])
            yield
            S.op('dve', lambda e: e.tensor_tensor(out=rr[b][:], in0=flat(aa[b]), in1=flat(aa[b]), op=ALU.mult), reads=[K_('rg_a')], writes=[K_('rg_r')])
            yield
            S.op('dve', lambda e: e.tensor_scalar(out=rr[b][:], in0=rr[b][:], scalar1=-1.0, scalar2=1.0, op0=ALU.mult, op1=ALU.add),
                 reads=[K_('rg_r')], writes=[K_('rg_r')])
            yield
            S.op('dve', lambda e: e.tensor_scalar(out=rr[b][:], in0=rr[b][:], scalar1=1e-30, scalar2=None, op0=ALU.max), reads=[K_('rg_r')], writes=[K_('rg_r')])
            yield
            S.op('act', lambda e: e.activation(out=rr[b][:], in_=rr[b][:], func=AF.Sqrt), reads=[K_('rg_r')], writes=[K_('rg_r')])
            yield
            S.op('dve', lambda e: e.tensor_tensor(out=flat(uu[b]), in0=ii[b][:], in1=rr[b][:], op=ALU.mult), reads=[K_('rg_i'), K_('rg_r')], writes=[K_('rg_u')])
            yield
            for n in range(nseq):
                S.op('dve', lambda e: e.tensor_tensor_scan(out=hh[:, b, n, :], data0=aa[b][:, n, :], data1=uu[b][:, n, :],
                                                           initial=hst[:, l, j, n:n + 1], op0=ALU.mult, op1=ALU.add),
                     reads=[K_('rg_a'), K_('rg_u'), hkey], writes=[('rg_h', b)])
            yield
            S.op('pool', lambda e: e.tensor_copy(out=hst[:, l, j, :], in_=hh[:, b, :, T - 1]), reads=[('rg_h', b)], writes=[hkey])
            ps, pk = self.ps()
            for k in range(8):
                self.mm(ps[:, 0:N], wg[:, k, b * 128:(b + 1) * 128], self.xT[:, k, 0:N], k == 0, k == 7,
                        reads=wgk + xkeys, writes=[pk])
            yield
            gbt = ii[b]
            S.op('act', lambda e: e.activation(out=gbt[:], in_=ps[:, 0:N], func=AF.Sigmoid), reads=[pk], writes=[K_('rg_i')])
            yield
            S.op('dve', lambda e: e.tensor_tensor(out=gbt[:], in0=gbt[:], in1=hh[:, b, :, :].rearrange("p n t -> p (n t)"), op=ALU.mult),
                 reads=[K_('rg_i'), ('rg_h', b)], writes=[K_('rg_i')])
            yield
            S.op('pool', lambda e: e.tensor_tensor(out=self.mixacc[:, j, 0:N], in0=self.mixacc[:, j, 0:N], in1=gbt[:], op=ALU.add),
                 reads=[K_('rg_i'), ('mixacc', j)], writes=[('mixacc', j)])
            yield

        for half in range(2):
            wb, wk = self.wtile('w_in', l, [(OFF_RNN + half * 512, 512)])
            wg, wgk = self.wtile('w_in', l, [(OFF_GATE + D + half * 512, 512)])
            self.interleave([block(half, b, wb, wk, wg, wgk) for b in range(4)])
        self.dump('mixAB' if stream == 'P' else 's_mixAB', self.mixacc[:, :, 0:N], [128, 8, N], [('mixacc', j) for j in range(8)])
        ph.close()

    def rwkv(self, l, stream, g, ngroups, nt, N):
        S, I = self.S, self.I
        ph = Phase(self)
        is_s = stream == 'S'
        nseq, T = (NSEQ_S, T_S) if is_s else (1, 512)
        shst = self.shst[stream]
        skey = ('shst', stream)
        xkeys = [('xT', t) for t in range(nt)]
        pch = ph.T("pch", [128, nseq, 1 + T], F32)
        dd = ph.T("rw_d", [128, N], F32)
        lwla = ph.T("lwla", [128, N], BF16)
        siglg = ph.T("siglg", [128, 2, N], BF16)
        self.wlora_l = ph.T("wlora", [128, D], BF16)
        self.wgup_l = ph.T("wgup", [128, 2, D], BF16)
        S.dma('act', self.wlora_l[:], self.scr_wlora[:, l], reads=['scr_wlora'], writes=['wlora'])
        S.dma('act', self.wgup_l[:], self.scr_wgup[:, l], reads=['scr_wgup'], writes=['wgup'])
        lnw = ph.T("lnw_t", [128, D], F32)
        lnb = ph.T("lnb_t", [128, D], F32)
        S.dma('act', lnw[:], I['rw_lnw'][l:l + 1, :].broadcast_to([128, D]), writes=['lnw_t'])
        S.dma('act', lnb[:], I['rw_lnb'][l:l + 1, :].broadcast_to([128, D]), writes=['lnb_t'])
        flat = lambda t3: t3[:].rearrange("p n t -> p (n t)")

        def shifted(ps, pk, blk, w, out_ap, out_key, func=None, bufs=None, S_=None):
            pch_, dd_ = bufs if bufs is not None else (pch, dd)
            So = S_ if S_ is not None else S
            So.op('act', lambda e: e.copy(out=pch_[0:w, :, 1:1 + T], in_=ps[0:w, 0:N].rearrange("p (n t) -> p n t", n=nseq)),
                 reads=[pk], writes=['pch'])
            yield
            So.op('pool', lambda e: e.tensor_copy(out=pch_[0:w, :, 0], in_=shst[0:w, l, blk, :]), reads=[skey], writes=['pch'])
            yield
            So.op('pool', lambda e: e.tensor_copy(out=shst[0:w, l, blk, :], in_=pch_[0:w, :, T]), reads=['pch'], writes=[skey])
            yield
            So.op('pool', lambda e: e.tensor_tensor(out=dd_[0:w, :].rearrange("p (n t) -> p n t", n=nseq), in0=pch_[0:w, :, 0:T],
                                                  in1=pch_[0:w, :, 1:1 + T], op=ALU.subtract), reads=['pch'], writes=['rw_d'])
            yield
            if func is None:
                So.op('dve', lambda e: e.scalar_tensor_tensor(out=out_ap.rearrange("p (n t) -> p n t", n=nseq), in0=dd_[0:w, :].rearrange("p (n t) -> p n t", n=nseq),
                                                             scalar=self.p_mu[0:w, l, blk:blk + 1], in1=pch_[0:w, :, 1:1 + T],
                                                             op0=ALU.mult, op1=ALU.add), reads=['rw_d', 'pch', 'p_mu'], writes=[out_key])
                yield
            else:
                So.op('dve', lambda e: e.scalar_tensor_tensor(out=dd_[0:w, :].rearrange("p (n t) -> p n t", n=nseq), in0=dd_[0:w, :].rearrange("p (n t) -> p n t", n=nseq),
                                                             scalar=self.p_mu[0:w, l, blk:blk + 1], in1=pch_[0:w, :, 1:1 + T],
                                                             op0=ALU.mult, op1=ALU.add), reads=['rw_d', 'pch', 'p_mu'], writes=['rw_d'])
                yield
                func(dd_)

        wb, wk = self.wtile('w_in', l, [(OFF_RW + 3072, 288)])
        for bi, (c0, w) in enumerate(((0, 128), (128, 128), (256, 32))):
            ps, pk = self.ps()
            for k in range(8):
                self.mm(ps[0:w, 0:N], wb[:, k, c0:c0 + w], self.xT[:, k, 0:N], k == 0, k == 7, reads=wk + xkeys, writes=[pk])
            if bi == 0:
                def f0(d_):
                    S.op('act', lambda e: e.activation(out=lwla[0:64, :], in_=d_[0:64, :], func=AF.Tanh), reads=['rw_d'], writes=['lwla'])
                    S.op('act', lambda e: e.copy(out=lwla[64:128, :], in_=d_[64:128, :]), reads=['rw_d'], writes=['lwla'])
                for _ in shifted(ps, pk, 24, 128, None, None, f0):
                    pass
            else:
                def f1(d_, bi=bi, w=w):
                    S.op('act', lambda e: e.activation(out=siglg[0:w, bi - 1, :], in_=d_[0:w, :], func=AF.Sigmoid), reads=['rw_d'], writes=['siglg'])
                for _ in shifted(ps, pk, 24 + bi, w, None, None, f1):
                    pass

        if not is_s:
            for rnd in range(4):
                self.rwkv_round(ph, l, stream, g, nt, N, rnd, shifted, lwla, siglg, lnw, lnb, xkeys, None)
        else:
            ytok = ph.T("s2_ytok", [128, D], F32)
            nats = []
            self.pre_nat = [(ph.T("w_vT%d" % r_, [128, 2, N], BF16), ph.T("w_pT%d" % r_, [128, 2, N], BF16),
                             ph.T("w_ggT%d" % r_, [128, 2, N], F32)) for r_ in range(4)]
            phA = Phase(self)
            V6 = phA.T("s2_V6", [128, 6, D], F32)
            for rnd in range(4):
                nats.append(self.rwkv_round(ph, l, stream, g, nt, N, rnd, shifted, lwla, siglg, lnw, lnb, xkeys, V6))
            for n in range(NSEQ_S):
                dstn = self.scr_v[l, n].rearrange("j hh q t d -> q t (j hh) d")
                for q in range(6):
                    S.dma('act', dstn[q], V6[n * 8:(n + 1) * 8, q, :].rearrange("t (jh d) -> t jh d", d=64),
                          reads=[('s2_V6', jx) for jx in range(8)], writes=[('scr_v', l, n, q)])
            phA.names.append('s2_V6')
            phA.close()
            svkeys = [('scr_v', l, n, q) for n in range(NSEQ_S) for q in range(6)]
            for hh in range(2):
                phB = Phase(self)
                Sst = phB.T("s2_S", [128, 64, 64], F32)
                tmp = phB.T("s2_tmp", [128, 64, 64], F32)
                vec = phB.T("s2_vec", [128, 6, T_S, 64], F32)
                yS = phB.T("s2_y", [128, T_S, 64], F32)
                sa = phB.T("s2_sa", [128, 64], F32)
                S.dma('act', Sst[:].rearrange("p v k -> p (v k)"),
                      self.I['swkv'][l].rearrange("n (j hh) v k -> hh (n j) (v k)", hh=2)[hh], writes=['s2_S'])
                S.dma('act', vec[:].rearrange("p q t d -> p (q t d)"),
                      self.scr_v[l].rearrange("n j hh q t d -> hh (n j) (q t d)")[hh], reads=svkeys, writes=['s2_vec'])
                bv = lambda q, t: vec[:, q, t, :].unsqueeze(1).to_broadcast([128, 64, 64])
                bo = lambda ap: ap.unsqueeze(2).to_broadcast([128, 64, 64])
                for t in range(T_S):
                    S.op('dve', lambda e: e.tensor_tensor(out=tmp[:], in0=Sst[:], in1=bv(4, t), op=ALU.mult), reads=['s2_S', 's2_vec'], writes=['s2_tmp'])
                    S.op('dve', lambda e: e.tensor_reduce(out=sa[:], in_=tmp[:], axis=AX.X, op=ALU.add), reads=['s2_tmp'], writes=['s2_sa'])
                    S.op('pool', lambda e: e.tensor_tensor(out=Sst[:], in0=Sst[:], in1=bv(1, t), op=ALU.mult), reads=['s2_S', 's2_vec'], writes=['s2_S'])
                    S.op('dve', lambda e: e.tensor_tensor(out=tmp[:], in0=bo(sa[:]), in1=bv(5, t), op=ALU.mult), reads=['s2_sa', 's2_vec'], writes=['s2_tmp'])
                    S.op('pool', lambda e: e.tensor_tensor(out=Sst[:], in0=Sst[:], in1=tmp[:], op=ALU.subtract), reads=['s2_S', 's2_tmp'], writes=['s2_S'])
                    S.op('dve', lambda e: e.tensor_tensor(out=tmp[:], in0=bo(vec[:, 3, t, :]), in1=bv(2, t), op=ALU.mult), reads=['s2_vec'], writes=['s2_tmp'])
                    S.op('pool', lambda e: e.tensor_tensor(out=Sst[:], in0=Sst[:], in1=tmp[:], op=ALU.add), reads=['s2_S', 's2_tmp'], writes=['s2_S'])
                    S.op('dve', lambda e: e.tensor_tensor(out=tmp[:], in0=Sst[:], in1=bv(0, t), op=ALU.mult), reads=['s2_S', 's2_vec'], writes=['s2_tmp'])
                    S.op('dve', lambda e: e.tensor_reduce(out=yS[:, t, :], in_=tmp[:], axis=AX.X, op=ALU.add), reads=['s2_tmp'], writes=['s2_y'])
                S.dma('act', self.O['swkv_o'][l].rearrange("n (j hh) v k -> hh (n j) (v k)", hh=2)[hh], Sst[:].rearrange("p v k -> p (v k)"),
                      reads=['s2_S'], writes=[('o_swkv', l, hh)])
                S.dma('act', self.scr_y[l].rearrange("n j hh t v -> hh (n j) (t v)")[hh], yS[:].rearrange("p t v -> p (t v)"),
                      reads=['s2_y'], writes=[('scr_y', l, hh)])
                phB.close()
            for n in range(NSEQ_S):
                S.dma('act', ytok[n * 8:(n + 1) * 8, :].rearrange("t (jh v) -> t jh v", v=64),
                      self.scr_y[l, n].rearrange("j hh t v -> t (j hh) v"), reads=[('scr_y', l, 0), ('scr_y', l, 1)], writes=[('s2_ytok', n)])
            for rnd in range(4):
                for _ in self.wkv_tile(ph, l, 2 * rnd, 0, nats[rnd], lnw, lnb,
                                       s2_y=(ytok[:, rnd * 256:(rnd + 1) * 256], [('s2_ytok', n) for n in range(NSEQ_S)])):
                    pass
        self.dump('mixed' if not is_s else 's_mixed', self.mixacc[:, :, 0:N], [128, 8, N], [('mixacc', j) for j in range(8)])
        ph.names.append('s2_ytok')
        ph.close()

    def rwkv_round(self, ph0, l, stream, g, nt, N, rnd, shifted, lwla, siglg, lnw, lnb, xkeys, V6):
        S, I = self.S, self.I
        is_s = stream == 'S'
        ph = Phase(self)
        j0 = 2 * rnd
        sf = str(rnd) if is_s else ''
        pho = ph0 if is_s else ph
        aT = ph.T("w_aT", [128, 2, N], BF16)
        rT = ph.T("w_rT", [128, 2, N], BF16)
        bT = ph.T("w_bT", [128, 2, N], BF16)
        kT = ph.T("w_kT", [128, 2, N], BF16)
        if is_s:
            vT, pT, ggT = self.pre_nat[rnd]
        else:
            vT = ph.T("w_vT", [128, 2, N], BF16)
            pT = ph.T("w_pT", [128, 2, N], BF16)
            ggT = ph.T("w_ggT", [128, 2, N], F32)
        gC = ph.T("w_gC", [128, 2, 4], F32)
        php = Phase(self)
        TK = ('w_r32', 'w_k32', 'w_v32', 'w_lw32', 'w_cl', 'w_a32', 'w_kk32', 'w_t32', 'w_tb', 'w_eI', 'w_eN', 'pch', 'rw_d')
        nseq_, T_ = (NSEQ_S, T_S) if is_s else (1, 512)
        pbufs = []
        for jj_ in range(2):
            d_ = {}
            for nm_ in ('r32', 'k32', 'v32', 'lw32', 'cl', 'a32', 'kk32', 't32', 'eI', 'eN'):
                d_[nm_] = php.T("w_%s_%d" % (nm_, jj_), [128, N], F32)
            d_['tb'] = php.T("w_tb_%d" % jj_, [128, N], BF16)
            d_['pch'] = php.T("pch_%d" % jj_, [128, nseq_, 1 + T_], F32)
            d_['dd'] = php.T("rw_d_%d" % jj_, [128, N], F32)
            pbufs.append(d_)
        php.names += [k_ + '_0' for k_ in TK] + [k_ + '_1' for k_ in TK]
        c0 = OFF_RW + j0 * 128
        wb, wk = self.wtile('w_in', l, [(c0, 256), (c0 + D, 256)])
        wb2, wk2 = self.wtile('w_in', l, [(c0 + 2 * D, 256), (OFF_GATE + 2 * D + j0 * 128, 256)])
        S0 = S

        def pair(jj):
            j = j0 + jj
            B_ = pbufs[jj]
            r32, k32, v32, lw32, cl, a32, kk32, t32, eI, eN, tb = (B_[k_] for k_ in ('r32', 'k32', 'v32', 'lw32', 'cl', 'a32', 'kk32', 't32', 'eI', 'eN', 'tb'))
            km = lambda ks: [((k_ + '_%d' % jj) if (isinstance(k_, str) and k_ in TK) else k_) for k_ in ks]

            class _S:
                @staticmethod
                def op(e, fn_, reads=(), writes=()):
                    return S0.op(e, fn_, reads=km(reads), writes=km(writes))
            S = _S()
            mm0 = self.mm

            def MM(out, lhsT, rhs, start, stop, reads, writes):
                return mm0(out, lhsT, rhs, start, stop, reads=km(reads), writes=km(writes))
            for qi, (dst, dkey, wbx, wkx, coff, blk) in enumerate(((r32, 'w_r32', wb, wk, jj * 128, j),
                                                                  (k32, 'w_k32', wb, wk, 256 + jj * 128, 8 + j),
                                                                  (v32, 'w_v32', wb2, wk2, jj * 128, 16 + j))):
                ps, pk = self.ps()
                for k in range(8):
                    MM(ps[:, 0:N], wbx[:, k, coff:coff + 128], self.xT[:, k, 0:N], k == 0, k == 7, reads=wkx + xkeys, writes=[pk])
                yield from shifted(ps, pk, blk, 128, dst[:], dkey, bufs=(B_['pch'], B_['dd']), S_=S)
            ps, pk = self.ps()
            MM(ps[:, 0:N], self.wlora_l[0:64, j * 128:(j + 1) * 128], lwla[0:64, :], True, True, reads=['wlora', 'lwla'], writes=[pk])
            S.op('act', lambda e: e.activation(out=lw32[:], in_=ps[:, 0:N], func=AF.Sigmoid, bias=self.p_w0[:, l, j:j + 1], scale=1.0),
                 reads=[pk, 'rw_w0'], writes=['w_lw32'])
            yield
            S.op('pool', lambda e: e.tensor_scalar(out=lw32[:], in0=lw32[:], scalar1=-math.exp(-0.5), scalar2=None, op0=ALU.mult),
                 reads=['w_lw32'], writes=['w_lw32'])
            yield
            ps, pk = self.ps()
            MM(ps[:, 0:N], self.wlora_l[64:128, j * 128:(j + 1) * 128], lwla[64:128, :], True, True, reads=['wlora', 'lwla'], writes=[pk])
            S.op('act', lambda e: e.activation(out=a32[:], in_=ps[:, 0:N], func=AF.Sigmoid, bias=self.p_a0[:, l, j:j + 1], scale=1.0),
                 reads=[pk, 'rw_a0'], writes=['w_a32'])
            yield
            S.op('act', lambda e: e.activation(out=kk32[:], in_=k32[:], func=AF.Copy, scale=self.p_kk[:, l, j:j + 1]),
                 reads=['w_k32', 'rw_kk'], writes=['w_kk32'])
            yield
            S.op('act', lambda e: e.activation(out=tb[:], in_=kk32[:], func=AF.Square), reads=['w_kk32'], writes=['w_tb'])
            yield
            ps, pk = self.ps()
            MM(ps[:, 0:N], self.blkb[:, :], tb[:], True, True, reads=['blkb', 'w_tb'], writes=[pk])
            S.op('act', lambda e: e.activation(out=t32[:], in_=ps[:, 0:N], func=AF.Sqrt), reads=[pk], writes=['w_t32'])
            yield
            S.op('dve', lambda e: e.tensor_scalar(out=t32[:], in0=t32[:], scalar1=1e-12, scalar2=None, op0=ALU.max), reads=['w_t32'], writes=['w_t32'])
            yield
            S.op('dve', lambda e: e.reciprocal(out=t32[:], in_=t32[:]), reads=['w_t32'], writes=['w_t32'])
            yield
            S.op('dve', lambda e: e.tensor_tensor(out=kk32[:], in0=kk32[:], in1=t32[:], op=ALU.mult), reads=['w_kk32', 'w_t32'], writes=['w_kk32'])
            yield
            S.op('dve', lambda e: e.tensor_scalar(out=t32[:], in0=a32[:], scalar1=self.p_ka[:, l, j:j + 1], scalar2=self.p_omka[:, l, j:j + 1],
                                                  op0=ALU.mult, op1=ALU.add), reads=['w_a32', 'rw_ka', 'p_omka'], writes=['w_t32'])
            yield
            S.op('pool', lambda e: e.tensor_tensor(out=k32[:], in0=k32[:], in1=t32[:], op=ALU.mult), reads=['w_k32', 'w_t32'], writes=['w_k32'])
            yield
            if is_s:
                S.op('pool', lambda e: e.tensor_tensor(out=t32[:], in0=r32[:], in1=k32[:], op=ALU.mult), reads=['w_r32', 'w_k32'], writes=['w_t32'])
                yield
                S.op('act', lambda e: e.activation(out=pT[:, jj, :], in_=t32[:], func=AF.Copy, scale=self.p_rk[:, l, j:j + 1]),
                     reads=['w_t32', 'rw_rk'], writes=['w_pT' + sf])
                yield
                S.op('act', lambda e: e.copy(out=vT[:, jj, :], in_=v32[:]), reads=['w_v32'], writes=['w_vT' + sf])
                yield
                S.op('act', lambda e: e.activation(out=eI[:], in_=lw32[:], func=AF.Exp), reads=['w_lw32'], writes=['w_eI'])
                yield
                S.op('dve', lambda e: e.tensor_tensor(out=eN[:], in0=kk32[:], in1=a32[:], op=ALU.mult), reads=['w_kk32', 'w_a32'], writes=['w_eN'])
                yield
                psa, pka = self.ps()
                for q, (src, skey) in enumerate(((r32, 'w_r32'), (eI, 'w_eI'), (k32, 'w_k32'), (v32, 'w_v32'))):
                    S.op('pe', lambda e: e.transpose(out=psa[:, q * 128:(q + 1) * 128], in_=src[:, :], identity=self.identf[:]),
                         reads=[skey, 'identf'], writes=[pka])
                    yield
                psb_, pkb = self.ps()
                for q, (src, skey) in enumerate(((kk32, 'w_kk32'), (eN, 'w_eN'))):
                    S.op('pe', lambda e: e.transpose(out=psb_[:, q * 128:(q + 1) * 128], in_=src[:, :], identity=self.identf[:]),
                         reads=[skey, 'identf'], writes=[pkb])
                    yield
                S.op('act', lambda e: e.copy(out=V6[:, 0:4, j * 128:(j + 1) * 128], in_=psa[:, :].rearrange("p (q c) -> p q c", q=4)),
                     reads=[pka], writes=[('s2_V6', j)])
                yield
                S.op('dve', lambda e: e.tensor_copy(out=V6[:, 4:6, j * 128:(j + 1) * 128], in_=psb_[:, 0:256].rearrange("p (q c) -> p q c", q=2)),
                     reads=[pkb], writes=[('s2_V6', j)])
                yield
            if not is_s:
                S.op('dve', lambda e: e.tensor_tensor_scan(out=cl[:], data0=self.rmaskP[:, 0:N], data1=lw32[:], initial=0.0, op0=ALU.mult, op1=ALU.add),
                     reads=['rmaskP', 'w_lw32'], writes=['w_cl'])
                yield
                S.op('act', lambda e: e.activation(out=eI[:], in_=cl[:], func=AF.Exp), reads=['w_cl'], writes=['w_eI'])
                yield
                S.op('act', lambda e: e.activation(out=eN[:], in_=cl[:], func=AF.Exp, scale=-1.0), reads=['w_cl'], writes=['w_eN'])
                yield
                S.op('pool', lambda e: e.tensor_copy(out=gC[:, jj, :], in_=eI[:, 127:N:128]), reads=['w_eI'], writes=['w_gC'])
                yield
                S.op('dve', lambda e: e.tensor_tensor(out=rT[:, jj, :], in0=r32[:], in1=eI[:], op=ALU.mult), reads=['w_r32', 'w_eI'], writes=['w_rT'])
                yield
                S.op('pool', lambda e: e.tensor_tensor(out=t32[:], in0=r32[:], in1=k32[:], op=ALU.mult), reads=['w_r32', 'w_k32'], writes=['w_t32'])
                yield
                S.op('act', lambda e: e.activation(out=pT[:, jj, :], in_=t32[:], func=AF.Copy, scale=self.p_rk[:, l, j:j + 1]),
                     reads=['w_t32', 'rw_rk'], writes=['w_pT' + sf])
                yield
                S.op('dve', lambda e: e.tensor_tensor(out=kT[:, jj, :], in0=k32[:], in1=eN[:], op=ALU.mult), reads=['w_k32', 'w_eN'], writes=['w_kT'])
                yield
                S.op('pool', lambda e: e.tensor_tensor(out=t32[:], in0=kk32[:], in1=a32[:], op=ALU.mult), reads=['w_kk32', 'w_a32'], writes=['w_t32'])
                yield
                S.op('dve', lambda e: e.tensor_tensor(out=bT[:, jj, :], in0=t32[:], in1=eN[:], op=ALU.mult), reads=['w_t32', 'w_eN'], writes=['w_bT'])
                yield
                S.op('pool', lambda e: e.tensor_tensor(out=t32[:], in0=cl[:], in1=lw32[:], op=ALU.subtract), reads=['w_cl', 'w_lw32'], writes=['w_t32'])
                yield
                S.op('act', lambda e: e.activation(out=t32[:], in_=t32[:], func=AF.Exp), reads=['w_t32'], writes=['w_t32'])
                yield
                S.op('dve', lambda e: e.scalar_tensor_tensor(out=aT[:, jj, :], in0=kk32[:], scalar=-1.0, in1=t32[:], op0=ALU.mult, op1=ALU.mult),
                     reads=['w_kk32', 'w_t32'], writes=['w_aT'])
                yield
                S.op('act', lambda e: e.copy(out=vT[:, jj, :], in_=v32[:]), reads=['w_v32'], writes=['w_vT' + sf])
                yield
            ps, pk = self.ps()
            MM(ps[:, 0:N], self.wgup_l[:, 0, j * 128:(j + 1) * 128], siglg[:, 0, :], True, False, reads=['wgup', 'siglg'], writes=[pk])
            MM(ps[:, 0:N], self.wgup_l[0:32, 1, j * 128:(j + 1) * 128], siglg[0:32, 1, :], False, True, reads=['wgup', 'siglg'], writes=[pk])
            ps2, pk2 = self.ps()
            for k in range(8):
                MM(ps2[:, 0:N], wb2[:, k, 256 + jj * 128:256 + (jj + 1) * 128], self.xT[:, k, 0:N], k == 0, k == 7, reads=wk2 + xkeys, writes=[pk2])
            S.op('act', lambda e: e.activation(out=t32[:], in_=ps2[:, 0:N], func=AF.Sigmoid), reads=[pk2], writes=['w_t32'])
            yield
            S.op('dve', lambda e: e.tensor_tensor(out=ggT[:, jj, :], in0=ps[:, 0:N], in1=t32[:], op=ALU.mult), reads=[pk, 'w_t32'], writes=['w_ggT' + sf])
            yield

        gens_ = [pair(0), pair(1)]
        if is_s:
            for g_ in gens_:
                for _ in g_:
                    pass
        else:
            self.interleave(gens_)
        php.close()
        nat = dict(aT=aT, rT=rT, bT=bT, kT=kT, vT=vT, pT=pT, ggT=ggT, gC=gC, sf=sf)
        if not is_s:
            for t0_ in range(0, nt, 2):
                self.interleave_staged([self.wkv_tile(ph, l, j0, t0_ + i_, nat, lnw, lnb, slot=i_) for i_ in range(2)])
        ph.close()
        return nat

    def wkv_tile(self, ph0, l, j0, t, nat, lnw, lnb, s2_y=None, slot=0):
        S0 = self.S
        sl = '_s%d' % slot
        km = lambda ks: [((k + sl) if isinstance(k, str) and k.startswith('k_') else k) for k in ks]

        class _S:
            @staticmethod
            def op(e, fn_, reads=(), writes=()):
                return S0.op(e, fn_, reads=km(reads), writes=km(writes))
        S = _S()
        mm0 = self.mm

        def MM(out, lhsT, rhs, start, stop, reads, writes):
            return mm0(out, lhsT, rhs, start, stop, reads=km(reads), writes=km(writes))

        class _Reuse:
            def T(_s, name, shape, dt):
                name = name + sl
                cache = ph0.__dict__.setdefault('_tile_cache', {})
                if name not in cache:
                    cache[name] = ph0.T(name, shape, dt)
                return cache[name]

            def close(_s):
                pass
        ph = _Reuse()
        aT, rT, bT, kT, vT, pT, ggT, gC = (nat[k] for k in ('aT', 'rT', 'bT', 'kT', 'vT', 'pT', 'ggT', 'gC'))
        sf = nat['sf']
        cs = slice(t * 128, (t + 1) * 128)
        tok = ph.T("k_tok", [128, 4, 2, 128], BF16)
        ps, pk = self.ps()
        psb = ps[:].bitcast(BF16)
        for qi, (src, skey) in enumerate(((vT, 'w_vT' + sf), (bT, 'w_bT'), (kT, 'w_kT'), (pT, 'w_pT' + sf))):
            if s2_y is not None and qi in (1, 2):
                src, skey = vT, 'w_vT' + sf
            for jj in range(2):
                S.op('pe', lambda e: e.transpose(out=psb[:, (qi * 2 + jj) * 128:(qi * 2 + jj + 1) * 128], in_=src[:, jj, cs],
                                                 identity=self.identb[:]), reads=[skey, 'identb'], writes=[pk])
                yield
        S.op('act', lambda e: e.copy(out=tok[:].rearrange("p q j c -> p (q j c)"), in_=psb), reads=[pk], writes=['k_tok'])
        yield
        if s2_y is None:
            blks = {}
            for nm, (src, skey) in (('a', (aT, 'w_aT')), ('r', (rT, 'w_rT')), ('b', (bT, 'w_bT'))):
                isnew = ("k_blk" + nm + sl) not in ph0.__dict__.get('_tile_cache', {})
                bl = ph.T("k_blk" + nm, [128, 2, 2, 128], BF16)
                if isnew:
                    S.op('pool', lambda e: e.memset(bl[:], 0.0), writes=['k_blk' + nm])
                    yield
                for hh in range(2):
                    eng = 'act' if hh else 'pool'
                    if eng == 'act':
                        S.op('act', lambda e: e.copy(out=bl[hh * 64:(hh + 1) * 64, :, hh, :], in_=src[hh * 64:(hh + 1) * 64, :, cs]),
                             reads=[skey, 'k_blk' + nm], writes=['k_blk' + nm])
                        yield
                    else:
                        S.op('pool', lambda e: e.tensor_copy(out=bl[hh * 64:(hh + 1) * 64, :, hh, :], in_=src[hh * 64:(hh + 1) * 64, :, cs]),
                             reads=[skey, 'k_blk' + nm], writes=['k_blk' + nm])
                        yield
                blks[nm] = bl
            mats = {}
            specs = (('AabT', bT, 'w_bT', 'a', 0), ('ArbT', bT, 'w_bT', 'r', 1), ('AakT', kT, 'w_kT', 'a', 0),
                     ('ArkT', kT, 'w_kT', 'r', 1), ('Aab', aT, 'w_aT', 'b', 2))
            for mi, (nm, lsrc, lkey, rb, mk) in enumerate(specs):
                ps, pk = self.ps()
                for jj in range(2):
                    MM(ps[:, jj * 256:(jj + 1) * 256], lsrc[:, jj, cs], blks[rb][:, jj, :, :].rearrange("p h c -> p (h c)"),
                            True, True, reads=[lkey, 'k_blk' + rb], writes=[pk])
                m = ph.T("k_" + nm, [128, 4, 128], BF16)
                S.op('dve', lambda e: e.tensor_tensor(out=m[:].rearrange("p h c -> p (h c)"), in0=ps[:, :], in1=self.maskb[:, mk, :], op=ALU.mult),
                     reads=[pk, 'maskb'], writes=['k_' + nm])
                yield
                mats[nm] = m
            Pm = [ph.T("k_P%d" % i, [128, 4, 128], BF16) for i in range(2)]
            Am = [ph.T("k_A%d" % i, [128, 4, 128], BF16) for i in range(2)]
            Bm = [ph.T("k_B%d" % i, [128, 4, 128], BF16) for i in range(2)]
            fl = lambda m_: m_[:].rearrange("p h c -> p (h c)")
            S.op('pool', lambda e: e.tensor_tensor(out=fl(Pm[0]), in0=fl(mats['AabT']), in1=self.maskb[:, 3, :], op=ALU.add),
                 reads=['k_AabT', 'maskb'], writes=['k_P0'])
            yield
            Acur, Akey, Bcur, Bkey = mats['Aab'], 'k_Aab', mats['AabT'], 'k_AabT'
            pcur = 0
            pend = None

            def p_update(Ap, Apk, pcur_):
                ps_, pk_ = self.ps()
                for h in range(4):
                    MM(ps_[:, h * 128:(h + 1) * 128], Ap[:, h, :], Pm[pcur_][:, h, :], True, True, reads=[Apk, 'k_P%d' % pcur_], writes=[pk_])
                S.op('dve', lambda e: e.tensor_tensor(out=fl(Pm[1 - pcur_]), in0=ps_[:, :], in1=fl(Pm[pcur_]), op=ALU.add),
                     reads=[pk_, 'k_P%d' % pcur_], writes=['k_P%d' % (1 - pcur_)])
                return 1 - pcur_
            Am3 = Am + [ph.T("k_A2", [128, 4, 128], BF16)]
            for step in range(6):
                An, Ank = Am3[step % 3], 'k_A%d' % (step % 3)
                psA, pkA = self.ps()
                for h in range(4):
                    MM(psA[:, h * 128:(h + 1) * 128], Bcur[:, h, :], Acur[:, h, :], True, True, reads=[Akey, Bkey], writes=[pkA])
                if step < 5:
                    Bn, Bnk = Bm[step % 2], 'k_B%d' % (step % 2)
                    psB, pkB = self.ps()
                    for h in range(4):
                        MM(psB[:, h * 128:(h + 1) * 128], Acur[:, h, :], Bcur[:, h, :], True, True, reads=[Akey, Bkey], writes=[pkB])
                S.op('act', lambda e: e.copy(out=fl(An), in_=psA[:, :]), reads=[pkA], writes=[Ank])
                yield
                if step < 5:
                    S.op('act', lambda e: e.copy(out=fl(Bn), in_=psB[:, :]), reads=[pkB], writes=[Bnk])
                    yield
                if pend is not None:
                    pcur = p_update(pend[0], pend[1], pcur)
                pend = (An, Ank)
                Acur, Akey = An, Ank
                if step < 5:
                    Bcur, Bkey = Bn, Bnk
            pcur = p_update(pend[0], pend[1], pcur)
            Pf, Pkey = Pm[pcur], 'k_P%d' % pcur
            yield 'CHAIN'
            Hb = self.Hbf
            Xs = ph.T("k_Xs", [128, 4, 64], BF16)
            Us = ph.T("k_Us", [128, 4, 64], BF16)
            psX, pkX = self.ps()
            for jj in range(2):
                j = j0 + jj
                MM(psX[:, jj * 128:(jj + 1) * 128], aT[:, jj, cs], Hb[:, l, j, :], True, False, reads=['w_aT', ('Hbf', l, j)], writes=[pkX])
                for hh in range(2):
                    h = jj * 2 + hh
                    MM(psX[:, h * 64:(h + 1) * 64], mats['AakT'][:, h, :], tok[:, 0, jj, hh * 64:(hh + 1) * 64], False, hh == 1,
                            reads=['k_AakT', 'k_tok'], writes=[pkX])
            S.op('act', lambda e: e.copy(out=Xs[:].rearrange("p h c -> p (h c)"), in_=psX[:, 0:256]), reads=[pkX], writes=['k_Xs'])
            yield
            psU, pkU = self.ps()
            for h in range(4):
                MM(psU[:, h * 64:(h + 1) * 64], Pf[:, h, :], Xs[:, h, :], True, True, reads=[Pkey, 'k_Xs'], writes=[pkU])
            S.op('act', lambda e: e.copy(out=Us[:].rearrange("p h c -> p (h c)"), in_=psU[:, 0:256]), reads=[pkU], writes=['k_Us'])
            yield
            psY, pkY = self.ps()
            for jj in range(2):
                j = j0 + jj
                MM(psY[:, jj * 128:(jj + 1) * 128], rT[:, jj, cs], Hb[:, l, j, :], True, False, reads=['w_rT', ('Hbf', l, j)], writes=[pkY])
                for hh in range(2):
                    h = jj * 2 + hh
                    MM(psY[:, h * 64:(h + 1) * 64], mats['ArbT'][:, h, :], Us[:, h, :], False, False, reads=['k_ArbT', 'k_Us'], writes=[pkY])
                    MM(psY[:, h * 64:(h + 1) * 64], mats['ArkT'][:, h, :], tok[:, 0, jj, hh * 64:(hh + 1) * 64], False, hh == 1,
                            reads=['k_ArkT', 'k_tok'], writes=[pkY])
            psH, pkH = self.ps()
            for jj in range(2):
                MM(psH[:, jj * 128:(jj + 1) * 128], tok[:, 1, jj, :], Us[:, 2 * jj:2 * jj + 2, :].rearrange("p h c -> p (h c)"), True, False,
                        reads=['k_tok', 'k_Us'], writes=[pkH])
                MM(psH[:, jj * 128:(jj + 1) * 128], tok[:, 2, jj, :], tok[:, 0, jj, :], False, True, reads=['k_tok'], writes=[pkH])
            hd = ph.T("k_hd", [128, 256], F32)
            S.op('dve', lambda e: e.tensor_tensor(out=hd[:], in0=psH[:, 0:256], in1=self.blkf[:], op=ALU.mult), reads=[pkH, 'blkf'], writes=['k_hd'])
            yield
            for jj in range(2):
                j = j0 + jj
                S.op('pool', lambda e: e.tensor_tensor(out=hd[:, jj * 128:(jj + 1) * 128], in0=hd[:, jj * 128:(jj + 1) * 128], in1=self.H32[:, l, j, :], op=ALU.add),
                     reads=['k_hd', ('H32', l, j)], writes=['k_hd'])
                yield
                S.op('act', lambda e: e.activation(out=self.H32[:, l, j, :], in_=hd[:, jj * 128:(jj + 1) * 128], func=AF.Copy, scale=gC[:, jj, t:t + 1]),
                     reads=['k_hd', 'w_gC'], writes=[('H32', l, j)])
                yield
                S.op('act', lambda e: e.copy(out=Hb[:, l, j, :], in_=self.H32[:, l, j, :]), reads=[('H32', l, j)], writes=[('Hbf', l, j)])
                yield
        yield 'POST'
        y = ph.T("k_y", [128, 4, 64], F32)
        sq = ph.T("k_sq", [128, 4, 64], F32)
        st = ph.T("k_st", [128, 4], F32)
        st2 = ph.T("k_st2", [128, 4], F32)
        if s2_y is None:
            S.op('act', lambda e: e.copy(out=y[:].rearrange("p h c -> p (h c)"), in_=psY[:, 0:256]), reads=[pkY], writes=['k_y'])
            yield
        else:
            S.op('act', lambda e: e.copy(out=y[:].rearrange("p h c -> p (h c)"), in_=s2_y[0]), reads=s2_y[1], writes=['k_y'])
            yield
        S.op('dve', lambda e: e.tensor_reduce(out=st[:], in_=y[:], axis=AX.X, op=ALU.add), reads=['k_y'], writes=['k_st'])
        yield
        S.op('dve', lambda e: e.tensor_scalar(out=st[:], in0=st[:], scalar1=-1.0 / 64, scalar2=None, op0=ALU.mult), reads=['k_st'], writes=['k_st'])
        yield
        S.op('dve', lambda e: e.tensor_tensor(out=y[:], in0=y[:], in1=st[:].unsqueeze(2).to_broadcast([128, 4, 64]), op=ALU.add),
             reads=['k_y', 'k_st'], writes=['k_y'])
        yield
        S.op('pool', lambda e: e.tensor_tensor(out=sq[:], in0=y[:], in1=y[:], op=ALU.mult), reads=['k_y'], writes=['k_sq'])
        yield
        S.op('dve', lambda e: e.tensor_reduce(out=st2[:], in_=sq[:], axis=AX.X, op=ALU.add), reads=['k_sq'], writes=['k_st2'])
        yield
        S.op('act', lambda e: e.activation(out=st2[:], in_=st2[:], func=AF.Sqrt, bias=self.epsln[:, 1:2], scale=1.0 / 64),
             reads=['k_st2', 'epsln'], writes=['k_st2'])
        yield
        S.op('dve', lambda e: e.reciprocal(out=st2[:], in_=st2[:]), reads=['k_st2'], writes=['k_st2'])
        yield
        S.op('dve', lambda e: e.tensor_tensor(out=y[:], in0=y[:], in1=st2[:].unsqueeze(2).to_broadcast([128, 4, 64]), op=ALU.mult),
             reads=['k_y', 'k_st2'], writes=['k_y'])
        yield
        yf = y[:].rearrange("p h c -> p (h c)")
        fsl = slice(j0 * 128, j0 * 128 + 256)
        S.op('pool', lambda e: e.tensor_tensor(out=yf, in0=yf, in1=lnw[:, fsl], op=ALU.mult), reads=['k_y', 'lnw_t'], writes=['k_y'])
        yield
        S.op('pool', lambda e: e.tensor_tensor(out=yf, in0=yf, in1=lnb[:, fsl], op=ALU.add), reads=['k_y', 'lnb_t'], writes=['k_y'])
        yield
        S.op('dve', lambda e: e.tensor_reduce(out=st[:], in_=tok[:, 3, :, :].rearrange("p j (h c) -> p (j h) c", h=2), axis=AX.X, op=ALU.add),
             reads=['k_tok'], writes=['k_st'])
        yield
        S.op('dve', lambda e: e.tensor_tensor(out=sq[:], in0=tok[:, 0, :, :].rearrange("p j (h c) -> p (j h) c", h=2),
                                              in1=st[:].unsqueeze(2).to_broadcast([128, 4, 64]), op=ALU.mult), reads=['k_tok', 'k_st'], writes=['k_sq'])
        yield
        S.op('pool', lambda e: e.tensor_tensor(out=y[:], in0=y[:], in1=sq[:], op=ALU.add), reads=['k_y', 'k_sq'], writes=['k_y'])
        yield
        psT, pkT = self.ps()
        for jj in range(2):
            S.op('pe', lambda e: e.transpose(out=psT[:, jj * 128:(jj + 1) * 128], in_=yf[:, jj * 128:(jj + 1) * 128], identity=self.identf[:]),
                 reads=['k_y', 'identf'], writes=[pkT])
            yield
        og = ph.T("k_og", [128, 2, 128], F32)
        S.op('dve', lambda e: e.tensor_tensor(out=og[:], in0=psT[:, 0:256].rearrange("p (j c) -> p j c", j=2), in1=ggT[:, :, cs], op=ALU.mult),
             reads=[pkT, 'w_ggT' + sf], writes=['k_og'])
        yield
        S.op('pool', lambda e: e.tensor_tensor(out=self.mixacc[:, j0:j0 + 2, cs], in0=self.mixacc[:, j0:j0 + 2, cs], in1=og[:], op=ALU.add),
             reads=['k_og', ('mixacc', j0), ('mixacc', j0 + 1)], writes=[('mixacc', j0), ('mixacc', j0 + 1)])
        yield
        ph.close()

    def attention(self, l, stream, g, ngroups, nt, N):
        S, I = self.S, self.I
        ph = Phase(self)
        qT = ph.T("qT", [128, 8, N], BF16)
        ET = [ph.T("ET%d" % b, [128, 512], BF16) for b in range(2)]
        sb = ph.T("sbt", [128, 512], F32)
        den = ph.T("den", [128, 256], F32)
        osb = ph.T("osb", [128, 256], F32)
        if stream == 'P':
            self.biasP = ph.T("biasP", [128, 2, NH, 128], F32)
            S.dma('act', self.biasP[:], I['c_biasP'], writes=['biasP'])
        kv32_pre = ph.T("kv32", [128, 512], F32)
        if stream == 'P':
            kT = self.kT_p[l]
            vtok = self.vtok_p[l]
            kkey, vkey = ('kTp', l), ('vtokp', l)
        else:
            kT = ph.T("kTs", [128, KVH, 128 + 128], BF16)
            vtok = ph.T("vtoks", [128, 2, 256], BF16)
            kkey, vkey = 'kTs', 'vtoks'
        xkeys = [('xT', t) for t in range(nt)]
        for half in range(2):
            wb, wk = self.wtile('w_in', l, [(half * 512, 512)])
            for b in range(4):
                ps, pk = self.ps()
                for k in range(8):
                    self.mm(ps[:, 0:N], wb[:, k, b * 128:(b + 1) * 128], self.xT[:, k, 0:N], k == 0, k == 7,
                            reads=wk + xkeys, writes=[pk])
                self.evac_copy('act' if b % 2 else 'dve', qT[:, half * 4 + b, :], ps[:, 0:N], [pk], ['qT'], scale=0.125)
        self.chk('q')
        i = self.wn
        self.wn = (self.wn + 1) % NWBUF
        wb = self.wbuf[i]
        wk = [('wbuf', i)]
        srcw = self.W['w_in'][l]
        kv = srcw[:, OFF_K:OFF_K + 256].rearrange("(k p) (c d) -> p k c d", p=128, d=64)
        wbv = wb[:, :, :].rearrange("p k (c two d) -> p k c two d", two=2, d=64)
        for dup in range(2):
            for k in range(8):
                S.dma('sp', wbv[:, k, :, dup, :], kv[:, k, :, :],
                      writes=wk if (dup == 0 and k == 0) else [('wbufx', i, dup, k)])
        wk_all = wk + [('wbufx', i, dup, k) for dup in range(2) for k in range(8) if not (dup == 0 and k == 0)]
        for c in range(KVH):
            ps, pk = self.ps()
            for k in range(8):
                self.mm(ps[:, 0:N], wb[:, k, c * 128:(c + 1) * 128], self.xT[:, k, 0:N], k == 0, k == 7,
                        reads=wk_all + xkeys, writes=[pk])
            self.evac_copy('act' if c % 2 else 'dve', kT[:, c, 128:128 + N], ps[:, 0:N], [pk], [kkey])
        self.chk('kdup')
        wb, wk = self.wtile('w_in', l, [(OFF_K, 512)])
        last_tile_needs_k = (stream == 'S') or (g == ngroups - 1)
        kv32 = None
        import os
        kvvar = os.environ.get('KVVAR', '')
        for t in range(nt):
            if kvvar == 'dmaonly':
                break
            ps, pk = self.ps()
            for k in range(8):
                self.mm(ps[:, :], self.xT[:, k, t * 128:(t + 1) * 128], wb[:, k, :], k == 0, k == 7,
                        reads=wk + [('xT', t)], writes=[pk])
            if kvvar == 'mmonly':
                continue
            if kvvar != 'noact' and not (kvvar == 'noact3' and t == 3):
                self.evac_copy('act', vtok[:, 1 + t, :], ps[:, 256:512], [pk], [vkey])
            if last_tile_needs_k and t == nt - 1 and kvvar != 'nokv32':
                kv32 = kv32_pre
                self.evac_copy('act', kv32[:], ps[:, :], [pk], ['kv32'])
        import os
        if kv32 is not None and stream == 'P' and not os.environ.get('NOKVOUT'):
            S.dma('act', self.O['pk'][l], kv32[:, 0:256], reads=['kv32'], writes=[('o_pk', l)])
            S.dma('act', self.O['pv'][l], kv32[:, 256:512], reads=['kv32'], writes=[('o_pv', l)])
        self.chk('kvtok')
        for half in range(2):
            wb, wk = self.wtile('w_in', l, [(OFF_GATE + half * 512, 512)])
            for b in range(4):
                ps, pk = self.ps()
                for k in range(8):
                    self.mm(ps[:, 0:N], wb[:, k, b * 128:(b + 1) * 128], self.xT[:, k, 0:N], k == 0, k == 7,
                            reads=wk + xkeys, writes=[pk])
                j = half * 4 + b
                S.op('act', lambda e: e.activation(out=self.mixacc[:, j, 0:N], in_=ps[:, 0:N], func=AF.Sigmoid),
                     reads=[pk], writes=[('mixacc', j)])
        self.chk('proj')
        if stream == 'P':
            sbc = [ph.T("sbt_c%d" % c, [128, 512], F32) for c in range(KVH)]
            ETcp = [[ph.T("ET_c%d_%d" % (c, p_), [128, 512], BF16) for p_ in range(2)] for c in range(KVH)]
            denc = [ph.T("den_c%d" % c, [128, 256], F32) for c in range(KVH)]
            osbc = [ph.T("osb_c%d" % c, [128, 256], F32) for c in range(KVH)]
            ph.names += ['sbt_c', 'ET_c', 'den_c', 'osb_c']

            def attn_c(t, c, blks):
                nb = len(blks)
                sb_, den_, osb_ = sbc[c], denc[c], osbc[c]
                for par in range(2):
                    ps, pk = self.ps()
                    hp = par * 64
                    for bi, blk in enumerate(blks):
                        koff = 128 * t if blk == 0 else 128 * (t + 1)
                        for i2 in range(2):
                            h = 4 * c + 2 * i2 + par
                            col = (bi * 2 + i2) * 128
                            self.mm(ps[:, col:col + 128], kT[hp:hp + 64, c, koff:koff + 128],
                                    qT[hp:hp + 64, h // 2, t * 128:(t + 1) * 128], True, True,
                                    reads=[kkey, 'qT'], writes=[pk])
                    yield
                    W_ = nb * 256
                    bias_ap = self.biasP[:, blks[0]:blks[0] + nb, 4 * c + par:4 * c + 4:2, :]
                    S.op('dve', lambda e: e.tensor_tensor(out=sb_[:, 0:W_].rearrange("p (b i q) -> p b i q", b=nb, i=2),
                                                          in0=ps[:, 0:W_].rearrange("p (b i q) -> p b i q", b=nb, i=2),
                                                          in1=bias_ap, op=ALU.add),
                         reads=[pk, 'biasP'], writes=[('sbt_c', c)])
                    yield
                    S.op('act', lambda e: e.activation(out=ETcp[c][par][:, 0:W_], in_=sb_[:, 0:W_], func=AF.Exp),
                         reads=[('sbt_c', c)], writes=[('ET_c', c, par)])
                    yield
                ps, pk = self.ps()
                for i4 in range(4):
                    par = i4 % 2
                    hp = par * 64
                    i2 = i4 // 2
                    jj = i4 // 2
                    for part in range(2):
                        for bi, blk in enumerate(blks):
                            slot = t if blk == 0 else t + 1
                            lhsT = vtok[:, slot, c * 64:(c + 1) * 64] if part == 0 else self.onesb[:, :]
                            col = (bi * 2 + i2) * 128
                            self.mm(ps[hp:hp + 64, part * 256 + jj * 128: part * 256 + (jj + 1) * 128], lhsT,
                                    ETcp[c][par][:, col:col + 128], bi == 0, bi == nb - 1,
                                    reads=[vkey, 'onesb', ('ET_c', c, par)], writes=[pk])
                yield
                for jj in range(2):
                    j = 2 * c + jj
                    S.op('dve', lambda e: e.tensor_scalar(out=den_[:, jj * 128:(jj + 1) * 128],
                                                          in0=ps[:, 256 + jj * 128:256 + (jj + 1) * 128],
                                                          scalar1=self.esink[:, l, j:j + 1], scalar2=None, op0=ALU.add),
                         reads=[pk, 'esink'], writes=[('den_c', c)])
                yield
                S.op('dve', lambda e: e.reciprocal(out=den_[:], in_=den_[:]), reads=[('den_c', c)], writes=[('den_c', c)])
                yield
                S.op('dve', lambda e: e.tensor_tensor(out=osb_[:], in0=ps[:, 0:256], in1=den_[:], op=ALU.mult),
                     reads=[pk, ('den_c', c)], writes=[('osb_c', c)])
                yield
                S.op('pool', lambda e: e.tensor_tensor(out=self.mixacc[:, 2 * c:2 * c + 2, t * 128:(t + 1) * 128],
                                                       in0=self.mixacc[:, 2 * c:2 * c + 2, t * 128:(t + 1) * 128],
                                                       in1=osb_[:].rearrange("p (j q) -> p j q", j=2), op=ALU.mult),
                     reads=[('osb_c', c), ('mixacc', 2 * c), ('mixacc', 2 * c + 1)],
                     writes=[('mixacc', 2 * c), ('mixacc', 2 * c + 1)])
                yield

            for t in range(nt):
                first = (g == 0 and t == 0)
                blks = [1] if first else [0, 1]
                self.interleave([attn_c(t, c, blks) for c in range(KVH)])
            S.op('pool', lambda e: e.tensor_copy(out=kT[:, :, 0:128], in_=kT[:, :, N:N + 128]), reads=[kkey], writes=[kkey])
            S.op('pool', lambda e: e.tensor_copy(out=vtok[:, 0, :], in_=vtok[:, nt, :]), reads=[vkey], writes=[vkey])
        else:
            bSn = ph.T("bSn", [128, NH, 128], F32)
            bSc = ph.T("bSc", [128, KVH, 2, 256], F32)
            S.dma('act', bSn[:], I['c_biasSn'], writes=['bSn'])
            S.dma('act', bSc[:], I['c_biasSc'], writes=['bSc'])
            ck32 = ph.T("ck32", [128, NSEQ_S, 256], F32)
            ckd = ph.T("ckd", [128, NSEQ_S, KVH, 2, 64], BF16)
            KcT = ph.T("KcT", [128, NSEQ_S, KVH, 128], BF16)
            Vc = ph.T("Vc", [128, NSEQ_S, 256], BF16)
            ETc = [ph.T("ETc%d" % b, [128, 256], BF16) for b in range(2)]
            t1 = ph.T("sa_t1", [128, 512], F32)
            S.dma('act', ck32[:], I['ck'][l].rearrange("n r c -> r n c"), writes=['ck32'])
            S.dma('act', self.O['sk'][l].rearrange("n r c -> r n c")[0:120], ck32[8:128, :, :], reads=['ck32'], writes=[('o_sk', l)])
            for dup in range(2):
                S.op('dve' if dup else 'pool', lambda e: e.tensor_copy(out=ckd[:, :, :, dup, :], in_=ck32[:].rearrange("p n (c d) -> p n c d", d=64)),
                     reads=['ck32'], writes=['ckd'])
            S.dma('act', ck32[:], I['cv'][l].rearrange("n r c -> r n c"), writes=['ck32'])
            S.dma('act', self.O['sv'][l].rearrange("n r c -> r n c")[0:120], ck32[8:128, :, :], reads=['ck32'], writes=[('o_sv', l)])
            S.op('act', lambda e: e.copy(out=Vc[:], in_=ck32[:]), reads=['ck32'], writes=['Vc'])
            for n0 in range(0, NSEQ_S, 2):
                ps, pk = self.ps()
                psb = ps[:].bitcast(BF16)
                for ni in range(2):
                    for c in range(KVH):
                        S.op('pe', lambda e: e.transpose(out=psb[:, (ni * 4 + c) * 128:(ni * 4 + c + 1) * 128],
                                                         in_=ckd[:, n0 + ni, c, :, :].rearrange("p a d -> p (a d)"), identity=self.identb[:]),
                             reads=['ckd', 'identb'], writes=[pk])
                S.op('act' if (n0 // 2) % 2 else 'dve',
                     (lambda e: e.copy(out=KcT[:, n0:n0 + 2, :, :].rearrange("p n c k -> p (n c k)"), in_=psb)) if (n0 // 2) % 2 else
                     (lambda e: e.tensor_copy(out=KcT[:, n0:n0 + 2, :, :].rearrange("p n c k -> p (n c k)"), in_=psb)),
                     reads=[pk], writes=['KcT'])
            for n in range(NSEQ_S):
                S.dma('act', self.O['sk'][l, n, 120:128, :], kv32[n * 8:(n + 1) * 8, 0:256], reads=['kv32'], writes=[('o_skn', l, n)])
                S.dma('act', self.O['sv'][l, n, 120:128, :], kv32[n * 8:(n + 1) * 8, 256:512], reads=['kv32'], writes=[('o_svn', l, n)])
            for c in range(KVH):
                for par in range(2):
                    hp = par * 64
                    ps, pk = self.ps()
                    for i2 in range(2):
                        h = 4 * c + 2 * i2 + par
                        self.mm(ps[:, i2 * 128:(i2 + 1) * 128], kT[hp:hp + 64, c, 128:256], qT[hp:hp + 64, h // 2, 0:128], True, True,
                                reads=[kkey, 'qT'], writes=[pk])
                    S.op('dve', lambda e: e.tensor_tensor(out=sb[:, 0:256].rearrange("p (i q) -> p i q", i=2),
                                                          in0=ps[:, 0:256].rearrange("p (i q) -> p i q", i=2),
                                                          in1=bSn[:, 4 * c + par:4 * c + 4:2, :], op=ALU.add), reads=[pk, 'bSn'], writes=['sbt'])
                    S.op('act', lambda e: e.activation(out=ET[par][:, 0:256], in_=sb[:, 0:256], func=AF.Exp), reads=['sbt'], writes=[('ET', par)])
                    ps, pk = self.ps()
                    for n in range(NSEQ_S):
                        for i2 in range(2):
                            h = 4 * c + 2 * i2 + par
                            col = (n * 2 + i2) * 8
                            self.mm(ps[:, col:col + 8], KcT[hp:hp + 64, n, c, :], qT[hp:hp + 64, h // 2, n * 8:(n + 1) * 8], True, True,
                                    reads=['KcT', 'qT'], writes=[pk])
                    S.op('dve', lambda e: e.tensor_tensor(out=sb[:, 256:512], in0=ps[:, 0:256], in1=bSc[:, c, par, :], op=ALU.add),
                         reads=[pk, 'bSc'], writes=['sbt'])
                    S.op('act', lambda e: e.activation(out=ETc[par][:], in_=sb[:, 256:512], func=AF.Exp), reads=['sbt'], writes=[('ETc', par)])
                ps1, pk1 = self.ps()
                ps2, pk2 = self.ps()
                for i4 in range(4):
                    par, i2 = i4 % 2, i4 // 2
                    hp = par * 64
                    for part in range(2):
                        lhs_n = vtok[:, 1, c * 64:(c + 1) * 64] if part == 0 else self.onesb[:, :]
                        self.mm(ps1[hp:hp + 64, part * 256 + i2 * 128: part * 256 + (i2 + 1) * 128], lhs_n,
                                ET[par][:, i2 * 128:(i2 + 1) * 128], True, True, reads=[vkey, 'onesb', ('ET', par)], writes=[pk1])
                        for n in range(NSEQ_S):
                            lhs_c = Vc[:, n, c * 64:(c + 1) * 64] if part == 0 else self.onesb[:, :]
                            col = part * 256 + i2 * 128 + n * 8
                            self.mm(ps2[hp:hp + 64, col:col + 8], lhs_c, ETc[par][:, (n * 2 + i2) * 8:(n * 2 + i2) * 8 + 8], True, True,
                                    reads=['Vc', 'onesb', ('ETc', par)], writes=[pk2])
                S.op('act', lambda e: e.copy(out=t1[:], in_=ps1[:, :]), reads=[pk1], writes=['sa_t1'])
                S.op('dve', lambda e: e.tensor_tensor(out=t1[:], in0=t1[:], in1=ps2[:, :], op=ALU.add), reads=['sa_t1', pk2], writes=['sa_t1'])
                for jj in range(2):
                    j = 2 * c + jj
                    S.op('dve', lambda e: e.tensor_scalar(out=den[:, jj * 128:(jj + 1) * 128], in0=t1[:, 256 + jj * 128:256 + (jj + 1) * 128],
                                                          scalar1=self.esink[:, l, j:j + 1], scalar2=None, op0=ALU.add),
                         reads=['sa_t1', 'esink'], writes=['den'])
                S.op('dve', lambda e: e.reciprocal(out=den[:], in_=den[:]), reads=['den'], writes=['den'])
                S.op('dve', lambda e: e.tensor_tensor(out=osb[:], in0=t1[:, 0:256], in1=den[:], op=ALU.mult), reads=['sa_t1', 'den'], writes=['osb'])
                S.op('pool', lambda e: e.tensor_tensor(out=self.mixacc[:, 2 * c:2 * c + 2, 0:128], in0=self.mixacc[:, 2 * c:2 * c + 2, 0:128],
                                                       in1=osb[:].rearrange("p (j q) -> p j q", j=2), op=ALU.mult),
                     reads=['osb', ('mixacc', 2 * c), ('mixacc', 2 * c + 1)], writes=[('mixacc', 2 * c), ('mixacc', 2 * c + 1)])
        self.dump('oa_g' if stream == 'P' else 's_oa_g', self.mixacc[:, :, 0:N], [128, 8, N], [('mixacc', j) for j in range(8)])
        ph.close()


_CACHE = {}


def consts():
    ident = np.eye(128, dtype=np.float32)
    slopes = 2.0 ** (-8.0 * np.arange(1, NH + 1) / NH)
    k = np.arange(128)[:, None]
    q = np.arange(128)[None, :]
    biasP = np.zeros((128, 2, NH, 128), np.float32)
    rel_prev = 128 + q - k
    rel_cur = q - k
    for h in range(NH):
        biasP[:, 0, h, :] = np.where(rel_prev < 128, -slopes[h] * rel_prev, NEG)
        biasP[:, 1, h, :] = np.where(rel_cur >= 0, -slopes[h] * rel_cur, NEG)
    kn, ks = np.arange(128)[:, None] // 8, np.arange(128)[:, None] % 8
    qn, qt = np.arange(128)[None, :] // 8, np.arange(128)[None, :] % 8
    biasSn = np.zeros((128, NH, 128), np.float32)
    for h in range(NH):
        biasSn[:, h, :] = np.where((kn == qn) & (ks <= qt), -slopes[h] * (qt - ks), NEG)
    biasSc = np.zeros((128, KVH, 2, NSEQ_S, 2, T_S), np.float32)
    j = np.arange(128)[:, None]
    t = np.arange(T_S)[None, :]
    for c in range(KVH):
        for i4 in range(4):
            h = 4 * c + i4
            biasSc[:, c, i4 % 2, :, i4 // 2, :] = np.where(j > t, -slopes[h] * (128 + t - j), NEG)[:, None, :]
    biasSc = biasSc.reshape(128, KVH, 2, 256)
    p = np.arange(128)[:, None]
    f = np.arange(128)[None, :]
    strictT = (p < f).astype(np.float32)
    inclT = (p <= f).astype(np.float32)
    strictA = (f < p).astype(np.float32)
    identb = (p == f).astype(np.float32)
    mask = np.stack([np.tile(m, (1, 4)) for m in (strictT, inclT, strictA, identb)], axis=1).astype(np.float32)
    rmask = np.ones((128, 512), np.float32)
    rmask[:, ::128] = 0.0
    blk = np.zeros((128, 128), np.float32)
    blk[:64, :64] = 1.0
    blk[64:, 64:] = 1.0
    return dict(c_ident=ident, c_biasP=biasP, c_biasSn=biasSn, c_biasSc=biasSc, c_mask=mask, c_rmask=rmask, c_blk=blk)


def get_prog(seq, **kw):
    key = (seq, tuple(sorted((k, str(v)) for k, v in kw.items())))
    if key not in _CACHE:
        kb = KB(seq, **kw)
        kb.build()
        _CACHE[key] = kb
    return _CACHE[key]


def make_in_maps(inputs, seq):
    cst = consts()
    maps = []
    f = lambda a: np.ascontiguousarray(a, dtype=np.float32)
    for c in range(NCORES):
        n0 = c * NSEQ_S
        m = {
            'xp': f(inputs['x_prompt'][c % 2, :seq]),
            'xs': f(inputs['x_sample'][n0:n0 + NSEQ_S].reshape(128, D)),
            'ck': f(inputs['cache_k'][:, n0:n0 + NSEQ_S].reshape(DEPTH, NSEQ_S, 128, 256)),
            'cv': f(inputs['cache_v'][:, n0:n0 + NSEQ_S].reshape(DEPTH, NSEQ_S, 128, 256)),
            'sconv': f(inputs['state_conv'][:, n0:n0 + NSEQ_S]),
            'srg': f(inputs['state_rglru'][:, n0:n0 + NSEQ_S]),
            'sshift': f(inputs['state_shift'][:, n0:n0 + NSEQ_S]),
            'swkv': f(inputs['state_wkv'][:, n0:n0 + NSEQ_S]),
        }
        for nm in ['w_in', 'attn_sinks', 'conv_w', 'conv_b', 'rg_wa', 'rg_ba', 'rg_wx', 'rg_bx', 'rg_lambda',
                   'rw_mu', 'rw_w0', 'rw_wup', 'rw_a0', 'rw_aup', 'rw_gup', 'rw_kk', 'rw_ka', 'rw_lnw', 'rw_lnb',
                   'w_out', 'ln1_g', 'ln1_b', 'w_gu', 'w_down', 'ln2_g', 'ln2_b']:
            m[nm] = f(inputs[nm])
        m['rw_rk'] = f(np.asarray(inputs['rw_rk']).reshape(DEPTH, D))
        m.update(cst)
        maps.append(m)
    return maps


def kernel(**inputs):
    seq = inputs['x_prompt'].shape[1]
    kb = get_prog(seq)
    maps = make_in_maps(inputs, seq)
    res = run_bass_kernel_spmd(kb.nc, maps, core_ids=list(range(NCORES)))
    R = res.results
    B = inputs['x_prompt'].shape[0]
    yp = np.stack([R[b]['yp'] for b in range(B)])
    ys = np.concatenate([R[c]['ys'].reshape(NSEQ_S, T_S, D) for c in range(NCORES)])

    def pst(nm, shp):
        return np.stack([R[b][nm] for b in range(B)], axis=1).reshape(shp)

    def sst(nm, shp):
        return np.concatenate([R[c][nm] for c in range(NCORES)], axis=1).reshape(shp)
    NS = NCORES * NSEQ_S
    outs = (yp, ys,
            pst('pk', (DEPTH, B, 128, KVH, HD)), pst('pv', (DEPTH, B, 128, KVH, HD)),
            pst('pconv', (DEPTH, B, 3, D)), pst('prg', (DEPTH, B, D)), pst('pshift', (DEPTH, B, SHW)),
            pst('pwkv', (DEPTH, B, NH, HD, HD)),
            sst('sk', (DEPTH, NS, 128, KVH, HD)), sst('sv', (DEPTH, NS, 128, KVH, HD)),
            sst('sconv_o', (DEPTH, NS, 3, D)), sst('srg_o', (DEPTH, NS, D)), sst('sshift_o', (DEPTH, NS, SHW)),
            sst('swkv_o', (DEPTH, NS, NH, HD, HD)))
    return tuple(np.ascontiguousarray(o, dtype=np.float32) for o in outs)
```

```python
import math
import numpy as np
from contextlib import ExitStack
import concourse.bass as bass
import concourse.mybir as mybir
from concourse.bass_utils import run_bass_kernel_spmd

F32 = mybir.dt.float32
BF16 = mybir.dt.bfloat16
ALU = mybir.AluOpType
AF = mybir.ActivationFunctionType
AX = mybir.AxisListType

D = 1024
NH = 16
HD = 64
KVH = 4
DFF = 2816
INC = 8992
OFF_K, OFF_V, OFF_RNN, OFF_RW, OFF_GATE = 1024, 1280, 1536, 2560, 5920
SHW = 3360
DEPTH = 2
ALPHA = (2 * DEPTH) ** 0.25
LN_EPS = 1e-5
GN_EPS = 64e-5
NEG = -30000.0
NCORES = 8
NSEQ_S = 16
T_S = 8
NWBUF = 3


class Sched:
    def __init__(self, nc, es, n_dma_sems=32):
        self.nc = nc
        self.es = es
        self.engs = {'pe': nc.tensor, 'dve': nc.vector, 'act': nc.scalar, 'pool': nc.gpsimd, 'sp': nc.sync}
        self.csem = {e: es.enter_context(nc.semaphore("cs_" + e)) for e in ('pe', 'dve', 'act', 'pool')}
        self.ccnt = {e: 0 for e in self.csem}
        self.dsem = {q: [es.enter_context(nc.semaphore("ds%s%d" % (q, i))) for i in range(n)]
                     for q, n in (('sp', 16), ('act', 12), ('pool', 4))}
        self.dcnt = {q: [0] * len(v) for q, v in self.dsem.items()}
        self.dnext = {q: 0 for q in self.dsem}
        self.known = {e: {} for e in self.engs}
        self.lastw = {}
        self.reads = {}
        self.ninst = 0
        self.psn = 0

    def _wait(self, e, tok):
        if tok is None:
            return
        sem, val, owner = tok
        if owner == 'pe' and e == 'pe':
            return
        if self.known[e].get(sem.name, 0) >= val:
            return
        self.engs[e].wait_ge(sem, val)
        self.known[e][sem.name] = val
        self.ninst += 1

    def _deps(self, e, reads, writes):
        for k in reads:
            self._wait(e, self.lastw.get(k))
        for k in writes:
            self._wait(e, self.lastw.get(k))
            for t in self.reads.get(k, ()):
                self._wait(e, t)

    def _commit(self, tok, reads, writes):
        for k in writes:
            self.lastw[k] = tok
            self.reads[k] = []
        for k in reads:
            if k in writes:
                continue
            lst = self.reads.setdefault(k, [])
            lst[:] = [t for t in lst if t[0].name != tok[0].name]
            lst.append(tok)

    def op(self, e, fn, reads=(), writes=()):
        self._deps(e, reads, writes)
        ins = fn(self.engs[e])
        self.ccnt[e] += 1
        ins.then_inc(self.csem[e], 1)
        tok = (self.csem[e], self.ccnt[e], e)
        self._commit(tok, reads, writes)
        self.ninst += 1
        return tok

    def dma(self, e, out, in_, reads=(), writes=(), **kw):
        i = self.dnext[e]
        self.dnext[e] = (i + 1) % len(self.dsem[e])
        if self.dcnt[e][i] > 0:
            self._wait(e, (self.dsem[e][i], self.dcnt[e][i], 'dma'))
        self._deps(e, reads, writes)
        ins = self.engs[e].dma_start(out=out, in_=in_, **kw)
        self.dcnt[e][i] += 16
        ins.then_inc(self.dsem[e][i], 16)
        tok = (self.dsem[e][i], self.dcnt[e][i], 'dma')
        self._commit(tok, reads, writes)
        self.ninst += 1
        return tok

    def release(self, names, engines=('pe', 'dve', 'act', 'pool', 'sp')):
        names = set(names)
        keys = [k for k in list(self.lastw.keys()) + list(self.reads.keys())
                if (k in names) or (isinstance(k, tuple) and k[0] in names)]
        for e in engines:
            for k in keys:
                self._wait(e, self.lastw.get(k))
                for t in self.reads.get(k, ()):
                    self._wait(e, t)
        for k in set(keys):
            self.lastw.pop(k, None)
            self.reads.pop(k, None)

    def finish(self, e='sp'):
        for k, t in list(self.lastw.items()):
            self._wait(e, t)


class Phase:
    def __init__(self, kb):
        self.kb = kb
        self.es = ExitStack()
        self.names = []
        kb.open_phases.append(self)

    _uid = [0]

    def T(self, name, shape, dt):
        self.names.append(name)
        Phase._uid[0] += 1
        return self.es.enter_context(self.kb.nc.sbuf_tensor("%s_%d" % (name, Phase._uid[0]), list(shape), dt))

    def close(self):
        self.kb.open_phases.remove(self)
        self.kb.S.release(self.names)
        self.es.close()


def w_in_tiles():
    t = []
    t.append(('q0', [(0, 512)]))
    t.append(('q1', [(512, 512)]))
    t.append(('kdup', None))
    t.append(('kv', [(OFF_K, 512)]))
    t.append(('gA0', [(OFF_GATE, 512)]))
    t.append(('gA1', [(OFF_GATE + 512, 512)]))
    return t


class _Stop(Exception):
    pass


class KB:
    def __init__(self, seq, depth=DEPTH, do_sample=True, debug=(), stop_after=None):
        self.seq = seq
        self.depth = depth
        self.do_sample = do_sample
        self.debug = set(debug)
        self.stop_after = stop_after
        self.dbg_outs = {}
        self.open_phases = []
        self.nc = bass.Bass("TRN2", target_bir_lowering=False)
        self.es = ExitStack()

    def chk(self, name):
        if self.stop_after == name:
            raise _Stop()

    def din(self, name, shape, dt=F32):
        return self.nc.dram_tensor(name, list(shape), dt, kind="ExternalInput").ap()

    def dout(self, name, shape, dt=F32):
        return self.nc.dram_tensor(name, list(shape), dt, kind="ExternalOutput").ap()

    def dscr(self, name, shape, dt):
        return self.nc.dram_tensor(name, list(shape), dt, kind="Internal").ap()

    def T(self, name, shape, dt):
        return self.es.enter_context(self.nc.sbuf_tensor(name, list(shape), dt))

    def ps(self):
        i = self.S.psn
        self.S.psn = (i + 1) % 8
        return self.psb[i], 'ps%d' % i

    def dump(self, name, ap, shape, key):
        if name not in self.debug:
            return
        o = self.dout("dbg_" + name, shape, ap.dtype if hasattr(ap, 'dtype') else F32)
        self.dbg_outs[name] = shape
        self.S.dma('act', o, ap, reads=key if isinstance(key, list) else [key], writes=['dbg_' + name])

    def mm(self, out, lhsT, rhs, start, stop, reads, writes):
        return self.S.op('pe', lambda e: e.matmul(out, lhsT=lhsT, rhs=rhs, start=start, stop=stop),
                         reads=reads, writes=writes)

    def evac_copy(self, eng, out, in_, reads, writes, scale=None):
        if eng == 'act':
            if scale is None:
                return self.S.op('act', lambda e: e.copy(out=out, in_=in_), reads=reads, writes=writes)
            return self.S.op('act', lambda e: e.activation(out=out, in_=in_, func=AF.Copy, scale=scale),
                             reads=reads, writes=writes)
        if scale is None:
            return self.S.op(eng, lambda e: e.tensor_copy(out=out, in_=in_), reads=reads, writes=writes)
        return self.S.op(eng, lambda e: e.tensor_scalar(out=out, in0=in_, scalar1=scale, scalar2=None, op0=ALU.mult),
                         reads=reads, writes=writes)

    def build(self):
        nc, es = self.nc, self.es
        self.S = S = Sched(nc, es)
        seq = self.seq
        L = self.depth
        I = self.I = {}
        I['xp'] = self.din('xp', [seq, D])
        I['xs'] = self.din('xs', [128, D])
        I['ck'] = self.din('ck', [DEPTH, NSEQ_S, 128, 256])
        I['cv'] = self.din('cv', [DEPTH, NSEQ_S, 128, 256])
        I['sconv'] = self.din('sconv', [DEPTH, NSEQ_S, 3, D])
        I['srg'] = self.din('srg', [DEPTH, NSEQ_S, D])
        I['sshift'] = self.din('sshift', [DEPTH, NSEQ_S, SHW])
        I['swkv'] = self.din('swkv', [DEPTH, NSEQ_S, NH, HD, HD])
        for nm, shp in [('w_in', [DEPTH, D, INC]), ('attn_sinks', [DEPTH, NH]), ('conv_w', [DEPTH, 4, D]),
                        ('conv_b', [DEPTH, D]), ('rg_wa', [DEPTH, 16, 64, 64]), ('rg_ba', [DEPTH, D]),
                        ('rg_wx', [DEPTH, 16, 64, 64]), ('rg_bx', [DEPTH, D]), ('rg_lambda', [DEPTH, D]),
                        ('rw_mu', [DEPTH, SHW]), ('rw_w0', [DEPTH, D]), ('rw_wup', [DEPTH, 64, D]),
                        ('rw_a0', [DEPTH, D]), ('rw_aup', [DEPTH, 64, D]), ('rw_gup', [DEPTH, 160, D]),
                        ('rw_kk', [DEPTH, D]), ('rw_ka', [DEPTH, D]), ('rw_rk', [DEPTH, D]),
                        ('rw_lnw', [DEPTH, D]), ('rw_lnb', [DEPTH, D]), ('w_out', [DEPTH, D, D]),
                        ('ln1_g', [DEPTH, D]), ('ln1_b', [DEPTH, D]), ('w_gu', [DEPTH, D, 2 * DFF]),
                        ('w_down', [DEPTH, DFF, D]), ('ln2_g', [DEPTH, D]), ('ln2_b', [DEPTH, D])]:
            I[nm] = self.din(nm, shp)
        I['c_ident'] = self.din('c_ident', [128, 128])
        I['c_biasP'] = self.din('c_biasP', [128, 2, NH, 128])
        I['c_biasSn'] = self.din('c_biasSn', [128, NH, 128])
        I['c_biasSc'] = self.din('c_biasSc', [128, KVH, 2, 256])
        I['c_mask'] = self.din('c_mask', [128, 4, 512])
        I['c_rmask'] = self.din('c_rmask', [128, 512])
        I['c_blk'] = self.din('c_blk', [128, 128])
        O = self.O = {}
        O['yp'] = self.dout('yp', [seq, D])
        O['ys'] = self.dout('ys', [128, D])
        O['pk'] = self.dout('pk', [DEPTH, 128, 256])
        O['pv'] = self.dout('pv', [DEPTH, 128, 256])
        O['pconv'] = self.dout('pconv', [DEPTH, 3, D])
        O['prg'] = self.dout('prg', [DEPTH, D])
        O['pshift'] = self.dout('pshift', [DEPTH, SHW])
        O['pwkv'] = self.dout('pwkv', [DEPTH, NH, HD, HD])
        O['sk'] = self.dout('sk', [DEPTH, NSEQ_S, 128, 256])
        O['sv'] = self.dout('sv', [DEPTH, NSEQ_S, 128, 256])
        O['sconv_o'] = self.dout('sconv_o', [DEPTH, NSEQ_S, 3, D])
        O['srg_o'] = self.dout('srg_o', [DEPTH, NSEQ_S, D])
        O['sshift_o'] = self.dout('sshift_o', [DEPTH, NSEQ_S, SHW])
        O['swkv_o'] = self.dout('swkv_o', [DEPTH, NSEQ_S, NH, HD, HD])
        W = self.W = {}
        W['w_in'] = self.dscr('wb_in', [DEPTH, D, INC], BF16)
        W['w_out'] = self.dscr('wb_out', [DEPTH, D, D], BF16)
        W['w_gu'] = self.dscr('wb_gu', [DEPTH, D, 2 * DFF], BF16)
        W['w_down'] = self.dscr('wb_down', [DEPTH, DFF, D], BF16)
        self.scr_wabd = self.dscr('scr_wabd', [128, DEPTH, 2, 8, 128], BF16)
        self.scr_wlora = self.dscr('scr_wlora', [128, DEPTH, D], BF16)
        self.scr_wgup = self.dscr('scr_wgup', [128, DEPTH, 2, D], BF16)
        self.scr_v = self.dscr('scr_v', [DEPTH, NSEQ_S, 8, 2, 6, T_S, 64], F32)
        self.scr_y = self.dscr('scr_y', [DEPTH, NSEQ_S, 8, 2, T_S, 64], F32)

        self.psb = [es.enter_context(nc.psum_tensor("psb%d" % i, [128, 512], F32)) for i in range(8)]
        self.wbuf = [self.T("wbuf%d" % i, [128, 8, 512], BF16) for i in range(NWBUF)]
        self.wn = 0
        self.xres = self.T("xres", [128, 4, D], F32)
        self.xT = self.T("xT", [128, 8, 512], BF16)
        self.mixacc = self.T("mixacc", [128, 8, 512], F32)
        self.identf = self.T("identf", [128, 128], F32)
        self.xb_tmp = [self.T("xb_tmp%d" % i, [128, D], BF16) for i in range(2)]
        self.identb = self.T("identb", [128, 128], BF16)
        self.onesb = self.T("onesb", [128, 64], BF16)
        self.esink = self.T("esink", [128, DEPTH, 8], F32)
        self.epsln = self.T("epsln", [128, 2], F32)
        self.p_cw = self.T("p_cw", [128, DEPTH, 4, 8], F32)
        self.p_cb = self.T("p_cb", [128, DEPTH, 8], F32)
        self.p_ba = self.T("p_ba", [128, DEPTH, 8], F32)
        self.p_bx = self.T("p_bx", [128, DEPTH, 8], F32)
        self.p_c8 = self.T("p_c8", [128, DEPTH, 8], F32)
        self.p_mu = self.T("p_mu", [128, DEPTH, 27], F32)
        self.p_w0 = self.T("p_w0", [128, DEPTH, 8], F32)
        self.p_a0 = self.T("p_a0", [128, DEPTH, 8], F32)
        self.p_kk = self.T("p_kk", [128, DEPTH, 8], F32)
        self.p_ka = self.T("p_ka", [128, DEPTH, 8], F32)
        self.p_rk = self.T("p_rk", [128, DEPTH, 8], F32)
        self.p_omka = self.T("p_omka", [128, DEPTH, 8], F32)
        self.blkb = self.T("blkb", [128, 128], BF16)
        self.blkf = self.T("blkf", [128, 256], F32)
        self.convst = {'P': self.T("convstP", [128, DEPTH, 8, 1, 3], F32), 'S': self.T("convstS", [128, DEPTH, 8, NSEQ_S, 3], F32)}
        self.hst = {'P': self.T("hstP", [128, DEPTH, 8, 1], F32), 'S': self.T("hstS", [128, DEPTH, 8, NSEQ_S], F32)}
        self.shst = {'P': self.T("shstP", [128, DEPTH, 27, 1], F32), 'S': self.T("shstS", [128, DEPTH, 27, NSEQ_S], F32)}

        self.phP = Phase(self)
        self.kT_p = [self.phP.T("kTp%d" % l, [128, KVH, 128 + 512], BF16) for l in range(L)]
        self.vtok_p = [self.phP.T("vtokp%d" % l, [128, 5, 256], BF16) for l in range(L)]
        self.H32 = self.phP.T("H32", [128, DEPTH, 8, 128], F32)
        self.Hbf = self.phP.T("Hbf", [128, DEPTH, 8, 128], BF16)
        self.maskb = self.phP.T("maskb", [128, 4, 512], BF16)
        self.rmaskP = self.phP.T("rmaskP", [128, 512], F32)
        self.phP.names = ['kTp', 'vtokp', 'H32', 'Hbf', 'maskb', 'rmaskP']
        try:
            self.main_body()
        except _Stop:
            for ph in reversed(list(self.open_phases)):
                ph.close()
        S.finish('sp')
        S.finish('act')
        self.es.close()
        return nc

    def main_body(self):
        S, I, O, L, seq = self.S, self.I, self.O, self.depth, self.seq
        self.setup()
        self.chk('setup')
        ngroups = seq // 512
        for g in range(ngroups):
            self.load_x(I['xp'][g * 512:(g + 1) * 512, :], 4)
            self.chk('loadx')
            for l in range(L):
                self.layer(l, 'P', g, ngroups)
            self.store_y(O['yp'][g * 512:(g + 1) * 512, :], 4)
        self.chk('pgroups')
        for l in range(L):
            self.prompt_state_out(l)
        self.chk('pstates')
        self.phP.close()
        if self.do_sample:
            self.load_x(I['xs'], 1)
            for l in range(L):
                self.sample_state_in(l)
            self.chk('S_in')
            for l in range(L):
                self.layer(l, 'S', 0, 1)
                self.sample_state_out(l)
            self.store_y(O['ys'], 1)

    def setup(self):
        S, I, W = self.S, self.I, self.W
        ph = Phase(self)
        st32 = [ph.T("wst32_%d" % i, [128, 2048], F32) for i in range(3)]
        st16 = [ph.T("wst16_%d" % i, [128, 2048], BF16) for i in range(3)]
        pc = 0
        for nm in ['w_in', 'w_out', 'w_gu', 'w_down']:
            src = I[nm]
            dst = W[nm]
            tot = 1
            for s_ in src.shape:
                tot *= s_
            sf = src.rearrange("l a b -> (l a b)").rearrange("(r c) -> r c", c=2048)
            df = dst.rearrange("l a b -> (l a b)").rearrange("(r c) -> r c", c=2048)
            R = tot // 2048
            r0 = 0
            while r0 < R:
                rr = min(128, R - r0)
                b = pc % 3
                S.dma('sp', st32[b][0:rr, :], sf[r0:r0 + rr, :], writes=[('wst32', b)])
                ce = ('dve', 'act', 'pool')[pc % 3]
                if ce == 'act':
                    S.op('act', lambda e: e.copy(out=st16[b][0:rr, :], in_=st32[b][0:rr, :]),
                         reads=[('wst32', b)], writes=[('wst16', b)])
                else:
                    S.op(ce, lambda e: e.tensor_copy(out=st16[b][0:rr, :], in_=st32[b][0:rr, :]),
                         reads=[('wst32', b)], writes=[('wst16', b)])
                S.dma('act', df[r0:r0 + rr, :], st16[b][0:rr, :], reads=[('wst16', b)], writes=[('W', nm, pc)])
                r0 += rr
                pc += 1
        ph.names = ['wst32', 'wst16']
        ph.close()
        for k, t in list(S.lastw.items()):
            if isinstance(k, tuple) and k[0] == 'W':
                S._wait('sp', t)
        S.dma('sp', self.identf[:], I['c_ident'], writes=['identf'])
        S.op('dve', lambda e: e.tensor_copy(out=self.identb[:], in_=self.identf[:]), reads=['identf'], writes=['identb'])
        S.op('dve', lambda e: e.memset(self.onesb[:], 1.0), writes=['onesb'])
        S.op('dve', lambda e: e.memset(self.epsln[:, 0:1], LN_EPS), writes=['epsln'])
        S.op('dve', lambda e: e.memset(self.epsln[:, 1:2], GN_EPS), writes=['epsln'])
        sk = I['attn_sinks'].rearrange("l (j two) -> two l j", two=2)
        for half in range(2):
            S.dma('sp', self.esink[half * 64:(half + 1) * 64, :, :],
                  sk[half:half + 1, :, :].broadcast_to([64, DEPTH, 8]), writes=['esink'],
                  allow_slow_non_contiguous=True)
        S.op('act', lambda e: e.activation(out=self.esink[:], in_=self.esink[:], func=AF.Exp),
             reads=['esink'], writes=['esink'])
        self.setup_params()
        for l in range(self.depth):
            S.op('dve', lambda e: e.memset(self.kT_p[l][:], 0.0), writes=[('kTp', l)])
            S.op('pool', lambda e: e.memset(self.vtok_p[l][:], 0.0), writes=[('vtokp', l)])


    def colload(self, dst, src_vec, key, nblk=8):
        self.S.dma('act', dst, src_vec.rearrange("(j p) -> p j", p=128), writes=[key], allow_slow_non_contiguous=True)

    def setup_params(self):
        S, I = self.S, self.I
        ph = Phase(self)
        for l in range(DEPTH):
            for i in range(4):
                self.colload(self.p_cw[:, l, i, :], I['conv_w'][l, i], 'p_cw')
            self.colload(self.p_cb[:, l, :], I['conv_b'][l], 'p_cb')
            self.colload(self.p_ba[:, l, :], I['rg_ba'][l], 'p_ba')
            self.colload(self.p_bx[:, l, :], I['rg_bx'][l], 'p_bx')
            self.colload(self.p_c8[:, l, :], I['rg_lambda'][l], 'p_c8')
            self.colload(self.p_mu[:, l, 0:26], I['rw_mu'][l, 0:3328], 'p_mu', 26)
            S.dma('act', self.p_mu[0:32, l, 26:27], I['rw_mu'][l, 3328:3360].rearrange("(p o) -> p o", o=1), writes=['p_mu'],
                  allow_slow_non_contiguous=True)
            for nm, t in (('rw_w0', self.p_w0), ('rw_a0', self.p_a0), ('rw_kk', self.p_kk), ('rw_ka', self.p_ka),
                          ('rw_rk', self.p_rk)):
                self.colload(t[:, l, :], I[nm][l], nm)
        c8 = self.p_c8[:].rearrange("p l j -> p (l j)")
        x = ph.T("sp_x", [128, 16], F32)
        z = ph.T("sp_z", [128, 16], F32)
        z2 = ph.T("sp_z2", [128, 16], F32)
        acc = ph.T("sp_acc", [128, 16], F32)
        S.op('act', lambda e: e.activation(out=x[:], in_=c8, func=AF.Exp, scale=-1.0), reads=['p_c8'], writes=['sp_x'])
        S.op('dve', lambda e: e.tensor_scalar(out=z[:], in0=x[:], scalar1=2.0, scalar2=None, op0=ALU.add), reads=['sp_x'], writes=['sp_z'])
        S.op('dve', lambda e: e.reciprocal(out=z[:], in_=z[:]), reads=['sp_z'], writes=['sp_z'])
        S.op('dve', lambda e: e.tensor_tensor(out=z[:], in0=z[:], in1=x[:], op=ALU.mult), reads=['sp_z', 'sp_x'], writes=['sp_z'])
        S.op('dve', lambda e: e.tensor_tensor(out=z2[:], in0=z[:], in1=z[:], op=ALU.mult), reads=['sp_z'], writes=['sp_z2'])
        S.op('dve', lambda e: e.tensor_scalar(out=acc[:], in0=z2[:], scalar1=1.0 / 9, scalar2=1.0 / 7, op0=ALU.mult, op1=ALU.add),
             reads=['sp_z2'], writes=['sp_acc'])
        for cst in (1.0 / 5, 1.0 / 3, 1.0):
            S.op('dve', lambda e: e.tensor_tensor(out=acc[:], in0=acc[:], in1=z2[:], op=ALU.mult), reads=['sp_acc', 'sp_z2'], writes=['sp_acc'])
            S.op('dve', lambda e: e.tensor_scalar(out=acc[:], in0=acc[:], scalar1=cst, scalar2=None, op0=ALU.add), reads=['sp_acc'], writes=['sp_acc'])
        S.op('dve', lambda e: e.tensor_tensor(out=acc[:], in0=acc[:], in1=z[:], op=ALU.mult), reads=['sp_acc', 'sp_z'], writes=['sp_acc'])
        S.op('dve', lambda e: e.tensor_scalar(out=c8, in0=acc[:], scalar1=-16.0, scalar2=None, op0=ALU.mult), reads=['sp_acc'], writes=['p_c8'])
        ph.close()
        ph = Phase(self)
        st = ph.T("wbd_st", [128, DEPTH, 2, 8, 128], F32)
        S.op('pool', lambda e: e.memset(st[:], 0.0), writes=['wbd_st'])
        for l in range(DEPTH):
            for gi, nm in enumerate(('rg_wa', 'rg_wx')):
                src = I[nm][l].rearrange("(j hh) i c -> hh i j c", hh=2)
                for hh in range(2):
                    S.dma('act', st[hh * 64:(hh + 1) * 64, l, gi, :, hh * 64:(hh + 1) * 64], src[hh], writes=['wbd_st'])
        wabd_t = ph.T("wabd_t", [128, DEPTH, 2, 8, 128], BF16)
        S.op('dve', lambda e: e.tensor_copy(out=wabd_t[:], in_=st[:]), reads=['wbd_st'], writes=['wabd_t'])
        S.dma('act', self.scr_wabd, wabd_t[:], reads=['wabd_t'], writes=['scr_wabd'])
        for st_ in ('P', 'S'):
            S.op('pool', lambda e: e.memset(self.convst[st_][:], 0.0), writes=[('convst', st_)])
            S.op('pool', lambda e: e.memset(self.hst[st_][:], 0.0), writes=[('hst', st_)])
            S.op('pool', lambda e: e.memset(self.shst[st_][:], 0.0), writes=[('shst', st_)])
        ph.close()
        ph = Phase(self)
        st2 = ph.T("lora_st", [128, DEPTH, D], F32)
        st3 = ph.T("gup_st", [128, DEPTH, 2, D], F32)
        S.op('pool', lambda e: e.memset(st3[:], 0.0), writes=['gup_st'])
        for l in range(DEPTH):
            S.dma('act', st2[0:64, l, :], I['rw_wup'][l], writes=['lora_st'])
            S.dma('act', st2[64:128, l, :], I['rw_aup'][l], writes=['lora_st'])
            S.dma('act', st3[:, l, 0, :], I['rw_gup'][l, 0:128, :], writes=['gup_st'])
            S.dma('act', st3[0:32, l, 1, :], I['rw_gup'][l, 128:160, :], writes=['gup_st'])
        wlora_t = ph.T("wlora_t", [128, DEPTH, D], BF16)
        wgup_t = ph.T("wgup_t", [128, DEPTH, 2, D], BF16)
        S.op('dve', lambda e: e.tensor_copy(out=wlora_t[:], in_=st2[:]), reads=['lora_st'], writes=['wlora_t'])
        S.op('dve', lambda e: e.tensor_copy(out=wgup_t[:], in_=st3[:]), reads=['gup_st'], writes=['wgup_t'])
        S.dma('act', self.scr_wlora, wlora_t[:], reads=['wlora_t'], writes=['scr_wlora'])
        S.dma('act', self.scr_wgup, wgup_t[:], reads=['wgup_t'], writes=['scr_wgup'])
        S.op('dve', lambda e: e.tensor_scalar(out=self.p_omka[:], in0=self.p_ka[:], scalar1=-1.0, scalar2=1.0, op0=ALU.mult, op1=ALU.add),
             reads=['rw_ka'], writes=['p_omka'])
        S.op('pool', lambda e: e.memset(self.H32[:], 0.0), writes=['H32'])
        S.op('pool', lambda e: e.memset(self.Hbf[:], 0.0), writes=['Hbf'])
        ph.close()
        ph = Phase(self)
        mk = ph.T("mask_st", [128, 4, 512], F32)
        S.dma('act', mk[:], I['c_mask'], writes=['mask_st'])
        S.op('dve', lambda e: e.tensor_copy(out=self.maskb[:], in_=mk[:]), reads=['mask_st'], writes=['maskb'])
        bk = ph.T("blk_st", [128, 128], F32)
        S.dma('act', bk[:], I['c_blk'], writes=['blk_st'])
        S.op('dve', lambda e: e.tensor_copy(out=self.blkb[:], in_=bk[:]), reads=['blk_st'], writes=['blkb'])
        for r2 in range(2):
            S.op('dve', lambda e: e.tensor_copy(out=self.blkf[:, r2 * 128:(r2 + 1) * 128], in_=bk[:]), reads=['blk_st'], writes=['blkf'])
        S.dma('act', self.rmaskP[:], I['c_rmask'], writes=['rmaskP'])
        ph.close()


    def prompt_state_out(self, l):
        S, O = self.S, self.O
        ph = Phase(self)
        cv, hs, sh = self.convst['P'], self.hst['P'], self.shst['P']
        self.fm_to_rows(ph, lambda blk, w: cv[0:w, l, blk, 0, :], 3, D, O['pconv'][l], [('convst', 'P')], ('o_pconv', l))
        self.fm_to_rows(ph, lambda blk, w: hs[0:w, l, blk, :], 1, D, O['prg'][l:l + 1, :], [('hst', 'P')], ('o_prg', l))
        self.fm_to_rows(ph, lambda blk, w: sh[0:w, l, blk, :], 1, SHW, O['pshift'][l:l + 1, :], [('shst', 'P')], ('o_pshift', l))
        for j in range(8):
            ps, pk = self.ps()
            S.op('pe', lambda e: e.transpose(out=ps[:, 0:128], in_=self.H32[:, l, j, :], identity=self.identf[:]),
                 reads=[('H32', l, j), 'identf'], writes=[pk])
            tmp = ph.T("pw_tmp%d" % j, [128, 128], F32)
            S.op('act', lambda e: e.copy(out=tmp[:], in_=ps[:, 0:128]), reads=[pk], writes=[('pw_tmp', j)])
            for hh in range(2):
                S.dma('act', O['pwkv'][l, 2 * j + hh], tmp[hh * 64:(hh + 1) * 64, hh * 64:(hh + 1) * 64],
                      reads=[('pw_tmp', j)], writes=[('o_pwkv', l, j, hh)])
        ph.names.append('pw_tmp')
        ph.close()

    def sample_state_in(self, l):
        S, I = self.S, self.I
        ph = Phase(self)
        cv, hs, sh = self.convst['S'], self.hst['S'], self.shst['S']
        self.rows_to_fm(ph, I['sconv'][l].rearrange("n i c -> (n i) c"), 48, D,
                        lambda b0, nb, w: cv[0:w, l, b0:b0 + nb, :, :].rearrange("p b n i -> p b (n i)"), [('convst', 'S')])
        ph.close()
        ph = Phase(self)
        self.rows_to_fm(ph, I['srg'][l], NSEQ_S, D, lambda b0, nb, w: hs[0:w, l, b0:b0 + nb, :], [('hst', 'S')])
        ph.close()
        ph = Phase(self)
        self.rows_to_fm(ph, I['sshift'][l], NSEQ_S, SHW, lambda b0, nb, w: sh[0:w, l, b0:b0 + nb, :], [('shst', 'S')])
        ph.close()

    def sample_state_out(self, l):
        S, O = self.S, self.O
        cv, hs, sh = self.convst['S'], self.hst['S'], self.shst['S']
        ph = Phase(self)
        self.fm_to_rows(ph, lambda blk, w: cv[0:w, l, blk, :, :].rearrange("p n i -> p (n i)"), 48, D,
                        O['sconv_o'][l].rearrange("n i c -> (n i) c"), [('convst', 'S')], ('o_sconv', l))
        ph.close()
        ph = Phase(self)
        self.fm_to_rows(ph, lambda blk, w: hs[0:w, l, blk, :], NSEQ_S, D, O['srg_o'][l], [('hst', 'S')], ('o_srg', l))
        ph.close()
        ph = Phase(self)
        self.fm_to_rows(ph, lambda blk, w: sh[0:w, l, blk, :], NSEQ_S, SHW, O['sshift_o'][l], [('shst', 'S')], ('o_sshift', l))
        ph.close()

    def rows_to_fm(self, ph, src_rows, R, F, dst_fn, keys_w):
        S = self.S
        tmp = ph.T("r2f_tmp", [128, F], F32)
        S.dma('act', tmp[0:R, :], src_rows, writes=['r2f_tmp'])
        nblk = (F + 127) // 128
        per = max(1, 512 // R)
        b0 = 0
        while b0 < nblk:
            nb = min(per, nblk - b0)
            ps, pk = self.ps()
            wlast = 128
            for b in range(nb):
                w = min(128, F - (b0 + b) * 128)
                wlast = w
                S.op('pe', lambda e: e.transpose(out=ps[0:w, b * R:(b + 1) * R], in_=tmp[0:R, (b0 + b) * 128:(b0 + b) * 128 + w],
                                                 identity=self.identf[0:R, 0:R]), reads=['r2f_tmp', 'identf'], writes=[pk])
            nfull = nb if wlast == 128 else nb - 1
            if nfull > 0:
                S.op('act', lambda e: e.copy(out=dst_fn(b0, nfull, 128), in_=ps[:, 0:nfull * R].rearrange("p (b r) -> p b r", r=R)),
                     reads=[pk], writes=keys_w)
            if nfull < nb:
                S.op('act', lambda e: e.copy(out=dst_fn(b0 + nfull, 1, wlast),
                                             in_=ps[0:wlast, nfull * R:(nfull + 1) * R].rearrange("p (b r) -> p b r", r=R)),
                     reads=[pk], writes=keys_w)
            b0 += nb

    def fm_to_rows(self, ph, src_fn, R, F, dst_rows, keys_r, okey):
        S = self.S
        tmp = ph.T("f2r_tmp", [128, F], F32)
        nblk = (F + 127) // 128
        b0 = 0
        while b0 < nblk:
            nb = min(4, nblk - b0)
            ps, pk = self.ps()
            tot = 0
            for b in range(nb):
                w = min(128, F - (b0 + b) * 128)
                S.op('pe', lambda e: e.transpose(out=ps[0:R, b * 128:b * 128 + w], in_=src_fn(b0 + b, w),
                                                 identity=self.identf[0:w, 0:w]), reads=keys_r + ['identf'], writes=[pk])
                tot += w
            S.op('act', lambda e: e.copy(out=tmp[0:R, b0 * 128:b0 * 128 + tot], in_=ps[0:R, 0:tot]), reads=[pk], writes=['f2r_tmp'])
            b0 += nb
        S.dma('act', dst_rows, tmp[0:R, :], reads=['f2r_tmp'], writes=[okey])

    def load_x(self, src, ntile):
        self.S.dma('act', self.xres[:, 0:ntile, :], src.rearrange("(t p) d -> p t d", p=128),
                   writes=[('xres', t) for t in range(ntile)])
        for t in range(ntile):
            self.make_xT(t)

    def store_y(self, dst, ntile):
        self.S.dma('act', dst.rearrange("(t p) d -> p t d", p=128), self.xres[:, 0:ntile, :],
                   reads=[('xres', t) for t in range(ntile)], writes=[('yout', id(dst))])

    def make_xT(self, t):
        S = self.S
        self.xbn = (getattr(self, 'xbn', 0) + 1) % 2
        xb = self.xb_tmp[self.xbn]
        xbk = ('xb_tmp', self.xbn)
        S.op('act', lambda e: e.copy(out=xb[:], in_=self.xres[:, t, :]), reads=[('xres', t)], writes=[xbk])
        ps, pk = self.ps()
        psb = ps[:].bitcast(BF16)
        for k in range(8):
            S.op('pe', lambda e: e.transpose(out=psb[:, k * 128:(k + 1) * 128], in_=xb[:, k * 128:(k + 1) * 128],
                                             identity=self.identb[:]),
                 reads=[xbk, 'identb'], writes=[pk])
        S.op('dve', lambda e: e.tensor_copy(out=self.xT[:, :, t * 128:(t + 1) * 128],
                                            in_=psb.rearrange("p (k c) -> p k c", k=8)),
             reads=[pk], writes=[('xT', t)])

    def wtile(self, wname, l, cols, rows=(0, 8)):
        S = self.S
        i = self.wn
        self.wn = (self.wn + 1) % NWBUF
        buf = self.wbuf[i]
        src = self.W[wname][l]
        k0, k1 = rows
        off = 0
        keys = []
        for ci, (c0, n) in enumerate(cols):
            key = ('wbuf', i) if ci == 0 else ('wbufx', i, ci)
            S.dma('sp', buf[:, 0:k1 - k0, off:off + n],
                  src[k0 * 128:k1 * 128, c0:c0 + n].rearrange("(k p) c -> p k c", p=128), writes=[key])
            keys.append(key)
            off += n
        return buf, keys

    def layer(self, l, stream, g, ngroups):
        S = self.S
        nt = 4 if stream == 'P' else 1
        N = nt * 128
        self.attention(l, stream, g, ngroups, nt, N)
        self.chk('attn')
        self.chk(stream + '_attn')
        self.rglru(l, stream, g, ngroups, nt, N)
        self.chk('rglru')
        self.chk(stream + '_rglru')
        self.rwkv(l, stream, g, ngroups, nt, N)
        self.chk('rwkv')
        self.chk(stream + '_rwkv')
        self.dense_tail(l, stream, nt, N)
        self.chk('dense')

    def load_ln(self, l):
        S, I = self.S, self.I
        for i, nm in enumerate(['ln1_g', 'ln1_b', 'ln2_g', 'ln2_b']):
            S.dma('act', self.lnbuf[:, i, :], I[nm][l:l + 1, :].broadcast_to([128, D]), writes=[('lnbuf', i)])

    def layer_norm_tile(self, ph, t, which):
        S = self.S
        x = self.xres[:, t, :]
        key = ('xres', t)
        cache = ph.__dict__.setdefault('_ln_cache', {})
        if t not in cache:
            cache[t] = (ph.T("lnst%d" % t, [128, 2, 6], F32), ph.T("lnmv%d" % t, [128, 2], F32), ph.T("lnrs%d" % t, [128, 1], F32))
            ph.names += ['lnst', 'lnmv', 'lnrs']
        st, mv, rs = cache[t]
        kst, kmv, krs = ('lnst', t), ('lnmv', t), ('lnrs', t)
        for hf in range(2):
            S.op('dve', lambda e: e.bn_stats(out=st[:, hf, :], in_=self.xres[:, t, hf * 512:(hf + 1) * 512]),
                 reads=[key], writes=[kst])
        yield
        S.op('dve', lambda e: e.bn_aggr(out=mv[:], in_=st[:].rearrange("p a b -> p (a b)")), reads=[kst], writes=[kmv])
        yield
        S.op('act', lambda e: e.activation(out=rs[:], in_=mv[:, 1:2], func=AF.Sqrt, bias=self.epsln[:, 0:1], scale=1.0),
             reads=[kmv, 'epsln'], writes=[krs])
        yield
        S.op('dve', lambda e: e.reciprocal(out=rs[:], in_=rs[:]), reads=[krs], writes=[krs])
        yield
        S.op('dve', lambda e: e.tensor_scalar(out=x, in0=x, scalar1=mv[:, 0:1], scalar2=rs[:, 0:1],
                                              op0=ALU.subtract, op1=ALU.mult), reads=[key, kmv, krs], writes=[key])
        yield
        S.op('pool', lambda e: e.tensor_tensor(out=x, in0=x, in1=self.lnbuf[:, 2 * which, :], op=ALU.mult),
             reads=[key, ('lnbuf', 2 * which)], writes=[key])
        yield
        S.op('pool', lambda e: e.tensor_tensor(out=x, in0=x, in1=self.lnbuf[:, 2 * which + 1, :], op=ALU.add),
             reads=[key, ('lnbuf', 2 * which + 1)], writes=[key])
        yield

    def dense_tail(self, l, stream, nt, N):
        S = self.S
        ph = Phase(self)
        self.lnbuf = ph.T("lnbuf", [128, 4, D], F32)
        self.load_ln(l)
        self.mixT = ph.T("mixT", [128, 8, N], BF16)
        for j in range(8):
            S.op('act', lambda e: e.copy(out=self.mixT[:, j, 0:N], in_=self.mixacc[:, j, 0:N]),
                 reads=[('mixacc', j)], writes=[('mixT', j)])
        mkeys = [('mixT', j) for j in range(8)]
        wbs = [self.wtile('w_out', l, [(half * 512, 512)]) for half in range(2)]
        for t in range(nt):
            for half in range(2):
                wb, wk = wbs[half]
                ps, pk = self.ps()
                for k in range(8):
                    self.mm(ps[:, :], self.mixT[:, k, t * 128:(t + 1) * 128], wb[:, k, :], k == 0, k == 7,
                            reads=wk + [('mixT', k)], writes=[pk])
                xs = self.xres[:, t, half * 512:(half + 1) * 512]
                S.op('dve', lambda e: e.scalar_tensor_tensor(out=xs, in0=xs, scalar=ALPHA, in1=ps[:, :],
                                                             op0=ALU.mult, op1=ALU.add),
                     reads=[pk, ('xres', t)], writes=[('xres', t)])
        self.interleave([self.layer_norm_tile(ph, t, 0) for t in range(nt)])
        for t in range(nt):
            self.make_xT(t)
        self.dump('x1', self.xres[:, 0:nt, :], [128, nt, D], [('xres', t) for t in range(nt)])
        hT = ph.T("hT", [128, 22, N], BF16)
        sg = ph.T("sgt", [128, N], F32)
        xkeys = [('xT', t) for t in range(nt)]
        NB = DFF // 128
        for j0 in range(0, NB, 4):
            nb = min(4, NB - j0)
            wg, wgk = self.wtile('w_gu', l, [(j0 * 128, nb * 128)])
            wu, wuk = self.wtile('w_gu', l, [(DFF + j0 * 128, nb * 128)])
            for b in range(nb):
                j = j0 + b
                psg, pgk = self.ps()
                for k in range(8):
                    self.mm(psg[:, 0:N], wg[:, k, b * 128:(b + 1) * 128], self.xT[:, k, 0:N], k == 0, k == 7,
                            reads=wgk + xkeys, writes=[pgk])
                psu, puk = self.ps()
                for k in range(8):
                    self.mm(psu[:, 0:N], wu[:, k, b * 128:(b + 1) * 128], self.xT[:, k, 0:N], k == 0, k == 7,
                            reads=wuk + xkeys, writes=[puk])
                S.op('act', lambda e: e.activation(out=sg[:], in_=psg[:, 0:N], func=AF.Silu), reads=[pgk], writes=['sgt'])
                S.op('dve', lambda e: e.tensor_tensor(out=hT[:, j, :], in0=sg[:], in1=psu[:, 0:N], op=ALU.mult),
                     reads=['sgt', puk], writes=[('hT', j)])
        for half in range(2):
            accs = [self.ps() for _ in range(nt)]
            pieces = [(0, 8), (8, 16), (16, 22)]
            for pi, (k0, k1) in enumerate(pieces):
                wb, wk = self.wtile('w_down', l, [(half * 512, 512)], rows=(k0, k1))
                for t in range(nt):
                    ps, pk = accs[t]
                    for k in range(k0, k1):
                        self.mm(ps[:, :], hT[:, k, t * 128:(t + 1) * 128], wb[:, k - k0, :], k == 0, k == NB - 1,
                                reads=wk + [('hT', k)], writes=[pk])
            for t in range(nt):
                ps, pk = accs[t]
                xs = self.xres[:, t, half * 512:(half + 1) * 512]
                S.op('dve', lambda e: e.scalar_tensor_tensor(out=xs, in0=xs, scalar=ALPHA, in1=ps[:, :],
                                                             op0=ALU.mult, op1=ALU.add),
                     reads=[pk, ('xres', t)], writes=[('xres', t)])
        self.interleave([self.layer_norm_tile(ph, t, 1) for t in range(nt)])
        for t in range(nt):
            if l < self.depth - 1:
                self.make_xT(t)
        self.dump('x2', self.xres[:, 0:nt, :], [128, nt, D], [('xres', t) for t in range(nt)])
        ph.close()

    def interleave(self, gens):
        gens = list(gens)
        while gens:
            for g_ in list(gens):
                try:
                    next(g_)
                except StopIteration:
                    gens.remove(g_)

    def interleave_staged(self, gens):
        gens = list(gens)
        waiting = []
        active = list(gens)
        while active:
            for g_ in list(active):
                try:
                    r = next(g_)
                except StopIteration:
                    active.remove(g_)
                    continue
                if r == 'CHAIN':
                    active.remove(g_)
                    waiting.append(g_)
        waiting.sort(key=lambda g_: gens.index(g_))
        post = []
        for g_ in waiting:
            while True:
                try:
                    r = next(g_)
                except StopIteration:
                    break
                if r == 'POST':
                    post.append(g_)
                    break
        self.interleave(post)

    def rglru(self, l, stream, g, ngroups, nt, N):
        S, I = self.S, self.I
        ph = Phase(self)
        nseq, T = (1, 512) if stream == 'P' else (NSEQ_S, T_S)
        convst, hst = self.convst[stream], self.hst[stream]
        ckey, hkey = ('convst', stream), ('hst', stream)
        xkeys = [('xT', t) for t in range(nt)]
        NW = 4
        xrh = [ph.T("xrh%d" % i, [128, nseq, 3 + T], F32) for i in range(NW)]
        xc = [ph.T("xc%d" % i, [128, nseq, T], F32) for i in range(NW)]
        xcb = [ph.T("xcb%d" % i, [128, N], BF16) for i in range(NW)]
        rr = [ph.T("rg_r%d" % i, [128, N], F32) for i in range(NW)]
        ii = [ph.T("rg_i%d" % i, [128, N], F32) for i in range(NW)]
        aa = [ph.T("rg_a%d" % i, [128, nseq, T], F32) for i in range(NW)]
        uu = [ph.T("rg_u%d" % i, [128, nseq, T], F32) for i in range(NW)]
        hh = ph.T("rg_h", [128, 4, nseq, T], F32)
        ph.names += ['xrh', 'xc', 'xcb', 'rg_r', 'rg_i', 'rg_a', 'rg_u']
        wa_l = ph.T("wa_bd", [128, 2, 8, 128], BF16)
        S.dma('act', wa_l[:], self.scr_wabd[:, l], reads=['scr_wabd'], writes=['wa_bd'])
        flat = lambda t3: t3[:].rearrange("p n t -> p (n t)")

        def block(half, b, wb, wk, wg, wgk):
            j = half * 4 + b
            K_ = lambda nm: (nm, b)
            ps, pk = self.ps()
            for k in range(8):
                self.mm(ps[:, 0:N], wb[:, k, b * 128:(b + 1) * 128], self.xT[:, k, 0:N], k == 0, k == 7,
                        reads=wk + xkeys, writes=[pk])
            yield
            S.op('act', lambda e: e.copy(out=xrh[b][:, :, 3:3 + T], in_=ps[:, 0:N].rearrange("p (n t) -> p n t", n=nseq)),
                 reads=[pk], writes=[K_('xrh')])
            S.op('pool', lambda e: e.tensor_copy(out=xrh[b][:, :, 0:3], in_=convst[:, l, j, :, :]), reads=[ckey], writes=[K_('xrh')])
            yield
            cw = self.p_cw
            S.op('dve', lambda e: e.tensor_scalar(out=xc[b][:], in0=xrh[b][:, :, 0:T], scalar1=cw[:, l, 0, j:j + 1],
                                                  scalar2=self.p_cb[:, l, j:j + 1], op0=ALU.mult, op1=ALU.add),
                 reads=[K_('xrh'), 'p_cw', 'p_cb'], writes=[K_('xc')])
            yield
            for i in range(1, 4):
                S.op('dve', lambda e: e.scalar_tensor_tensor(out=xc[b][:], in0=xrh[b][:, :, i:i + T], scalar=cw[:, l, i, j:j + 1],
                                                             in1=xc[b][:], op0=ALU.mult, op1=ALU.add),
                     reads=[K_('xrh'), 'p_cw', K_('xc')], writes=[K_('xc')])
                yield
            S.op('pool', lambda e: e.tensor_copy(out=convst[:, l, j, :, :], in_=xrh[b][:, :, T:T + 3]), reads=[K_('xrh')], writes=[ckey])
            S.op('act', lambda e: e.copy(out=xcb[b][:], in_=flat(xc[b])), reads=[K_('xc')], writes=[K_('xcb')])
            yield
            pss = []
            for gi in range(2):
                ps2, pk2 = self.ps()
                self.mm(ps2[:, 0:N], wa_l[:, gi, j, :], xcb[b][:], True, True, reads=['wa_bd', K_('xcb')], writes=[pk2])
                pss.append((ps2, pk2))
            yield
            for gi, (dst, dk, bias) in enumerate(((rr[b], K_('rg_r'), self.p_ba), (ii[b], K_('rg_i'), self.p_bx))):
                ps2, pk2 = pss[gi]
                S.op('act', lambda e: e.activation(out=dst[:], in_=ps2[:, 0:N], func=AF.Sigmoid, bias=bias[:, l, j:j + 1], scale=1.0),
                     reads=[pk2, 'p_ba', 'p_bx'], writes=[dk])
            yield
            S.op('act', lambda e: e.activation(out=flat(aa[b]), in_=rr[b][:], func=AF.Exp, scale=self.p_c8[:, l, j:j + 1]),
                 reads=[K_('rg_r'), 'p_c8'], writes=[K_('rg_a')])
            S.op('pool', lambda e: e.tensor_tensor(out=ii[b][:], in0=ii[b][:], in1=flat(xc[b]), op=ALU.mult), reads=[K_('rg_i'), K_('xc')], writes=[K_('rg_i')])
            yield
            S.op('dve', lambda e: e.tensor_tensor(out=rr[b][:], in0=flat(aa[b]), in1=flat(aa[b]), op=ALU.mult), reads=[K_('rg_a')], writes=[K_('rg_r')])
            yield
            S.op('dve', lambda e: e.tensor_scalar(out=rr[b][:], in0=rr[b][:], scalar1=-1.0, scalar2=1.0, op0=ALU.mult, op1=ALU.add),
                 reads=[K_('rg_r')], writes=[K_('rg_r')])
            yield
            S.op('dve', lambda e: e.tensor_scalar(out=rr[b][:], in0=rr[b][:], scalar1=1e-30, scalar2=None, op0=ALU.max), reads=[K_('rg_r')], writes=[K_('rg_r')])
            yield
            S.op('act', lambda e: e.activation(out=rr[b][:], in_=rr[b][:], func=AF.Sqrt), reads=[K_('rg_r')], writes=[K_('rg_r')])
            yield
            S.op('dve', lambda e: e.tensor_tensor(out=flat(uu[b]), in0=ii[b][:], in1=rr[b][:], op=ALU.mult), reads=[K_('rg_i'), K_('rg_r')], writes=[K_('rg_u')])
            yield
            for n in range(nseq):
                S.op('dve', lambda e: e.tensor_tensor_scan(out=hh[:, b, n, :], data0=aa[b][:, n, :], data1=uu[b][:, n, :],
                                                           initial=hst[:, l, j, n:n + 1], op0=ALU.mult, op1=ALU.add),
                     reads=[K_('rg_a'), K_('rg_u'), hkey], writes=[('rg_h', b)])
            yield
            S.op('pool', lambda e: e.tensor_copy(out=hst[:, l, j, :], in_=hh[:, b, :, T - 1]), reads=[('rg_h', b)], writes=[hkey])
            ps, pk = self.ps()
            for k in range(8):
                self.mm(ps[:, 0:N], wg[:, k, b * 128:(b + 1) * 128], self.xT[:, k, 0:N], k == 0, k == 7,
                        reads=wgk + xkeys, writes=[pk])
            yield
            gbt = ii[b]
            S.op('act', lambda e: e.activation(out=gbt[:], in_=ps[:, 0:N], func=AF.Sigmoid), reads=[pk], writes=[K_('rg_i')])
            yield
            S.op('dve', lambda e: e.tensor_tensor(out=gbt[:], in0=gbt[:], in1=hh[:, b, :, :].rearrange("p n t -> p (n t)"), op=ALU.mult),
                 reads=[K_('rg_i'), ('rg_h', b)], writes=[K_('rg_i')])
            yield
            S.op('pool', lambda e: e.tensor_tensor(out=self.mixacc[:, j, 0:N], in0=self.mixacc[:, j, 0:N], in1=gbt[:], op=ALU.add),
                 reads=[K_('rg_i'), ('mixacc', j)], writes=[('mixacc', j)])
            yield

        for half in range(2):
            wb, wk = self.wtile('w_in', l, [(OFF_RNN + half * 512, 512)])
            wg, wgk = self.wtile('w_in', l, [(OFF_GATE + D + half * 512, 512)])
            self.interleave([block(half, b, wb, wk, wg, wgk) for b in range(4)])
        self.dump('mixAB' if stream == 'P' else 's_mixAB', self.mixacc[:, :, 0:N], [128, 8, N], [('mixacc', j) for j in range(8)])
        ph.close()

    def rwkv(self, l, stream, g, ngroups, nt, N):
        S, I = self.S, self.I
        ph = Phase(self)
        is_s = stream == 'S'
        nseq, T = (NSEQ_S, T_S) if is_s else (1, 512)
        shst = self.shst[stream]
        skey = ('shst', stream)
        xkeys = [('xT', t) for t in range(nt)]
        pch = ph.T("pch", [128, nseq, 1 + T], F32)
        dd = ph.T("rw_d", [128, N], F32)
        lwla = ph.T("lwla", [128, N], BF16)
        siglg = ph.T("siglg", [128, 2, N], BF16)
        self.wlora_l = ph.T("wlora", [128, D], BF16)
        self.wgup_l = ph.T("wgup", [128, 2, D], BF16)
        S.dma('act', self.wlora_l[:], self.scr_wlora[:, l], reads=['scr_wlora'], writes=['wlora'])
        S.dma('act', self.wgup_l[:], self.scr_wgup[:, l], reads=['scr_wgup'], writes=['wgup'])
        lnw = ph.T("lnw_t", [128, D], F32)
        lnb = ph.T("lnb_t", [128, D], F32)
        S.dma('act', lnw[:], I['rw_lnw'][l:l + 1, :].broadcast_to([128, D]), writes=['lnw_t'])
        S.dma('act', lnb[:], I['rw_lnb'][l:l + 1, :].broadcast_to([128, D]), writes=['lnb_t'])
        flat = lambda t3: t3[:].rearrange("p n t -> p (n t)")

        def shifted(ps, pk, blk, w, out_ap, out_key, func=None, bufs=None, S_=None):
            pch_, dd_ = bufs if bufs is not None else (pch, dd)
            So = S_ if S_ is not None else S
            So.op('act', lambda e: e.copy(out=pch_[0:w, :, 1:1 + T], in_=ps[0:w, 0:N].rearrange("p (n t) -> p n t", n=nseq)),
                 reads=[pk], writes=['pch'])
            yield
            So.op('pool', lambda e: e.tensor_copy(out=pch_[0:w, :, 0], in_=shst[0:w, l, blk, :]), reads=[skey], writes=['pch'])
            yield
            So.op('pool', lambda e: e.tensor_copy(out=shst[0:w, l, blk, :], in_=pch_[0:w, :, T]), reads=['pch'], writes=[skey])
            yield
            So.op('pool', lambda e: e.tensor_tensor(out=dd_[0:w, :].rearrange("p (n t) -> p n t", n=nseq), in0=pch_[0:w, :, 0:T],
                                                  in1=pch_[0:w, :, 1:1 + T], op=ALU.subtract), reads=['pch'], writes=['rw_d'])
            yield
            if func is None:
                So.op('dve', lambda e: e.scalar_tensor_tensor(out=out_ap.rearrange("p (n t) -> p n t", n=nseq), in0=dd_[0:w, :].rearrange("p (n t) -> p n t", n=nseq),
                                                             scalar=self.p_mu[0:w, l, blk:blk + 1], in1=pch_[0:w, :, 1:1 + T],
                                                             op0=ALU.mult, op1=ALU.add), reads=['rw_d', 'pch', 'p_mu'], writes=[out_key])
                yield
            else:
                So.op('dve', lambda e: e.scalar_tensor_tensor(out=dd_[0:w, :].rearrange("p (n t) -> p n t", n=nseq), in0=dd_[0:w, :].rearrange("p (n t) -> p n t", n=nseq),
                                                             scalar=self.p_mu[0:w, l, blk:blk + 1], in1=pch_[0:w, :, 1:1 + T],
                                                             op0=ALU.mult, op1=ALU.add), reads=['rw_d', 'pch', 'p_mu'], writes=['rw_d'])
                yield
                func(dd_)

        wb, wk = self.wtile('w_in', l, [(OFF_RW + 3072, 288)])
        for bi, (c0, w) in enumerate(((0, 128), (128, 128), (256, 32))):
            ps, pk = self.ps()
            for k in range(8):
                self.mm(ps[0:w, 0:N], wb[:, k, c0:c0 + w], self.xT[:, k, 0:N], k == 0, k == 7, reads=wk + xkeys, writes=[pk])
            if bi == 0:
                def f0(d_):
                    S.op('act', lambda e: e.activation(out=lwla[0:64, :], in_=d_[0:64, :], func=AF.Tanh), reads=['rw_d'], writes=['lwla'])
                    S.op('act', lambda e: e.copy(out=lwla[64:128, :], in_=d_[64:128, :]), reads=['rw_d'], writes=['lwla'])
                for _ in shifted(ps, pk, 24, 128, None, None, f0):
                    pass
            else:
                def f1(d_, bi=bi, w=w):
                    S.op('act', lambda e: e.activation(out=siglg[0:w, bi - 1, :], in_=d_[0:w, :], func=AF.Sigmoid), reads=['rw_d'], writes=['siglg'])
                for _ in shifted(ps, pk, 24 + bi, w, None, None, f1):
                    pass

        if not is_s:
            for rnd in range(4):
                self.rwkv_round(ph, l, stream, g, nt, N, rnd, shifted, lwla, siglg, lnw, lnb, xkeys, None)
        else:
            ytok = ph.T("s2_ytok", [128, D], F32)
            nats = []
            self.pre_nat = [(ph.T("w_vT%d" % r_, [128, 2, N], BF16), ph.T("w_pT%d" % r_, [128, 2, N], BF16),
                             ph.T("w_ggT%d" % r_, [128, 2, N], F32)) for r_ in range(4)]
            phA = Phase(self)
            V6 = phA.T("s2_V6", [128, 6, D], F32)
            for rnd in range(4):
                nats.append(self.rwkv_round(ph, l, stream, g, nt, N, rnd, shifted, lwla, siglg, lnw, lnb, xkeys, V6))
            for n in range(NSEQ_S):
                dstn = self.scr_v[l, n].rearrange("j hh q t d -> q t (j hh) d")
                for q in range(6):
                    S.dma('act', dstn[q], V6[n * 8:(n + 1) * 8, q, :].rearrange("t (jh d) -> t jh d", d=64),
                          reads=[('s2_V6', jx) for jx in range(8)], writes=[('scr_v', l, n, q)])
            phA.names.append('s2_V6')
            phA.close()
            svkeys = [('scr_v', l, n, q) for n in range(NSEQ_S) for q in range(6)]
            for hh in range(2):
                phB = Phase(self)
                Sst = phB.T("s2_S", [128, 64, 64], F32)
                tmp = phB.T("s2_tmp", [128, 64, 64], F32)
                vec = phB.T("s2_vec", [128, 6, T_S, 64], F32)
                yS = phB.T("s2_y", [128, T_S, 64], F32)
                sa = phB.T("s2_sa", [128, 64], F32)
                S.dma('act', Sst[:].rearrange("p v k -> p (v k)"),
                      self.I['swkv'][l].rearrange("n (j hh) v k -> hh (n j) (v k)", hh=2)[hh], writes=['s2_S'])
                S.dma('act', vec[:].rearrange("p q t d -> p (q t d)"),
                      self.scr_v[l].rearrange("n j hh q t d -> hh (n j) (q t d)")[hh], reads=svkeys, writes=['s2_vec'])
                bv = lambda q, t: vec[:, q, t, :].unsqueeze(1).to_broadcast([128, 64, 64])
                bo = lambda ap: ap.unsqueeze(2).to_broadcast([128, 64, 64])
                for t in range(T_S):
                    S.op('dve', lambda e: e.tensor_tensor(out=tmp[:], in0=Sst[:], in1=bv(4, t), op=ALU.mult), reads=['s2_S', 's2_vec'], writes=['s2_tmp'])
                    S.op('dve', lambda e: e.tensor_reduce(out=sa[:], in_=tmp[:], axis=AX.X, op=ALU.add), reads=['s2_tmp'], writes=['s2_sa'])
                    S.op('pool', lambda e: e.tensor_tensor(out=Sst[:], in0=Sst[:], in1=bv(1, t), op=ALU.mult), reads=['s2_S', 's2_vec'], writes=['s2_S'])
                    S.op('dve', lambda e: e.tensor_tensor(out=tmp[:], in0=bo(sa[:]), in1=bv(5, t), op=ALU.mult), reads=['s2_sa', 's2_vec'], writes=['s2_tmp'])
                    S.op('pool', lambda e: e.tensor_tensor(out=Sst[:], in0=Sst[:], in1=tmp[:], op=ALU.subtract), reads=['s2_S', 's2_tmp'], writes=['s2_S'])
                    S.op('dve', lambda e: e.tensor_tensor(out=tmp[:], in0=bo(vec[:, 3, t, :]), in1=bv(2, t), op=ALU.mult), reads=['s2_vec'], writes=['s2_tmp'])
                    S.op('pool', lambda e: e.tensor_tensor(out=Sst[:], in0=Sst[:], in1=tmp[:], op=ALU.add), reads=['s2_S', 's2_tmp'], writes=['s2_S'])
                    S.op('dve', lambda e: e.tensor_tensor(out=tmp[:], in0=Sst[:], in1=bv(0, t), op=ALU.mult), reads=['s2_S', 's2_vec'], writes=['s2_tmp'])
                    S.op('dve', lambda e: e.tensor_reduce(out=yS[:, t, :], in_=tmp[:], axis=AX.X, op=ALU.add), reads=['s2_tmp'], writes=['s2_y'])
                S.dma('act', self.O['swkv_o'][l].rearrange("n (j hh) v k -> hh (n j) (v k)", hh=2)[hh], Sst[:].rearrange("p v k -> p (v k)"),
                      reads=['s2_S'], writes=[('o_swkv', l, hh)])
                S.dma('act', self.scr_y[l].rearrange("n j hh t v -> hh (n j) (t v)")[hh], yS[:].rearrange("p t v -> p (t v)"),
                      reads=['s2_y'], writes=[('scr_y', l, hh)])
                phB.close()
            for n in range(NSEQ_S):
                S.dma('act', ytok[n * 8:(n + 1) * 8, :].rearrange("t (jh v) -> t jh v", v=64),
                      self.scr_y[l, n].rearrange("j hh t v -> t (j hh) v"), reads=[('scr_y', l, 0), ('scr_y', l, 1)], writes=[('s2_ytok', n)])
            for rnd in range(4):
                for _ in self.wkv_tile(ph, l, 2 * rnd, 0, nats[rnd], lnw, lnb,
                                       s2_y=(ytok[:, rnd * 256:(rnd + 1) * 256], [('s2_ytok', n) for n in range(NSEQ_S)])):
                    pass
        self.dump('mixed' if not is_s else 's_mixed', self.mixacc[:, :, 0:N], [128, 8, N], [('mixacc', j) for j in range(8)])
        ph.names.append('s2_ytok')
        ph.close()

    def rwkv_round(self, ph0, l, stream, g, nt, N, rnd, shifted, lwla, siglg, lnw, lnb, xkeys, V6):
        S, I = self.S, self.I
        is_s = stream == 'S'
        ph = Phase(self)
        j0 = 2 * rnd
        sf = str(rnd) if is_s else ''
        pho = ph0 if is_s else ph
        aT = ph.T("w_aT", [128, 2, N], BF16)
        rT = ph.T("w_rT", [128, 2, N], BF16)
        bT = ph.T("w_bT", [128, 2, N], BF16)
        kT = ph.T("w_kT", [128, 2, N], BF16)
        if is_s:
            vT, pT, ggT = self.pre_nat[rnd]
        else:
            vT = ph.T("w_vT", [128, 2, N], BF16)
            pT = ph.T("w_pT", [128, 2, N], BF16)
            ggT = ph.T("w_ggT", [128, 2, N], F32)
        gC = ph.T("w_gC", [128, 2, 4], F32)
        php = Phase(self)
        TK = ('w_r32', 'w_k32', 'w_v32', 'w_lw32', 'w_cl', 'w_a32', 'w_kk32', 'w_t32', 'w_tb', 'w_eI', 'w_eN', 'pch', 'rw_d')
        nseq_, T_ = (NSEQ_S, T_S) if is_s else (1, 512)
        pbufs = []
        for jj_ in range(2):
            d_ = {}
            for nm_ in ('r32', 'k32', 'v32', 'lw32', 'cl', 'a32', 'kk32', 't32', 'eI', 'eN'):
                d_[nm_] = php.T("w_%s_%d" % (nm_, jj_), [128, N], F32)
            d_['tb'] = php.T("w_tb_%d" % jj_, [128, N], BF16)
            d_['pch'] = php.T("pch_%d" % jj_, [128, nseq_, 1 + T_], F32)
            d_['dd'] = php.T("rw_d_%d" % jj_, [128, N], F32)
            pbufs.append(d_)
        php.names += [k_ + '_0' for k_ in TK] + [k_ + '_1' for k_ in TK]
        c0 = OFF_RW + j0 * 128
        wb, wk = self.wtile('w_in', l, [(c0, 256), (c0 + D, 256)])
        wb2, wk2 = self.wtile('w_in', l, [(c0 + 2 * D, 256), (OFF_GATE + 2 * D + j0 * 128, 256)])
        S0 = S

        def pair(jj):
            j = j0 + jj
            B_ = pbufs[jj]
            r32, k32, v32, lw32, cl, a32, kk32, t32, eI, eN, tb = (B_[k_] for k_ in ('r32', 'k32', 'v32', 'lw32', 'cl', 'a32', 'kk32', 't32', 'eI', 'eN', 'tb'))
            km = lambda ks: [((k_ + '_%d' % jj) if (isinstance(k_, str) and k_ in TK) else k_) for k_ in ks]

            class _S:
                @staticmethod
                def op(e, fn_, reads=(), writes=()):
                    return S0.op(e, fn_, reads=km(reads), writes=km(writes))
            S = _S()
            mm0 = self.mm

            def MM(out, lhsT, rhs, start, stop, reads, writes):
                return mm0(out, lhsT, rhs, start, stop, reads=km(reads), writes=km(writes))
            for qi, (dst, dkey, wbx, wkx, coff, blk) in enumerate(((r32, 'w_r32', wb, wk, jj * 128, j),
                                                                  (k32, 'w_k32', wb, wk, 256 + jj * 128, 8 + j),
                                                                  (v32, 'w_v32', wb2, wk2, jj * 128, 16 + j))):
                ps, pk = self.ps()
                for k in range(8):
                    MM(ps[:, 0:N], wbx[:, k, coff:coff + 128], self.xT[:, k, 0:N], k == 0, k == 7, reads=wkx + xkeys, writes=[pk])
                yield from shifted(ps, pk, blk, 128, dst[:], dkey, bufs=(B_['pch'], B_['dd']), S_=S)
            ps, pk = self.ps()
            MM(ps[:, 0:N], self.wlora_l[0:64, j * 128:(j + 1) * 128], lwla[0:64, :], True, True, reads=['wlora', 'lwla'], writes=[pk])
            S.op('act', lambda e: e.activation(out=lw32[:], in_=ps[:, 0:N], func=AF.Sigmoid, bias=self.p_w0[:, l, j:j + 1], scale=1.0),
                 reads=[pk, 'rw_w0'], writes=['w_lw32'])
            yield
            S.op('pool', lambda e: e.tensor_scalar(out=lw32[:], in0=lw32[:], scalar1=-math.exp(-0.5), scalar2=None, op0=ALU.mult),
                 reads=['w_lw32'], writes=['w_lw32'])
            yield
            ps, pk = self.ps()
            MM(ps[:, 0:N], self.wlora_l[64:128, j * 128:(j + 1) * 128], lwla[64:128, :], True, True, reads=['wlora', 'lwla'], writes=[pk])
            S.op('act', lambda e: e.activation(out=a32[:], in_=ps[:, 0:N], func=AF.Sigmoid, bias=self.p_a0[:, l, j:j + 1], scale=1.0),
                 reads=[pk, 'rw_a0'], writes=['w_a32'])
            yield
            ps, pk = self.ps()
            MM(ps[:, 0:N], self.wgup_l[:, 0, j * 128:(j + 1) * 128], siglg[:, 0, :], True, False, reads=['wgup', 'siglg'], writes=[pk])
            MM(ps[:, 0:N], self.wgup_l[0:32, 1, j * 128:(j + 1) * 128], siglg[0:32, 1, :], False, True, reads=['wgup', 'siglg'], writes=[pk])
            ps2, pk2 = self.ps()
            for k in range(8):
                MM(ps2[:, 0:N], wb2[:, k, 256 + jj * 128:256 + (jj + 1) * 128], self.xT[:, k, 0:N], k == 0, k == 7, reads=wk2 + xkeys, writes=[pk2])
            S.op('act', lambda e: e.activation(out=ggT[:, jj, :], in_=ps2[:, 0:N], func=AF.Sigmoid), reads=[pk2], writes=['w_ggT' + sf])
            yield
            S.op('dve', lambda e: e.tensor_tensor(out=ggT[:, jj, :], in0=ps[:, 0:N], in1=ggT[:, jj, :], op=ALU.mult), reads=[pk, 'w_ggT' + sf], writes=['w_ggT' + sf])
            yield
            S.op('act', lambda e: e.activation(out=kk32[:], in_=k32[:], func=AF.Copy, scale=self.p_kk[:, l, j:j + 1]),
                 reads=['w_k32', 'rw_kk'], writes=['w_kk32'])
            yield
            S.op('act', lambda e: e.activation(out=tb[:], in_=kk32[:], func=AF.Square), reads=['w_kk32'], writes=['w_tb'])
            yield
            ps, pk = self.ps()
            MM(ps[:, 0:N], self.blkb[:, :], tb[:], True, True, reads=['blkb', 'w_tb'], writes=[pk])
            S.op('act', lambda e: e.activation(out=t32[:], in_=ps[:, 0:N], func=AF.Sqrt), reads=[pk], writes=['w_t32'])
            yield
            S.op('dve', lambda e: e.tensor_scalar(out=t32[:], in0=t32[:], scalar1=1e-12, scalar2=None, op0=ALU.max), reads=['w_t32'], writes=['w_t32'])
            yield
            S.op('dve', lambda e: e.reciprocal(out=t32[:], in_=t32[:]), reads=['w_t32'], writes=['w_t32'])
            yield
            S.op('dve', lambda e: e.tensor_tensor(out=kk32[:], in0=kk32[:], in1=t32[:], op=ALU.mult), reads=['w_kk32', 'w_t32'], writes=['w_kk32'])
            yield
            S.op('dve', lambda e: e.tensor_scalar(out=t32[:], in0=a32[:], scalar1=self.p_ka[:, l, j:j + 1], scalar2=self.p_omka[:, l, j:j + 1],
                                                  op0=ALU.mult, op1=ALU.add), reads=['w_a32', 'rw_ka', 'p_omka'], writes=['w_t32'])
            yield
            S.op('pool', lambda e: e.tensor_tensor(out=k32[:], in0=k32[:], in1=t32[:], op=ALU.mult), reads=['w_k32', 'w_t32'], writes=['w_k32'])
            yield
            if is_s:
                S.op('pool', lambda e: e.tensor_tensor(out=t32[:], in0=r32[:], in1=k32[:], op=ALU.mult), reads=['w_r32', 'w_k32'], writes=['w_t32'])
                yield
                S.op('act', lambda e: e.activation(out=pT[:, jj, :], in_=t32[:], func=AF.Copy, scale=self.p_rk[:, l, j:j + 1]),
                     reads=['w_t32', 'rw_rk'], writes=['w_pT' + sf])
                yield
                S.op('act', lambda e: e.copy(out=vT[:, jj, :], in_=v32[:]), reads=['w_v32'], writes=['w_vT' + sf])
                yield
                S.op('act', lambda e: e.activation(out=eI[:], in_=lw32[:], func=AF.Exp), reads=['w_lw32'], writes=['w_eI'])
                yield
                S.op('dve', lambda e: e.tensor_tensor(out=eN[:], in0=kk32[:], in1=a32[:], op=ALU.mult), reads=['w_kk32', 'w_a32'], writes=['w_eN'])
                yield
                psa, pka = self.ps()
                for q, (src, skey) in enumerate(((r32, 'w_r32'), (eI, 'w_eI'), (k32, 'w_k32'), (v32, 'w_v32'))):
                    S.op('pe', lambda e: e.transpose(out=psa[:, q * 128:(q + 1) * 128], in_=src[:, :], identity=self.identf[:]),
                         reads=[skey, 'identf'], writes=[pka])
                    yield
                psb_, pkb = self.ps()
                for q, (src, skey) in enumerate(((kk32, 'w_kk32'), (eN, 'w_eN'))):
                    S.op('pe', lambda e: e.transpose(out=psb_[:, q * 128:(q + 1) * 128], in_=src[:, :], identity=self.identf[:]),
                         reads=[skey, 'identf'], writes=[pkb])
                    yield
                S.op('act', lambda e: e.copy(out=V6[:, 0:4, j * 128:(j + 1) * 128], in_=psa[:, :].rearrange("p (q c) -> p q c", q=4)),
                     reads=[pka], writes=[('s2_V6', j)])
                yield
                S.op('dve', lambda e: e.tensor_copy(out=V6[:, 4:6, j * 128:(j + 1) * 128], in_=psb_[:, 0:256].rearrange("p (q c) -> p q c", q=2)),
                     reads=[pkb], writes=[('s2_V6', j)])
                yield
            if not is_s:
                S.op('dve', lambda e: e.tensor_tensor_scan(out=cl[:], data0=self.rmaskP[:, 0:N], data1=lw32[:], initial=0.0, op0=ALU.mult, op1=ALU.add),
                     reads=['rmaskP', 'w_lw32'], writes=['w_cl'])
                yield
                S.op('act', lambda e: e.activation(out=eI[:], in_=cl[:], func=AF.Exp), reads=['w_cl'], writes=['w_eI'])
                yield
                S.op('act', lambda e: e.activation(out=eN[:], in_=cl[:], func=AF.Exp, scale=-1.0), reads=['w_cl'], writes=['w_eN'])
                yield
                S.op('pool', lambda e: e.tensor_copy(out=gC[:, jj, :], in_=eI[:, 127:N:128]), reads=['w_eI'], writes=['w_gC'])
                yield
                S.op('dve', lambda e: e.tensor_tensor(out=rT[:, jj, :], in0=r32[:], in1=eI[:], op=ALU.mult), reads=['w_r32', 'w_eI'], writes=['w_rT'])
                yield
                S.op('pool', lambda e: e.tensor_tensor(out=t32[:], in0=r32[:], in1=k32[:], op=ALU.mult), reads=['w_r32', 'w_k32'], writes=['w_t32'])
                yield
                S.op('act', lambda e: e.activation(out=pT[:, jj, :], in_=t32[:], func=AF.Copy, scale=self.p_rk[:, l, j:j + 1]),
                     reads=['w_t32', 'rw_rk'], writes=['w_pT' + sf])
                yield
                S.op('dve', lambda e: e.tensor_tensor(out=kT[:, jj, :], in0=k32[:], in1=eN[:], op=ALU.mult), reads=['w_k32', 'w_eN'], writes=['w_kT'])
                yield
                S.op('pool', lambda e: e.tensor_tensor(out=t32[:], in0=kk32[:], in1=a32[:], op=ALU.mult), reads=['w_kk32', 'w_a32'], writes=['w_t32'])
                yield
                S.op('dve', lambda e: e.tensor_tensor(out=bT[:, jj, :], in0=t32[:], in1=eN[:], op=ALU.mult), reads=['w_t32', 'w_eN'], writes=['w_bT'])
                yield
                S.op('pool', lambda e: e.tensor_tensor(out=t32[:], in0=cl[:], in1=lw32[:], op=ALU.subtract), reads=['w_cl', 'w_lw32'], writes=['w_t32'])
                yield
                S.op('act', lambda e: e.activation(out=t32[:], in_=t32[:], func=AF.Exp), reads=['w_t32'], writes=['w_t32'])
                yield
                S.op('dve', lambda e: e.scalar_tensor_tensor(out=aT[:, jj, :], in0=kk32[:], scalar=-1.0, in1=t32[:], op0=ALU.mult, op1=ALU.mult),
                     reads=['w_kk32', 'w_t32'], writes=['w_aT'])
                yield
                S.op('act', lambda e: e.copy(out=vT[:, jj, :], in_=v32[:]), reads=['w_v32'], writes=['w_vT' + sf])
                yield

        gens_ = [pair(0), pair(1)]
        if is_s:
            for g_ in gens_:
                for _ in g_:
                    pass
        else:
            self.interleave(gens_)
        php.close()
        nat = dict(aT=aT, rT=rT, bT=bT, kT=kT, vT=vT, pT=pT, ggT=ggT, gC=gC, sf=sf)
        if not is_s:
            for t0_ in range(0, nt, 2):
                self.interleave_staged([self.wkv_tile(ph, l, j0, t0_ + i_, nat, lnw, lnb, slot=i_) for i_ in range(2)])
        ph.close()
        return nat

    def wkv_tile(self, ph0, l, j0, t, nat, lnw, lnb, s2_y=None, slot=0):
        S0 = self.S
        sl = '_s%d' % slot
        km = lambda ks: [((k + sl) if isinstance(k, str) and k.startswith('k_') else k) for k in ks]

        class _S:
            @staticmethod
            def op(e, fn_, reads=(), writes=()):
                return S0.op(e, fn_, reads=km(reads), writes=km(writes))
        S = _S()
        mm0 = self.mm

        def MM(out, lhsT, rhs, start, stop, reads, writes):
            return mm0(out, lhsT, rhs, start, stop, reads=km(reads), writes=km(writes))

        class _Reuse:
            def T(_s, name, shape, dt):
                name = name + sl
                cache = ph0.__dict__.setdefault('_tile_cache', {})
                if name not in cache:
                    cache[name] = ph0.T(name, shape, dt)
                return cache[name]

            def close(_s):
                pass
        ph = _Reuse()
        aT, rT, bT, kT, vT, pT, ggT, gC = (nat[k] for k in ('aT', 'rT', 'bT', 'kT', 'vT', 'pT', 'ggT', 'gC'))
        sf = nat['sf']
        cs = slice(t * 128, (t + 1) * 128)
        tok = ph.T("k_tok", [128, 4, 2, 128], BF16)
        ps, pk = self.ps()
        psb = ps[:].bitcast(BF16)
        for qi, (src, skey) in enumerate(((vT, 'w_vT' + sf), (bT, 'w_bT'), (kT, 'w_kT'), (pT, 'w_pT' + sf))):
            if s2_y is not None and qi in (1, 2):
                src, skey = vT, 'w_vT' + sf
            for jj in range(2):
                S.op('pe', lambda e: e.transpose(out=psb[:, (qi * 2 + jj) * 128:(qi * 2 + jj + 1) * 128], in_=src[:, jj, cs],
                                                 identity=self.identb[:]), reads=[skey, 'identb'], writes=[pk])
                yield
        S.op('act', lambda e: e.copy(out=tok[:].rearrange("p q j c -> p (q j c)"), in_=psb), reads=[pk], writes=['k_tok'])
        yield
        if s2_y is None:
            blks = {}
            for nm, (src, skey) in (('a', (aT, 'w_aT')), ('r', (rT, 'w_rT')), ('b', (bT, 'w_bT'))):
                isnew = ("k_blk" + nm + sl) not in ph0.__dict__.get('_tile_cache', {})
                bl = ph.T("k_blk" + nm, [128, 2, 2, 128], BF16)
                if isnew:
                    S.op('pool', lambda e: e.memset(bl[:], 0.0), writes=['k_blk' + nm])
                    yield
                for hh in range(2):
                    eng = 'act' if hh else 'pool'
                    if eng == 'act':
                        S.op('act', lambda e: e.copy(out=bl[hh * 64:(hh + 1) * 64, :, hh, :], in_=src[hh * 64:(hh + 1) * 64, :, cs]),
                             reads=[skey, 'k_blk' + nm], writes=['k_blk' + nm])
                        yield
                    else:
                        S.op('pool', lambda e: e.tensor_copy(out=bl[hh * 64:(hh + 1) * 64, :, hh, :], in_=src[hh * 64:(hh + 1) * 64, :, cs]),
                             reads=[skey, 'k_blk' + nm], writes=['k_blk' + nm])
                        yield
                blks[nm] = bl
            mats = {}
            specs = (('AabT', bT, 'w_bT', 'a', 0), ('ArbT', bT, 'w_bT', 'r', 1), ('AakT', kT, 'w_kT', 'a', 0),
                     ('ArkT', kT, 'w_kT', 'r', 1), ('Aab', aT, 'w_aT', 'b', 2))
            for mi, (nm, lsrc, lkey, rb, mk) in enumerate(specs):
                ps, pk = self.ps()
                for jj in range(2):
                    MM(ps[:, jj * 256:(jj + 1) * 256], lsrc[:, jj, cs], blks[rb][:, jj, :, :].rearrange("p h c -> p (h c)"),
                            True, True, reads=[lkey, 'k_blk' + rb], writes=[pk])
                m = ph.T("k_" + nm, [128, 4, 128], BF16)
                S.op('dve', lambda e: e.tensor_tensor(out=m[:].rearrange("p h c -> p (h c)"), in0=ps[:, :], in1=self.maskb[:, mk, :], op=ALU.mult),
                     reads=[pk, 'maskb'], writes=['k_' + nm])
                yield
                mats[nm] = m
            Pm = [ph.T("k_P%d" % i, [128, 4, 128], BF16) for i in range(2)]
            Am = [ph.T("k_A%d" % i, [128, 4, 128], BF16) for i in range(2)]
            Bm = [ph.T("k_B%d" % i, [128, 4, 128], BF16) for i in range(2)]
            fl = lambda m_: m_[:].rearrange("p h c -> p (h c)")
            S.op('pool', lambda e: e.tensor_tensor(out=fl(Pm[0]), in0=fl(mats['AabT']), in1=self.maskb[:, 3, :], op=ALU.add),
                 reads=['k_AabT', 'maskb'], writes=['k_P0'])
            yield
            Acur, Akey, Bcur, Bkey = mats['Aab'], 'k_Aab', mats['AabT'], 'k_AabT'
            pcur = 0
            pend = None

            def p_update(Ap, Apk, pcur_):
                ps_, pk_ = self.ps()
                for h in range(4):
                    MM(ps_[:, h * 128:(h + 1) * 128], Ap[:, h, :], Pm[pcur_][:, h, :], True, True, reads=[Apk, 'k_P%d' % pcur_], writes=[pk_])
                S.op('dve', lambda e: e.tensor_tensor(out=fl(Pm[1 - pcur_]), in0=ps_[:, :], in1=fl(Pm[pcur_]), op=ALU.add),
                     reads=[pk_, 'k_P%d' % pcur_], writes=['k_P%d' % (1 - pcur_)])
                return 1 - pcur_
            Am3 = Am + [ph.T("k_A2", [128, 4, 128], BF16)]
            for step in range(6):
                An, Ank = Am3[step % 3], 'k_A%d' % (step % 3)
                psA, pkA = self.ps()
                for h in range(4):
                    MM(psA[:, h * 128:(h + 1) * 128], Bcur[:, h, :], Acur[:, h, :], True, True, reads=[Akey, Bkey], writes=[pkA])
                if step < 5:
                    Bn, Bnk = Bm[step % 2], 'k_B%d' % (step % 2)
                    psB, pkB = self.ps()
                    for h in range(4):
                        MM(psB[:, h * 128:(h + 1) * 128], Acur[:, h, :], Bcur[:, h, :], True, True, reads=[Akey, Bkey], writes=[pkB])
                S.op('act', lambda e: e.copy(out=fl(An), in_=psA[:, :]), reads=[pkA], writes=[Ank])
                yield
                if step < 5:
                    S.op('act', lambda e: e.copy(out=fl(Bn), in_=psB[:, :]), reads=[pkB], writes=[Bnk])
                    yield
                if pend is not None:
                    pcur = p_update(pend[0], pend[1], pcur)
                pend = (An, Ank)
                Acur, Akey = An, Ank
                if step < 5:
                    Bcur, Bkey = Bn, Bnk
            pcur = p_update(pend[0], pend[1], pcur)
            Pf, Pkey = Pm[pcur], 'k_P%d' % pcur
            yield 'CHAIN'
            Hb = self.Hbf
            Xs = ph.T("k_Xs", [128, 4, 64], BF16)
            Us = ph.T("k_Us", [128, 4, 64], BF16)
            psX, pkX = self.ps()
            for jj in range(2):
                j = j0 + jj
                MM(psX[:, jj * 128:(jj + 1) * 128], aT[:, jj, cs], Hb[:, l, j, :], True, False, reads=['w_aT', ('Hbf', l, j)], writes=[pkX])
                for hh in range(2):
                    h = jj * 2 + hh
                    MM(psX[:, h * 64:(h + 1) * 64], mats['AakT'][:, h, :], tok[:, 0, jj, hh * 64:(hh + 1) * 64], False, hh == 1,
                            reads=['k_AakT', 'k_tok'], writes=[pkX])
            S.op('act', lambda e: e.copy(out=Xs[:].rearrange("p h c -> p (h c)"), in_=psX[:, 0:256]), reads=[pkX], writes=['k_Xs'])
            yield
            psU, pkU = self.ps()
            for h in range(4):
                MM(psU[:, h * 64:(h + 1) * 64], Pf[:, h, :], Xs[:, h, :], True, True, reads=[Pkey, 'k_Xs'], writes=[pkU])
            S.op('act', lambda e: e.copy(out=Us[:].rearrange("p h c -> p (h c)"), in_=psU[:, 0:256]), reads=[pkU], writes=['k_Us'])
            yield
            psY, pkY = self.ps()
            for jj in range(2):
                j = j0 + jj
                MM(psY[:, jj * 128:(jj + 1) * 128], rT[:, jj, cs], Hb[:, l, j, :], True, False, reads=['w_rT', ('Hbf', l, j)], writes=[pkY])
                for hh in range(2):
                    h = jj * 2 + hh
                    MM(psY[:, h * 64:(h + 1) * 64], mats['ArbT'][:, h, :], Us[:, h, :], False, False, reads=['k_ArbT', 'k_Us'], writes=[pkY])
                    MM(psY[:, h * 64:(h + 1) * 64], mats['ArkT'][:, h, :], tok[:, 0, jj, hh * 64:(hh + 1) * 64], False, hh == 1,
                            reads=['k_ArkT', 'k_tok'], writes=[pkY])
            psH, pkH = self.ps()
            for jj in range(2):
                MM(psH[:, jj * 128:(jj + 1) * 128], tok[:, 1, jj, :], Us[:, 2 * jj:2 * jj + 2, :].rearrange("p h c -> p (h c)"), True, False,
                        reads=['k_tok', 'k_Us'], writes=[pkH])
                MM(psH[:, jj * 128:(jj + 1) * 128], tok[:, 2, jj, :], tok[:, 0, jj, :], False, True, reads=['k_tok'], writes=[pkH])
            hd = ph.T("k_hd", [128, 256], F32)
            S.op('dve', lambda e: e.tensor_tensor(out=hd[:], in0=psH[:, 0:256], in1=self.blkf[:], op=ALU.mult), reads=[pkH, 'blkf'], writes=['k_hd'])
            yield
            for jj in range(2):
                j = j0 + jj
                S.op('pool', lambda e: e.tensor_tensor(out=hd[:, jj * 128:(jj + 1) * 128], in0=hd[:, jj * 128:(jj + 1) * 128], in1=self.H32[:, l, j, :], op=ALU.add),
                     reads=['k_hd', ('H32', l, j)], writes=['k_hd'])
                yield
                S.op('act', lambda e: e.activation(out=self.H32[:, l, j, :], in_=hd[:, jj * 128:(jj + 1) * 128], func=AF.Copy, scale=gC[:, jj, t:t + 1]),
                     reads=['k_hd', 'w_gC'], writes=[('H32', l, j)])
                yield
                S.op('act', lambda e: e.copy(out=Hb[:, l, j, :], in_=self.H32[:, l, j, :]), reads=[('H32', l, j)], writes=[('Hbf', l, j)])
                yield
        yield 'POST'
        y = ph.T("k_y", [128, 4, 64], F32)
        sq = ph.T("k_sq", [128, 4, 64], F32)
        st = ph.T("k_st", [128, 4], F32)
        st2 = ph.T("k_st2", [128, 4], F32)
        if s2_y is None:
            S.op('act', lambda e: e.copy(out=y[:].rearrange("p h c -> p (h c)"), in_=psY[:, 0:256]), reads=[pkY], writes=['k_y'])
            yield
        else:
            S.op('act', lambda e: e.copy(out=y[:].rearrange("p h c -> p (h c)"), in_=s2_y[0]), reads=s2_y[1], writes=['k_y'])
            yield
        S.op('dve', lambda e: e.tensor_reduce(out=st[:], in_=y[:], axis=AX.X, op=ALU.add), reads=['k_y'], writes=['k_st'])
        yield
        S.op('dve', lambda e: e.tensor_scalar(out=st[:], in0=st[:], scalar1=-1.0 / 64, scalar2=None, op0=ALU.mult), reads=['k_st'], writes=['k_st'])
        yield
        S.op('dve', lambda e: e.tensor_tensor(out=y[:], in0=y[:], in1=st[:].unsqueeze(2).to_broadcast([128, 4, 64]), op=ALU.add),
             reads=['k_y', 'k_st'], writes=['k_y'])
        yield
        S.op('pool', lambda e: e.tensor_tensor(out=sq[:], in0=y[:], in1=y[:], op=ALU.mult), reads=['k_y'], writes=['k_sq'])
        yield
        S.op('dve', lambda e: e.tensor_reduce(out=st2[:], in_=sq[:], axis=AX.X, op=ALU.add), reads=['k_sq'], writes=['k_st2'])
        yield
        S.op('act', lambda e: e.activation(out=st2[:], in_=st2[:], func=AF.Sqrt, bias=self.epsln[:, 1:2], scale=1.0 / 64),
             reads=['k_st2', 'epsln'], writes=['k_st2'])
        yield
        S.op('dve', lambda e: e.reciprocal(out=st2[:], in_=st2[:]), reads=['k_st2'], writes=['k_st2'])
        yield
        S.op('dve', lambda e: e.tensor_tensor(out=y[:], in0=y[:], in1=st2[:].unsqueeze(2).to_broadcast([128, 4, 64]), op=ALU.mult),
             reads=['k_y', 'k_st2'], writes=['k_y'])
        yield
        yf = y[:].rearrange("p h c -> p (h c)")
        fsl = slice(j0 * 128, j0 * 128 + 256)
        S.op('pool', lambda e: e.tensor_tensor(out=yf, in0=yf, in1=lnw[:, fsl], op=ALU.mult), reads=['k_y', 'lnw_t'], writes=['k_y'])
        yield
        S.op('pool', lambda e: e.tensor_tensor(out=yf, in0=yf, in1=lnb[:, fsl], op=ALU.add), reads=['k_y', 'lnb_t'], writes=['k_y'])
        yield
        S.op('dve', lambda e: e.tensor_reduce(out=st[:], in_=tok[:, 3, :, :].rearrange("p j (h c) -> p (j h) c", h=2), axis=AX.X, op=ALU.add),
             reads=['k_tok'], writes=['k_st'])
        yield
        S.op('dve', lambda e: e.tensor_tensor(out=sq[:], in0=tok[:, 0, :, :].rearrange("p j (h c) -> p (j h) c", h=2),
                                              in1=st[:].unsqueeze(2).to_broadcast([128, 4, 64]), op=ALU.mult), reads=['k_tok', 'k_st'], writes=['k_sq'])
        yield
        S.op('pool', lambda e: e.tensor_tensor(out=y[:], in0=y[:], in1=sq[:], op=ALU.add), reads=['k_y', 'k_sq'], writes=['k_y'])
        yield
        psT, pkT = self.ps()
        for jj in range(2):
            S.op('pe', lambda e: e.transpose(out=psT[:, jj * 128:(jj + 1) * 128], in_=yf[:, jj * 128:(jj + 1) * 128], identity=self.identf[:]),
                 reads=['k_y', 'identf'], writes=[pkT])
            yield
        og = ph.T("k_og", [128, 2, 128], F32)
        S.op('dve', lambda e: e.tensor_tensor(out=og[:], in0=psT[:, 0:256].rearrange("p (j c) -> p j c", j=2), in1=ggT[:, :, cs], op=ALU.mult),
             reads=[pkT, 'w_ggT' + sf], writes=['k_og'])
        yield
        S.op('pool', lambda e: e.tensor_tensor(out=self.mixacc[:, j0:j0 + 2, cs], in0=self.mixacc[:, j0:j0 + 2, cs], in1=og[:], op=ALU.add),
             reads=['k_og', ('mixacc', j0), ('mixacc', j0 + 1)], writes=[('mixacc', j0), ('mixacc', j0 + 1)])
        yield
        ph.close()

    def attention(self, l, stream, g, ngroups, nt, N):
        S, I = self.S, self.I
        ph = Phase(self)
        qT = ph.T("qT", [128, 8, N], BF16)
        ET = [ph.T("ET%d" % b, [128, 512], BF16) for b in range(2)]
        sb = ph.T("sbt", [128, 512], F32)
        den = ph.T("den", [128, 256], F32)
        osb = ph.T("osb", [128, 256], F32)
        if stream == 'P':
            self.biasP = ph.T("biasP", [128, 2, NH, 128], F32)
            S.dma('act', self.biasP[:], I['c_biasP'], writes=['biasP'])
        kv32_pre = ph.T("kv32", [128, 512], F32)
        if stream == 'P':
            kT = self.kT_p[l]
            vtok = self.vtok_p[l]
            kkey, vkey = ('kTp', l), ('vtokp', l)
        else:
            kT = ph.T("kTs", [128, KVH, 128 + 128], BF16)
            vtok = ph.T("vtoks", [128, 2, 256], BF16)
            kkey, vkey = 'kTs', 'vtoks'
        xkeys = [('xT', t) for t in range(nt)]
        for half in range(2):
            wb, wk = self.wtile('w_in', l, [(half * 512, 512)])
            for b in range(4):
                ps, pk = self.ps()
                for k in range(8):
                    self.mm(ps[:, 0:N], wb[:, k, b * 128:(b + 1) * 128], self.xT[:, k, 0:N], k == 0, k == 7,
                            reads=wk + xkeys, writes=[pk])
                self.evac_copy('act' if b % 2 else 'dve', qT[:, half * 4 + b, :], ps[:, 0:N], [pk], ['qT'], scale=0.125)
        self.chk('q')
        i = self.wn
        self.wn = (self.wn + 1) % NWBUF
        wb = self.wbuf[i]
        wk = [('wbuf', i)]
        srcw = self.W['w_in'][l]
        kv = srcw[:, OFF_K:OFF_K + 256].rearrange("(k p) (c d) -> p k c d", p=128, d=64)
        wbv = wb[:, :, :].rearrange("p k (c two d) -> p k c two d", two=2, d=64)
        for dup in range(2):
            for k in range(8):
                S.dma('sp', wbv[:, k, :, dup, :], kv[:, k, :, :],
                      writes=wk if (dup == 0 and k == 0) else [('wbufx', i, dup, k)])
        wk_all = wk + [('wbufx', i, dup, k) for dup in range(2) for k in range(8) if not (dup == 0 and k == 0)]
        for c in range(KVH):
            ps, pk = self.ps()
            for k in range(8):
                self.mm(ps[:, 0:N], wb[:, k, c * 128:(c + 1) * 128], self.xT[:, k, 0:N], k == 0, k == 7,
                        reads=wk_all + xkeys, writes=[pk])
            self.evac_copy('act' if c % 2 else 'dve', kT[:, c, 128:128 + N], ps[:, 0:N], [pk], [kkey])
        self.chk('kdup')
        wb, wk = self.wtile('w_in', l, [(OFF_K, 512)])
        last_tile_needs_k = (stream == 'S') or (g == ngroups - 1)
        kv32 = None
        import os
        kvvar = os.environ.get('KVVAR', '')
        for t in range(nt):
            if kvvar == 'dmaonly':
                break
            ps, pk = self.ps()
            for k in range(8):
                self.mm(ps[:, :], self.xT[:, k, t * 128:(t + 1) * 128], wb[:, k, :], k == 0, k == 7,
                        reads=wk + [('xT', t)], writes=[pk])
            if kvvar == 'mmonly':
                continue
            if kvvar != 'noact' and not (kvvar == 'noact3' and t == 3):
                self.evac_copy('act', vtok[:, 1 + t, :], ps[:, 256:512], [pk], [vkey])
            if last_tile_needs_k and t == nt - 1 and kvvar != 'nokv32':
                kv32 = kv32_pre
                self.evac_copy('act', kv32[:], ps[:, :], [pk], ['kv32'])
        import os
        if kv32 is not None and stream == 'P' and not os.environ.get('NOKVOUT'):
            S.dma('act', self.O['pk'][l], kv32[:, 0:256], reads=['kv32'], writes=[('o_pk', l)])
            S.dma('act', self.O['pv'][l], kv32[:, 256:512], reads=['kv32'], writes=[('o_pv', l)])
        self.chk('kvtok')
        for half in range(2):
            wb, wk = self.wtile('w_in', l, [(OFF_GATE + half * 512, 512)])
            for b in range(4):
                ps, pk = self.ps()
                for k in range(8):
                    self.mm(ps[:, 0:N], wb[:, k, b * 128:(b + 1) * 128], self.xT[:, k, 0:N], k == 0, k == 7,
                            reads=wk + xkeys, writes=[pk])
                j = half * 4 + b
                S.op('act', lambda e: e.activation(out=self.mixacc[:, j, 0:N], in_=ps[:, 0:N], func=AF.Sigmoid),
                     reads=[pk], writes=[('mixacc', j)])
        self.chk('proj')
        if stream == 'P':
            sbc = [ph.T("sbt_c%d" % c, [128, 512], F32) for c in range(KVH)]
            ETcp = [[ph.T("ET_c%d_%d" % (c, p_), [128, 512], BF16) for p_ in range(2)] for c in range(KVH)]
            denc = [ph.T("den_c%d" % c, [128, 256], F32) for c in range(KVH)]
            osbc = [ph.T("osb_c%d" % c, [128, 256], F32) for c in range(KVH)]
            ph.names += ['sbt_c', 'ET_c', 'den_c', 'osb_c']

            def attn_c(t, c, blks):
                nb = len(blks)
                sb_, den_, osb_ = sbc[c], denc[c], osbc[c]
                for par in range(2):
                    ps, pk = self.ps()
                    hp = par * 64
                    for bi, blk in enumerate(blks):
                        koff = 128 * t if blk == 0 else 128 * (t + 1)
                        for i2 in range(2):
                            h = 4 * c + 2 * i2 + par
                            col = (bi * 2 + i2) * 128
                            self.mm(ps[:, col:col + 128], kT[hp:hp + 64, c, koff:koff + 128],
                                    qT[hp:hp + 64, h // 2, t * 128:(t + 1) * 128], True, True,
                                    reads=[kkey, 'qT'], writes=[pk])
                    yield
                    W_ = nb * 256
                    bias_ap = self.biasP[:, blks[0]:blks[0] + nb, 4 * c + par:4 * c + 4:2, :]
                    S.op('dve', lambda e: e.tensor_tensor(out=sb_[:, 0:W_].rearrange("p (b i q) -> p b i q", b=nb, i=2),
                                                          in0=ps[:, 0:W_].rearrange("p (b i q) -> p b i q", b=nb, i=2),
                                                          in1=bias_ap, op=ALU.add),
                         reads=[pk, 'biasP'], writes=[('sbt_c', c)])
                    yield
                    S.op('act', lambda e: e.activation(out=ETcp[c][par][:, 0:W_], in_=sb_[:, 0:W_], func=AF.Exp),
                         reads=[('sbt_c', c)], writes=[('ET_c', c, par)])
                    yield
                ps, pk = self.ps()
                for i4 in range(4):
                    par = i4 % 2
                    hp = par * 64
                    i2 = i4 // 2
                    jj = i4 // 2
                    for part in range(2):
                        for bi, blk in enumerate(blks):
                            slot = t if blk == 0 else t + 1
                            lhsT = vtok[:, slot, c * 64:(c + 1) * 64] if part == 0 else self.onesb[:, :]
                            col = (bi * 2 + i2) * 128
                            self.mm(ps[hp:hp + 64, part * 256 + jj * 128: part * 256 + (jj + 1) * 128], lhsT,
                                    ETcp[c][par][:, col:col + 128], bi == 0, bi == nb - 1,
                                    reads=[vkey, 'onesb', ('ET_c', c, par)], writes=[pk])
                yield
                for jj in range(2):
                    j = 2 * c + jj
                    S.op('dve', lambda e: e.tensor_scalar(out=den_[:, jj * 128:(jj + 1) * 128],
                                                          in0=ps[:, 256 + jj * 128:256 + (jj + 1) * 128],
                                                          scalar1=self.esink[:, l, j:j + 1], scalar2=None, op0=ALU.add),
                         reads=[pk, 'esink'], writes=[('den_c', c)])
                yield
                S.op('dve', lambda e: e.reciprocal(out=den_[:], in_=den_[:]), reads=[('den_c', c)], writes=[('den_c', c)])
                yield
                S.op('dve', lambda e: e.tensor_tensor(out=osb_[:], in0=ps[:, 0:256], in1=den_[:], op=ALU.mult),
                     reads=[pk, ('den_c', c)], writes=[('osb_c', c)])
                yield
                S.op('pool', lambda e: e.tensor_tensor(out=self.mixacc[:, 2 * c:2 * c + 2, t * 128:(t + 1) * 128],
                                                       in0=self.mixacc[:, 2 * c:2 * c + 2, t * 128:(t + 1) * 128],
                                                       in1=osb_[:].rearrange("p (j q) -> p j q", j=2), op=ALU.mult),
                     reads=[('osb_c', c), ('mixacc', 2 * c), ('mixacc', 2 * c + 1)],
                     writes=[('mixacc', 2 * c), ('mixacc', 2 * c + 1)])
                yield

            for t in range(nt):
                first = (g == 0 and t == 0)
                blks = [1] if first else [0, 1]
                self.interleave([attn_c(t, c, blks) for c in range(KVH)])
            S.op('pool', lambda e: e.tensor_copy(out=kT[:, :, 0:128], in_=kT[:, :, N:N + 128]), reads=[kkey], writes=[kkey])
            S.op('pool', lambda e: e.tensor_copy(out=vtok[:, 0, :], in_=vtok[:, nt, :]), reads=[vkey], writes=[vkey])
        else:
            bSn = ph.T("bSn", [128, NH, 128], F32)
            bSc = ph.T("bSc", [128, KVH, 2, 256], F32)
            S.dma('act', bSn[:], I['c_biasSn'], writes=['bSn'])
            S.dma('act', bSc[:], I['c_biasSc'], writes=['bSc'])
            ck32 = ph.T("ck32", [128, NSEQ_S, 256], F32)
            ckd = ph.T("ckd", [128, NSEQ_S, KVH, 2, 64], BF16)
            KcT = ph.T("KcT", [128, NSEQ_S, KVH, 128], BF16)
            Vc = ph.T("Vc", [128, NSEQ_S, 256], BF16)
            ETc = [ph.T("ETc%d" % b, [128, 256], BF16) for b in range(2)]
            t1 = ph.T("sa_t1", [128, 512], F32)
            S.dma('act', ck32[:], I['ck'][l].rearrange("n r c -> r n c"), writes=['ck32'])
            S.dma('act', self.O['sk'][l].rearrange("n r c -> r n c")[0:120], ck32[8:128, :, :], reads=['ck32'], writes=[('o_sk', l)])
            for dup in range(2):
                S.op('dve' if dup else 'pool', lambda e: e.tensor_copy(out=ckd[:, :, :, dup, :], in_=ck32[:].rearrange("p n (c d) -> p n c d", d=64)),
                     reads=['ck32'], writes=['ckd'])
            S.dma('act', ck32[:], I['cv'][l].rearrange("n r c -> r n c"), writes=['ck32'])
            S.dma('act', self.O['sv'][l].rearrange("n r c -> r n c")[0:120], ck32[8:128, :, :], reads=['ck32'], writes=[('o_sv', l)])
            S.op('act', lambda e: e.copy(out=Vc[:], in_=ck32[:]), reads=['ck32'], writes=['Vc'])
            for n0 in range(0, NSEQ_S, 2):
                ps, pk = self.ps()
                psb = ps[:].bitcast(BF16)
                for ni in range(2):
                    for c in range(KVH):
                        S.op('pe', lambda e: e.transpose(out=psb[:, (ni * 4 + c) * 128:(ni * 4 + c + 1) * 128],
                                                         in_=ckd[:, n0 + ni, c, :, :].rearrange("p a d -> p (a d)"), identity=self.identb[:]),
                             reads=['ckd', 'identb'], writes=[pk])
                S.op('act' if (n0 // 2) % 2 else 'dve',
                     (lambda e: e.copy(out=KcT[:, n0:n0 + 2, :, :].rearrange("p n c k -> p (n c k)"), in_=psb)) if (n0 // 2) % 2 else
                     (lambda e: e.tensor_copy(out=KcT[:, n0:n0 + 2, :, :].rearrange("p n c k -> p (n c k)"), in_=psb)),
                     reads=[pk], writes=['KcT'])
            for n in range(NSEQ_S):
                S.dma('act', self.O['sk'][l, n, 120:128, :], kv32[n * 8:(n + 1) * 8, 0:256], reads=['kv32'], writes=[('o_skn', l, n)])
                S.dma('act', self.O['sv'][l, n, 120:128, :], kv32[n * 8:(n + 1) * 8, 256:512], reads=['kv32'], writes=[('o_svn', l, n)])
            for c in range(KVH):
                for par in range(2):
                    hp = par * 64
                    ps, pk = self.ps()
                    for i2 in range(2):
                        h = 4 * c + 2 * i2 + par
                        self.mm(ps[:, i2 * 128:(i2 + 1) * 128], kT[hp:hp + 64, c, 128:256], qT[hp:hp + 64, h // 2, 0:128], True, True,
                                reads=[kkey, 'qT'], writes=[pk])
                    S.op('dve', lambda e: e.tensor_tensor(out=sb[:, 0:256].rearrange("p (i q) -> p i q", i=2),
                                                          in0=ps[:, 0:256].rearrange("p (i q) -> p i q", i=2),
                                                          in1=bSn[:, 4 * c + par:4 * c + 4:2, :], op=ALU.add), reads=[pk, 'bSn'], writes=['sbt'])
                    S.op('act', lambda e: e.activation(out=ET[par][:, 0:256], in_=sb[:, 0:256], func=AF.Exp), reads=['sbt'], writes=[('ET', par)])
                    ps, pk = self.ps()
                    for n in range(NSEQ_S):
                        for i2 in range(2):
                            h = 4 * c + 2 * i2 + par
                            col = (n * 2 + i2) * 8
                            self.mm(ps[:, col:col + 8], KcT[hp:hp + 64, n, c, :], qT[hp:hp + 64, h // 2, n * 8:(n + 1) * 8], True, True,
                                    reads=['KcT', 'qT'], writes=[pk])
                    S.op('dve', lambda e: e.tensor_tensor(out=sb[:, 256:512], in0=ps[:, 0:256], in1=bSc[:, c, par, :], op=ALU.add),
                         reads=[pk, 'bSc'], writes=['sbt'])
                    S.op('act', lambda e: e.activation(out=ETc[par][:], in_=sb[:, 256:512], func=AF.Exp), reads=['sbt'], writes=[('ETc', par)])
                ps1, pk1 = self.ps()
                ps2, pk2 = self.ps()
                for i4 in range(4):
                    par, i2 = i4 % 2, i4 // 2
                    hp = par * 64
                    for part in range(2):
                        lhs_n = vtok[:, 1, c * 64:(c + 1) * 64] if part == 0 else self.onesb[:, :]
                        self.mm(ps1[hp:hp + 64, part * 256 + i2 * 128: part * 256 + (i2 + 1) * 128], lhs_n,
                                ET[par][:, i2 * 128:(i2 + 1) * 128], True, True, reads=[vkey, 'onesb', ('ET', par)], writes=[pk1])
                        for n in range(NSEQ_S):
                            lhs_c = Vc[:, n, c * 64:(c + 1) * 64] if part == 0 else self.onesb[:, :]
                            col = part * 256 + i2 * 128 + n * 8
                            self.mm(ps2[hp:hp + 64, col:col + 8], lhs_c, ETc[par][:, (n * 2 + i2) * 8:(n * 2 + i2) * 8 + 8], True, True,
                                    reads=['Vc', 'onesb', ('ETc', par)], writes=[pk2])
                S.op('act', lambda e: e.copy(out=t1[:], in_=ps1[:, :]), reads=[pk1], writes=['sa_t1'])
                S.op('dve', lambda e: e.tensor_tensor(out=t1[:], in0=t1[:], in1=ps2[:, :], op=ALU.add), reads=['sa_t1', pk2], writes=['sa_t1'])
                for jj in range(2):
                    j = 2 * c + jj
                    S.op('dve', lambda e: e.tensor_scalar(out=den[:, jj * 128:(jj + 1) * 128], in0=t1[:, 256 + jj * 128:256 + (jj + 1) * 128],
                                                          scalar1=self.esink[:, l, j:j + 1], scalar2=None, op0=ALU.add),
                         reads=['sa_t1', 'esink'], writes=['den'])
                S.op('dve', lambda e: e.reciprocal(out=den[:], in_=den[:]), reads=['den'], writes=['den'])
                S.op('dve', lambda e: e.tensor_tensor(out=osb[:], in0=t1[:, 0:256], in1=den[:], op=ALU.mult), reads=['sa_t1', 'den'], writes=['osb'])
                S.op('pool', lambda e: e.tensor_tensor(out=self.mixacc[:, 2 * c:2 * c + 2, 0:128], in0=self.mixacc[:, 2 * c:2 * c + 2, 0:128],
                                                       in1=osb[:].rearrange("p (j q) -> p j q", j=2), op=ALU.mult),
                     reads=['osb', ('mixacc', 2 * c), ('mixacc', 2 * c + 1)], writes=[('mixacc', 2 * c), ('mixacc', 2 * c + 1)])
        self.dump('oa_g' if stream == 'P' else 's_oa_g', self.mixacc[:, :, 0:N], [128, 8, N], [('mixacc', j) for j in range(8)])
        ph.close()


_CACHE = {}


def consts():
    ident = np.eye(128, dtype=np.float32)
    slopes = 2.0 ** (-8.0 * np.arange(1, NH + 1) / NH)
    k = np.arange(128)[:, None]
    q = np.arange(128)[None, :]
    biasP = np.zeros((128, 2, NH, 128), np.float32)
    rel_prev = 128 + q - k
    rel_cur = q - k
    for h in range(NH):
        biasP[:, 0, h, :] = np.where(rel_prev < 128, -slopes[h] * rel_prev, NEG)
        biasP[:, 1, h, :] = np.where(rel_cur >= 0, -slopes[h] * rel_cur, NEG)
    kn, ks = np.arange(128)[:, None] // 8, np.arange(128)[:, None] % 8
    qn, qt = np.arange(128)[None, :] // 8, np.arange(128)[None, :] % 8
    biasSn = np.zeros((128, NH, 128), np.float32)
    for h in range(NH):
        biasSn[:, h, :] = np.where((kn == qn) & (ks <= qt), -slopes[h] * (qt - ks), NEG)
    biasSc = np.zeros((128, KVH, 2, NSEQ_S, 2, T_S), np.float32)
    j = np.arange(128)[:, None]
    t = np.arange(T_S)[None, :]
    for c in range(KVH):
        for i4 in range(4):
            h = 4 * c + i4
            biasSc[:, c, i4 % 2, :, i4 // 2, :] = np.where(j > t, -slopes[h] * (128 + t - j), NEG)[:, None, :]
    biasSc = biasSc.reshape(128, KVH, 2, 256)
    p = np.arange(128)[:, None]
    f = np.arange(128)[None, :]
    strictT = (p < f).astype(np.float32)
    inclT = (p <= f).astype(np.float32)
    strictA = (f < p).astype(np.float32)
    identb = (p == f).astype(np.float32)
    mask = np.stack([np.tile(m, (1, 4)) for m in (strictT, inclT, strictA, identb)], axis=1).astype(np.float32)
    rmask = np.ones((128, 512), np.float32)
    rmask[:, ::128] = 0.0
    blk = np.zeros((128, 128), np.float32)
    blk[:64, :64] = 1.0
    blk[64:, 64:] = 1.0
    return dict(c_ident=ident, c_biasP=biasP, c_biasSn=biasSn, c_biasSc=biasSc, c_mask=mask, c_rmask=rmask, c_blk=blk)


def get_prog(seq, **kw):
    key = (seq, tuple(sorted((k, str(v)) for k, v in kw.items())))
    if key not in _CACHE:
        kb = KB(seq, **kw)
        kb.build()
        _CACHE[key] = kb
    return _CACHE[key]


def make_in_maps(inputs, seq):
    cst = consts()
    maps = []
    f = lambda a: np.ascontiguousarray(a, dtype=np.float32)
    for c in range(NCORES):
        n0 = c * NSEQ_S
        m = {
            'xp': f(inputs['x_prompt'][c % 2, :seq]),
            'xs': f(inputs['x_sample'][n0:n0 + NSEQ_S].reshape(128, D)),
            'ck': f(inputs['cache_k'][:, n0:n0 + NSEQ_S].reshape(DEPTH, NSEQ_S, 128, 256)),
            'cv': f(inputs['cache_v'][:, n0:n0 + NSEQ_S].reshape(DEPTH, NSEQ_S, 128, 256)),
            'sconv': f(inputs['state_conv'][:, n0:n0 + NSEQ_S]),
            'srg': f(inputs['state_rglru'][:, n0:n0 + NSEQ_S]),
            'sshift': f(inputs['state_shift'][:, n0:n0 + NSEQ_S]),
            'swkv': f(inputs['state_wkv'][:, n0:n0 + NSEQ_S]),
        }
        for nm in ['w_in', 'attn_sinks', 'conv_w', 'conv_b', 'rg_wa', 'rg_ba', 'rg_wx', 'rg_bx', 'rg_lambda',
                   'rw_mu', 'rw_w0', 'rw_wup', 'rw_a0', 'rw_aup', 'rw_gup', 'rw_kk', 'rw_ka', 'rw_lnw', 'rw_lnb',
                   'w_out', 'ln1_g', 'ln1_b', 'w_gu', 'w_down', 'ln2_g', 'ln2_b']:
            m[nm] = f(inputs[nm])
        m['rw_rk'] = f(np.asarray(inputs['rw_rk']).reshape(DEPTH, D))
        m.update(cst)
        maps.append(m)
    return maps


def kernel(**inputs):
    seq = inputs['x_prompt'].shape[1]
    kb = get_prog(seq)
    maps = make_in_maps(inputs, seq)
    res = run_bass_kernel_spmd(kb.nc, maps, core_ids=list(range(NCORES)))
    R = res.results
    B = inputs['x_prompt'].shape[0]
    yp = np.stack([R[b]['yp'] for b in range(B)])
    ys = np.concatenate([R[c]['ys'].reshape(NSEQ_S, T_S, D) for c in range(NCORES)])

    def pst(nm, shp):
        return np.stack([R[b][nm] for b in range(B)], axis=1).reshape(shp)

    def sst(nm, shp):
        return np.concatenate([R[c][nm] for c in range(NCORES)], axis=1).reshape(shp)
    NS = NCORES * NSEQ_S
    outs = (yp, ys,
            pst('pk', (DEPTH, B, 128, KVH, HD)), pst('pv', (DEPTH, B, 128, KVH, HD)),
            pst('pconv', (DEPTH, B, 3, D)), pst('prg', (DEPTH, B, D)), pst('pshift', (DEPTH, B, SHW)),
            pst('pwkv', (DEPTH, B, NH, HD, HD)),
            sst('sk', (DEPTH, NS, 128, KVH, HD)), sst('sv', (DEPTH, NS, 128, KVH, HD)),
            sst('sconv_o', (DEPTH, NS, 3, D)), sst('srg_o', (DEPTH, NS, D)), sst('sshift_o', (DEPTH, NS, SHW)),
            sst('swkv_o', (DEPTH, NS, NH, HD, HD)))
    return tuple(np.ascontiguousarray(o, dtype=np.float32) for o in outs)
```
